# Optimizing a Trainium2 kernel written in Bass

```python
import math
import jax
import jax.numpy as jnp
from jax import lax
import numpy as np

D_MODEL = 1024
BATCH = 8
SEQ = 8192
DEPTH = 4

MIX_WIDTH = D_MODEL
GROUP_WIDTH = MIX_WIDTH // 4
PLE_DIM = 256
RMS_EPS = 1e-6
Q_BLOCK = 128

HY_CH = GROUP_WIDTH
HY_ORDER = 2
HY_SHORT = 3
HY_BANDS = 16
HY_FEAT = 1 + 2 * HY_BANDS
HY_FILTER_HIDDEN = 64
HY_FILTER_OUT = HY_ORDER * 2 * HY_CH

S5_CH = GROUP_WIDTH
S5_GROUP = 16
S5_NGROUPS = S5_CH // S5_GROUP
S5_STATE = 64

DA_HEADS = 4
DA_QK = 32
DA_V = 2 * DA_QK
DA_QK_WIDTH = DA_HEADS * 2 * DA_QK
DA_WIDTH = DA_HEADS * DA_V

MLA_HEADS = 4
MLA_NOPE = 64
MLA_ROPE = 32
MLA_V = 64
MLA_Q_RANK = 192
MLA_KV_RANK = 128
ROPE_THETA = 10000.0

IN_WIDTH = 3 * HY_CH + S5_CH + 2 * DA_QK_WIDTH + DA_WIDTH + MLA_Q_RANK + MLA_KV_RANK + MLA_ROPE
FFN_HIDDEN = -(-8 * D_MODEL // (3 * 256)) * 256

kernel_name = "hybrid_parallel_group_encoder"


def rmsnorm(x, g):
    xf = x.astype(jnp.float32)
    y = xf * lax.rsqrt(jnp.mean(xf * xf, axis=-1, keepdims=True) + RMS_EPS)
    return (y * g.astype(jnp.float32)).astype(x.dtype)


def split_mixer_columns(z):
    sizes = (3 * HY_CH, S5_CH, DA_QK_WIDTH, DA_QK_WIDTH, DA_WIDTH, MLA_Q_RANK, MLA_KV_RANK, MLA_ROPE)
    out, start = [], 0
    for s in sizes:
        out.append(z[..., start:start + s])
        start += s
    return out


def to_blocks(t):
    b, l = t.shape[:2]
    return t.reshape((b, l // Q_BLOCK, Q_BLOCK) + t.shape[2:]).swapaxes(0, 1)


def from_blocks(t):
    nb, b = t.shape[:2]
    return t.swapaxes(0, 1).reshape((b, nb * Q_BLOCK) + t.shape[3:])


def short_conv(u, w, b):
    l = u.shape[1]
    up = jnp.pad(u, ((0, 0), (1, 1), (0, 0)))
    return up[:, :l] * w[0] + up[:, 1:l + 1] * w[1] + up[:, 2:] * w[2] + b


def hyena_filter_spectrum(l, w1, b1, w2, b2, w3, log_decay):
    f32 = jnp.float32
    t = jnp.arange(l, dtype=f32) / l
    bands = jnp.arange(1, HY_BANDS + 1, dtype=f32)
    ang = (2.0 * math.pi) * t[:, None] * bands[None, :]
    feats = jnp.concatenate([t[:, None], jnp.cos(ang), jnp.sin(ang)], axis=-1)
    hid = jnp.sin(feats @ w1.astype(f32) + b1.astype(f32))
    hid = jnp.sin(hid @ w2.astype(f32) + b2.astype(f32))
    h = hid @ w3.astype(f32)
    h = h * jnp.exp(-jnp.exp(log_decay.astype(f32))[None, :] * t[:, None])
    h = h.reshape(l, HY_ORDER, 2, HY_CH)
    fwd, bwd = h[:, :, 0], h[:, :, 1]
    full = jnp.concatenate([fwd, jnp.zeros_like(fwd[:1]), bwd[1:][::-1]], axis=0)
    full = full / jnp.sum(jnp.abs(full), axis=0, keepdims=True)
    return jnp.fft.rfft(full, axis=0)


def fft_long_conv(u, h_freq, skip):
    l = u.shape[1]
    uf = jnp.fft.rfft(u, n=2 * l, axis=1)
    y = jnp.fft.irfft(uf * h_freq[None], n=2 * l, axis=1)[:, :l]
    return y + u * skip


def hyena_mixer(z, conv_w, conv_b, w1, b1, w2, b2, w3, log_decay, skip):
    f32 = jnp.float32
    l = z.shape[1]
    zc = short_conv(z.astype(f32), conv_w.astype(f32), conv_b.astype(f32))
    v, x1, x2 = jnp.split(zc, 3, axis=-1)
    hf = hyena_filter_spectrum(l, w1, b1, w2, b2, w3, log_decay)
    s = skip.astype(f32)
    y = x1 * fft_long_conv(v, hf[:, 0], s[0])
    y = x2 * fft_long_conv(y, hf[:, 1], s[1])
    return y


def _ssm_combine(e1, e2):
    a1r, a1i, b1r, b1i = e1
    a2r, a2i, b2r, b2i = e2
    return (a2r * a1r - a2i * a1i,
            a2r * a1i + a2i * a1r,
            a2r * b1r - a2i * b1i + b2r,
            a2r * b1i + a2i * b1r + b2i)


def s5_mixer(u, a_re, a_im, log_dt, b_re, b_im, c_re, c_im, d, w_glu, b_glu):
    f32 = jnp.float32
    bsz, l, _ = u.shape
    uf = u.astype(f32)
    ug = uf.reshape(bsz, l, S5_NGROUPS, S5_GROUP)
    y = d.astype(f32) * uf
    for direction in range(2):
        ar = a_re[direction].astype(f32)
        ai = a_im[direction].astype(f32)
        dt = jnp.exp(log_dt[direction].astype(f32))[:, None]
        mag = jnp.exp(ar * dt)
        abr, abi = mag * jnp.cos(ai * dt), mag * jnp.sin(ai * dt)
        nr, ni = abr - 1.0, abi
        den = ar * ar + ai * ai
        fr, fi = (nr * ar + ni * ai) / den, (ni * ar - nr * ai) / den
        br, bi = b_re[direction].astype(f32), b_im[direction].astype(f32)
        bbr = fr[..., None] * br - fi[..., None] * bi
        bbi = fr[..., None] * bi + fi[..., None] * br
        bu_r = jnp.einsum('blgh,gph->blgp', ug, bbr)
        bu_i = jnp.einsum('blgh,gph->blgp', ug, bbi)
        a_r = jnp.broadcast_to(abr[None, None], (1, l) + abr.shape)
        a_i = jnp.broadcast_to(abi[None, None], (1, l) + abi.shape)
        _, _, xr, xi = lax.associative_scan(_ssm_combine, (a_r, a_i, bu_r, bu_i),
                                            reverse=(direction == 1), axis=1)
        yc = (jnp.einsum('blgp,ghp->blgh', xr, c_re[direction].astype(f32))
              - jnp.einsum('blgp,ghp->blgh', xi, c_im[direction].astype(f32)))
        y = y + yc.reshape(bsz, l, S5_CH)
    y = jax.nn.gelu(y)
    return y * jax.nn.sigmoid(y @ w_glu.astype(f32) + b_glu.astype(f32))


def alibi_slopes(n):
    return 2.0 ** (-8.0 * jnp.arange(1, n + 1, dtype=jnp.float32) / n)


def diff_attention(q, k, v, lam, lam_init, g_head):
    f32 = jnp.float32
    bsz, l = q.shape[:2]
    scale = DA_QK ** -0.5
    slopes = alibi_slopes(DA_HEADS)
    kpos = jnp.arange(l, dtype=f32)
    qpos = kpos.reshape(l // Q_BLOCK, Q_BLOCK)
    k1, k2 = k[..., 0, :], k[..., 1, :]

    def block(args):
        qb, qp = args
        bias = -slopes[:, None, None] * jnp.abs(qp[:, None] - kpos[None, :])[None]
        s1 = jnp.einsum('bqhd,bkhd->bhqk', qb[..., 0, :], k1, preferred_element_type=f32) * scale + bias
        s2 = jnp.einsum('bqhd,bkhd->bhqk', qb[..., 1, :], k2, preferred_element_type=f32) * scale + bias
        attn = jax.nn.softmax(s1, axis=-1) - lam * jax.nn.softmax(s2, axis=-1)
        return jnp.einsum('bhqk,bkhd->bqhd', attn.astype(v.dtype), v)

    o = from_blocks(lax.map(block, (to_blocks(q), qpos)))
    o = rmsnorm(o, g_head) * (1.0 - lam_init)
    return o.reshape(bsz, l, DA_WIDTH)


def rope_tables(l):
    inv = ROPE_THETA ** (-jnp.arange(0, MLA_ROPE, 2, dtype=jnp.float32) / MLA_ROPE)
    ang = jnp.arange(l, dtype=jnp.float32)[:, None] * inv[None, :]
    return jnp.cos(ang), jnp.sin(ang)


def apply_rope(x, cos, sin):
    cos, sin = cos.astype(x.dtype), sin.astype(x.dtype)
    x1, x2 = jnp.split(x, 2, axis=-1)
    return jnp.concatenate([x1 * cos - x2 * sin, x2 * cos + x1 * sin], axis=-1)


def mla_attention(c_q, c_kv, k_rope, g_q, g_kv, w_uq, w_ukv, cos, sin):
    f32 = jnp.float32
    bsz, l = c_q.shape[:2]
    q = (rmsnorm(c_q, g_q) @ w_uq).reshape(bsz, l, MLA_HEADS, MLA_NOPE + MLA_ROPE)
    q_nope = q[..., :MLA_NOPE]
    q_rope = apply_rope(q[..., MLA_NOPE:], cos[None, :, None], sin[None, :, None])
    kv = (rmsnorm(c_kv, g_kv) @ w_ukv).reshape(bsz, l, MLA_HEADS, MLA_NOPE + MLA_V)
    k_nope, v = kv[..., :MLA_NOPE], kv[..., MLA_NOPE:]
    k_r = apply_rope(k_rope, cos[None], sin[None])
    scale = (MLA_NOPE + MLA_ROPE) ** -0.5

    def block(args):
        qn, qr = args
        s = (jnp.einsum('bqhd,bkhd->bhqk', qn, k_nope, preferred_element_type=f32)
             + jnp.einsum('bqhr,bkr->bhqk', qr, k_r, preferred_element_type=f32)) * scale
        a = jax.nn.softmax(s, axis=-1)
        return jnp.einsum('bhqk,bkhd->bqhd', a.astype(v.dtype), v)

    o = from_blocks(lax.map(block, (to_blocks(q_nope), to_blocks(q_rope))))
    return o.reshape(bsz, l, MLA_HEADS * MLA_V)


def setup_inputs(seed: int = 0) -> dict:
    key = jax.random.key(seed)
    k = jax.random.split(key, 40)
    f32 = jnp.float32

    def nrm(i, shape, scale):
        return jax.random.normal(k[i], shape, f32) * scale

    def gain(i, shape):
        return 1.0 + nrm(i, shape, 0.02)

    dp, g, p_, hs, fh = DEPTH, S5_NGROUPS, S5_STATE, S5_GROUP, HY_FILTER_HIDDEN
    state_n = jnp.arange(p_, dtype=f32)
    decay_rates = jnp.tile(jnp.linspace(3.0, 15.0, HY_CH, dtype=f32), HY_ORDER * 2)
    return {
        "x": nrm(0, (BATCH, SEQ, D_MODEL), 1.0),
        "p": nrm(1, (dp, BATCH, SEQ, PLE_DIM), 1.0),
        "g_mix": gain(2, (dp, D_MODEL)),
        "w_in": nrm(3, (dp, D_MODEL, IN_WIDTH), D_MODEL ** -0.5),
        "w_out": nrm(4, (dp, MIX_WIDTH, D_MODEL), MIX_WIDTH ** -0.5),
        "g_branch": gain(5, (dp, 3, GROUP_WIDTH)),
        "hy_conv_w": nrm(6, (dp, HY_SHORT, 3 * HY_CH), 0.5),
        "hy_conv_b": nrm(7, (dp, 3 * HY_CH), 0.02),
        "hy_f_w1": nrm(8, (dp, HY_FEAT, fh), HY_FEAT ** -0.5),
        "hy_f_b1": nrm(9, (dp, fh), 0.5),
        "hy_f_w2": nrm(10, (dp, fh, fh), fh ** -0.5),
        "hy_f_b2": nrm(11, (dp, fh), 0.5),
        "hy_f_w3": nrm(12, (dp, fh, HY_FILTER_OUT), fh ** -0.5),
        "hy_log_decay": jnp.log(decay_rates)[None] + nrm(13, (dp, HY_FILTER_OUT), 0.05),
        "hy_skip": nrm(14, (dp, HY_ORDER, HY_CH), 0.5),
        "s5_a_re": -0.5 * jnp.exp(nrm(15, (dp, 2, g, p_), 0.02)),
        "s5_a_im": math.pi * state_n + nrm(16, (dp, 2, g, p_), 0.02),
        "s5_log_dt": jax.random.uniform(k[17], (dp, 2, g), f32, math.log(1e-3), math.log(1e-1)),
        "s5_b_re": nrm(18, (dp, 2, g, p_, hs), (2 * hs) ** -0.5),
        "s5_b_im": nrm(19, (dp, 2, g, p_, hs), (2 * hs) ** -0.5),
        "s5_c_re": nrm(20, (dp, 2, g, hs, p_), (2 * p_) ** -0.5),
        "s5_c_im": nrm(21, (dp, 2, g, hs, p_), (2 * p_) ** -0.5),
        "s5_d": nrm(22, (dp, S5_CH), 1.0),
        "s5_w_glu": nrm(23, (dp, S5_CH, S5_CH), S5_CH ** -0.5),
        "s5_b_glu": nrm(24, (dp, S5_CH), 0.02),
        "da_lambda": nrm(25, (dp, 4, DA_QK), 0.1),
        "da_g_head": gain(26, (dp, DA_V)),
        "mla_g_q": gain(27, (dp, MLA_Q_RANK)),
        "mla_g_kv": gain(28, (dp, MLA_KV_RANK)),
        "mla_w_uq": nrm(29, (dp, MLA_Q_RANK, MLA_HEADS * (MLA_NOPE + MLA_ROPE)), MLA_Q_RANK ** -0.5),
        "mla_w_ukv": nrm(30, (dp, MLA_KV_RANK, MLA_HEADS * (MLA_NOPE + MLA_V)), MLA_KV_RANK ** -0.5),
        "g_ffn": gain(31, (dp, D_MODEL)),
        "w_gate": nrm(32, (dp, D_MODEL, FFN_HIDDEN), D_MODEL ** -0.5),
        "w_up": nrm(33, (dp, D_MODEL, FFN_HIDDEN), D_MODEL ** -0.5),
        "w_down": nrm(34, (dp, FFN_HIDDEN, D_MODEL), FFN_HIDDEN ** -0.5),
        "g_ple": gain(35, (dp, D_MODEL)),
        "w_ple_gate": nrm(36, (dp, D_MODEL, D_MODEL), D_MODEL ** -0.5),
        "w_ple": nrm(37, (dp, PLE_DIM, D_MODEL), PLE_DIM ** -0.5),
        "g_final": gain(38, (D_MODEL,)),
    }


def reference(x, p, g_mix, w_in, w_out, g_branch, hy_conv_w, hy_conv_b, hy_f_w1, hy_f_b1,
              hy_f_w2, hy_f_b2, hy_f_w3, hy_log_decay, hy_skip, s5_a_re, s5_a_im, s5_log_dt,
              s5_b_re, s5_b_im, s5_c_re, s5_c_im, s5_d, s5_w_glu, s5_b_glu, da_lambda, da_g_head,
              mla_g_q, mla_g_kv, mla_w_uq, mla_w_ukv, g_ffn, w_gate, w_up, w_down, g_ple,
              w_ple_gate, w_ple, g_final):
    f32 = jnp.float32
    bsz, l, _ = x.shape
    cos, sin = rope_tables(l)
    h = x
    for i in range(DEPTH):
        a = rmsnorm(h, g_mix[i])
        z = a @ w_in[i]
        z_hy, z_s5, z_q, z_k, z_v, c_q, c_kv, k_r = split_mixer_columns(z)

        y_hy = hyena_mixer(z_hy, hy_conv_w[i], hy_conv_b[i], hy_f_w1[i], hy_f_b1[i], hy_f_w2[i],
                           hy_f_b2[i], hy_f_w3[i], hy_log_decay[i], hy_skip[i]).astype(h.dtype)
        y_s5 = s5_mixer(z_s5, s5_a_re[i], s5_a_im[i], s5_log_dt[i], s5_b_re[i], s5_b_im[i],
                        s5_c_re[i], s5_c_im[i], s5_d[i], s5_w_glu[i], s5_b_glu[i]).astype(h.dtype)

        lam_init = 0.8 - 0.6 * math.exp(-0.3 * i)
        lam_vec = da_lambda[i].astype(f32)
        lam = (jnp.exp(jnp.sum(lam_vec[0] * lam_vec[1]))
               - jnp.exp(jnp.sum(lam_vec[2] * lam_vec[3])) + lam_init)
        y_da = diff_attention(z_q.reshape(bsz, l, DA_HEADS, 2, DA_QK),
                              z_k.reshape(bsz, l, DA_HEADS, 2, DA_QK),
                              z_v.reshape(bsz, l, DA_HEADS, DA_V),
                              lam, lam_init, da_g_head[i])
        y_mla = mla_attention(c_q, c_kv, k_r, mla_g_q[i], mla_g_kv[i], mla_w_uq[i], mla_w_ukv[i], cos, sin)

        mix = jnp.concatenate([rmsnorm(y_hy, g_branch[i, 0]),
                               rmsnorm(y_s5, g_branch[i, 1]),
                               y_da,
                               rmsnorm(y_mla, g_branch[i, 2])], axis=-1)
        h = h + mix @ w_out[i]

        f = rmsnorm(h, g_ffn[i])
        h = h + (jax.nn.silu(f @ w_gate[i]) * (f @ w_up[i])) @ w_down[i]

        gate = jax.nn.sigmoid(rmsnorm(h, g_ple[i]) @ w_ple_gate[i])
        h = h + gate * (p[i] @ w_ple[i])
    return rmsnorm(h, g_final)
```

```python
import math
from contextlib import ExitStack
import numpy as np
import ml_dtypes
import concourse.bass as bass
import concourse.mybir as mybir
from concourse.bass_utils import run_bass_kernel_spmd

F32 = mybir.dt.float32
BF16 = mybir.dt.bfloat16
AF = mybir.ActivationFunctionType
ALU = mybir.AluOpType
AX = mybir.AxisListType


class Buf:
    __slots__ = ("lw", "rd", "name")

    def __init__(self, name=""):
        self.lw = None
        self.rd = {}
        self.name = name


class Stream:
    def __init__(self, eng):
        self.eng = eng
        self.seen = {}
        self.ev = []


class CQ:
    def __init__(self, stream, sem):
        self.stream = stream
        self.sem = sem
        self.count = 0
        self.is_dma = False


class DQ:
    def __init__(self, stream, sems):
        self.stream = stream
        self.sems = sems
        self.cnt = [0] * len(sems)
        self.idx = 0
        self.is_dma = True
        self.outst = []


class FW:
    def __init__(self, nc, es):
        self.nc = nc
        self.es = es
        S = lambda n: es.enter_context(nc.semaphore(n))
        self.s_pe = Stream(nc.tensor)
        self.s_act = Stream(nc.scalar)
        self.s_dve = Stream(nc.vector)
        self.s_pool = Stream(nc.gpsimd)
        self.s_sp = Stream(nc.sync)
        self.streams = [self.s_pe, self.s_act, self.s_dve, self.s_pool, self.s_sp]
        self.PE = CQ(self.s_pe, S("q_pe"))
        self.ACT = CQ(self.s_act, S("q_act"))
        self.DVE = CQ(self.s_dve, S("q_dve"))
        self.POOL = CQ(self.s_pool, S("q_pool"))
        self.SP = DQ(self.s_sp, [S(f"q_sp{i}") for i in range(12)])
        self.GD = DQ(self.s_pool, [S(f"q_gd{i}") for i in range(12)])
        self.AD = DQ(self.s_act, [S(f"q_ad{i}") for i in range(4)])
        self.cqs = [self.PE, self.ACT, self.DVE, self.POOL]
        self.dqs = [self.SP, self.GD, self.AD]
        self.n_inst = 0

    def _wait(self, stream, deps, own=None):
        need = {}
        for d in deps:
            if d is None:
                continue
            (sem, val, q), raw = d
            if own is not None and q is own and not own.is_dma:
                if not raw or own is self.PE:
                    continue
            k = id(sem)
            if stream.seen.get(k, 0) >= val:
                continue
            if k not in need or need[k][1] < val:
                need[k] = (sem, val)
        for k, (sem, val) in need.items():
            stream.eng.wait_ge(sem, val)
            stream.seen[k] = val
            self.n_inst += 1
            stream.ev.append(("w", k, val))

    @staticmethod
    def _deps(R, W):
        deps = []
        for b in R:
            if b.lw is not None:
                deps.append((b.lw, True))
        for b in W:
            if b.lw is not None:
                deps.append((b.lw, False))
            deps.extend((t, False) for t in b.rd.values())
        return deps

    @staticmethod
    def _mark(tok, R, W):
        k = id(tok[0])
        for b in R:
            o = b.rd.get(k)
            if o is None or o[1] < tok[1]:
                b.rd[k] = tok
        for b in W:
            b.lw = tok
            b.rd = {}

    def op(self, q, fn, R=(), W=(), inc=True):
        self._wait(q.stream, self._deps(R, W), own=q)
        inst = fn()
        self.n_inst += 1
        if inc:
            inst.then_inc(q.sem, 1)
            q.count += 1
            tok = (q.sem, q.count, q)
            q.stream.ev.append(("i", id(q.sem), 1))
        else:
            tok = (q.sem, q.count + 1, q)
        self._mark(tok, R, W)
        return inst

    def dma(self, q, out, in_, R=(), W=(), **kw):
        st = q.stream
        self._wait(st, self._deps(R, W), own=None)
        nd = 1
        for d_ in tuple(out.shape)[:-1]:
            nd *= int(d_)
        LIM = 1536
        q.outst = [o for o in q.outst if st.seen.get(id(o[0]), 0) < o[1]]
        while q.outst and sum(o[2] for o in q.outst) + nd > LIM:
            osem, oval, _ = q.outst.pop(0)
            if st.seen.get(id(osem), 0) < oval:
                st.eng.wait_ge(osem, oval)
                st.seen[id(osem)] = oval
                st.ev.append(("w", id(osem), oval))
        slot = q.idx % len(q.sems)
        q.idx += 1
        sem = q.sems[slot]
        if q.cnt[slot] > 0 and st.seen.get(id(sem), 0) < q.cnt[slot]:
            st.eng.wait_ge(sem, q.cnt[slot])
            st.seen[id(sem)] = q.cnt[slot]
            st.ev.append(("w", id(sem), q.cnt[slot]))
        inst = st.eng.dma_start(out=out, in_=in_, **kw)
        inst.then_inc(sem, 16)
        st.ev.append(("i", id(sem), 16))
        self.n_inst += 1
        q.cnt[slot] += 16
        tok = (sem, q.cnt[slot], q)
        q.outst.append((sem, q.cnt[slot], nd))
        self._mark(tok, R, W)
        return inst

    def barrier(self, streams=None):
        deps = []
        for q in self.cqs:
            if q.count > 0:
                deps.append(((q.sem, q.count, q), True))
        for q in self.dqs:
            for s, c in zip(q.sems, q.cnt):
                if c > 0:
                    deps.append(((s, c, q), True))
        for st in (streams or self.streams):
            self._wait(st, deps, own=None)

    def check_deadlock(self):
        vals = {}
        pos = [0] * len(self.streams)
        progress = True
        while progress:
            progress = False
            for si, st in enumerate(self.streams):
                while pos[si] < len(st.ev):
                    kind, k, v = st.ev[pos[si]]
                    if kind == "w":
                        if vals.get(k, 0) >= v:
                            pos[si] += 1
                            progress = True
                        else:
                            break
                    else:
                        vals[k] = vals.get(k, 0) + v
                        pos[si] += 1
                        progress = True
        stuck = [(si, pos[si], len(st.ev)) for si, st in enumerate(self.streams) if pos[si] < len(st.ev)]
        if stuck:
            for si, p, n in stuck:
                kind, k, v = self.streams[si].ev[p]
                print("DEADLOCK stream", si, "at", p, "/", n, "waiting sem", k, "val", v, "cur", vals.get(k, 0))
            raise RuntimeError("deadlock detected in emitted program")
        return True

    def final_wait(self):
        self.barrier(streams=[self.s_sp])


L = 8192
D = 1024
NCH = 16
CH = 512
IN_W = 2144
FFN = 2816
EPS = 1e-6


class Tl:
    __slots__ = ("t", "b", "cb")

    def __init__(self, t, name=""):
        self.t = t
        self.b = Buf(name)


class Ctx:
    pass


def new_phase(cx):
    cx.fw.barrier()
    if cx.phase_es is not None:
        cx.phase_es.close()
    cx.phase_es = ExitStack()
    cx.es.callback(lambda e=cx.phase_es: e.close())
    cx.tcount += 1000


def SB(cx, shape, dt, name=None):
    cx.tcount += 1
    nm = f"{name or 't'}_{cx.tcount}"
    return Tl(cx.phase_es.enter_context(cx.nc.sbuf_tensor(nm, list(shape), dt)), nm)


def dram(cx, name, shape, dt):
    t = cx.nc.dram_tensor(name, list(shape), dt, kind="Internal").ap()
    tl = Tl(t, name)
    tl.cb = [Buf(name + str(i)) for i in range(NCH)]
    return tl


def rstd_chunk(cx, parts, dim, n=CH):
    fw, nc = cx.fw, cx.nc
    ps = cx.ps[0]
    np_ = len(parts)
    for i, (ap, K, bufs) in enumerate(parts):
        sq = cx.sq[i % 2]
        fw.op(fw.ACT, lambda: nc.scalar.activation(out=sq.t[0:K, 0:n], in_=ap, func=AF.Square), R=bufs, W=[sq.b])
        fw.op(fw.PE, lambda: nc.tensor.matmul(ps.t[:, 0:n], lhsT=cx.ones_f.t[0:K, :], rhs=sq.t[0:K, 0:n], start=(i == 0), stop=(i == np_ - 1)),
              R=[sq.b, cx.ones_f.b], W=[ps.b], inc=True)
    r = cx.rstd[cx.rstd_i % 2]
    cx.rstd_i += 1
    fw.op(fw.ACT, lambda: nc.scalar.activation(out=r.t[:, 0:n], in_=ps.t[:, 0:n], func=AF.Sqrt, bias=cx.eps_t.t[:, 0:1], scale=1.0 / dim), R=[ps.b, cx.eps_t.b], W=[r.b])
    fw.op(fw.DVE, lambda: nc.vector.reciprocal(out=r.t[:, 0:n], in_=r.t[:, 0:n]), R=[r.b], W=[r.b])
    return r


def evac(cx, k, out_ap, in_ap, R, W, scale=None):
    fw, nc = cx.fw, cx.nc
    if k % 2 == 0:
        if scale is None:
            fw.op(fw.ACT, lambda: nc.scalar.copy(out=out_ap, in_=in_ap), R=R, W=W)
        else:
            fw.op(fw.ACT, lambda: nc.scalar.mul(out=out_ap, in_=in_ap, mul=scale), R=R, W=W)
    else:
        if scale is None:
            fw.op(fw.DVE, lambda: nc.vector.tensor_copy(out=out_ap, in_=in_ap), R=R, W=W)
        else:
            fw.op(fw.DVE, lambda: nc.vector.tensor_scalar(out=out_ap, in0=in_ap, scalar1=scale, scalar2=None, op0=ALU.mult), R=R, W=W)


def load_w_bf16(cx, dst, src_ap, kt, ncols, gain=None, stg_cols=2144):
    fw, nc = cx.fw, cx.nc
    for j in range(kt):
        stg = cx.wstg[j % 2]
        fw.dma(fw.SP, out=stg.t[:, 0:ncols], in_=src_ap[j * 128:(j + 1) * 128, :], W=[stg.b])
        if gain is None:
            evac(cx, j, dst.t[:, j, :], stg.t[:, 0:ncols], R=[stg.b], W=[dst.b])
        else:
            if j % 2 == 0:
                fw.op(fw.ACT, lambda: nc.scalar.mul(out=dst.t[:, j, :], in_=stg.t[:, 0:ncols], mul=gain.t[:, j:j + 1]), R=[stg.b, gain.b], W=[dst.b])
            else:
                fw.op(fw.DVE, lambda: nc.vector.tensor_scalar(out=dst.t[:, j, :], in0=stg.t[:, 0:ncols], scalar1=gain.t[:, j:j + 1], scalar2=None, op0=ALU.mult),
                      R=[stg.b, gain.b], W=[dst.b])


def phase_x0(cx, x_ap):
    fw, nc = cx.fw, cx.nc
    new_phase(cx)
    xts = [SB(cx, [128, 4, D], F32, "xt") for _ in range(2)]
    hts = [SB(cx, [128, 8, CH], F32, "ht") for _ in range(2)]
    k = 0
    for c in range(NCH):
        xt, ht = xts[c % 2], hts[c % 2]
        fw.dma(fw.SP, out=xt.t[:], in_=x_ap[c * CH:(c + 1) * CH, :].rearrange("(s p) d -> p s d", p=128), W=[xt.b])
        for j in range(8):
            ps = cx.ps[2 + (k % 6)]
            for s in range(4):
                fw.op(fw.PE, lambda: nc.tensor.transpose(out=ps.t[:, s * 128:(s + 1) * 128], in_=xt.t[:, s, j * 128:(j + 1) * 128], identity=cx.ident_f.t[:]),
                      R=[xt.b, cx.ident_f.b], W=[ps.b], inc=(s == 3))
            evac(cx, k, ht.t[:, j, :], ps.t[:], R=[ps.b], W=[ht.b])
            k += 1
        fw.dma(fw.GD, out=cx.HT.t[:, c * CH:(c + 1) * CH].rearrange("(j p) t -> p j t", p=128), in_=ht.t[:], R=[ht.b], W=[cx.HT.cb[c]])


def phase_out(cx, gfin_ap, out_ap):
    fw, nc = cx.fw, cx.nc
    new_phase(cx)
    gf = SB(cx, [128, 8], F32, "gf")
    fw.dma(fw.SP, out=gf.t[:], in_=gfin_ap, W=[gf.b])
    hts = [SB(cx, [128, 8, CH], F32, "ht") for _ in range(2)]
    ots = [SB(cx, [128, 4, D], F32, "ot") for _ in range(2)]
    k = 0
    for c in range(NCH):
        ht, ot = hts[c % 2], ots[c % 2]
        fw.dma(fw.SP, out=ht.t[:], in_=cx.HT.t[:, c * CH:(c + 1) * CH].rearrange("(j p) t -> p j t", p=128), R=[cx.HT.cb[c]], W=[ht.b])
        r = rstd_chunk(cx, [(ht.t[:, j, :], 128, [ht.b]) for j in range(8)], D)
        for j in range(8):
            fw.op(fw.DVE, lambda: nc.vector.scalar_tensor_tensor(out=ht.t[:, j, :], in0=ht.t[:, j, :], scalar=gf.t[:, j:j + 1], in1=r.t[:],
                                                                 op0=ALU.mult, op1=ALU.mult), R=[ht.b, r.b, gf.b], W=[ht.b])
        for s in range(4):
            for jj in range(2):
                ps = cx.ps[2 + (k % 6)]
                for j4 in range(4):
                    j = jj * 4 + j4
                    fw.op(fw.PE, lambda: nc.tensor.transpose(out=ps.t[:, j4 * 128:(j4 + 1) * 128], in_=ht.t[:, j, s * 128:(s + 1) * 128], identity=cx.ident_f.t[:]),
                          R=[ht.b, cx.ident_f.b], W=[ps.b], inc=(j4 == 3))
                evac(cx, k, ot.t[:, s, jj * 512:(jj + 1) * 512], ps.t[:], R=[ps.b], W=[ot.b])
                k += 1
        fw.dma(fw.GD, out=out_ap[c * CH:(c + 1) * CH, :].rearrange("(s p) d -> p s d", p=128), in_=ot.t[:], R=[ot.b], W=[cx.OUTB[c]])


def phase_a(cx, W, li):
    fw, nc = cx.fw, cx.nc
    new_phase(cx)
    sc = cx.sc
    cx.wstg = [SB(cx, [128, IN_W], F32, "wstg") for _ in range(2)]
    gmix = SB(cx, [128, 8], F32, "gmix")
    fw.dma(fw.SP, out=gmix.t[:], in_=W["g_mix"], W=[gmix.b])
    win = SB(cx, [128, 8, IN_W], BF16, "win")
    load_w_bf16(cx, win, W["w_in"], 8, IN_W, gain=gmix)
    gq = SB(cx, [128, 2], F32, "gq")
    fw.dma(fw.SP, out=gq.t[:], in_=W["mla_g_q"], W=[gq.b])
    gkv = SB(cx, [128, 1], F32, "gkv")
    fw.dma(fw.SP, out=gkv.t[:], in_=W["mla_g_kv"], W=[gkv.b])
    wuq = SB(cx, [128, 2, 384], BF16, "wuq")
    wuq_src = W["mla_w_uq"]
    for kt, (r0, rn) in enumerate([(0, 128), (128, 64)]):
        stg = cx.wstg[kt % 2]
        src = wuq_src[r0:r0 + rn, :].rearrange("r (h c) -> r h c", c=96)
        fw.dma(fw.SP, out=stg.t[0:rn, 0:256].rearrange("r (h c) -> r h c", c=64), in_=src[:, :, 0:64], W=[stg.b])
        fw.dma(fw.SP, out=stg.t[0:rn, 256:320].rearrange("r (h c) -> r h c", c=16), in_=src[:, :, 64:80], W=[stg.b])
        fw.dma(fw.SP, out=stg.t[0:rn, 320:384].rearrange("r (h c) -> r h c", c=16), in_=src[:, :, 80:96], W=[stg.b])
        fw.op(fw.DVE, lambda: nc.vector.tensor_scalar(out=wuq.t[0:rn, kt, :], in0=stg.t[0:rn, 0:384], scalar1=gq.t[0:rn, kt:kt + 1], scalar2=None, op0=ALU.mult),
              R=[stg.b, gq.b], W=[wuq.b])
    wukv = SB(cx, [128, 512], BF16, "wukv")
    stg = cx.wstg[0]
    src = W["mla_w_ukv"].rearrange("r (h c) -> r h c", c=128)
    fw.dma(fw.SP, out=stg.t[:, 0:256].rearrange("r (h c) -> r h c", c=64), in_=src[:, :, 0:64], W=[stg.b])
    fw.dma(fw.SP, out=stg.t[:, 256:512].rearrange("r (h c) -> r h c", c=64), in_=src[:, :, 64:128], W=[stg.b])
    fw.op(fw.DVE, lambda: nc.vector.tensor_scalar(out=wukv.t[:], in0=stg.t[:, 0:512], scalar1=gkv.t[:, 0:1], scalar2=None, op0=ALU.mult),
          R=[stg.b, gkv.b], W=[wukv.b])

    hts = [SB(cx, [128, 8, CH], F32, "ht") for _ in range(2)]
    aT = SB(cx, [128, 8, CH], BF16, "aT")
    zhy = [SB(cx, [128, 6, CH], F32, "zhy") for _ in range(2)]
    zs5 = [SB(cx, [128, 2, CH], F32, "zs5") for _ in range(2)]
    qda = [SB(cx, [128, 2, CH], BF16, "qda") for _ in range(2)]
    kda = [SB(cx, [128, 2, CH], BF16, "kda") for _ in range(2)]
    vda = [SB(cx, [128, 4, 256], BF16, "vda") for _ in range(2)]
    cq = SB(cx, [128, 2, CH], F32, "cq")
    ckv = SB(cx, [128, CH], F32, "ckv")
    cqn = SB(cx, [128, 2, CH], BF16, "cqn")
    ckvn = SB(cx, [128, CH], BF16, "ckvn")
    qn_st = [SB(cx, [128, 2, CH], BF16, "qnst") for _ in range(2)]
    kn_st = [SB(cx, [128, 2, CH], BF16, "knst") for _ in range(2)]
    vm_st = [SB(cx, [128, 4, 256], BF16, "vmst") for _ in range(2)]
    csq = [SB(cx, [64, 2, CH], F32, "csq") for _ in range(2)]
    csk = [SB(cx, [16, 2, CH], F32, "csk") for _ in range(2)]
    t1 = SB(cx, [64, CH], F32, "t1")
    t2 = SB(cx, [64, CH], F32, "t2")
    qr_st = [SB(cx, [64, 2, CH], BF16, "qrst") for _ in range(2)]
    kr_st = [SB(cx, [16, 2, CH], BF16, "krst") for _ in range(2)]

    kk = [0]

    def nextps():
        p = cx.ps[2 + (kk[0] % 6)]
        kk[0] += 1
        return p

    def mm_group(ps_ap, psb, lhs_list, rhs_list, Rb):
        n = len(lhs_list)
        for i in range(n):
            fw.op(fw.PE, lambda: nc.tensor.matmul(ps_ap, lhsT=lhs_list[i], rhs=rhs_list[i], start=(i == 0), stop=(i == n - 1)),
                  R=Rb, W=[psb], inc=(i == n - 1))

    for c in range(NCH):
        tsl = slice(c * CH, (c + 1) * CH)
        ht = hts[c % 2]
        fw.dma(fw.SP, out=ht.t[:], in_=cx.HT.t[:, tsl].rearrange("(j p) t -> p j t", p=128), R=[cx.HT.cb[c]], W=[ht.b])
        fw.dma(fw.SP, out=csq[c % 2].t[:], in_=cx.C["cs_q"][:, :, tsl].rearrange("a p t -> p a t"), W=[csq[c % 2].b])
        fw.dma(fw.SP, out=csk[c % 2].t[:], in_=cx.C["cs_k"][:, :, tsl].rearrange("a p t -> p a t"), W=[csk[c % 2].b])
        r = rstd_chunk(cx, [(ht.t[:, j, :], 128, [ht.b]) for j in range(8)], D)
        fw.op(fw.DVE, lambda: nc.vector.tensor_tensor(out=aT.t[:], in0=ht.t[:], in1=r.t[:].unsqueeze(1).broadcast_to([128, 8, CH]), op=ALU.mult),
              R=[ht.b, r.b], W=[aT.b])
        rhs8 = [aT.t[:, j, :] for j in range(8)]

        def colmm(c0, m):
            ps = nextps()
            mm_group(ps.t[0:m, :], ps.b, [win.t[:, j, c0:c0 + m] for j in range(8)], rhs8, [win.b, aT.b])
            return ps
        for i in range(6):
            ps = colmm(i * 128, 128)
            evac(cx, kk[0], zhy[c % 2].t[:, i, :], ps.t[:], R=[ps.b], W=[zhy[c % 2].b])
        fw.dma(fw.GD, out=sc["ZHY"].t[:, tsl].rearrange("(j p) t -> p j t", p=128), in_=zhy[c % 2].t[:], R=[zhy[c % 2].b], W=[sc["ZHY"].cb[c]])
        for i in range(2):
            ps = colmm(768 + i * 128, 128)
            evac(cx, kk[0], zs5[c % 2].t[:, i, :], ps.t[:], R=[ps.b], W=[zs5[c % 2].b])
        fw.dma(fw.GD, out=sc["ZS5"].t[:, tsl].rearrange("(j p) t -> p j t", p=128), in_=zs5[c % 2].t[:], R=[zs5[c % 2].b], W=[sc["ZS5"].cb[c]])
        for i in range(2):
            ps = colmm(1024 + i * 128, 128)
            evac(cx, kk[0], qda[c % 2].t[:, i, :], ps.t[:], R=[ps.b], W=[qda[c % 2].b], scale=32 ** -0.5)
        fw.dma(fw.GD, out=sc["QDA"].t[:, tsl].rearrange("(j p) t -> p j t", p=128), in_=qda[c % 2].t[:], R=[qda[c % 2].b], W=[sc["QDA"].cb[c]])
        for i in range(2):
            ps = colmm(1280 + i * 128, 128)
            evac(cx, kk[0], kda[c % 2].t[:, i, :], ps.t[:], R=[ps.b], W=[kda[c % 2].b])
        fw.dma(fw.GD, out=sc["KDA"].t[:, tsl].rearrange("(j p) t -> p j t", p=128), in_=kda[c % 2].t[:], R=[kda[c % 2].b], W=[sc["KDA"].cb[c]])
        for s in range(4):
            ps = nextps()
            mm_group(ps.t[:, 0:256], ps.b, [aT.t[:, j, s * 128:(s + 1) * 128] for j in range(8)], [win.t[:, j, 1536:1792] for j in range(8)], [win.b, aT.b])
            evac(cx, kk[0], vda[c % 2].t[:, s, :], ps.t[:, 0:256], R=[ps.b], W=[vda[c % 2].b])
        fw.dma(fw.GD, out=sc["VDA"].t[tsl, :].rearrange("(s p) d -> p s d", p=128), in_=vda[c % 2].t[:], R=[vda[c % 2].b], W=[sc["VDA"].cb[c]])
        ps = colmm(1792, 128)
        evac(cx, kk[0], cq.t[:, 0, :], ps.t[:], R=[ps.b], W=[cq.b])
        ps = colmm(1920, 64)
        evac(cx, kk[0], cq.t[0:64, 1, :], ps.t[0:64, :], R=[ps.b], W=[cq.b])
        ps = colmm(1984, 128)
        evac(cx, kk[0], ckv.t[:], ps.t[:], R=[ps.b], W=[ckv.b])
        rq = rstd_chunk(cx, [(cq.t[:, 0, :], 128, [cq.b]), (cq.t[0:64, 1, :], 64, [cq.b])], 192)
        fw.op(fw.DVE, lambda: nc.vector.tensor_tensor(out=cqn.t[:, 0, :], in0=cq.t[:, 0, :], in1=rq.t[:], op=ALU.mult), R=[cq.b, rq.b], W=[cqn.b])
        fw.op(fw.DVE, lambda: nc.vector.tensor_tensor(out=cqn.t[0:64, 1, :], in0=cq.t[0:64, 1, :], in1=rq.t[0:64, :], op=ALU.mult), R=[cq.b, rq.b], W=[cqn.b])
        rkv = rstd_chunk(cx, [(ckv.t[:], 128, [ckv.b])], 128)
        fw.op(fw.DVE, lambda: nc.vector.tensor_tensor(out=ckvn.t[:], in0=ckv.t[:], in1=rkv.t[:], op=ALU.mult), R=[ckv.b, rkv.b], W=[ckvn.b])
        qrhs = [cqn.t[:, 0, :], cqn.t[0:64, 1, :]]
        for i in range(2):
            ps = nextps()
            mm_group(ps.t[:, :], ps.b, [wuq.t[:, 0, i * 128:(i + 1) * 128], wuq.t[0:64, 1, i * 128:(i + 1) * 128]], qrhs, [wuq.b, cqn.b])
            evac(cx, kk[0], qn_st[c % 2].t[:, i, :], ps.t[:], R=[ps.b], W=[qn_st[c % 2].b], scale=96 ** -0.5)
        for h in range(4):
            fw.dma(fw.GD, out=sc["QMLA"].t[h, 0:64, tsl], in_=qn_st[c % 2].t[(h % 2) * 64:(h % 2) * 64 + 64, h // 2, :], R=[qn_st[c % 2].b], W=[sc["QMLA"].cb[c]])
        ps1 = nextps()
        mm_group(ps1.t[0:64, :], ps1.b, [wuq.t[:, 0, 256:320], wuq.t[0:64, 1, 256:320]], qrhs, [wuq.b, cqn.b])
        ps2 = nextps()
        mm_group(ps2.t[0:64, :], ps2.b, [wuq.t[:, 0, 320:384], wuq.t[0:64, 1, 320:384]], qrhs, [wuq.b, cqn.b])
        cs = csq[c % 2]
        qr = qr_st[c % 2]
        fw.op(fw.DVE, lambda: nc.vector.tensor_tensor(out=t1.t[:], in0=ps1.t[0:64, :], in1=cs.t[:, 0, :], op=ALU.mult), R=[ps1.b, cs.b], W=[t1.b])
        fw.op(fw.DVE, lambda: nc.vector.tensor_tensor(out=t2.t[:], in0=ps2.t[0:64, :], in1=cs.t[:, 1, :], op=ALU.mult), R=[ps2.b, cs.b], W=[t2.b])
        fw.op(fw.DVE, lambda: nc.vector.tensor_tensor(out=qr.t[:, 0, :], in0=t1.t[:], in1=t2.t[:], op=ALU.subtract), R=[t1.b, t2.b], W=[qr.b])
        fw.op(fw.DVE, lambda: nc.vector.tensor_tensor(out=t1.t[:], in0=ps2.t[0:64, :], in1=cs.t[:, 0, :], op=ALU.mult), R=[ps2.b, cs.b], W=[t1.b])
        fw.op(fw.DVE, lambda: nc.vector.tensor_tensor(out=t2.t[:], in0=ps1.t[0:64, :], in1=cs.t[:, 1, :], op=ALU.mult), R=[ps1.b, cs.b], W=[t2.b])
        fw.op(fw.DVE, lambda: nc.vector.tensor_tensor(out=qr.t[:, 1, :], in0=t1.t[:], in1=t2.t[:], op=ALU.add), R=[t1.b, t2.b], W=[qr.b])
        for h in range(4):
            fw.dma(fw.GD, out=sc["QMLA"].t[h, 64:80, tsl], in_=qr.t[h * 16:(h + 1) * 16, 0, :], R=[qr.b], W=[sc["QMLA"].cb[c]])
            fw.dma(fw.GD, out=sc["QMLA"].t[h, 80:96, tsl], in_=qr.t[h * 16:(h + 1) * 16, 1, :], R=[qr.b], W=[sc["QMLA"].cb[c]])
        for i in range(2):
            ps = nextps()
            mm_group(ps.t[:, :], ps.b, [wukv.t[:, i * 128:(i + 1) * 128]], [ckvn.t[:]], [wukv.b, ckvn.b])
            evac(cx, kk[0], kn_st[c % 2].t[:, i, :], ps.t[:], R=[ps.b], W=[kn_st[c % 2].b])
        for h in range(4):
            fw.dma(fw.GD, out=sc["KMLA"].t[h, 0:64, tsl], in_=kn_st[c % 2].t[(h % 2) * 64:(h % 2) * 64 + 64, h // 2, :], R=[kn_st[c % 2].b], W=[sc["KMLA"].cb[c]])
        for s in range(4):
            ps = nextps()
            mm_group(ps.t[:, 0:256], ps.b, [ckvn.t[:, s * 128:(s + 1) * 128]], [wukv.t[:, 256:512]], [wukv.b, ckvn.b])
            evac(cx, kk[0], vm_st[c % 2].t[:, s, :], ps.t[:, 0:256], R=[ps.b], W=[vm_st[c % 2].b])
        fw.dma(fw.GD, out=sc["VMLA"].t[tsl, :].rearrange("(s p) d -> p s d", p=128), in_=vm_st[c % 2].t[:], R=[vm_st[c % 2].b], W=[sc["VMLA"].cb[c]])
        ps1 = colmm(2112, 16)
        ps2 = colmm(2128, 16)
        ck = csk[c % 2]
        kr = kr_st[c % 2]
        fw.op(fw.DVE, lambda: nc.vector.tensor_tensor(out=t1.t[0:16, :], in0=ps1.t[0:16, :], in1=ck.t[:, 0, :], op=ALU.mult), R=[ps1.b, ck.b], W=[t1.b])
        fw.op(fw.DVE, lambda: nc.vector.tensor_tensor(out=t2.t[0:16, :], in0=ps2.t[0:16, :], in1=ck.t[:, 1, :], op=ALU.mult), R=[ps2.b, ck.b], W=[t2.b])
        fw.op(fw.DVE, lambda: nc.vector.tensor_tensor(out=kr.t[:, 0, :], in0=t1.t[0:16, :], in1=t2.t[0:16, :], op=ALU.subtract), R=[t1.b, t2.b], W=[kr.b])
        fw.op(fw.DVE, lambda: nc.vector.tensor_tensor(out=t1.t[0:16, :], in0=ps2.t[0:16, :], in1=ck.t[:, 0, :], op=ALU.mult), R=[ps2.b, ck.b], W=[t1.b])
        fw.op(fw.DVE, lambda: nc.vector.tensor_tensor(out=t2.t[0:16, :], in0=ps1.t[0:16, :], in1=ck.t[:, 1, :], op=ALU.mult), R=[ps1.b, ck.b], W=[t2.b])
        fw.op(fw.DVE, lambda: nc.vector.tensor_tensor(out=kr.t[:, 1, :], in0=t1.t[0:16, :], in1=t2.t[0:16, :], op=ALU.add), R=[t1.b, t2.b], W=[kr.b])
        for h in range(4):
            fw.dma(fw.GD, out=sc["KMLA"].t[h, 64:80, tsl], in_=kr.t[:, 0, :], R=[kr.b], W=[sc["KMLA"].cb[c]])
            fw.dma(fw.GD, out=sc["KMLA"].t[h, 80:96, tsl], in_=kr.t[:, 1, :], R=[kr.b], W=[sc["KMLA"].cb[c]])


def attn_load_qk(cx, at, i, QT_src, KT_src):
    fw = cx.fw
    QT, KT = at["QT"][i], at["KT"][i]
    for (ap, r0, nr) in QT_src:
        fw.dma(fw.SP, out=QT.t[r0:r0 + nr, :], in_=ap, W=[QT.b])
    for (ap, r0, nr) in KT_src:
        fw.dma(fw.SP, out=KT.t[r0:r0 + nr, :], in_=ap, W=[KT.b])


def attn_head(cx, at, qi, r0, V_src, dk, slope_idx, on_out):
    fw, nc = cx.fw, cx.nc
    i = at["n"] % 2
    at["n"] += 1
    QT, KT, V1 = at["QT"][qi], at["KT"][qi], at["V1"][i]
    r1 = r0 + dk
    VALL = at["VALL"]
    fw.op(fw.POOL, lambda: nc.gpsimd.tensor_copy(out=V1.t[:, :, 0:64], in_=VALL.t[:, :, V_src * 64:(V_src + 1) * 64]), R=[VALL.b], W=[V1.b])
    fw.op(fw.POOL, lambda: nc.gpsimd.memset(V1.t[:, :, 64:65], 1.0), W=[V1.b])
    mx = at["mx"]
    for which, T_ in enumerate([QT, KT]):
        for c in range(NCH):
            sq = cx.sq[c % 2]
            ps = cx.ps[c % 2]
            fw.op(fw.ACT, lambda: nc.scalar.activation(out=sq.t[0:dk, :], in_=T_.t[r0:r1, c * CH:(c + 1) * CH], func=AF.Square), R=[T_.b], W=[sq.b])
            fw.op(fw.PE, lambda: nc.tensor.matmul(ps.t[:, :], lhsT=cx.ones_f.t[0:dk, :], rhs=sq.t[0:dk, :], start=True, stop=True), R=[sq.b, cx.ones_f.b], W=[ps.b])
            fw.op(fw.DVE, lambda: nc.vector.reduce_max(out=mx.t[:, which * 16 + c:which * 16 + c + 1], in_=ps.t[:, :], axis=AX.X), R=[ps.b], W=[mx.b])
    m2 = at["m2"]
    fw.op(fw.DVE, lambda: nc.vector.reduce_max(out=m2.t[:, 0:1], in_=mx.t[:, 0:16], axis=AX.X), R=[mx.b], W=[m2.b])
    fw.op(fw.DVE, lambda: nc.vector.reduce_max(out=m2.t[:, 1:2], in_=mx.t[:, 16:32], axis=AX.X), R=[mx.b], W=[m2.b])
    fw.op(fw.DVE, lambda: nc.vector.tensor_tensor(out=m2.t[:, 2:3], in0=m2.t[:, 0:1], in1=m2.t[:, 1:2], op=ALU.mult), R=[m2.b], W=[m2.b])
    fw.op(fw.ACT, lambda: nc.scalar.activation(out=m2.t[:, 3:4], in_=m2.t[:, 2:3], func=AF.Sqrt), R=[m2.b], W=[m2.b])
    negC = at["negC"][i]
    fw.op(fw.DVE, lambda: nc.vector.tensor_scalar(out=negC.t[:, 0:1], in0=m2.t[:, 3:4], scalar1=-1.0, scalar2=None, op0=ALU.mult), R=[m2.b], W=[negC.b])
    if slope_idx is not None:
        BT = at["BT"][i]
        fw.dma(fw.SP, out=BT.t[:], in_=cx.C["da_bt"][slope_idx], W=[BT.b])
        fw.op(fw.DVE, lambda: nc.vector.tensor_scalar(out=BT.t[:], in0=BT.t[:], scalar1=negC.t[:, 0:1], scalar2=None, op0=ALU.add), R=[BT.b, negC.b], W=[BT.b])
        DG, FLR = at["DG"][slope_idx % 2], at["FLR"][slope_idx % 2]
    for qc in range(NCH):
        qsl = slice(qc * CH, (qc + 1) * CH)
        aset = at["acc_i"] % 2
        at["acc_i"] += 1
        accs = [cx.ps[2 + aset * 3 + r] for r in range(3)]
        def region(kc):
            if slope_idx is None:
                return 1
            dl = kc - 4 * qc
            return 0 if dl < 0 else (1 if dl < 4 else 2)
        kcs = list(range(64))
        if slope_idx is not None and at.get("window") is not None:
            thr = at["window"] / cx.slopes[slope_idx]
            keep = []
            for kc in kcs:
                dl = kc - 4 * qc
                if dl < 0:
                    dmin = qc * CH - (kc * 128 + 127)
                elif dl >= 4:
                    dmin = kc * 128 - (qc * CH + 511)
                else:
                    dmin = 0
                if dmin <= thr:
                    keep.append(kc)
            kcs = keep
        regs = [region(kc) for kc in kcs]
        first = {}
        last = {}
        for idx, rg in enumerate(regs):
            first.setdefault(rg, idx)
            last[rg] = idx
        for idx, kc in enumerate(kcs):
            rg = regs[idx]
            st = cx.ps[at["st_i"] % 2]
            at["st_i"] += 1
            fw.op(fw.PE, lambda: nc.tensor.matmul(st.t[:, :], lhsT=KT.t[r0:r1, kc * 128:(kc + 1) * 128], rhs=QT.t[r0:r1, qsl], start=True, stop=True),
                  R=[KT.b, QT.b], W=[st.b])
            pt = at["PT"][at["pt_i"] % len(at["PT"])]
            at["pt_i"] += 1
            if slope_idx is None:
                fw.op(fw.ACT, lambda: nc.scalar.activation(out=pt.t[:], in_=st.t[:], func=AF.Exp, bias=negC.t[:, 0:1], scale=1.0), R=[st.b, negC.b], W=[pt.b])
            elif rg == 1:
                dl = kc - 4 * qc
                tmp = at["tmp"][at["tmp_i"] % 2]
                at["tmp_i"] += 1
                fw.op(fw.DVE, lambda: nc.vector.tensor_tensor(out=tmp.t[:], in0=st.t[:], in1=DG.t[:, dl, :], op=ALU.add), R=[st.b, DG.b], W=[tmp.b])
                fw.op(fw.ACT, lambda: nc.scalar.activation(out=pt.t[:], in_=tmp.t[:], func=AF.Exp, bias=negC.t[:, 0:1], scale=1.0), R=[tmp.b, negC.b], W=[pt.b])
            else:
                dl = kc - 4 * qc
                fw.op(fw.ACT, lambda: nc.scalar.activation(out=pt.t[:], in_=st.t[:], func=AF.Exp, bias=BT.t[:, dl + 63:dl + 64], scale=1.0), R=[st.b, BT.b], W=[pt.b])
            acc = accs[rg]
            fw.op(fw.PE, lambda: nc.tensor.matmul(acc.t[0:65, :], lhsT=V1.t[:, kc, :], rhs=pt.t[:], start=(first[rg] == idx), stop=(last[rg] == idx)),
                  R=[V1.b, pt.b], W=[acc.b], inc=True)
        comb = at["comb"][at["comb_i"] % 2]
        at["comb_i"] += 1
        if slope_idx is None:
            fw.op(fw.DVE, lambda: nc.vector.tensor_copy(out=comb.t[:], in_=accs[1].t[0:65, :]), R=[accs[1].b], W=[comb.b])
        else:
            fw.op(fw.DVE, lambda: nc.vector.tensor_copy(out=comb.t[:], in_=accs[1].t[0:65, :]), R=[accs[1].b], W=[comb.b])
            t65 = at["t65"]
            if 0 in first:
                fw.op(fw.DVE, lambda: nc.vector.tensor_tensor(out=t65.t[:], in0=accs[0].t[0:65, :], in1=FLR.t[:, 0, :], op=ALU.mult), R=[accs[0].b, FLR.b], W=[t65.b])
                fw.op(fw.DVE, lambda: nc.vector.tensor_tensor(out=comb.t[:], in0=comb.t[:], in1=t65.t[:], op=ALU.add), R=[comb.b, t65.b], W=[comb.b])
            if 2 in first:
                fw.op(fw.DVE, lambda: nc.vector.tensor_tensor(out=t65.t[:], in0=accs[2].t[0:65, :], in1=FLR.t[:, 1, :], op=ALU.mult), R=[accs[2].b, FLR.b], W=[t65.b])
                fw.op(fw.DVE, lambda: nc.vector.tensor_tensor(out=comb.t[:], in0=comb.t[:], in1=t65.t[:], op=ALU.add), R=[comb.b, t65.b], W=[comb.b])
        bs = cx.ps[at["st_i"] % 2]
        at["st_i"] += 1
        fw.op(fw.PE, lambda: nc.tensor.matmul(bs.t[0:64, :], lhsT=at["sel"].t[0:65, 0:64], rhs=comb.t[0:65, :], start=True, stop=True), R=[at["sel"].b, comb.b], W=[bs.b])
        rs = at["rs"]
        fw.op(fw.DVE, lambda: nc.vector.reciprocal(out=rs.t[:], in_=bs.t[0:64, :]), R=[bs.b], W=[rs.b])
        on = at["on"][at["on_i"] % 2]
        at["on_i"] += 1
        fw.op(fw.DVE, lambda: nc.vector.tensor_tensor(out=on.t[:], in0=comb.t[0:64, :], in1=rs.t[:], op=ALU.mult), R=[comb.b, rs.b], W=[on.b])
        on_out(qc, on)


def attn_setup(cx, alibi, V_all):
    fw, nc = cx.fw, cx.nc
    at = {"n": 0, "acc_i": 0, "st_i": 0, "pt_i": 0, "tmp_i": 0, "comb_i": 0, "on_i": 0}
    nqk = 1 if alibi else 2
    at["QT"] = [SB(cx, [96, L], BF16, "QT") for _ in range(nqk)]
    at["KT"] = [SB(cx, [96, L], BF16, "KT") for _ in range(nqk)]
    at["V1"] = [SB(cx, [128, 64, 65], BF16, "V1") for _ in range(2)]
    at["PT"] = [SB(cx, [128, CH], BF16, "PT") for _ in range(4)]
    at["mx"] = SB(cx, [128, 32], F32, "mx")
    at["m2"] = SB(cx, [128, 4], F32, "m2")
    at["negC"] = [SB(cx, [128, 1], F32, "negC") for _ in range(2)]
    at["comb"] = [SB(cx, [65, CH], F32, "comb") for _ in range(2)]
    at["rs"] = SB(cx, [64, CH], F32, "rs")
    at["on"] = [SB(cx, [64, CH], F32, "on") for _ in range(2)]
    at["sel"] = SB(cx, [65, 64], F32, "sel")
    at["VALL"] = SB(cx, [128, 64, 256], BF16, "VALL")
    vsrc = V_all.rearrange("(kc p) d -> p kc d", p=128)
    for i in range(8):
        fw.dma(fw.SP, out=at["VALL"].t[:, i * 8:(i + 1) * 8, :], in_=vsrc[:, i * 8:(i + 1) * 8, :], W=[at["VALL"].b])
    fw.dma(fw.SP, out=at["sel"].t[:], in_=cx.C["sel"], W=[at["sel"].b])
    if alibi:
        at["BT"] = [SB(cx, [128, 127], F32, "BT") for _ in range(2)]
        at["tmp"] = [SB(cx, [128, CH], F32, "tmp") for _ in range(2)]
        at["t65"] = SB(cx, [65, CH], F32, "t65")
        at["DG"] = [SB(cx, [128, 4, CH], F32, "DG") for _ in range(2)]
        at["FLR"] = [SB(cx, [65, 2, CH], F32, "FLR") for _ in range(2)]
    return at


def phase_mla(cx):
    fw, nc = cx.fw, cx.nc
    new_phase(cx)
    sc = cx.sc
    at = attn_setup(cx, False, sc["VMLA"].t)
    for h in range(4):
        def on_out(qc, on, h=h):
            fw.dma(fw.GD, out=sc["YMLA"].t[h * 64:(h + 1) * 64, qc * CH:(qc + 1) * CH], in_=on.t[:], R=[on.b], W=[sc["YMLA"].cb[qc]])
        attn_load_qk(cx, at, h % 2, [(sc["QMLA"].t[h], 0, 96)], [(sc["KMLA"].t[h], 0, 96)])
        attn_head(cx, at, h % 2, 0, h, 96, None, on_out)
        if getattr(cx, "dbgt", False) and h < 2:
            for nm, tl in [("m2", at["m2"]), ("mx", at["mx"]), ("comb", at["comb"][1]), ("rs", at["rs"]), ("on", at["on"][1]), ("negC", at["negC"][h % 2])]:
                dst = nc.dram_tensor(f"dbg_{nm}{h}", list(tl.t.shape), F32, kind="ExternalOutput").ap()
                fw.dma(fw.SP, out=dst, in_=tl.t[:], R=[tl.b])


def phase_da(cx, W, li):
    fw, nc = cx.fw, cx.nc
    new_phase(cx)
    sc = cx.sc
    at = attn_setup(cx, True, sc["VDA"].t)
    at["window"] = cx.window
    lam_init = 0.8 - 0.6 * math.exp(-0.3 * li)
    lv = SB(cx, [64, 128], F32, "lv")
    fw.dma(fw.SP, out=lv.t[:], in_=W["da_lambda"].rearrange("a b -> (a b)").partition_broadcast(64), W=[lv.b])
    lp = SB(cx, [64, 64], F32, "lp")
    fw.op(fw.DVE, lambda: nc.vector.tensor_tensor(out=lp.t[:, 0:32], in0=lv.t[:, 0:32], in1=lv.t[:, 32:64], op=ALU.mult), R=[lv.b], W=[lp.b])
    fw.op(fw.DVE, lambda: nc.vector.tensor_tensor(out=lp.t[:, 32:64], in0=lv.t[:, 64:96], in1=lv.t[:, 96:128], op=ALU.mult), R=[lv.b], W=[lp.b])
    ls = SB(cx, [64, 4], F32, "ls")
    fw.op(fw.DVE, lambda: nc.vector.reduce_sum(out=ls.t[:, 0:1], in_=lp.t[:, 0:32], axis=AX.X), R=[lp.b], W=[ls.b])
    fw.op(fw.DVE, lambda: nc.vector.reduce_sum(out=ls.t[:, 1:2], in_=lp.t[:, 32:64], axis=AX.X), R=[lp.b], W=[ls.b])
    fw.op(fw.ACT, lambda: nc.scalar.activation(out=ls.t[:, 0:2], in_=ls.t[:, 0:2], func=AF.Exp), R=[ls.b], W=[ls.b])
    fw.op(fw.DVE, lambda: nc.vector.tensor_tensor(out=ls.t[:, 2:3], in0=ls.t[:, 1:2], in1=ls.t[:, 0:1], op=ALU.subtract), R=[ls.b], W=[ls.b])
    fw.op(fw.DVE, lambda: nc.vector.tensor_scalar(out=ls.t[:, 3:4], in0=ls.t[:, 2:3], scalar1=-lam_init, scalar2=None, op0=ALU.add), R=[ls.b], W=[ls.b])
    gh = SB(cx, [64, 1], F32, "gh")
    fw.dma(fw.SP, out=gh.t[:], in_=W["da_g_head"], W=[gh.b])
    fw.op(fw.DVE, lambda: nc.vector.tensor_scalar(out=gh.t[:], in0=gh.t[:], scalar1=1.0 - lam_init, scalar2=None, op0=ALU.mult), R=[gh.b], W=[gh.b])
    O1 = SB(cx, [64, L], F32, "O1")
    dd = [SB(cx, [64, CH], F32, "dd") for _ in range(2)]
    for h in range(4):
        def out1(qc, on):
            fw.op(fw.ACT, lambda: nc.scalar.copy(out=O1.t[:, qc * CH:(qc + 1) * CH], in_=on.t[:]), R=[on.b], W=[O1.b])

        def out2(qc, on, h=h):
            d_ = dd[qc % 2]
            fw.op(fw.DVE, lambda: nc.vector.scalar_tensor_tensor(out=d_.t[:], in0=on.t[:], scalar=ls.t[:, 3:4], in1=O1.t[:, qc * CH:(qc + 1) * CH],
                                                                 op0=ALU.mult, op1=ALU.add), R=[on.b, ls.b, O1.b], W=[d_.b])
            r = rstd_chunk(cx, [(d_.t[:], 64, [d_.b])], 64)
            fw.op(fw.DVE, lambda: nc.vector.scalar_tensor_tensor(out=d_.t[:], in0=d_.t[:], scalar=gh.t[:, 0:1], in1=r.t[0:64, :], op0=ALU.mult, op1=ALU.mult),
                  R=[d_.b, gh.b, r.b], W=[d_.b])
            fw.dma(fw.GD, out=sc["YDA"].t[h * 64:(h + 1) * 64, qc * CH:(qc + 1) * CH], in_=d_.t[:], R=[d_.b], W=[sc["YDA"].cb[qc]])
        attn_load_qk(cx, at, 0, [(sc["QDA"].t[h * 64:(h + 1) * 64, :], 0, 64)], [(sc["KDA"].t[h * 64:(h + 1) * 64, :], 0, 64)])
        dg, fl = at["DG"][h % 2], at["FLR"][h % 2]
        fw.dma(fw.SP, out=dg.t[:], in_=cx.C["da_dg"][h].rearrange("a p f -> p a f"), W=[dg.b])
        fw.dma(fw.SP, out=fl.t[:], in_=cx.C["da_flr"][h].rearrange("a p f -> p a f"), W=[fl.b])
        for j, cb in enumerate([out1, out2]):
            attn_head(cx, at, 0, j * 32, h, 32, h, cb)


def phase_s5(cx, W, li):
    fw, nc = cx.fw, cx.nc
    new_phase(cx)
    sc = cx.sc
    V = fw.DVE
    T = CH

    def tt(out, a, b, op, R, Wb):
        fw.op(V, lambda: nc.vector.tensor_tensor(out=out, in0=a, in1=b, op=op), R=R, W=Wb)

    def ts(out, a, s1, op0, R, Wb, s2=None, op1=None):
        if op1 is None:
            fw.op(V, lambda: nc.vector.tensor_scalar(out=out, in0=a, scalar1=s1, scalar2=None, op0=op0), R=R, W=Wb)
        else:
            fw.op(V, lambda: nc.vector.tensor_scalar(out=out, in0=a, scalar1=s1, scalar2=s2, op0=op0, op1=op1), R=R, W=Wb)

    def stt(out, a, sc_, b, op0, op1, R, Wb):
        fw.op(V, lambda: nc.vector.scalar_tensor_tensor(out=out, in0=a, scalar=sc_, in1=b, op0=op0, op1=op1), R=R, W=Wb)

    NP = 40
    pr = SB(cx, [128, NP, 16], F32, "s5p")
    PB = pr.b
    col = lambda i: pr.t[:, i, :]
    AR, AI, LDT, DT, ARDT, TH, SS, CC, M32, ZR, ZI, T1, T2, ABR, ABI, NR, DEN, FR, FI, ATR, ATI, N2 = range(22)
    fw.dma(fw.SP, out=col(AR), in_=W["s5_ar"], W=[PB])
    fw.dma(fw.SP, out=col(AI), in_=W["s5_ai"], W=[PB])
    fw.dma(fw.SP, out=col(LDT), in_=W["s5_ldt"], W=[PB])
    halfpi = SB(cx, [128, 1], F32, "halfpi")
    fw.op(V, lambda: nc.vector.memset(halfpi.t[:], math.pi / 2), W=[halfpi.b])
    fw.op(fw.ACT, lambda: nc.scalar.activation(out=col(DT), in_=col(LDT), func=AF.Exp), R=[PB], W=[PB])
    tt(col(ARDT), col(AR), col(DT), ALU.mult, [PB], [PB])
    tt(col(TH), col(AI), col(DT), ALU.mult, [PB], [PB])
    fw.op(fw.ACT, lambda: nc.scalar.activation(out=col(SS), in_=col(TH), func=AF.Sin, scale=1.0 / 32), R=[PB], W=[PB])
    fw.op(fw.ACT, lambda: nc.scalar.activation(out=col(CC), in_=col(TH), func=AF.Sin, scale=1.0 / 32, bias=halfpi.t[:, 0:1]), R=[PB, halfpi.b], W=[PB])
    fw.op(fw.ACT, lambda: nc.scalar.activation(out=col(M32), in_=col(ARDT), func=AF.Exp, scale=1.0 / 32), R=[PB], W=[PB])
    tt(col(ZR), col(M32), col(CC), ALU.mult, [PB], [PB])
    tt(col(ZI), col(M32), col(SS), ALU.mult, [PB], [PB])
    pw = SB(cx, [128, 10, 2, 16], F32, "s5pw")

    def square():
        tt(col(T1), col(ZR), col(ZR), ALU.mult, [PB], [PB])
        tt(col(T2), col(ZI), col(ZI), ALU.mult, [PB], [PB])
        tt(col(T2), col(T1), col(T2), ALU.subtract, [PB], [PB])
        tt(col(T1), col(ZR), col(ZI), ALU.mult, [PB], [PB])
        ts(col(ZI), col(T1), 2.0, ALU.mult, [PB], [PB])
        fw.op(V, lambda: nc.vector.tensor_copy(out=col(ZR), in_=col(T2)), R=[PB], W=[PB])
    for _ in range(5):
        square()
    for j in range(10):
        fw.op(V, lambda: nc.vector.tensor_copy(out=pw.t[:, j, 0, :], in_=col(ZR)), R=[PB], W=[pw.b])
        fw.op(V, lambda: nc.vector.tensor_copy(out=pw.t[:, j, 1, :], in_=col(ZI)), R=[PB], W=[pw.b])
        if j < 9:
            square()
    ABRa, ABIa = pw.t[:, 0, 0, :], pw.t[:, 0, 1, :]
    ts(col(NR), ABRa, -1.0, ALU.add, [pw.b], [PB])
    tt(col(T1), col(AR), col(AR), ALU.mult, [PB], [PB])
    tt(col(T2), col(AI), col(AI), ALU.mult, [PB], [PB])
    tt(col(DEN), col(T1), col(T2), ALU.add, [PB], [PB])
    fw.op(V, lambda: nc.vector.reciprocal(out=col(DEN), in_=col(DEN)), R=[PB], W=[PB])
    tt(col(T1), col(NR), col(AR), ALU.mult, [PB], [PB])
    tt(col(T2), ABIa, col(AI), ALU.mult, [PB, pw.b], [PB])
    tt(col(T1), col(T1), col(T2), ALU.add, [PB], [PB])
    tt(col(FR), col(T1), col(DEN), ALU.mult, [PB], [PB])
    tt(col(T1), ABIa, col(AR), ALU.mult, [PB, pw.b], [PB])
    tt(col(T2), col(NR), col(AI), ALU.mult, [PB], [PB])
    tt(col(T1), col(T1), col(T2), ALU.subtract, [PB], [PB])
    tt(col(FI), col(T1), col(DEN), ALU.mult, [PB], [PB])
    ts(col(N2), col(ARDT), -2.0, ALU.mult, [PB], [PB])
    bsrc = SB(cx, [128, 16, 2, 16], F32, "s5b")
    csrc = SB(cx, [128, 16, 2, 16], F32, "s5c")
    fw.dma(fw.SP, out=bsrc.t[:], in_=W["s5_b"], W=[bsrc.b])
    fw.dma(fw.SP, out=csrc.t[:], in_=W["s5_c"], W=[csrc.b])
    dsk = SB(cx, [32, 8], F32, "s5d")
    fw.dma(fw.SP, out=dsk.t[:], in_=W["s5_d"], W=[dsk.b])
    tau = SB(cx, [128, T], F32, "tau")
    fw.dma(fw.SP, out=tau.t[:], in_=cx.C["tau"], W=[tau.b])
    ones = SB(cx, [128, T], F32, "ones512")
    fw.op(V, lambda: nc.vector.memset(ones.t[:], 1.0), W=[ones.b])
    Ep = [SB(cx, [128, 2, T], F32, "Ep") for _ in range(2)]
    Em = [SB(cx, [128, 2, T], F32, "Em") for _ in range(2)]
    mg = SB(cx, [128, T], F32, "mg")
    Uf = [SB(cx, [32, L], F32, "Uf") for _ in range(2)]
    Ub = [SB(cx, [32, L], BF16, "Ub") for _ in range(2)]
    Y = SB(cx, [32, L], F32, "Y")
    bpad = SB(cx, [128, 2, 32], F32, "bpad")
    cpad = SB(cx, [128, 2, 32], F32, "cpad")
    bb = SB(cx, [128, 2, 16], F32, "bb")
    BT_ = [SB(cx, [32, 2, 128], BF16, "BTl") for _ in range(2)]
    CT_ = [SB(cx, [128, 2, 32], BF16, "CTl") for _ in range(2)]
    Pt = [SB(cx, [128, 2, T], F32, "Pt") for _ in range(2)]
    St = [SB(cx, [128, 2, T], F32, "St") for _ in range(2)]
    tm = [SB(cx, [128, 2, T], F32, "tm") for _ in range(2)]
    xb = [SB(cx, [128, 2, T], BF16, "xb") for _ in range(2)]
    ax = SB(cx, [128, 4], F32, "ax")
    it = 0
    for k in range(8):
        uf, ub = Uf[k % 2], Ub[k % 2]
        fw.dma(fw.SP, out=uf.t[:], in_=sc["ZS5"].t[32 * k:32 * k + 32, :], W=[uf.b])
        fw.op(fw.ACT, lambda: nc.scalar.copy(out=ub.t[:], in_=uf.t[:]), R=[uf.b], W=[ub.b])
        for dr in range(2):
            cl = dr * 8 + k
            c1 = slice(cl, cl + 1)
            ep, em = Ep[(k * 2 + dr) % 2], Em[(k * 2 + dr) % 2]
            btl, ctl = BT_[(k * 2 + dr) % 2], CT_[(k * 2 + dr) % 2]
            fw.op(V, lambda: nc.vector.memset(ep.t[:, 0, 0:1], 1.0), W=[ep.b])
            fw.op(V, lambda: nc.vector.memset(ep.t[:, 1, 0:1], 0.0), W=[ep.b])
            for j in range(9):
                n_ = 1 << j
                p_r, p_i = pw.t[:, j, 0, c1], pw.t[:, j, 1, c1]
                ts(mg.t[:, 0:n_], ep.t[:, 1, 0:n_], p_i, ALU.mult, [ep.b, pw.b], [mg.b])
                stt(ep.t[:, 0, n_:2 * n_], ep.t[:, 0, 0:n_], p_r, mg.t[:, 0:n_], ALU.mult, ALU.subtract, [ep.b, pw.b, mg.b], [ep.b])
                ts(mg.t[:, 0:n_], ep.t[:, 1, 0:n_], p_r, ALU.mult, [ep.b, pw.b], [mg.b])
                stt(ep.t[:, 1, n_:2 * n_], ep.t[:, 0, 0:n_], p_i, mg.t[:, 0:n_], ALU.mult, ALU.add, [ep.b, pw.b, mg.b], [ep.b])
            fw.op(fw.ACT, lambda: nc.scalar.activation(out=mg.t[:], in_=tau.t[:], func=AF.Exp, scale=pr.t[:, N2, c1]), R=[tau.b, PB, mg.b], W=[mg.b])
            tt(em.t[:, 0, :], ep.t[:, 0, :], mg.t[:], ALU.mult, [ep.b, mg.b], [em.b])
            stt(em.t[:, 1, :], ep.t[:, 1, :], -1.0, mg.t[:], ALU.mult, ALU.mult, [ep.b, mg.b], [em.b])
            f_r, f_i = pr.t[:, FR, c1], pr.t[:, FI, c1]
            b_r, b_i = bsrc.t[:, cl, 0, :], bsrc.t[:, cl, 1, :]
            ts(bb.t[:, 0, :], b_i, f_i, ALU.mult, [bsrc.b, PB], [bb.b])
            stt(bb.t[:, 0, :], b_r, f_r, bb.t[:, 0, :], ALU.mult, ALU.subtract, [bsrc.b, PB, bb.b], [bb.b])
            ts(bb.t[:, 1, :], b_r, f_i, ALU.mult, [bsrc.b, PB], [bb.b])
            stt(bb.t[:, 1, :], b_i, f_r, bb.t[:, 1, :], ALU.mult, ALU.add, [bsrc.b, PB, bb.b], [bb.b])
            fw.op(V, lambda: nc.vector.memset(bpad.t[:], 0.0), W=[bpad.b])
            fw.op(V, lambda: nc.vector.memset(cpad.t[:], 0.0), W=[cpad.b])
            for hf in range(2):
                ps_ = slice(hf * 64, hf * 64 + 64)
                cs_ = slice(hf * 16, hf * 16 + 16)
                fw.op(V, lambda: nc.vector.tensor_copy(out=bpad.t[ps_, :, cs_], in_=bb.t[ps_, :, :]), R=[bb.b], W=[bpad.b])
                fw.op(V, lambda: nc.vector.tensor_copy(out=cpad.t[ps_, 0, cs_], in_=csrc.t[ps_, cl, 0, :]), R=[csrc.b], W=[cpad.b])
                ts(cpad.t[ps_, 1, cs_], csrc.t[ps_, cl, 1, :], -1.0, ALU.mult, [csrc.b], [cpad.b])
            fw.op(V, lambda: nc.vector.tensor_copy(out=ctl.t[:], in_=cpad.t[:]), R=[cpad.b], W=[ctl.b])
            psb = cx.ps[6]
            for ri in range(2):
                fw.op(fw.PE, lambda: nc.tensor.transpose(out=psb.t[0:32, ri * 128:(ri + 1) * 128], in_=bpad.t[:, ri, :], identity=cx.ident_f.t[:]),
                      R=[bpad.b, cx.ident_f.b], W=[psb.b], inc=(ri == 1))
            fw.op(V, lambda: nc.vector.tensor_copy(out=btl.t[:].rearrange("p a b -> p (a b)"), in_=psb.t[0:32, 0:256]), R=[psb.b], W=[btl.b])
            fw.op(V, lambda: nc.vector.memset(ax.t[:], 0.0), W=[ax.b])
            at_r, at_i = pw.t[:, 9, 0, c1], pw.t[:, 9, 1, c1]
            for ci in range(NCH):
                c = ci if dr == 0 else NCH - 1 - ci
                tsl = slice(c * T, (c + 1) * T)
                rv = (lambda a: a) if dr == 0 else (lambda a: a[:, ::-1])
                bur, bui = cx.ps[(it % 2) * 2], cx.ps[(it % 2) * 2 + 1]
                yps = cx.ps[4 + it % 2]
                P_, S_, t_, x_ = Pt[it % 2], St[it % 2], tm[it % 2], xb[it % 2]
                it += 1
                fw.op(fw.PE, lambda: nc.tensor.matmul(bur.t[:, :], lhsT=btl.t[:, 0, :], rhs=ub.t[:, tsl], start=True, stop=True), R=[btl.b, ub.b], W=[bur.b])
                fw.op(fw.PE, lambda: nc.tensor.matmul(bui.t[:, :], lhsT=btl.t[:, 1, :], rhs=ub.t[:, tsl], start=True, stop=True), R=[btl.b, ub.b], W=[bui.b])
                emr, emi = rv(em.t[:, 0, :]), rv(em.t[:, 1, :])
                epr, epi = rv(ep.t[:, 0, :]), rv(ep.t[:, 1, :])
                tt(t_.t[:, 0, :], bui.t[:, :], emi, ALU.mult, [bui.b, em.b], [t_.b])
                tt(P_.t[:, 0, :], bur.t[:, :], emr, ALU.mult, [bur.b, em.b], [P_.b])
                tt(P_.t[:, 0, :], P_.t[:, 0, :], t_.t[:, 0, :], ALU.subtract, [P_.b, t_.b], [P_.b])
                tt(t_.t[:, 1, :], bur.t[:, :], emi, ALU.mult, [bur.b, em.b], [t_.b])
                tt(P_.t[:, 1, :], bui.t[:, :], emr, ALU.mult, [bui.b, em.b], [P_.b])
                tt(P_.t[:, 1, :], P_.t[:, 1, :], t_.t[:, 1, :], ALU.add, [P_.b, t_.b], [P_.b])
                for ri in range(2):
                    fw.op(V, lambda: nc.vector.tensor_tensor_scan(out=rv(S_.t[:, ri, :]), data0=ones.t[:, :], data1=rv(P_.t[:, ri, :]), initial=ax.t[:, ri:ri + 1],
                                                                  op0=ALU.mult, op1=ALU.add), R=[ones.b, P_.b, ax.b], W=[S_.b])
                tt(t_.t[:, 0, :], S_.t[:, 1, :], epi, ALU.mult, [S_.b, ep.b], [t_.b])
                tt(t_.t[:, 1, :], S_.t[:, 0, :], epr, ALU.mult, [S_.b, ep.b], [t_.b])
                tt(x_.t[:, 0, :], t_.t[:, 1, :], t_.t[:, 0, :], ALU.subtract, [t_.b], [x_.b])
                tt(t_.t[:, 0, :], S_.t[:, 0, :], epi, ALU.mult, [S_.b, ep.b], [t_.b])
                tt(t_.t[:, 1, :], S_.t[:, 1, :], epr, ALU.mult, [S_.b, ep.b], [t_.b])
                tt(x_.t[:, 1, :], t_.t[:, 1, :], t_.t[:, 0, :], ALU.add, [t_.b], [x_.b])
                e_ = T - 1 if dr == 0 else 0
                sre, sie = S_.t[:, 0, e_:e_ + 1], S_.t[:, 1, e_:e_ + 1]
                ts(ax.t[:, 2:3], sie, at_i, ALU.mult, [S_.b, pw.b], [ax.b])
                ts(ax.t[:, 3:4], sie, at_r, ALU.mult, [S_.b, pw.b], [ax.b])
                stt(ax.t[:, 0:1], sre, at_r, ax.t[:, 2:3], ALU.mult, ALU.subtract, [S_.b, pw.b, ax.b], [ax.b])
                stt(ax.t[:, 1:2], sre, at_i, ax.t[:, 3:4], ALU.mult, ALU.add, [S_.b, pw.b, ax.b], [ax.b])
                fw.op(fw.PE, lambda: nc.tensor.matmul(yps.t[0:32, :], lhsT=ctl.t[:, 0, :], rhs=x_.t[:, 0, :], start=True, stop=False), R=[ctl.b, x_.b], W=[yps.b], inc=False)
                fw.op(fw.PE, lambda: nc.tensor.matmul(yps.t[0:32, :], lhsT=ctl.t[:, 1, :], rhs=x_.t[:, 1, :], start=False, stop=True), R=[ctl.b, x_.b], W=[yps.b])
                if dr == 0:
                    fw.op(fw.ACT, lambda: nc.scalar.copy(out=Y.t[:, tsl], in_=yps.t[0:32, :]), R=[yps.b], W=[Y.b])
                else:
                    fw.op(fw.POOL, lambda: nc.gpsimd.tensor_copy(out=t_.t[0:32, 0, :], in_=t_.t[0:32, 0, :]), R=[t_.b], W=[t_.b]) if False else None
                    fw.op(fw.ACT, lambda: nc.scalar.copy(out=S_.t[0:32, 0, :], in_=yps.t[0:32, :]), R=[yps.b, S_.b], W=[S_.b])
                    fw.op(fw.POOL, lambda: nc.gpsimd.tensor_tensor(out=Y.t[:, tsl], in0=Y.t[:, tsl], in1=S_.t[0:32, 0, :], op=ALU.add), R=[Y.b, S_.b], W=[Y.b])
        for c in range(NCH):
            tsl = slice(c * T, (c + 1) * T)
            fw.op(V, lambda: nc.vector.scalar_tensor_tensor(out=Y.t[:, tsl], in0=uf.t[:, tsl], scalar=dsk.t[:, k:k + 1], in1=Y.t[:, tsl], op0=ALU.mult, op1=ALU.add),
                  R=[uf.b, dsk.b, Y.b], W=[Y.b])
        fw.op(fw.ACT, lambda: nc.scalar.activation(out=Y.t[:], in_=Y.t[:], func=AF.Gelu), R=[Y.b], W=[Y.b])
        fw.dma(fw.GD, out=sc["GS5"].t[32 * k:32 * k + 32, :], in_=Y.t[:], R=[Y.b], W=sc["GS5"].cb)

    new_phase(cx)
    cx.wstg = [SB(cx, [128, 1024], F32, "wstg") for _ in range(2)]
    wgl = SB(cx, [128, 2, 256], BF16, "wgl")
    load_w_bf16(cx, wgl, W["s5_w_glu"], 2, 256)
    bgl = SB(cx, [128, 2], F32, "bgl")
    fw.dma(fw.SP, out=bgl.t[:], in_=W["s5_b_glu"], W=[bgl.b])
    gts = [SB(cx, [128, 2, CH], F32, "gt") for _ in range(2)]
    gbs = [SB(cx, [128, 2, CH], BF16, "gb") for _ in range(2)]
    sgs = [SB(cx, [128, CH], F32, "sg") for _ in range(2)]
    kq = 0
    for c in range(NCH):
        tsl = slice(c * CH, (c + 1) * CH)
        g, gbf = gts[c % 2], gbs[c % 2]
        fw.dma(fw.SP, out=g.t[:], in_=sc["GS5"].t[:, tsl].rearrange("(j p) t -> p j t", p=128), R=[sc["GS5"].cb[c]], W=[g.b])
        fw.op(fw.ACT, lambda: nc.scalar.copy(out=gbf.t[:], in_=g.t[:]), R=[g.b], W=[gbf.b])
        for m in range(2):
            ps = cx.ps[2 + kq % 6]
            kq += 1
            for j in range(2):
                fw.op(fw.PE, lambda: nc.tensor.matmul(ps.t[:, :], lhsT=wgl.t[:, j, m * 128:(m + 1) * 128], rhs=gbf.t[:, j, :], start=(j == 0), stop=(j == 1)),
                      R=[wgl.b, gbf.b], W=[ps.b], inc=(j == 1))
            sg = sgs[m]
            fw.op(fw.ACT, lambda: nc.scalar.activation(out=sg.t[:], in_=ps.t[:], func=AF.Sigmoid, bias=bgl.t[:, m:m + 1], scale=1.0), R=[ps.b, bgl.b], W=[sg.b])
            fw.op(V, lambda: nc.vector.tensor_tensor(out=g.t[:, m, :], in0=g.t[:, m, :], in1=sg.t[:], op=ALU.mult), R=[g.b, sg.b], W=[g.b])
        fw.dma(fw.GD, out=sc["YS5"].t[:, tsl].rearrange("(j p) t -> p j t", p=128), in_=g.t[:], R=[g.b], W=[sc["YS5"].cb[c]])


NK1 = 65
HC = 32


def fft_fwd_group(cx, hy, lhs_of, K, on_bank):
    fw, nc = cx.fw, cx.nc
    Yb = hy["Ybuf"]
    for c in range(HC):
        ps = cx.ps[hy["pi"] % 4]
        hy["pi"] += 1
        fw.op(fw.PE, lambda: nc.tensor.matmul(ps.t[:, 0:4 * NK1], lhsT=lhs_of(c), rhs=hy["F4"].t[0:K, :], start=True, stop=True), R=hy["lhsR"] + [hy["F4"].b], W=[ps.b])
        src = ps.t[:, 0:4 * NK1].rearrange("p (a k) -> p k a", a=4)
        if c % 2 == 0:
            fw.op(fw.ACT, lambda: nc.scalar.copy(out=Yb.t[:, :, :, c], in_=src), R=[ps.b], W=[Yb.b])
        else:
            fw.op(fw.DVE, lambda: nc.vector.tensor_copy(out=Yb.t[:, :, :, c], in_=src), R=[ps.b], W=[Yb.b])
    G = hy["G"]
    nb = (NK1 + 7) // 8
    for b in range(nb):
        k1s = list(range(b * 8, min(b * 8 + 8, NK1)))
        ps = cx.ps[4 + hy["pj"] % 4]
        hy["pj"] += 1
        for i, k1 in enumerate(k1s):
            o = ps.t[:, i * 2 * HC:(i + 1) * 2 * HC]
            fw.op(fw.PE, lambda: nc.tensor.matmul(o, lhsT=G.t[:, k1, 0, :], rhs=Yb.t[:, k1, 0:2, :].rearrange("p a c -> p (a c)"), start=True, stop=False),
                  R=[G.b, Yb.b], W=[ps.b], inc=False)
            fw.op(fw.PE, lambda: nc.tensor.matmul(o, lhsT=G.t[:, k1, 1, :], rhs=Yb.t[:, k1, 2:4, :].rearrange("p a c -> p (a c)"), start=False, stop=True),
                  R=[G.b, Yb.b], W=[ps.b], inc=(i == len(k1s) - 1))
        on_bank(b, k1s, ps)


def phase_hy_filter(cx, W, li):
    fw, nc = cx.fw, cx.nc
    new_phase(cx)
    sc = cx.sc
    V = fw.DVE
    hy = {"pi": 0, "pj": 0}
    hy["G"] = SB(cx, [128, NK1, 2, 128], BF16, "G")
    for i in range(5):
        fw.dma(fw.SP, out=hy["G"].t[:, i * 13:(i + 1) * 13], in_=cx.C["fftG"][:, i * 13:(i + 1) * 13], W=[hy["G"].b])
    hy["F4"] = SB(cx, [128, 4 * NK1], BF16, "F4")
    fw.dma(fw.SP, out=hy["F4"].t[:], in_=cx.C["fftF4"], W=[hy["F4"].b])
    hy["Ybuf"] = SB(cx, [128, NK1, 4, HC], BF16, "Ybuf")
    w1 = SB(cx, [33, 64], F32, "w1")
    w2 = SB(cx, [64, 64], F32, "w2")
    b12 = SB(cx, [64, 2], F32, "b12")
    fw.dma(fw.SP, out=w1.t[:], in_=W["hy_f_w1"], W=[w1.b])
    fw.dma(fw.SP, out=w2.t[:], in_=W["hy_f_w2"], W=[w2.b])
    fw.dma(fw.SP, out=b12.t[:], in_=W["hy_f_b12"], W=[b12.b])
    w3s = SB(cx, [64, 2, 512], F32, "w3s")
    nr = SB(cx, [128, 512], F32, "nr")
    for half in range(2):
        for o in range(2):
            c0 = o * 512 + half * 256
            fw.dma(fw.SP, out=w3s.t[:, half, o * 256:(o + 1) * 256], in_=W["hy_f_w3"][:, c0:c0 + 256], W=[w3s.b])
            fw.dma(fw.SP, out=nr.t[half * 64:(half + 1) * 64, o * 256:(o + 1) * 256], in_=W["hy_log_decay"][c0:c0 + 256].partition_broadcast(64), W=[nr.b])
    fw.op(fw.ACT, lambda: nc.scalar.activation(out=nr.t[:], in_=nr.t[:], func=AF.Exp), R=[nr.b], W=[nr.b])
    fw.op(V, lambda: nc.vector.tensor_scalar(out=nr.t[:], in0=nr.t[:], scalar1=-1.0, scalar2=None, op0=ALU.mult), R=[nr.b], W=[nr.b])
    tpos = SB(cx, [128, 128], F32, "tpos")
    fw.dma(fw.SP, out=tpos.t[:], in_=cx.C["tpos"], W=[tpos.b])
    h2T = SB(cx, [64, 2 * L], F32, "h2T")
    fts = [SB(cx, [33, CH], F32, "ft")] * 2
    xf = [SB(cx, [64, CH], F32, "xf") for _ in range(2)]
    xi = [SB(cx, [64, CH], mybir.dt.int32, "xi")] * 2
    h1 = [SB(cx, [64, CH], F32, "h1") for _ in range(2)]
    twopi = 2 * math.pi

    def sin_from(ps_ap, psb, bias_ap, out_ap, outb, i):
        x_, xi_ = xf[i % 2], xi[i % 2]
        fw.op(V, lambda: nc.vector.tensor_scalar(out=x_.t[:], in0=ps_ap, scalar1=bias_ap, scalar2=None, op0=ALU.add), R=[psb, b12.b], W=[x_.b])
        fw.op(V, lambda: nc.vector.tensor_scalar(out=xi_.t[:], in0=x_.t[:], scalar1=1.0 / twopi, scalar2=None, op0=ALU.mult), R=[x_.b], W=[xi_.b])
        fw.op(V, lambda: nc.vector.tensor_copy(out=out_ap, in_=xi_.t[:]), R=[xi_.b], W=[outb])
        fw.op(V, lambda: nc.vector.scalar_tensor_tensor(out=x_.t[:], in0=out_ap, scalar=-twopi, in1=x_.t[:], op0=ALU.mult, op1=ALU.add), R=[outb, x_.b], W=[x_.b])
        fw.op(fw.ACT, lambda: nc.scalar.activation(out=out_ap, in_=x_.t[:], func=AF.Sin), R=[x_.b], W=[outb])
    for c in range(32):
        ft = fts[c % 2]
        fw.dma(fw.SP, out=ft.t[:], in_=cx.C["featsT"][:, c * CH:(c + 1) * CH], W=[ft.b])
        ps = cx.ps[c % 2]
        fw.op(fw.PE, lambda: nc.tensor.matmul(ps.t[0:64, :], lhsT=w1.t[:], rhs=ft.t[:], start=True, stop=True), R=[w1.b, ft.b], W=[ps.b])
        sin_from(ps.t[0:64, :], ps.b, b12.t[:, 0:1], h1[c % 2].t[:], h1[c % 2].b, c)
        ps2 = cx.ps[2 + c % 2]
        fw.op(fw.PE, lambda: nc.tensor.matmul(ps2.t[0:64, :], lhsT=w2.t[:], rhs=h1[c % 2].t[:], start=True, stop=True), R=[w2.b, h1[c % 2].b], W=[ps2.b])
        sin_from(ps2.t[0:64, :], ps2.b, b12.t[:, 1:2], h2T.t[:, c * CH:(c + 1) * CH], h2T.b, c)
    HB = SB(cx, [128, 128, 128], BF16, "HB")
    acc4 = SB(cx, [128, 512], F32, "acc4")
    dec = [SB(cx, [128, 512], F32, "dec") for _ in range(2)]
    hh = [SB(cx, [128, 512], F32, "hh") for _ in range(2)]
    ab = [SB(cx, [128, 512], F32, "ab") for _ in range(1)] * 2
    rnt = SB(cx, [128, 512], F32, "rnt")
    hst = [SB(cx, [128, NK1, 2, HC], BF16, "hst") for _ in range(1)]
    hy["lhsR"] = [HB.b]
    for q in range(4):
        qs = slice(q * 128, (q + 1) * 128)
        fw.op(V, lambda: nc.vector.memset(acc4.t[:], 0.0), W=[acc4.b])
        for bt in range(32):
            ps = cx.ps[bt % 2]
            d_, h_, a_ = dec[bt % 2], hh[bt % 2], ab[bt % 2]
            for i in range(4):
                n2 = bt * 4 + i
                fw.op(fw.PE, lambda: nc.tensor.matmul(ps.t[0:64, i * 128:(i + 1) * 128], lhsT=h2T.t[:, n2:L:128], rhs=w3s.t[:, 0, qs], start=True, stop=True),
                      R=[h2T.b, w3s.b], W=[ps.b], inc=False)
                fw.op(fw.PE, lambda: nc.tensor.matmul(ps.t[64:128, i * 128:(i + 1) * 128], lhsT=h2T.t[:, L + n2:2 * L:128], rhs=w3s.t[:, 1, qs], start=True, stop=True),
                      R=[h2T.b, w3s.b], W=[ps.b], inc=(i == 3))
                fw.op(fw.ACT, lambda: nc.scalar.activation(out=d_.t[:, i * 128:(i + 1) * 128], in_=nr.t[:, qs], func=AF.Exp, scale=tpos.t[:, n2:n2 + 1]),
                      R=[nr.b, tpos.b], W=[d_.b])
            fw.op(V, lambda: nc.vector.tensor_tensor(out=h_.t[:], in0=ps.t[:], in1=d_.t[:], op=ALU.mult), R=[ps.b, d_.b], W=[h_.b])
            if bt == 0:
                fw.op(V, lambda: nc.vector.memset(h_.t[64:65, 0:128], 0.0), W=[h_.b])
            fw.op(fw.ACT, lambda: nc.scalar.activation(out=a_.t[:], in_=h_.t[:], func=AF.Abs), R=[h_.b], W=[a_.b])
            fw.op(fw.POOL, lambda: nc.gpsimd.tensor_tensor(out=acc4.t[:], in0=acc4.t[:], in1=a_.t[:], op=ALU.add), R=[acc4.b, a_.b], W=[acc4.b])
            fw.op(V, lambda: nc.vector.tensor_copy(out=HB.t[:, :, bt * 4:(bt + 1) * 4].rearrange("p c n -> p n c"), in_=h_.t[:].rearrange("p (n c) -> p n c", c=128)),
                  R=[h_.b], W=[HB.b])
        ps = cx.ps[2]
        for i in range(4):
            fw.op(fw.PE, lambda: nc.tensor.matmul(ps.t[:, 0:128], lhsT=cx.ones_f.t[:], rhs=acc4.t[:, i * 128:(i + 1) * 128], start=(i == 0), stop=(i == 3)),
                  R=[cx.ones_f.b, acc4.b], W=[ps.b], inc=(i == 3))
        fw.op(V, lambda: nc.vector.reciprocal(out=rnt.t[:, qs], in_=ps.t[:, 0:128]), R=[ps.b], W=[rnt.b])
        for g in range(4):
            gi = q * 4 + g
            hs = hst[0]

            def on_bank(b, k1s, ps, hs=hs):
                src = ps.t[:, 0:len(k1s) * 2 * HC]
                dst = hs.t[:, k1s[0]:k1s[-1] + 1, :, :].rearrange("p k a c -> p (k a c)")
                if b % 2 == 0:
                    fw.op(fw.ACT, lambda: nc.scalar.copy(out=dst, in_=src), R=[ps.b], W=[hs.b])
                else:
                    fw.op(V, lambda: nc.vector.tensor_copy(out=dst, in_=src), R=[ps.b], W=[hs.b])
            fft_fwd_group(cx, hy, lambda c, g=g: HB.t[:, g * HC + c, :], 128, on_bank)
            fw.dma(fw.GD, out=sc["HSPEC"].t[gi], in_=hs.t[:].rearrange("p k a c -> p (k a c)"), R=[hs.b], W=[sc["HSPEC"].cb[gi]])
    fw.dma(fw.GD, out=sc["RN"].t, in_=rnt.t[:], R=[rnt.b], W=sc["RN"].cb)


def phase_hy(cx, W, li):
    fw, nc = cx.fw, cx.nc
    sc = cx.sc
    V = fw.DVE
    new_phase(cx)
    cw = SB(cx, [128, 6, 4], F32, "cw")
    fw.dma(fw.SP, out=cw.t[:], in_=W["hy_conv"], W=[cw.b])
    zt = [SB(cx, [128, L], F32, "zt") for _ in range(2)]
    zo = [SB(cx, [128, L], F32, "zo") for _ in range(2)]
    for j in range(6):
        z, o = zt[j % 2], zo[j % 2]
        fw.dma(fw.SP, out=z.t[:], in_=sc["ZHY"].t[j * 128:(j + 1) * 128, :], W=[z.b])
        fw.op(V, lambda: nc.vector.tensor_scalar(out=o.t[:], in0=z.t[:], scalar1=cw.t[:, j, 1:2], scalar2=cw.t[:, j, 3:4], op0=ALU.mult, op1=ALU.add),
              R=[z.b, cw.b], W=[o.b])
        fw.op(V, lambda: nc.vector.scalar_tensor_tensor(out=o.t[:, 1:L], in0=z.t[:, 0:L - 1], scalar=cw.t[:, j, 0:1], in1=o.t[:, 1:L], op0=ALU.mult, op1=ALU.add),
              R=[z.b, cw.b, o.b], W=[o.b])
        fw.op(V, lambda: nc.vector.scalar_tensor_tensor(out=o.t[:, 0:L - 1], in0=z.t[:, 1:L], scalar=cw.t[:, j, 2:3], in1=o.t[:, 0:L - 1], op0=ALU.mult, op1=ALU.add),
              R=[z.b, cw.b, o.b], W=[o.b])
        fw.dma(fw.GD, out=sc["ZC"].t[j * 128:(j + 1) * 128, :], in_=o.t[:], R=[o.b], W=sc["ZC"].cb)
    if getattr(cx, "hy_sc_only", False):
        return
    new_phase(cx)
    hy = {"pi": 0, "pj": 0}
    hy["G"] = SB(cx, [128, NK1, 2, 128], BF16, "G")
    for i in range(5):
        fw.dma(fw.SP, out=hy["G"].t[:, i * 13:(i + 1) * 13], in_=cx.C["fftG"][:, i * 13:(i + 1) * 13], W=[hy["G"].b])
    hy["F4"] = SB(cx, [128, 4 * NK1], BF16, "F4")
    fw.dma(fw.SP, out=hy["F4"].t[:], in_=cx.C["fftF4"], W=[hy["F4"].b])
    Q = SB(cx, [NK1, 128, 2, 64], BF16, "Q")
    for i in range(4):
        fw.dma(fw.SP, out=Q.t[:, i * 32:(i + 1) * 32], in_=cx.C["fftQ"][:, i * 32:(i + 1) * 32], W=[Q.b])
    FI = SB(cx, [128, 2, 256], BF16, "FI")
    fw.dma(fw.SP, out=FI.t[:], in_=cx.C["fftFI"], W=[FI.b])
    hy["Ybuf"] = SB(cx, [128, NK1, 4, HC], BF16, "Ybuf")
    Yb = hy["Ybuf"]
    Tb_ap = Yb.t[0:NK1].rearrange("p k a c -> p (k a c)")[:, 0:2 * 128 * HC].rearrange("p (a n c) -> p a n c", a=2, c=HC)
    Hg = [SB(cx, [128, NK1, 2, HC], BF16, "Hg") for _ in range(2)]
    Zb = SB(cx, [128, HC, 2, NK1], BF16, "Zb")
    Vf = SB(cx, [64, HC, 128], F32, "Vf")
    X1 = SB(cx, [64, HC, 128], F32, "X1")
    G1 = SB(cx, [64, HC, 128], F32, "G1")
    Ub = SB(cx, [64, HC, 128], BF16, "Ub")
    U2 = SB(cx, [64, HC, 128], BF16, "U2")
    RN = SB(cx, [64, 512], F32, "RN")
    fw.dma(fw.SP, out=RN.t[:], in_=sc["RN"].t[0:64, :], R=sc["RN"].cb, W=[RN.b])
    SK = SB(cx, [64, 512], F32, "SK")
    fw.dma(fw.SP, out=SK.t[:], in_=W["hy_skip"].rearrange("a b -> (a b)").partition_broadcast(64), W=[SK.b])
    tA = [SB(cx, [128, 8 * HC], F32, "tA") for _ in range(2)]
    tB = [SB(cx, [128, 8 * HC], F32, "tB") for _ in range(2)]
    e1 = [SB(cx, [64, 16 * HC], F32, "e1") for _ in range(2)]
    e2 = [SB(cx, [64, 16 * HC], F32, "e2") for _ in range(2)]
    zc = sc["ZC"].t

    def ld(dst, row0):
        src = zc[row0:row0 + HC, :].rearrange("c (a b) -> a c b", b=128)
        for i in range(2):
            fw.dma(fw.SP, out=dst.t[:, i * 16:(i + 1) * 16, :], in_=src[:, i * 16:(i + 1) * 16, :], R=sc["ZC"].cb, W=[dst.b])

    def conv(g, o, ub, v_f, gate, out_f32, out_bf):
        hg = Hg[o]
        fw.dma(fw.SP, out=hg.t[:].rearrange("p k a c -> p (k a c)"), in_=sc["HSPEC"].t[o * 8 + g], R=[sc["HSPEC"].cb[o * 8 + g]], W=[hg.b])
        hy["lhsR"] = [ub.b]

        def on_bank(b, k1s, ps):
            nk = len(k1s)
            ks = slice(k1s[0], k1s[-1] + 1)
            pv = ps.t[:, 0:nk * 2 * HC].rearrange("p (k a c) -> p k a c", a=2, c=HC)
            xr, xi_ = pv[:, :, 0, :], pv[:, :, 1, :]
            hr, hi = hg.t[:, ks, 0, :], hg.t[:, ks, 1, :]
            a_, b_ = tA[b % 2], tB[b % 2]
            av = a_.t[:, 0:nk * HC].rearrange("p (k c) -> p k c", c=HC)
            bv = b_.t[:, 0:nk * HC].rearrange("p (k c) -> p k c", c=HC)
            zr = Zb.t[:, :, 0, ks].rearrange("p c k -> p k c")
            zi = Zb.t[:, :, 1, ks].rearrange("p c k -> p k c")
            fw.op(V, lambda: nc.vector.tensor_tensor(out=av, in0=xr, in1=hr, op=ALU.mult), R=[ps.b, hg.b], W=[a_.b])
            fw.op(V, lambda: nc.vector.tensor_tensor(out=bv, in0=xi_, in1=hi, op=ALU.mult), R=[ps.b, hg.b], W=[b_.b])
            fw.op(fw.POOL, lambda: nc.gpsimd.tensor_tensor(out=zr, in0=av, in1=bv, op=ALU.subtract), R=[a_.b, b_.b], W=[Zb.b])
            a2, b2 = tA[(b + 1) % 2], tB[(b + 1) % 2]
            av2 = a2.t[:, 0:nk * HC].rearrange("p (k c) -> p k c", c=HC)
            bv2 = b2.t[:, 0:nk * HC].rearrange("p (k c) -> p k c", c=HC)
            fw.op(V, lambda: nc.vector.tensor_tensor(out=av2, in0=xr, in1=hi, op=ALU.mult), R=[ps.b, hg.b], W=[a2.b])
            fw.op(V, lambda: nc.vector.tensor_tensor(out=bv2, in0=xi_, in1=hr, op=ALU.mult), R=[ps.b, hg.b], W=[b2.b])
            fw.op(fw.POOL, lambda: nc.gpsimd.tensor_tensor(out=zi, in0=av2, in1=bv2, op=ALU.add), R=[a2.b, b2.b], W=[Zb.b])
        if cx.hy_stop <= 1:
            return
        fft_fwd_group(cx, hy, lambda c: ub.t[:, c, :], 64, on_bank)
        if cx.hy_stop <= 2:
            return
        for c in range(HC):
            ps = cx.ps[hy["pi"] % 4]
            hy["pi"] += 1
            fw.op(fw.PE, lambda: nc.tensor.matmul(ps.t[0:NK1, 0:256], lhsT=Zb.t[:, c, 0, :], rhs=FI.t[:, 0, :], start=True, stop=False), R=[Zb.b, FI.b], W=[ps.b], inc=False)
            fw.op(fw.PE, lambda: nc.tensor.matmul(ps.t[0:NK1, 0:256], lhsT=Zb.t[:, c, 1, :], rhs=FI.t[:, 1, :], start=False, stop=True), R=[Zb.b, FI.b], W=[ps.b])
            src = ps.t[0:NK1, 0:256].rearrange("p (a n) -> p a n", a=2)
            if c % 2 == 0:
                fw.op(fw.ACT, lambda: nc.scalar.copy(out=Tb_ap[:, :, :, c], in_=src), R=[ps.b], W=[Yb.b])
            else:
                fw.op(V, lambda: nc.vector.tensor_copy(out=Tb_ap[:, :, :, c], in_=src), R=[ps.b], W=[Yb.b])
        if cx.hy_stop <= 3:
            return
        cs = slice(o * 256 + g * HC, o * 256 + (g + 1) * HC)
        for nb in range(8):
            ps = cx.ps[4 + hy["pj"] % 4]
            hy["pj"] += 1
            for j in range(16):
                n2 = nb * 16 + j
                oo = ps.t[0:64, j * HC:(j + 1) * HC]
                fw.op(fw.PE, lambda: nc.tensor.matmul(oo, lhsT=Q.t[:, n2, 0, :], rhs=Tb_ap[:, 0, n2, :], start=True, stop=False), R=[Q.b, Yb.b], W=[ps.b], inc=False)
                fw.op(fw.PE, lambda: nc.tensor.matmul(oo, lhsT=Q.t[:, n2, 1, :], rhs=Tb_ap[:, 1, n2, :], start=False, stop=True), R=[Q.b, Yb.b], W=[ps.b], inc=(j == 15))
            pv = ps.t[0:64, 0:16 * HC].rearrange("p (n c) -> p n c", c=HC)
            ns = slice(nb * 16, (nb + 1) * 16)
            a_, b_ = e1[nb % 2], e2[nb % 2]
            av = a_.t[:].rearrange("p (n c) -> p n c", c=HC)
            bv = b_.t[:].rearrange("p (n c) -> p n c", c=HC)
            fw.op(V, lambda: nc.vector.tensor_tensor(out=av, in0=pv, in1=RN.t[:, cs].unsqueeze(1).broadcast_to([64, 16, HC]), op=ALU.mult), R=[ps.b, RN.b], W=[a_.b])
            fw.op(fw.POOL, lambda: nc.gpsimd.tensor_tensor(out=bv, in0=v_f.t[:, :, ns].rearrange("p c n -> p n c"), in1=SK.t[:, cs].unsqueeze(1).broadcast_to([64, 16, HC]), op=ALU.mult),
                  R=[v_f.b, SK.b], W=[b_.b])
            fw.op(V, lambda: nc.vector.tensor_tensor(out=av, in0=av, in1=bv, op=ALU.add), R=[a_.b, b_.b], W=[a_.b])
            fw.op(V, lambda: nc.vector.tensor_tensor(out=out_f32.t[:, :, ns].rearrange("p c n -> p n c"), in0=av, in1=gate.t[:, :, ns].rearrange("p c n -> p n c"), op=ALU.mult),
                  R=[a_.b, gate.b], W=[out_f32.b])
            if out_bf is not None:
                fw.op(fw.ACT, lambda: nc.scalar.copy(out=out_bf.t[:, :, ns], in_=out_f32.t[:, :, ns]), R=[out_f32.b], W=[out_bf.b])

    for g_ in range(cx.hy_lim):
        g = cx.hy_g0 if cx.hy_same else g_
        ld(Vf, g * HC)
        ld(X1, 256 + g * HC)
        fw.op(fw.ACT, lambda: nc.scalar.copy(out=Ub.t[:], in_=Vf.t[:]), R=[Vf.b], W=[Ub.b])
        conv(g, 0, Ub, Vf, X1, G1, U2)
        ld(X1, 512 + g * HC)
        conv(g, 1, U2, G1, X1, Vf, None)
        dst = sc["YHY"].t[g * HC:(g + 1) * HC, :].rearrange("c (a b) -> a c b", b=128)
        for i in range(2):
            fw.dma(fw.GD, out=dst[:, i * 16:(i + 1) * 16, :], in_=Vf.t[:, i * 16:(i + 1) * 16, :], R=[Vf.b], W=sc["YHY"].cb)


def phase_f1(cx, W, li):
    fw, nc = cx.fw, cx.nc
    new_phase(cx)
    sc = cx.sc
    cx.wstg = [SB(cx, [128, 1024], F32, "wstg") for _ in range(2)]
    gb = SB(cx, [128, 8], F32, "gb")
    fw.dma(fw.SP, out=gb.t[:], in_=W["g_branch"], W=[gb.b])
    wout = SB(cx, [128, 8, D], BF16, "wout")
    load_w_bf16(cx, wout, W["w_out"], 8, D, gain=gb)
    ys = [SB(cx, [128, 8, CH], F32, "ys") for _ in range(2)]
    hts = [SB(cx, [128, 8, CH], F32, "ht") for _ in range(2)]
    mixT = SB(cx, [128, 8, CH], BF16, "mixT")
    k = 0
    srcs = ["YHY", "YS5", "YDA", "YMLA"]
    for c in range(NCH):
        tsl = slice(c * CH, (c + 1) * CH)
        y, ht = ys[c % 2], hts[c % 2]
        for bi, nm in enumerate(srcs):
            fw.dma(fw.SP, out=y.t[:, 2 * bi:2 * bi + 2, :], in_=sc[nm].t[:, tsl].rearrange("(j p) t -> p j t", p=128), R=[sc[nm].cb[c]], W=[y.b])
        fw.dma(fw.SP, out=ht.t[:], in_=cx.HT.t[:, tsl].rearrange("(j p) t -> p j t", p=128), R=[cx.HT.cb[c]], W=[ht.b])
        for bi in range(4):
            if bi == 2:
                fw.op(fw.ACT, lambda: nc.scalar.copy(out=mixT.t[:, 4:6, :], in_=y.t[:, 4:6, :]), R=[y.b], W=[mixT.b])
                continue
            r = rstd_chunk(cx, [(y.t[:, 2 * bi + jj, :], 128, [y.b]) for jj in range(2)], 256)
            fw.op(fw.DVE, lambda: nc.vector.tensor_tensor(out=mixT.t[:, 2 * bi:2 * bi + 2, :], in0=y.t[:, 2 * bi:2 * bi + 2, :],
                                                          in1=r.t[:].unsqueeze(1).broadcast_to([128, 2, CH]), op=ALU.mult), R=[y.b, r.b], W=[mixT.b])
        for m in range(8):
            ps = cx.ps[2 + (k % 6)]
            k += 1
            for j in range(8):
                fw.op(fw.PE, lambda: nc.tensor.matmul(ps.t[:, :], lhsT=wout.t[:, j, m * 128:(m + 1) * 128], rhs=mixT.t[:, j, :], start=(j == 0), stop=(j == 7)),
                      R=[wout.b, mixT.b], W=[ps.b], inc=(j == 7))
            fw.op(fw.DVE, lambda: nc.vector.tensor_tensor(out=ht.t[:, m, :], in0=ht.t[:, m, :], in1=ps.t[:, :], op=ALU.add), R=[ht.b, ps.b], W=[ht.b])
        fw.dma(fw.GD, out=cx.HT.t[:, tsl].rearrange("(j p) t -> p j t", p=128), in_=ht.t[:], R=[ht.b], W=[cx.HT.cb[c]])


def phase_f2(cx, W, li):
    fw, nc = cx.fw, cx.nc
    new_phase(cx)
    n = 256
    nchunks = L // n
    cx.wstg = [SB(cx, [128, 1024], F32, "wstg") for _ in range(2)]
    gf = SB(cx, [128, 8], F32, "gf")
    fw.dma(fw.SP, out=gf.t[:], in_=W["g_ffn"], W=[gf.b])
    wg = SB(cx, [128, 8, FFN], BF16, "wg")
    wu = SB(cx, [128, 8, FFN], BF16, "wu")
    wd = SB(cx, [128, 22, D], BF16, "wd")
    kq = 0
    for wt, src in [(wg, W["w_gate"]), (wu, W["w_up"])]:
        for j in range(8):
            for c0 in range(0, FFN, 1024):
                cn = min(1024, FFN - c0)
                stg = cx.wstg[kq % 2]
                fw.dma(fw.SP, out=stg.t[:, 0:cn], in_=src[j * 128:(j + 1) * 128, c0:c0 + cn], W=[stg.b])
                if kq % 2 == 0:
                    fw.op(fw.ACT, lambda: nc.scalar.mul(out=wt.t[:, j, c0:c0 + cn], in_=stg.t[:, 0:cn], mul=gf.t[:, j:j + 1]), R=[stg.b, gf.b], W=[wt.b])
                else:
                    fw.op(fw.DVE, lambda: nc.vector.tensor_scalar(out=wt.t[:, j, c0:c0 + cn], in0=stg.t[:, 0:cn], scalar1=gf.t[:, j:j + 1], scalar2=None, op0=ALU.mult),
                          R=[stg.b, gf.b], W=[wt.b])
                kq += 1
    for i in range(22):
        stg = cx.wstg[kq % 2]
        fw.dma(fw.SP, out=stg.t[:, :], in_=W["w_down"][i * 128:(i + 1) * 128, :], W=[stg.b])
        evac(cx, kq, wd.t[:, i, :], stg.t[:, :], R=[stg.b], W=[wd.b])
        kq += 1
    hts = [SB(cx, [128, 8, n], F32, "ht") for _ in range(2)]
    fT = SB(cx, [128, 8, n], BF16, "fT")
    act = SB(cx, [128, 22, n], BF16, "act")
    sg = [SB(cx, [128, n], F32, "sg") for _ in range(2)]
    k = 0
    for c in range(nchunks):
        tsl = slice(c * n, (c + 1) * n)
        cb = cx.HT.cb[c // 2]
        ht = hts[c % 2]
        fw.dma(fw.SP, out=ht.t[:], in_=cx.HT.t[:, tsl].rearrange("(j p) t -> p j t", p=128), R=[cb], W=[ht.b])
        r = rstd_chunk(cx, [(ht.t[:, j, :], 128, [ht.b]) for j in range(8)], D, n=n)
        fw.op(fw.DVE, lambda: nc.vector.tensor_tensor(out=fT.t[:], in0=ht.t[:], in1=r.t[:, 0:n].unsqueeze(1).broadcast_to([128, 8, n]), op=ALU.mult),
              R=[ht.b, r.b], W=[fT.b])
        for i in range(22):
            ps = cx.ps[2 + (k % 6)]
            k += 1
            for half, wt in enumerate([wg, wu]):
                for j in range(8):
                    fw.op(fw.PE, lambda: nc.tensor.matmul(ps.t[:, half * n:(half + 1) * n], lhsT=wt.t[:, j, i * 128:(i + 1) * 128], rhs=fT.t[:, j, :],
                                                          start=(j == 0), stop=(j == 7)), R=[wt.b, fT.b], W=[ps.b], inc=(j == 7))
            s_ = sg[i % 2]
            fw.op(fw.ACT, lambda: nc.scalar.activation(out=s_.t[:], in_=ps.t[:, 0:n], func=AF.Silu), R=[ps.b], W=[s_.b])
            fw.op(fw.DVE, lambda: nc.vector.tensor_tensor(out=act.t[:, i, :], in0=s_.t[:], in1=ps.t[:, n:2 * n], op=ALU.mult), R=[s_.b, ps.b], W=[act.b])
        for m in range(8):
            ps = cx.ps[2 + (k % 6)]
            k += 1
            for i in range(22):
                fw.op(fw.PE, lambda: nc.tensor.matmul(ps.t[:, 0:n], lhsT=wd.t[:, i, m * 128:(m + 1) * 128], rhs=act.t[:, i, :], start=(i == 0), stop=(i == 21)),
                      R=[wd.b, act.b], W=[ps.b], inc=(i == 21))
            fw.op(fw.DVE, lambda: nc.vector.tensor_tensor(out=ht.t[:, m, :], in0=ht.t[:, m, :], in1=ps.t[:, 0:n], op=ALU.add), R=[ht.b, ps.b], W=[ht.b])
        fw.dma(fw.GD, out=cx.HT.t[:, tsl].rearrange("(j p) t -> p j t", p=128), in_=ht.t[:], R=[ht.b], W=[cb])


def phase_f3(cx, W, li, p_ap):
    fw, nc = cx.fw, cx.nc
    new_phase(cx)
    cx.wstg = [SB(cx, [128, 1024], F32, "wstg") for _ in range(2)]
    gp = SB(cx, [128, 8], F32, "gp")
    fw.dma(fw.SP, out=gp.t[:], in_=W["g_ple"], W=[gp.b])
    wpg = SB(cx, [128, 8, D], BF16, "wpg")
    load_w_bf16(cx, wpg, W["w_ple_gate"], 8, D, gain=gp)
    wpl = SB(cx, [128, 2, D], BF16, "wpl")
    load_w_bf16(cx, wpl, W["w_ple"], 2, D)
    hts = [SB(cx, [128, 8, CH], F32, "ht") for _ in range(2)]
    pcs = [SB(cx, [128, 4, 256], F32, "pc") for _ in range(2)]
    fT = SB(cx, [128, 8, CH], BF16, "fT")
    pT = SB(cx, [128, 2, CH], BF16, "pT")
    sg = [SB(cx, [128, CH], F32, "sg") for _ in range(2)]
    k = 0
    for c in range(NCH):
        tsl = slice(c * CH, (c + 1) * CH)
        ht, pc = hts[c % 2], pcs[c % 2]
        fw.dma(fw.SP, out=ht.t[:], in_=cx.HT.t[:, tsl].rearrange("(j p) t -> p j t", p=128), R=[cx.HT.cb[c]], W=[ht.b])
        fw.dma(fw.SP, out=pc.t[:], in_=p_ap[li, tsl, :].rearrange("(s p) d -> p s d", p=128), W=[pc.b])
        r = rstd_chunk(cx, [(ht.t[:, j, :], 128, [ht.b]) for j in range(8)], D)
        fw.op(fw.DVE, lambda: nc.vector.tensor_tensor(out=fT.t[:], in0=ht.t[:], in1=r.t[:].unsqueeze(1).broadcast_to([128, 8, CH]), op=ALU.mult),
              R=[ht.b, r.b], W=[fT.b])
        for ct in range(2):
            ps = cx.ps[2 + (k % 6)]
            k += 1
            for s in range(4):
                fw.op(fw.PE, lambda: nc.tensor.transpose(out=ps.t[:, s * 128:(s + 1) * 128], in_=pc.t[:, s, ct * 128:(ct + 1) * 128], identity=cx.ident_f.t[:]),
                      R=[pc.b, cx.ident_f.b], W=[ps.b], inc=(s == 3))
            evac(cx, k, pT.t[:, ct, :], ps.t[:], R=[ps.b], W=[pT.b])
        for m in range(8):
            ps = cx.ps[2 + (k % 6)]
            k += 1
            for j in range(8):
                fw.op(fw.PE, lambda: nc.tensor.matmul(ps.t[:, :], lhsT=wpg.t[:, j, m * 128:(m + 1) * 128], rhs=fT.t[:, j, :], start=(j == 0), stop=(j == 7)),
                      R=[wpg.b, fT.b], W=[ps.b], inc=(j == 7))
            ps2 = cx.ps[2 + (k % 6)]
            k += 1
            for ct in range(2):
                fw.op(fw.PE, lambda: nc.tensor.matmul(ps2.t[:, :], lhsT=wpl.t[:, ct, m * 128:(m + 1) * 128], rhs=pT.t[:, ct, :], start=(ct == 0), stop=(ct == 1)),
                      R=[wpl.b, pT.b], W=[ps2.b], inc=(ct == 1))
            s_ = sg[m % 2]
            fw.op(fw.ACT, lambda: nc.scalar.activation(out=s_.t[:], in_=ps.t[:], func=AF.Sigmoid), R=[ps.b], W=[s_.b])
            fw.op(fw.DVE, lambda: nc.vector.tensor_tensor(out=s_.t[:], in0=s_.t[:], in1=ps2.t[:], op=ALU.mult), R=[s_.b, ps2.b], W=[s_.b])
            fw.op(fw.DVE, lambda: nc.vector.tensor_tensor(out=ht.t[:, m, :], in0=ht.t[:, m, :], in1=s_.t[:], op=ALU.add), R=[ht.b, s_.b], W=[ht.b])
        fw.dma(fw.GD, out=cx.HT.t[:, tsl].rearrange("(j p) t -> p j t", p=128), in_=ht.t[:], R=[ht.b], W=[cx.HT.cb[c]])


def host_consts():
    C = {}
    C["ident_f"] = np.eye(128, dtype=np.float32)
    C["ones_f"] = np.ones((128, 128), dtype=np.float32)
    inv = 10000.0 ** (-np.arange(0, 32, 2, dtype=np.float32) / 32)
    ang = (np.arange(L, dtype=np.float32)[:, None] * inv[None, :]).astype(np.float32)
    cos, sin = np.cos(ang).T.astype(np.float32), np.sin(ang).T.astype(np.float32)
    sq = np.float32(96 ** -0.5)
    C["cs_q"] = np.stack([np.tile(cos, (4, 1)) * sq, np.tile(sin, (4, 1)) * sq]).astype(np.float32)
    C["cs_k"] = np.stack([cos, sin]).astype(np.float32)
    sel = np.zeros((65, 64), np.float32)
    sel[64, :] = 1.0
    C["sel"] = sel
    bf = ml_dtypes.bfloat16
    N = 2 * L
    n1 = np.arange(128, dtype=np.float64)
    k1 = np.arange(NK1, dtype=np.float64)
    a1 = 2 * np.pi * np.outer(n1, k1) / 128.0
    Fr, Fi = np.cos(a1), -np.sin(a1)
    C["fftF4"] = np.concatenate([Fr, Fi, -Fi, Fr], axis=1).astype(bf)
    n2 = np.arange(128, dtype=np.float64)
    k2 = np.arange(128, dtype=np.float64)
    kk = k1[None, :, None] + 128.0 * k2[None, None, :]
    ag = 2 * np.pi * ((n2[:, None, None] * kk) % N) / N
    C["fftG"] = np.stack([np.cos(ag), -np.sin(ag)], axis=2).astype(bf)
    ai_ = 2 * np.pi * np.outer(k2, n2) / 128.0
    Fc, Fs = np.cos(ai_), np.sin(ai_)
    C["fftFI"] = np.stack([np.concatenate([Fc, Fs], axis=1), np.concatenate([-Fs, Fc], axis=1)], axis=1).astype(bf)
    wk = np.where((k1 == 0) | (k1 == 64), 1.0, 2.0) / N
    nn = 128.0 * np.arange(64, dtype=np.float64)[None, None, :] + n2[None, :, None]
    aq = 2 * np.pi * ((k1[:, None, None] * nn) % N) / N
    C["fftQ"] = np.stack([np.cos(aq) * wk[:, None, None], -np.sin(aq) * wk[:, None, None]], axis=2).astype(bf)
    n = np.arange(N)
    pos = np.where(n < L, n, N - n).astype(np.float64)
    t = (pos.astype(np.float32) / np.float32(L)).astype(np.float32)
    bands = np.arange(1, 17, dtype=np.float32)
    ang2 = (np.float32(2.0 * math.pi) * t[:, None] * bands[None, :]).astype(np.float32)
    C["featsT"] = np.ascontiguousarray(np.concatenate([t[:, None], np.cos(ang2), np.sin(ang2)], axis=-1).T.astype(np.float32))
    C["tpos"] = np.ascontiguousarray(t.reshape(128, 128))
    C["tau"] = np.tile(np.arange(512, dtype=np.float32)[None, :], (128, 1))
    slopes = [2.0 ** (-8.0 * (h + 1) / 4) for h in range(4)]
    p = np.arange(128, dtype=np.float64)[:, None]
    dl = np.arange(-63, 64, dtype=np.float64)[None, :]
    f = np.arange(512, dtype=np.float64)
    bt, dg, flr = [], [], []
    for sl in slopes:
        left = sl * (p + 128 * dl)
        right = -sl * (p + 128 * dl - 511)
        bt.append(np.where(dl < 0, left, np.where(dl >= 4, right, 0.0)))
        dg.append(np.stack([-sl * np.abs(f[None, :] - 128 * d_ - p) for d_ in range(4)]))
        flr.append(np.stack([np.tile(np.exp(-sl * f)[None, :], (65, 1)), np.tile(np.exp(-sl * (511 - f))[None, :], (65, 1))]))
    C["da_bt"] = np.stack(bt).astype(np.float32)
    C["da_dg"] = np.stack(dg).astype(np.float32)
    C["da_flr"] = np.stack(flr).astype(np.float32)
    return C


PER_LAYER = ["g_mix", "w_in", "w_out", "g_branch", "hy_conv_w", "hy_conv_b", "hy_f_w1", "hy_f_b1", "hy_f_w2", "hy_f_b2", "hy_f_w3",
             "hy_log_decay", "hy_skip", "s5_a_re", "s5_a_im", "s5_log_dt", "s5_b_re", "s5_b_im", "s5_c_re", "s5_c_im", "s5_d",
             "s5_w_glu", "s5_b_glu", "da_lambda", "da_g_head", "mla_g_q", "mla_g_kv", "mla_w_uq", "mla_w_ukv", "g_ffn", "w_gate",
             "w_up", "w_down", "g_ple", "w_ple_gate", "w_ple"]


def pj(v, kt):
    v = np.asarray(v, dtype=np.float32).reshape(-1)
    out = np.zeros((kt * 128,), np.float32)
    out[:v.shape[0]] = v
    return np.ascontiguousarray(out.reshape(kt, 128).T)


def host_layout(inputs, nl):
    Wd = {}
    for li in range(nl):
        g = lambda n: np.asarray(inputs[n][li], dtype=np.float32)
        Wd[f"g_mix__{li}"] = pj(g("g_mix"), 8)
        Wd[f"w_in__{li}"] = g("w_in")
        Wd[f"mla_g_q__{li}"] = pj(g("mla_g_q"), 2)
        Wd[f"mla_g_kv__{li}"] = pj(g("mla_g_kv"), 1)
        Wd[f"mla_w_uq__{li}"] = g("mla_w_uq")
        Wd[f"mla_w_ukv__{li}"] = g("mla_w_ukv")
        Wd[f"da_lambda__{li}"] = g("da_lambda")
        def s5l(a):
            a = np.asarray(a, np.float32)
            rest = a.shape[2:]
            a = a.reshape((2, 8, 128) + rest)
            a = np.moveaxis(a, 2, 0)
            return np.ascontiguousarray(a.reshape((128, 16) + rest))
        Wd[f"s5_ar__{li}"] = s5l(g("s5_a_re").reshape(2, 1024))
        Wd[f"s5_ai__{li}"] = s5l(g("s5_a_im").reshape(2, 1024))
        Wd[f"s5_ldt__{li}"] = s5l(np.repeat(g("s5_log_dt"), 64, axis=1))
        Wd[f"s5_b__{li}"] = s5l(np.stack([g("s5_b_re").reshape(2, 1024, 16), g("s5_b_im").reshape(2, 1024, 16)], axis=2))
        cre = np.transpose(g("s5_c_re"), (0, 1, 3, 2)).reshape(2, 1024, 16)
        cim = np.transpose(g("s5_c_im"), (0, 1, 3, 2)).reshape(2, 1024, 16)
        Wd[f"s5_c__{li}"] = s5l(np.stack([cre, cim], axis=2))
        Wd[f"s5_d__{li}"] = np.ascontiguousarray(g("s5_d").reshape(8, 32).T)
        Wd[f"s5_w_glu__{li}"] = g("s5_w_glu")
        Wd[f"s5_b_glu__{li}"] = pj(g("s5_b_glu"), 2)
        Wd[f"hy_f_w1__{li}"] = g("hy_f_w1")
        Wd[f"hy_f_w2__{li}"] = g("hy_f_w2")
        Wd[f"hy_f_b12__{li}"] = np.ascontiguousarray(np.stack([g("hy_f_b1"), g("hy_f_b2")], axis=1))
        Wd[f"hy_f_w3__{li}"] = g("hy_f_w3")
        Wd[f"hy_log_decay__{li}"] = g("hy_log_decay")
        Wd[f"hy_skip__{li}"] = g("hy_skip")
        cwb = np.concatenate([g("hy_conv_w"), g("hy_conv_b")[None, :]], axis=0)
        Wd[f"hy_conv__{li}"] = np.ascontiguousarray(cwb.T.reshape(6, 128, 4).transpose(1, 0, 2))
        gbr = g("g_branch")
        Wd[f"g_branch__{li}"] = pj(np.concatenate([gbr[0], gbr[1], np.ones(256, np.float32), gbr[2]]), 8)
        Wd[f"w_out__{li}"] = g("w_out")
        Wd[f"g_ffn__{li}"] = pj(g("g_ffn"), 8)
        Wd[f"w_gate__{li}"] = g("w_gate")
        Wd[f"w_up__{li}"] = g("w_up")
        Wd[f"w_down__{li}"] = g("w_down")
        Wd[f"g_ple__{li}"] = pj(g("g_ple"), 8)
        Wd[f"w_ple_gate__{li}"] = g("w_ple_gate")
        Wd[f"w_ple__{li}"] = g("w_ple")
        Wd[f"da_g_head__{li}"] = g("da_g_head").reshape(64, 1)
    Wd["g_final"] = pj(inputs["g_final"], 8)
    return Wd


def build(nl, Wd_shapes, C_shapes, stages, dbg=()):
    nc = bass.Bass("TRN2", target_bir_lowering=False)
    cx = Ctx()
    cx.nc = nc
    x_ap = nc.dram_tensor("x", [L, D], F32, kind="ExternalInput").ap()
    p_ap = nc.dram_tensor("p", [nl, L, 256], F32, kind="ExternalInput").ap()
    out_ap = nc.dram_tensor("out", [L, D], F32, kind="ExternalOutput").ap()
    WA = {k: nc.dram_tensor(k, list(shp), F32, kind="ExternalInput").ap() for k, shp in Wd_shapes.items()}
    cx.C = {k: nc.dram_tensor("c_" + k, list(shp), dt, kind="ExternalInput").ap() for k, (shp, dt) in C_shapes.items()}
    with ExitStack() as es:
        cx.es = es
        cx.phase_es = None
        cx.tcount = 0
        cx.fw = fw = FW(nc, es)
        cx.ps = [Tl(es.enter_context(nc.psum_tensor(f"ps{i}", [128, 512], F32)), f"ps{i}") for i in range(8)]
        P = lambda n, shp, dt: Tl(es.enter_context(nc.sbuf_tensor(n, shp, dt)), n)
        cx.ones_f = P("ones_f", [128, 128], F32)
        cx.ident_f = P("ident_f", [128, 128], F32)
        cx.eps_t = P("eps_t", [128, 1], F32)
        cx.sq = [P(f"sq{i}", [128, CH], F32) for i in range(2)]
        cx.rstd = [P(f"rstd{i}", [128, CH], F32) for i in range(2)]
        cx.rstd_i = 0
        fw.dma(fw.SP, out=cx.ones_f.t[:], in_=cx.C["ones_f"], W=[cx.ones_f.b])
        fw.dma(fw.SP, out=cx.ident_f.t[:], in_=cx.C["ident_f"], W=[cx.ident_f.b])
        fw.op(fw.DVE, lambda: nc.vector.memset(cx.eps_t.t[:], EPS), W=[cx.eps_t.b])
        cx.OUTB = [Buf() for _ in range(NCH)]
        cx.HT = dram(cx, "HT", [D, L], F32)
        sc = cx.sc = {"HT": cx.HT}
        sc["ZHY"] = dram(cx, "ZHY", [768, L], F32)
        sc["ZS5"] = dram(cx, "ZS5", [256, L], F32)
        sc["QDA"] = dram(cx, "QDA", [256, L], BF16)
        sc["KDA"] = dram(cx, "KDA", [256, L], BF16)
        sc["VDA"] = dram(cx, "VDA", [L, 256], BF16)
        sc["QMLA"] = dram(cx, "QMLA", [4, 96, L], BF16)
        sc["KMLA"] = dram(cx, "KMLA", [4, 96, L], BF16)
        sc["VMLA"] = dram(cx, "VMLA", [L, 256], BF16)
        sc["YMLA"] = dram(cx, "YMLA", [256, L], F32)
        sc["YHY"] = dram(cx, "YHY", [256, L], F32)
        sc["YS5"] = dram(cx, "YS5", [256, L], F32)
        sc["GS5"] = dram(cx, "GS5", [256, L], F32)
        sc["ZC"] = dram(cx, "ZC", [768, L], F32)
        sc["HSPEC"] = dram(cx, "HSPEC", [16, 128, NK1 * 2 * HC], BF16)
        sc["RN"] = dram(cx, "RN", [128, 512], F32)
        sc["YDA"] = dram(cx, "YDA", [256, L], F32)
        cx.slopes = [2.0 ** (-8.0 * (h + 1) / 4) for h in range(4)]
        cx.window = None
        cx.dbgt = 'dbgt' in stages
        cx.hy_sc_only = 'hysc' in stages
        cx.hy_lim = 8
        cx.hy_stop = 9
        cx.hy_same = 'hysame' in stages
        cx.hy_g0 = 1 if 'hyg1' in stages else 0
        for st_ in stages:
            if st_.startswith('hystop'):
                cx.hy_stop = int(st_[6:])
                cx.hy_lim = 1
            if st_.startswith('hylim'):
                cx.hy_lim = int(st_[5:])
        if "x0" in stages:
            phase_x0(cx, x_ap)
        for li in range(nl):
            W = {k.rsplit("__", 1)[0]: v for k, v in WA.items() if k.endswith(f"__{li}")}
            if "a" in stages:
                phase_a(cx, W, li)
            if "mla" in stages:
                phase_mla(cx)
            if "da" in stages:
                phase_da(cx, W, li)
            if "s5" in stages:
                phase_s5(cx, W, li)
            if "hyf" in stages:
                phase_hy_filter(cx, W, li)
            if "hy" in stages:
                phase_hy(cx, W, li)
            if "inj" in stages:
                new_phase(cx)
                for nm in ["YHY", "YS5"]:
                    src = nc.dram_tensor("inj_" + nm, [256, L], F32, kind="ExternalInput").ap()
                    fw.dma(fw.SP, out=sc[nm].t, in_=src)
            if "f1" in stages:
                phase_f1(cx, W, li)
            if "f2" in stages:
                phase_f2(cx, W, li)
            if "f3" in stages:
                phase_f3(cx, W, li, p_ap)
        if "out" in stages:
            phase_out(cx, WA["g_final"], out_ap)
        new_phase(cx)
        for name in dbg:
            src = sc[name].t
            dst = nc.dram_tensor("dbg_" + name, list(src.shape), src.dtype, kind="ExternalOutput").ap()
            fw.dma(fw.SP, out=dst, in_=src)
        fw.final_wait()
        fw.check_deadlock()
        cx.phase_es.close()
    print("instructions:", fw.n_inst)
    return nc


ALL_STAGES = ["x0", "a", "mla", "da", "s5", "hyf", "hy", "f1", "f2", "f3", "out"]
_CACHE = {}


def kernel(**inputs):
    nl = 4
    nb = 8
    inputs = {k: np.asarray(v) for k, v in inputs.items()}
    Wd = host_layout(inputs, nl)
    C = host_consts()
    key = "prog"
    if key not in _CACHE:
        _CACHE[key] = build(nl, {k: v.shape for k, v in Wd.items()},
                            {k: (v.shape, BF16 if v.dtype == ml_dtypes.bfloat16 else F32) for k, v in C.items()}, ALL_STAGES)
    nc = _CACHE[key]
    x = np.asarray(inputs["x"], dtype=np.float32)
    p = np.asarray(inputs["p"], dtype=np.float32)
    in_maps = []
    for b in range(nb):
        m = {"x": np.ascontiguousarray(x[b]), "p": np.ascontiguousarray(p[:, b])}
        m.update(Wd)
        m.update({"c_" + k: v for k, v in C.items()})
        in_maps.append(m)
    res = run_bass_kernel_spmd(nc, in_maps, core_ids=list(range(nb)))
    return np.stack([np.asarray(res.results[b]["out"], dtype=np.float32) for b in range(nb)], axis=0)
```

```python
import math
from contextlib import ExitStack
import numpy as np
import ml_dtypes
import concourse.bass as bass
import concourse.mybir as mybir
from concourse.bass_utils import run_bass_kernel_spmd

F32 = mybir.dt.float32
BF16 = mybir.dt.bfloat16
AF = mybir.ActivationFunctionType
ALU = mybir.AluOpType
AX = mybir.AxisListType


class Buf:
    __slots__ = ("lw", "rd", "name")

    def __init__(self, name=""):
        self.lw = None
        self.rd = {}
        self.name = name


class Stream:
    def __init__(self, eng):
        self.eng = eng
        self.seen = {}
        self.ev = []


class CQ:
    def __init__(self, stream, sem):
        self.stream = stream
        self.sem = sem
        self.count = 0
        self.is_dma = False


class DQ:
    def __init__(self, stream, sems):
        self.stream = stream
        self.sems = sems
        self.cnt = [0] * len(sems)
        self.idx = 0
        self.is_dma = True
        self.outst = []


class FW:
    def __init__(self, nc, es):
        self.nc = nc
        self.es = es
        S = lambda n: es.enter_context(nc.semaphore(n))
        self.s_pe = Stream(nc.tensor)
        self.s_act = Stream(nc.scalar)
        self.s_dve = Stream(nc.vector)
        self.s_pool = Stream(nc.gpsimd)
        self.s_sp = Stream(nc.sync)
        self.streams = [self.s_pe, self.s_act, self.s_dve, self.s_pool, self.s_sp]
        self.PE = CQ(self.s_pe, S("q_pe"))
        self.ACT = CQ(self.s_act, S("q_act"))
        self.DVE = CQ(self.s_dve, S("q_dve"))
        self.POOL = CQ(self.s_pool, S("q_pool"))
        self.SP = DQ(self.s_sp, [S(f"q_sp{i}") for i in range(12)])
        self.GD = DQ(self.s_pool, [S(f"q_gd{i}") for i in range(12)])
        self.AD = DQ(self.s_act, [S(f"q_ad{i}") for i in range(4)])
        self.cqs = [self.PE, self.ACT, self.DVE, self.POOL]
        self.dqs = [self.SP, self.GD, self.AD]
        self.n_inst = 0

    def _wait(self, stream, deps, own=None):
        need = {}
        for d in deps:
            if d is None:
                continue
            (sem, val, q), raw = d
            if own is not None and q is own and not own.is_dma:
                if not raw or own is self.PE:
                    continue
            k = id(sem)
            if stream.seen.get(k, 0) >= val:
                continue
            if k not in need or need[k][1] < val:
                need[k] = (sem, val)
        for k, (sem, val) in need.items():
            stream.eng.wait_ge(sem, val)
            stream.seen[k] = val
            self.n_inst += 1
            stream.ev.append(("w", k, val))

    @staticmethod
    def _deps(R, W):
        deps = []
        for b in R:
            if b.lw is not None:
                deps.append((b.lw, True))
        for b in W:
            if b.lw is not None:
                deps.append((b.lw, False))
            deps.extend((t, False) for t in b.rd.values())
        return deps

    @staticmethod
    def _mark(tok, R, W):
        k = id(tok[0])
        for b in R:
            o = b.rd.get(k)
            if o is None or o[1] < tok[1]:
                b.rd[k] = tok
        for b in W:
            b.lw = tok
            b.rd = {}

    def op(self, q, fn, R=(), W=(), inc=True):
        self._wait(q.stream, self._deps(R, W), own=q)
        inst = fn()
        self.n_inst += 1
        if inc:
            inst.then_inc(q.sem, 1)
            q.count += 1
            tok = (q.sem, q.count, q)
            q.stream.ev.append(("i", id(q.sem), 1))
        else:
            tok = (q.sem, q.count + 1, q)
        self._mark(tok, R, W)
        return inst

    def dma(self, q, out, in_, R=(), W=(), **kw):
        st = q.stream
        self._wait(st, self._deps(R, W), own=None)
        nd = 1
        for d_ in tuple(out.shape)[:-1]:
            nd *= int(d_)
        LIM = 1536
        q.outst = [o for o in q.outst if st.seen.get(id(o[0]), 0) < o[1]]
        while q.outst and sum(o[2] for o in q.outst) + nd > LIM:
            osem, oval, _ = q.outst.pop(0)
            if st.seen.get(id(osem), 0) < oval:
                st.eng.wait_ge(osem, oval)
                st.seen[id(osem)] = oval
                st.ev.append(("w", id(osem), oval))
        slot = q.idx % len(q.sems)
        q.idx += 1
        sem = q.sems[slot]
        if q.cnt[slot] > 0 and st.seen.get(id(sem), 0) < q.cnt[slot]:
            st.eng.wait_ge(sem, q.cnt[slot])
            st.seen[id(sem)] = q.cnt[slot]
            st.ev.append(("w", id(sem), q.cnt[slot]))
        inst = st.eng.dma_start(out=out, in_=in_, **kw)
        inst.then_inc(sem, 16)
        st.ev.append(("i", id(sem), 16))
        self.n_inst += 1
        q.cnt[slot] += 16
        tok = (sem, q.cnt[slot], q)
        q.outst.append((sem, q.cnt[slot], nd))
        self._mark(tok, R, W)
        return inst

    def barrier(self, streams=None):
        deps = []
        for q in self.cqs:
            if q.count > 0:
                deps.append(((q.sem, q.count, q), True))
        for q in self.dqs:
            for s, c in zip(q.sems, q.cnt):
                if c > 0:
                    deps.append(((s, c, q), True))
        for st in (streams or self.streams):
            self._wait(st, deps, own=None)

    def check_deadlock(self):
        vals = {}
        pos = [0] * len(self.streams)
        progress = True
        while progress:
            progress = False
            for si, st in enumerate(self.streams):
                while pos[si] < len(st.ev):
                    kind, k, v = st.ev[pos[si]]
                    if kind == "w":
                        if vals.get(k, 0) >= v:
                            pos[si] += 1
                            progress = True
                        else:
                            break
                    else:
                        vals[k] = vals.get(k, 0) + v
                        pos[si] += 1
                        progress = True
        stuck = [(si, pos[si], len(st.ev)) for si, st in enumerate(self.streams) if pos[si] < len(st.ev)]
        if stuck:
            for si, p, n in stuck:
                kind, k, v = self.streams[si].ev[p]
                print("DEADLOCK stream", si, "at", p, "/", n, "waiting sem", k, "val", v, "cur", vals.get(k, 0))
            raise RuntimeError("deadlock detected in emitted program")
        return True

    def final_wait(self):
        self.barrier(streams=[self.s_sp])


L = 8192
D = 1024
NCH = 16
CH = 512
IN_W = 2144
FFN = 2816
EPS = 1e-6


class Tl:
    __slots__ = ("t", "b", "cb")

    def __init__(self, t, name=""):
        self.t = t
        self.b = Buf(name)


class Ctx:
    pass


def new_phase(cx):
    cx.fw.barrier()
    if cx.phase_es is not None:
        cx.phase_es.close()
    cx.phase_es = ExitStack()
    cx.es.callback(lambda e=cx.phase_es: e.close())
    cx.tcount += 1000


def SB(cx, shape, dt, name=None):
    cx.tcount += 1
    nm = f"{name or 't'}_{cx.tcount}"
    return Tl(cx.phase_es.enter_context(cx.nc.sbuf_tensor(nm, list(shape), dt)), nm)


def dram(cx, name, shape, dt):
    t = cx.nc.dram_tensor(name, list(shape), dt, kind="Internal").ap()
    tl = Tl(t, name)
    tl.cb = [Buf(name + str(i)) for i in range(NCH)]
    return tl


def rstd_chunk(cx, parts, dim, n=CH):
    fw, nc = cx.fw, cx.nc
    ps = cx.ps[0]
    np_ = len(parts)
    for i, (ap, K, bufs) in enumerate(parts):
        sq = cx.sq[i % 2]
        fw.op(fw.ACT, lambda: nc.scalar.activation(out=sq.t[0:K, 0:n], in_=ap, func=AF.Square), R=bufs, W=[sq.b])
        fw.op(fw.PE, lambda: nc.tensor.matmul(ps.t[:, 0:n], lhsT=cx.ones_f.t[0:K, :], rhs=sq.t[0:K, 0:n], start=(i == 0), stop=(i == np_ - 1)),
              R=[sq.b, cx.ones_f.b], W=[ps.b], inc=True)
    r = cx.rstd[cx.rstd_i % 2]
    cx.rstd_i += 1
    fw.op(fw.ACT, lambda: nc.scalar.activation(out=r.t[:, 0:n], in_=ps.t[:, 0:n], func=AF.Sqrt, bias=cx.eps_t.t[:, 0:1], scale=1.0 / dim), R=[ps.b, cx.eps_t.b], W=[r.b])
    fw.op(fw.DVE, lambda: nc.vector.reciprocal(out=r.t[:, 0:n], in_=r.t[:, 0:n]), R=[r.b], W=[r.b])
    return r


def evac(cx, k, out_ap, in_ap, R, W, scale=None):
    fw, nc = cx.fw, cx.nc
    if k % 2 == 0:
        if scale is None:
            fw.op(fw.ACT, lambda: nc.scalar.copy(out=out_ap, in_=in_ap), R=R, W=W)
        else:
            fw.op(fw.ACT, lambda: nc.scalar.mul(out=out_ap, in_=in_ap, mul=scale), R=R, W=W)
    else:
        if scale is None:
            fw.op(fw.DVE, lambda: nc.vector.tensor_copy(out=out_ap, in_=in_ap), R=R, W=W)
        else:
            fw.op(fw.DVE, lambda: nc.vector.tensor_scalar(out=out_ap, in0=in_ap, scalar1=scale, scalar2=None, op0=ALU.mult), R=R, W=W)


def load_w_bf16(cx, dst, src_ap, kt, ncols, gain=None, stg_cols=2144):
    fw, nc = cx.fw, cx.nc
    for j in range(kt):
        stg = cx.wstg[j % 2]
        fw.dma(fw.SP, out=stg.t[:, 0:ncols], in_=src_ap[j * 128:(j + 1) * 128, :], W=[stg.b])
        if gain is None:
            evac(cx, j, dst.t[:, j, :], stg.t[:, 0:ncols], R=[stg.b], W=[dst.b])
        else:
            if j % 2 == 0:
                fw.op(fw.ACT, lambda: nc.scalar.mul(out=dst.t[:, j, :], in_=stg.t[:, 0:ncols], mul=gain.t[:, j:j + 1]), R=[stg.b, gain.b], W=[dst.b])
            else:
                fw.op(fw.DVE, lambda: nc.vector.tensor_scalar(out=dst.t[:, j, :], in0=stg.t[:, 0:ncols], scalar1=gain.t[:, j:j + 1], scalar2=None, op0=ALU.mult),
                      R=[stg.b, gain.b], W=[dst.b])


def phase_x0(cx, x_ap):
    fw, nc = cx.fw, cx.nc
    new_phase(cx)
    xts = [SB(cx, [128, 4, D], F32, "xt") for _ in range(2)]
    hts = [SB(cx, [128, 8, CH], F32, "ht") for _ in range(2)]
    k = 0
    for c in range(NCH):
        xt, ht = xts[c % 2], hts[c % 2]
        fw.dma(fw.SP, out=xt.t[:], in_=x_ap[c * CH:(c + 1) * CH, :].rearrange("(s p) d -> p s d", p=128), W=[xt.b])
        for j in range(8):
            ps = cx.ps[2 + (k % 6)]
            for s in range(4):
                fw.op(fw.PE, lambda: nc.tensor.transpose(out=ps.t[:, s * 128:(s + 1) * 128], in_=xt.t[:, s, j * 128:(j + 1) * 128], identity=cx.ident_f.t[:]),
                      R=[xt.b, cx.ident_f.b], W=[ps.b], inc=(s == 3))
            evac(cx, k, ht.t[:, j, :], ps.t[:], R=[ps.b], W=[ht.b])
            k += 1
        fw.dma(fw.GD, out=cx.HT.t[:, c * CH:(c + 1) * CH].rearrange("(j p) t -> p j t", p=128), in_=ht.t[:], R=[ht.b], W=[cx.HT.cb[c]])


def phase_out(cx, gfin_ap, out_ap):
    fw, nc = cx.fw, cx.nc
    new_phase(cx)
    gf = SB(cx, [128, 8], F32, "gf")
    fw.dma(fw.SP, out=gf.t[:], in_=gfin_ap, W=[gf.b])
    hts = [SB(cx, [128, 8, CH], F32, "ht") for _ in range(2)]
    ots = [SB(cx, [128, 4, D], F32, "ot") for _ in range(2)]
    k = 0
    for c in range(NCH):
        ht, ot = hts[c % 2], ots[c % 2]
        fw.dma(fw.SP, out=ht.t[:], in_=cx.HT.t[:, c * CH:(c + 1) * CH].rearrange("(j p) t -> p j t", p=128), R=[cx.HT.cb[c]], W=[ht.b])
        r = rstd_chunk(cx, [(ht.t[:, j, :], 128, [ht.b]) for j in range(8)], D)
        for j in range(8):
            fw.op(fw.DVE, lambda: nc.vector.scalar_tensor_tensor(out=ht.t[:, j, :], in0=ht.t[:, j, :], scalar=gf.t[:, j:j + 1], in1=r.t[:],
                                                                 op0=ALU.mult, op1=ALU.mult), R=[ht.b, r.b, gf.b], W=[ht.b])
        for s in range(4):
            for jj in range(2):
                ps = cx.ps[2 + (k % 6)]
                for j4 in range(4):
                    j = jj * 4 + j4
                    fw.op(fw.PE, lambda: nc.tensor.transpose(out=ps.t[:, j4 * 128:(j4 + 1) * 128], in_=ht.t[:, j, s * 128:(s + 1) * 128], identity=cx.ident_f.t[:]),
                          R=[ht.b, cx.ident_f.b], W=[ps.b], inc=(j4 == 3))
                evac(cx, k, ot.t[:, s, jj * 512:(jj + 1) * 512], ps.t[:], R=[ps.b], W=[ot.b])
                k += 1
        fw.dma(fw.GD, out=out_ap[c * CH:(c + 1) * CH, :].rearrange("(s p) d -> p s d", p=128), in_=ot.t[:], R=[ot.b], W=[cx.OUTB[c]])


def phase_a(cx, W, li):
    fw, nc = cx.fw, cx.nc
    new_phase(cx)
    sc = cx.sc
    cx.wstg = [SB(cx, [128, IN_W], F32, "wstg") for _ in range(2)]
    gmix = SB(cx, [128, 8], F32, "gmix")
    fw.dma(fw.SP, out=gmix.t[:], in_=W["g_mix"], W=[gmix.b])
    win = SB(cx, [128, 8, IN_W], BF16, "win")
    load_w_bf16(cx, win, W["w_in"], 8, IN_W, gain=gmix)
    gq = SB(cx, [128, 2], F32, "gq")
    fw.dma(fw.SP, out=gq.t[:], in_=W["mla_g_q"], W=[gq.b])
    gkv = SB(cx, [128, 1], F32, "gkv")
    fw.dma(fw.SP, out=gkv.t[:], in_=W["mla_g_kv"], W=[gkv.b])
    wuq = SB(cx, [128, 2, 384], BF16, "wuq")
    wuq_src = W["mla_w_uq"]
    for kt, (r0, rn) in enumerate([(0, 128), (128, 64)]):
        stg = cx.wstg[kt % 2]
        src = wuq_src[r0:r0 + rn, :].rearrange("r (h c) -> r h c", c=96)
        fw.dma(fw.SP, out=stg.t[0:rn, 0:256].rearrange("r (h c) -> r h c", c=64), in_=src[:, :, 0:64], W=[stg.b])
        fw.dma(fw.SP, out=stg.t[0:rn, 256:320].rearrange("r (h c) -> r h c", c=16), in_=src[:, :, 64:80], W=[stg.b])
        fw.dma(fw.SP, out=stg.t[0:rn, 320:384].rearrange("r (h c) -> r h c", c=16), in_=src[:, :, 80:96], W=[stg.b])
        fw.op(fw.DVE, lambda: nc.vector.tensor_scalar(out=wuq.t[0:rn, kt, :], in0=stg.t[0:rn, 0:384], scalar1=gq.t[0:rn, kt:kt + 1], scalar2=None, op0=ALU.mult),
              R=[stg.b, gq.b], W=[wuq.b])
    wukv = SB(cx, [128, 512], BF16, "wukv")
    stg = cx.wstg[0]
    src = W["mla_w_ukv"].rearrange("r (h c) -> r h c", c=128)
    fw.dma(fw.SP, out=stg.t[:, 0:256].rearrange("r (h c) -> r h c", c=64), in_=src[:, :, 0:64], W=[stg.b])
    fw.dma(fw.SP, out=stg.t[:, 256:512].rearrange("r (h c) -> r h c", c=64), in_=src[:, :, 64:128], W=[stg.b])
    fw.op(fw.DVE, lambda: nc.vector.tensor_scalar(out=wukv.t[:], in0=stg.t[:, 0:512], scalar1=gkv.t[:, 0:1], scalar2=None, op0=ALU.mult),
          R=[stg.b, gkv.b], W=[wukv.b])

    hts = [SB(cx, [128, 8, CH], F32, "ht") for _ in range(2)]
    aT = SB(cx, [128, 8, CH], BF16, "aT")
    zhy = [SB(cx, [128, 6, CH], F32, "zhy") for _ in range(2)]
    zs5 = [SB(cx, [128, 2, CH], F32, "zs5") for _ in range(2)]
    qda = [SB(cx, [128, 2, CH], BF16, "qda") for _ in range(2)]
    kda = [SB(cx, [128, 2, CH], BF16, "kda") for _ in range(2)]
    vda = [SB(cx, [128, 4, 256], BF16, "vda") for _ in range(2)]
    cq = SB(cx, [128, 2, CH], F32, "cq")
    ckv = SB(cx, [128, CH], F32, "ckv")
    cqn = SB(cx, [128, 2, CH], BF16, "cqn")
    ckvn = SB(cx, [128, CH], BF16, "ckvn")
    qn_st = [SB(cx, [128, 2, CH], BF16, "qnst") for _ in range(2)]
    kn_st = [SB(cx, [128, 2, CH], BF16, "knst") for _ in range(2)]
    vm_st = [SB(cx, [128, 4, 256], BF16, "vmst") for _ in range(2)]
    csq = [SB(cx, [64, 2, CH], F32, "csq") for _ in range(2)]
    csk = [SB(cx, [16, 2, CH], F32, "csk") for _ in range(2)]
    t1 = SB(cx, [64, CH], F32, "t1")
    t2 = SB(cx, [64, CH], F32, "t2")
    qr_st = [SB(cx, [64, 2, CH], BF16, "qrst") for _ in range(2)]
    kr_st = [SB(cx, [16, 2, CH], BF16, "krst") for _ in range(2)]

    kk = [0]

    def nextps():
        p = cx.ps[2 + (kk[0] % 6)]
        kk[0] += 1
        return p

    def mm_group(ps_ap, psb, lhs_list, rhs_list, Rb):
        n = len(lhs_list)
        for i in range(n):
            fw.op(fw.PE, lambda: nc.tensor.matmul(ps_ap, lhsT=lhs_list[i], rhs=rhs_list[i], start=(i == 0), stop=(i == n - 1)),
                  R=Rb, W=[psb], inc=(i == n - 1))

    for c in range(NCH):
        tsl = slice(c * CH, (c + 1) * CH)
        ht = hts[c % 2]
        fw.dma(fw.SP, out=ht.t[:], in_=cx.HT.t[:, tsl].rearrange("(j p) t -> p j t", p=128), R=[cx.HT.cb[c]], W=[ht.b])
        fw.dma(fw.SP, out=csq[c % 2].t[:], in_=cx.C["cs_q"][:, :, tsl].rearrange("a p t -> p a t"), W=[csq[c % 2].b])
        fw.dma(fw.SP, out=csk[c % 2].t[:], in_=cx.C["cs_k"][:, :, tsl].rearrange("a p t -> p a t"), W=[csk[c % 2].b])
        r = rstd_chunk(cx, [(ht.t[:, j, :], 128, [ht.b]) for j in range(8)], D)
        fw.op(fw.DVE, lambda: nc.vector.tensor_tensor(out=aT.t[:], in0=ht.t[:], in1=r.t[:].unsqueeze(1).broadcast_to([128, 8, CH]), op=ALU.mult),
              R=[ht.b, r.b], W=[aT.b])
        rhs8 = [aT.t[:, j, :] for j in range(8)]

        def colmm(c0, m):
            ps = nextps()
            mm_group(ps.t[0:m, :], ps.b, [win.t[:, j, c0:c0 + m] for j in range(8)], rhs8, [win.b, aT.b])
            return ps
        for i in range(6):
            ps = colmm(i * 128, 128)
            evac(cx, kk[0], zhy[c % 2].t[:, i, :], ps.t[:], R=[ps.b], W=[zhy[c % 2].b])
        fw.dma(fw.GD, out=sc["ZHY"].t[:, tsl].rearrange("(j p) t -> p j t", p=128), in_=zhy[c % 2].t[:], R=[zhy[c % 2].b], W=[sc["ZHY"].cb[c]])
        for i in range(2):
            ps = colmm(768 + i * 128, 128)
            evac(cx, kk[0], zs5[c % 2].t[:, i, :], ps.t[:], R=[ps.b], W=[zs5[c % 2].b])
        fw.dma(fw.GD, out=sc["ZS5"].t[:, tsl].rearrange("(j p) t -> p j t", p=128), in_=zs5[c % 2].t[:], R=[zs5[c % 2].b], W=[sc["ZS5"].cb[c]])
        for i in range(2):
            ps = colmm(1024 + i * 128, 128)
            evac(cx, kk[0], qda[c % 2].t[:, i, :], ps.t[:], R=[ps.b], W=[qda[c % 2].b], scale=32 ** -0.5)
        fw.dma(fw.GD, out=sc["QDA"].t[:, tsl].rearrange("(j p) t -> p j t", p=128), in_=qda[c % 2].t[:], R=[qda[c % 2].b], W=[sc["QDA"].cb[c]])
        for i in range(2):
            ps = colmm(1280 + i * 128, 128)
            evac(cx, kk[0], kda[c % 2].t[:, i, :], ps.t[:], R=[ps.b], W=[kda[c % 2].b])
        fw.dma(fw.GD, out=sc["KDA"].t[:, tsl].rearrange("(j p) t -> p j t", p=128), in_=kda[c % 2].t[:], R=[kda[c % 2].b], W=[sc["KDA"].cb[c]])
        for s in range(4):
            ps = nextps()
            mm_group(ps.t[:, 0:256], ps.b, [aT.t[:, j, s * 128:(s + 1) * 128] for j in range(8)], [win.t[:, j, 1536:1792] for j in range(8)], [win.b, aT.b])
            evac(cx, kk[0], vda[c % 2].t[:, s, :], ps.t[:, 0:256], R=[ps.b], W=[vda[c % 2].b])
        fw.dma(fw.GD, out=sc["VDA"].t[tsl, :].rearrange("(s p) d -> p s d", p=128), in_=vda[c % 2].t[:], R=[vda[c % 2].b], W=[sc["VDA"].cb[c]])
        ps = colmm(1792, 128)
        evac(cx, kk[0], cq.t[:, 0, :], ps.t[:], R=[ps.b], W=[cq.b])
        ps = colmm(1920, 64)
        evac(cx, kk[0], cq.t[0:64, 1, :], ps.t[0:64, :], R=[ps.b], W=[cq.b])
        ps = colmm(1984, 128)
        evac(cx, kk[0], ckv.t[:], ps.t[:], R=[ps.b], W=[ckv.b])
        rq = rstd_chunk(cx, [(cq.t[:, 0, :], 128, [cq.b]), (cq.t[0:64, 1, :], 64, [cq.b])], 192)
        fw.op(fw.DVE, lambda: nc.vector.tensor_tensor(out=cqn.t[:, 0, :], in0=cq.t[:, 0, :], in1=rq.t[:], op=ALU.mult), R=[cq.b, rq.b], W=[cqn.b])
        fw.op(fw.DVE, lambda: nc.vector.tensor_tensor(out=cqn.t[0:64, 1, :], in0=cq.t[0:64, 1, :], in1=rq.t[0:64, :], op=ALU.mult), R=[cq.b, rq.b], W=[cqn.b])
        rkv = rstd_chunk(cx, [(ckv.t[:], 128, [ckv.b])], 128)
        fw.op(fw.DVE, lambda: nc.vector.tensor_tensor(out=ckvn.t[:], in0=ckv.t[:], in1=rkv.t[:], op=ALU.mult), R=[ckv.b, rkv.b], W=[ckvn.b])
        qrhs = [cqn.t[:, 0, :], cqn.t[0:64, 1, :]]
        for i in range(2):
            ps = nextps()
            mm_group(ps.t[:, :], ps.b, [wuq.t[:, 0, i * 128:(i + 1) * 128], wuq.t[0:64, 1, i * 128:(i + 1) * 128]], qrhs, [wuq.b, cqn.b])
            evac(cx, kk[0], qn_st[c % 2].t[:, i, :], ps.t[:], R=[ps.b], W=[qn_st[c % 2].b], scale=96 ** -0.5)
        for h in range(4):
            fw.dma(fw.GD, out=sc["QMLA"].t[h, 0:64, tsl], in_=qn_st[c % 2].t[(h % 2) * 64:(h % 2) * 64 + 64, h // 2, :], R=[qn_st[c % 2].b], W=[sc["QMLA"].cb[c]])
        ps1 = nextps()
        mm_group(ps1.t[0:64, :], ps1.b, [wuq.t[:, 0, 256:320], wuq.t[0:64, 1, 256:320]], qrhs, [wuq.b, cqn.b])
        ps2 = nextps()
        mm_group(ps2.t[0:64, :], ps2.b, [wuq.t[:, 0, 320:384], wuq.t[0:64, 1, 320:384]], qrhs, [wuq.b, cqn.b])
        cs = csq[c % 2]
        qr = qr_st[c % 2]
        fw.op(fw.DVE, lambda: nc.vector.tensor_tensor(out=t1.t[:], in0=ps1.t[0:64, :], in1=cs.t[:, 0, :], op=ALU.mult), R=[ps1.b, cs.b], W=[t1.b])
        fw.op(fw.DVE, lambda: nc.vector.tensor_tensor(out=t2.t[:], in0=ps2.t[0:64, :], in1=cs.t[:, 1, :], op=ALU.mult), R=[ps2.b, cs.b], W=[t2.b])
        fw.op(fw.DVE, lambda: nc.vector.tensor_tensor(out=qr.t[:, 0, :], in0=t1.t[:], in1=t2.t[:], op=ALU.subtract), R=[t1.b, t2.b], W=[qr.b])
        fw.op(fw.DVE, lambda: nc.vector.tensor_tensor(out=t1.t[:], in0=ps2.t[0:64, :], in1=cs.t[:, 0, :], op=ALU.mult), R=[ps2.b, cs.b], W=[t1.b])
        fw.op(fw.DVE, lambda: nc.vector.tensor_tensor(out=t2.t[:], in0=ps1.t[0:64, :], in1=cs.t[:, 1, :], op=ALU.mult), R=[ps1.b, cs.b], W=[t2.b])
        fw.op(fw.DVE, lambda: nc.vector.tensor_tensor(out=qr.t[:, 1, :], in0=t1.t[:], in1=t2.t[:], op=ALU.add), R=[t1.b, t2.b], W=[qr.b])
        for h in range(4):
            fw.dma(fw.GD, out=sc["QMLA"].t[h, 64:80, tsl], in_=qr.t[h * 16:(h + 1) * 16, 0, :], R=[qr.b], W=[sc["QMLA"].cb[c]])
            fw.dma(fw.GD, out=sc["QMLA"].t[h, 80:96, tsl], in_=qr.t[h * 16:(h + 1) * 16, 1, :], R=[qr.b], W=[sc["QMLA"].cb[c]])
        for i in range(2):
            ps = nextps()
            mm_group(ps.t[:, :], ps.b, [wukv.t[:, i * 128:(i + 1) * 128]], [ckvn.t[:]], [wukv.b, ckvn.b])
            evac(cx, kk[0], kn_st[c % 2].t[:, i, :], ps.t[:], R=[ps.b], W=[kn_st[c % 2].b])
        for h in range(4):
            fw.dma(fw.GD, out=sc["KMLA"].t[h, 0:64, tsl], in_=kn_st[c % 2].t[(h % 2) * 64:(h % 2) * 64 + 64, h // 2, :], R=[kn_st[c % 2].b], W=[sc["KMLA"].cb[c]])
        for s in range(4):
            ps = nextps()
            mm_group(ps.t[:, 0:256], ps.b, [ckvn.t[:, s * 128:(s + 1) * 128]], [wukv.t[:, 256:512]], [wukv.b, ckvn.b])
            evac(cx, kk[0], vm_st[c % 2].t[:, s, :], ps.t[:, 0:256], R=[ps.b], W=[vm_st[c % 2].b])
        fw.dma(fw.GD, out=sc["VMLA"].t[tsl, :].rearrange("(s p) d -> p s d", p=128), in_=vm_st[c % 2].t[:], R=[vm_st[c % 2].b], W=[sc["VMLA"].cb[c]])
        ps1 = colmm(2112, 16)
        ps2 = colmm(2128, 16)
        ck = csk[c % 2]
        kr = kr_st[c % 2]
        fw.op(fw.DVE, lambda: nc.vector.tensor_tensor(out=t1.t[0:16, :], in0=ps1.t[0:16, :], in1=ck.t[:, 0, :], op=ALU.mult), R=[ps1.b, ck.b], W=[t1.b])
        fw.op(fw.DVE, lambda: nc.vector.tensor_tensor(out=t2.t[0:16, :], in0=ps2.t[0:16, :], in1=ck.t[:, 1, :], op=ALU.mult), R=[ps2.b, ck.b], W=[t2.b])
        fw.op(fw.DVE, lambda: nc.vector.tensor_tensor(out=kr.t[:, 0, :], in0=t1.t[0:16, :], in1=t2.t[0:16, :], op=ALU.subtract), R=[t1.b, t2.b], W=[kr.b])
        fw.op(fw.DVE, lambda: nc.vector.tensor_tensor(out=t1.t[0:16, :], in0=ps2.t[0:16, :], in1=ck.t[:, 0, :], op=ALU.mult), R=[ps2.b, ck.b], W=[t1.b])
        fw.op(fw.DVE, lambda: nc.vector.tensor_tensor(out=t2.t[0:16, :], in0=ps1.t[0:16, :], in1=ck.t[:, 1, :], op=ALU.mult), R=[ps1.b, ck.b], W=[t2.b])
        fw.op(fw.DVE, lambda: nc.vector.tensor_tensor(out=kr.t[:, 1, :], in0=t1.t[0:16, :], in1=t2.t[0:16, :], op=ALU.add), R=[t1.b, t2.b], W=[kr.b])
        for h in range(4):
            fw.dma(fw.GD, out=sc["KMLA"].t[h, 64:80, tsl], in_=kr.t[:, 0, :], R=[kr.b], W=[sc["KMLA"].cb[c]])
            fw.dma(fw.GD, out=sc["KMLA"].t[h, 80:96, tsl], in_=kr.t[:, 1, :], R=[kr.b], W=[sc["KMLA"].cb[c]])


def attn_load_qk(cx, at, i, QT_src, KT_src):
    fw = cx.fw
    QT, KT = at["QT"][i], at["KT"][i]
    for (ap, r0, nr) in QT_src:
        fw.dma(fw.SP, out=QT.t[r0:r0 + nr, :], in_=ap, W=[QT.b])
    for (ap, r0, nr) in KT_src:
        fw.dma(fw.SP, out=KT.t[r0:r0 + nr, :], in_=ap, W=[KT.b])


def attn_head(cx, at, qi, r0, V_src, dk, slope_idx, on_out, ki=None, kfull=None):
    fw, nc = cx.fw, cx.nc
    i = at["n"] % 2
    at["n"] += 1
    QT, KT, V1 = at["QT"][qi], at["KT"][qi if ki is None else ki], at["V1"][i]
    r1 = r0 + dk
    m0, m1 = (r0, r1) if kfull is None else (0, kfull)
    VALL = at["VALL"]
    fw.op(fw.POOL, lambda: nc.gpsimd.tensor_copy(out=V1.t[:, :, 0:64], in_=VALL.t[:, :, V_src * 64:(V_src + 1) * 64]), R=[VALL.b], W=[V1.b])
    fw.op(fw.POOL, lambda: nc.gpsimd.memset(V1.t[:, :, 64:65], 1.0), W=[V1.b])
    mx = at["mx"]
    for which, T_ in enumerate([QT, KT]):
        for c in range(NCH):
            sq = cx.sq[c % 2]
            ps = cx.ps[c % 2]
            fw.op(fw.ACT, lambda: nc.scalar.activation(out=sq.t[0:dk, :], in_=T_.t[r0:r1, c * CH:(c + 1) * CH], func=AF.Square), R=[T_.b], W=[sq.b])
            fw.op(fw.PE, lambda: nc.tensor.matmul(ps.t[:, :], lhsT=cx.ones_f.t[0:dk, :], rhs=sq.t[0:dk, :], start=True, stop=True), R=[sq.b, cx.ones_f.b], W=[ps.b])
            fw.op(fw.DVE, lambda: nc.vector.reduce_max(out=mx.t[:, which * 16 + c:which * 16 + c + 1], in_=ps.t[:, :], axis=AX.X), R=[ps.b], W=[mx.b])
    m2 = at["m2"]
    fw.op(fw.DVE, lambda: nc.vector.reduce_max(out=m2.t[:, 0:1], in_=mx.t[:, 0:16], axis=AX.X), R=[mx.b], W=[m2.b])
    fw.op(fw.DVE, lambda: nc.vector.reduce_max(out=m2.t[:, 1:2], in_=mx.t[:, 16:32], axis=AX.X), R=[mx.b], W=[m2.b])
    fw.op(fw.DVE, lambda: nc.vector.tensor_tensor(out=m2.t[:, 2:3], in0=m2.t[:, 0:1], in1=m2.t[:, 1:2], op=ALU.mult), R=[m2.b], W=[m2.b])
    fw.op(fw.ACT, lambda: nc.scalar.activation(out=m2.t[:, 3:4], in_=m2.t[:, 2:3], func=AF.Sqrt), R=[m2.b], W=[m2.b])
    negC = at["negC"][i]
    fw.op(fw.DVE, lambda: nc.vector.tensor_scalar(out=negC.t[:, 0:1], in0=m2.t[:, 3:4], scalar1=-1.0, scalar2=None, op0=ALU.mult), R=[m2.b], W=[negC.b])
    if slope_idx is not None:
        BT = at["BT"][i]
        fw.dma(fw.SP, out=BT.t[:], in_=cx.C["da_bt"][slope_idx], W=[BT.b])
        fw.op(fw.DVE, lambda: nc.vector.tensor_scalar(out=BT.t[:], in0=BT.t[:], scalar1=negC.t[:, 0:1], scalar2=None, op0=ALU.add), R=[BT.b, negC.b], W=[BT.b])
        DG, FLR = at["DG"][slope_idx % 2], at["FLR"][slope_idx % 2]
    for qc in range(NCH):
        qsl = slice(qc * CH, (qc + 1) * CH)
        aset = at["acc_i"] % 2
        at["acc_i"] += 1
        if slope_idx is None:
            accs = [None, cx.ps[4 + aset], None]
        else:
            accs = [cx.ps[4 + r] for r in range(3)]
        def region(kc):
            if slope_idx is None:
                return 1
            dl = kc - 4 * qc
            return 0 if dl < 0 else (1 if dl < 4 else 2)
        kcs = list(range(64))
        if slope_idx is not None and at.get("window") is not None:
            thr = at["window"] / cx.slopes[slope_idx]
            keep = []
            for kc in kcs:
                dl = kc - 4 * qc
                if dl < 0:
                    dmin = qc * CH - (kc * 128 + 127)
                elif dl >= 4:
                    dmin = kc * 128 - (qc * CH + 511)
                else:
                    dmin = 0
                if dmin <= thr:
                    keep.append(kc)
            kcs = keep
        regs = [region(kc) for kc in kcs]
        first = {}
        last = {}
        for idx, rg in enumerate(regs):
            first.setdefault(rg, idx)
            last[rg] = idx
        nst = at["nst"]

        def emit_qk(idx):
            kc = kcs[idx]
            st = cx.ps[at["st_i"] % nst]
            at["st_i"] += 1
            fw.op(fw.PE, lambda: nc.tensor.matmul(st.t[:, :], lhsT=KT.t[m0:m1, kc * 128:(kc + 1) * 128], rhs=QT.t[m0:m1, qsl], start=True, stop=True),
                  R=[KT.b, QT.b], W=[st.b])
            return st

        def emit_exp(idx, st):
            kc = kcs[idx]
            rg = regs[idx]
            pt = at["PT"][at["pt_i"] % len(at["PT"])]
            at["pt_i"] += 1
            if slope_idx is None:
                fw.op(fw.ACT, lambda: nc.scalar.activation(out=pt.t[:], in_=st.t[:], func=AF.Exp, bias=negC.t[:, 0:1], scale=1.0), R=[st.b, negC.b], W=[pt.b])
            elif rg == 1:
                dl = kc - 4 * qc
                tmp = at["tmp"][at["tmp_i"] % 2]
                at["tmp_i"] += 1
                fw.op(fw.DVE, lambda: nc.vector.tensor_tensor(out=tmp.t[:], in0=st.t[:], in1=DG.t[:, dl, :], op=ALU.add), R=[st.b, DG.b], W=[tmp.b])
                fw.op(fw.ACT, lambda: nc.scalar.activation(out=pt.t[:], in_=tmp.t[:], func=AF.Exp, bias=negC.t[:, 0:1], scale=1.0), R=[tmp.b, negC.b], W=[pt.b])
            else:
                dl = kc - 4 * qc
                fw.op(fw.ACT, lambda: nc.scalar.activation(out=pt.t[:], in_=st.t[:], func=AF.Exp, bias=BT.t[:, dl + 63:dl + 64], scale=1.0), R=[st.b, BT.b], W=[pt.b])
            return pt

        def emit_pv(idx, pt):
            kc = kcs[idx]
            rg = regs[idx]
            acc = accs[rg]
            fw.op(fw.PE, lambda: nc.tensor.matmul(acc.t[0:65, :], lhsT=V1.t[:, kc, :], rhs=pt.t[:], start=(first[rg] == idx), stop=(last[rg] == idx)),
                  R=[V1.b, pt.b], W=[acc.b], inc=True)
        look = nst - 1
        sts = {}
        for idx in range(min(look, len(kcs))):
            sts[idx] = emit_qk(idx)
        for idx in range(len(kcs)):
            if idx + look < len(kcs):
                sts[idx + look] = emit_qk(idx + look)
            pt = emit_exp(idx, sts.pop(idx))
            emit_pv(idx, pt)
        comb = at["comb"][at["comb_i"] % 2]
        at["comb_i"] += 1
        if slope_idx is None:
            fw.op(fw.DVE, lambda: nc.vector.tensor_copy(out=comb.t[:], in_=accs[1].t[0:65, :]), R=[accs[1].b], W=[comb.b])
        else:
            fw.op(fw.DVE, lambda: nc.vector.tensor_copy(out=comb.t[:], in_=accs[1].t[0:65, :]), R=[accs[1].b], W=[comb.b])
            t65 = at["t65"]
            if 0 in first:
                fw.op(fw.DVE, lambda: nc.vector.tensor_tensor(out=t65.t[:], in0=accs[0].t[0:65, :], in1=FLR.t[:, 0, :], op=ALU.mult), R=[accs[0].b, FLR.b], W=[t65.b])
                fw.op(fw.DVE, lambda: nc.vector.tensor_tensor(out=comb.t[:], in0=comb.t[:], in1=t65.t[:], op=ALU.add), R=[comb.b, t65.b], W=[comb.b])
            if 2 in first:
                fw.op(fw.DVE, lambda: nc.vector.tensor_tensor(out=t65.t[:], in0=accs[2].t[0:65, :], in1=FLR.t[:, 1, :], op=ALU.mult), R=[accs[2].b, FLR.b], W=[t65.b])
                fw.op(fw.DVE, lambda: nc.vector.tensor_tensor(out=comb.t[:], in0=comb.t[:], in1=t65.t[:], op=ALU.add), R=[comb.b, t65.b], W=[comb.b])
        bs = cx.ps[6 + at["acc_i"] % 2] if slope_idx is None else cx.ps[7]
        fw.op(fw.PE, lambda: nc.tensor.matmul(bs.t[0:64, :], lhsT=at["sel"].t[0:65, 0:64], rhs=comb.t[0:65, :], start=True, stop=True), R=[at["sel"].b, comb.b], W=[bs.b])
        rs = at["rs"]
        fw.op(fw.DVE, lambda: nc.vector.reciprocal(out=rs.t[:], in_=bs.t[0:64, :]), R=[bs.b], W=[rs.b])
        on = at["on"][at["on_i"] % 2]
        at["on_i"] += 1
        fw.op(fw.DVE, lambda: nc.vector.tensor_tensor(out=on.t[:], in0=comb.t[0:64, :], in1=rs.t[:], op=ALU.mult), R=[comb.b, rs.b], W=[on.b])
        on_out(qc, on)


def attn_setup(cx, alibi, V_all):
    fw, nc = cx.fw, cx.nc
    at = {"n": 0, "acc_i": 0, "st_i": 0, "pt_i": 0, "tmp_i": 0, "comb_i": 0, "on_i": 0}
    at["nst"] = 4
    if alibi:
        at["QT"] = [SB(cx, [128, L], BF16, "QT") for _ in range(2)]
        at["KT"] = [SB(cx, [128, L], BF16, "KT") for _ in range(1)]
        for t_ in at["QT"] + at["KT"]:
            fw.op(fw.POOL, lambda: nc.gpsimd.memset(t_.t[:], 0.0), W=[t_.b])
    else:
        at["QT"] = [SB(cx, [96, L], BF16, "QT") for _ in range(2)]
        at["KT"] = [SB(cx, [96, L], BF16, "KT") for _ in range(2)]
    at["V1"] = [SB(cx, [128, 64, 65], BF16, "V1") for _ in range(2)]
    at["PT"] = [SB(cx, [128, CH], BF16, "PT") for _ in range(6)]
    at["mx"] = SB(cx, [128, 32], F32, "mx")
    at["m2"] = SB(cx, [128, 4], F32, "m2")
    at["negC"] = [SB(cx, [128, 1], F32, "negC") for _ in range(2)]
    at["comb"] = [SB(cx, [65, CH], F32, "comb") for _ in range(2)]
    at["rs"] = SB(cx, [64, CH], F32, "rs")
    at["on"] = [SB(cx, [64, CH], F32, "on") for _ in range(2)]
    at["sel"] = SB(cx, [65, 64], F32, "sel")
    at["VALL"] = SB(cx, [128, 64, 256], BF16, "VALL")
    vsrc = V_all.rearrange("(kc p) d -> p kc d", p=128)
    for i in range(8):
        fw.dma(fw.SP, out=at["VALL"].t[:, i * 8:(i + 1) * 8, :], in_=vsrc[:, i * 8:(i + 1) * 8, :], W=[at["VALL"].b])
    fw.dma(fw.SP, out=at["sel"].t[:], in_=cx.C["sel"], W=[at["sel"].b])
    if alibi:
        at["BT"] = [SB(cx, [128, 127], F32, "BT") for _ in range(2)]
        at["tmp"] = [SB(cx, [128, CH], F32, "tmp") for _ in range(2)]
        at["t65"] = SB(cx, [65, CH], F32, "t65")
        at["DG"] = [SB(cx, [128, 4, CH], F32, "DG") for _ in range(2)]
        at["FLR"] = [SB(cx, [65, 2, CH], F32, "FLR") for _ in range(2)]
    return at


def phase_mla(cx):
    fw, nc = cx.fw, cx.nc
    new_phase(cx)
    sc = cx.sc
    at = attn_setup(cx, False, sc["VMLA"].t)
    for h in range(4):
        def on_out(qc, on, h=h):
            fw.dma(fw.GD, out=sc["YMLA"].t[h * 64:(h + 1) * 64, qc * CH:(qc + 1) * CH], in_=on.t[:], R=[on.b], W=[sc["YMLA"].cb[qc]])
        attn_load_qk(cx, at, h % 2, [(sc["QMLA"].t[h], 0, 96)], [(sc["KMLA"].t[h], 0, 96)])
        attn_head(cx, at, h % 2, 0, h, 96, None, on_out)
        if getattr(cx, "dbgt", False) and h < 2:
            for nm, tl in [("m2", at["m2"]), ("mx", at["mx"]), ("comb", at["comb"][1]), ("rs", at["rs"]), ("on", at["on"][1]), ("negC", at["negC"][h % 2])]:
                dst = nc.dram_tensor(f"dbg_{nm}{h}", list(tl.t.shape), F32, kind="ExternalOutput").ap()
                fw.dma(fw.SP, out=dst, in_=tl.t[:], R=[tl.b])


def phase_da(cx, W, li):
    fw, nc = cx.fw, cx.nc
    new_phase(cx)
    sc = cx.sc
    at = attn_setup(cx, True, sc["VDA"].t)
    at["window"] = cx.window
    lam_init = 0.8 - 0.6 * math.exp(-0.3 * li)
    lv = SB(cx, [64, 128], F32, "lv")
    fw.dma(fw.SP, out=lv.t[:], in_=W["da_lambda"].rearrange("a b -> (a b)").partition_broadcast(64), W=[lv.b])
    lp = SB(cx, [64, 64], F32, "lp")
    fw.op(fw.DVE, lambda: nc.vector.tensor_tensor(out=lp.t[:, 0:32], in0=lv.t[:, 0:32], in1=lv.t[:, 32:64], op=ALU.mult), R=[lv.b], W=[lp.b])
    fw.op(fw.DVE, lambda: nc.vector.tensor_tensor(out=lp.t[:, 32:64], in0=lv.t[:, 64:96], in1=lv.t[:, 96:128], op=ALU.mult), R=[lv.b], W=[lp.b])
    ls = SB(cx, [64, 4], F32, "ls")
    fw.op(fw.DVE, lambda: nc.vector.reduce_sum(out=ls.t[:, 0:1], in_=lp.t[:, 0:32], axis=AX.X), R=[lp.b], W=[ls.b])
    fw.op(fw.DVE, lambda: nc.vector.reduce_sum(out=ls.t[:, 1:2], in_=lp.t[:, 32:64], axis=AX.X), R=[lp.b], W=[ls.b])
    fw.op(fw.ACT, lambda: nc.scalar.activation(out=ls.t[:, 0:2], in_=ls.t[:, 0:2], func=AF.Exp), R=[ls.b], W=[ls.b])
    fw.op(fw.DVE, lambda: nc.vector.tensor_tensor(out=ls.t[:, 2:3], in0=ls.t[:, 1:2], in1=ls.t[:, 0:1], op=ALU.subtract), R=[ls.b], W=[ls.b])
    fw.op(fw.DVE, lambda: nc.vector.tensor_scalar(out=ls.t[:, 3:4], in0=ls.t[:, 2:3], scalar1=-lam_init, scalar2=None, op0=ALU.add), R=[ls.b], W=[ls.b])
    gh = SB(cx, [64, 1], F32, "gh")
    fw.dma(fw.SP, out=gh.t[:], in_=W["da_g_head"], W=[gh.b])
    fw.op(fw.DVE, lambda: nc.vector.tensor_scalar(out=gh.t[:], in0=gh.t[:], scalar1=1.0 - lam_init, scalar2=None, op0=ALU.mult), R=[gh.b], W=[gh.b])
    O1 = SB(cx, [64, L], F32, "O1")
    dd = [SB(cx, [64, CH], F32, "dd") for _ in range(2)]
    for h in range(4):
        def out1(qc, on):
            fw.op(fw.ACT, lambda: nc.scalar.copy(out=O1.t[:, qc * CH:(qc + 1) * CH], in_=on.t[:]), R=[on.b], W=[O1.b])

        def out2(qc, on, h=h):
            d_ = dd[qc % 2]
            fw.op(fw.DVE, lambda: nc.vector.scalar_tensor_tensor(out=d_.t[:], in0=on.t[:], scalar=ls.t[:, 3:4], in1=O1.t[:, qc * CH:(qc + 1) * CH],
                                                                 op0=ALU.mult, op1=ALU.add), R=[on.b, ls.b, O1.b], W=[d_.b])
            r = rstd_chunk(cx, [(d_.t[:], 64, [d_.b])], 64)
            fw.op(fw.DVE, lambda: nc.vector.scalar_tensor_tensor(out=d_.t[:], in0=d_.t[:], scalar=gh.t[:, 0:1], in1=r.t[0:64, :], op0=ALU.mult, op1=ALU.mult),
                  R=[d_.b, gh.b, r.b], W=[d_.b])
            fw.dma(fw.GD, out=sc["YDA"].t[h * 64:(h + 1) * 64, qc * CH:(qc + 1) * CH], in_=d_.t[:], R=[d_.b], W=[sc["YDA"].cb[qc]])
        fw.dma(fw.SP, out=at["QT"][0].t[0:32, :], in_=sc["QDA"].t[h * 64:h * 64 + 32, :], W=[at["QT"][0].b])
        fw.dma(fw.SP, out=at["QT"][1].t[32:64, :], in_=sc["QDA"].t[h * 64 + 32:h * 64 + 64, :], W=[at["QT"][1].b])
        fw.dma(fw.SP, out=at["KT"][0].t[0:64, :], in_=sc["KDA"].t[h * 64:(h + 1) * 64, :], W=[at["KT"][0].b])
        dg, fl = at["DG"][h % 2], at["FLR"][h % 2]
        fw.dma(fw.SP, out=dg.t[:], in_=cx.C["da_dg"][h].rearrange("a p f -> p a f"), W=[dg.b])
        fw.dma(fw.SP, out=fl.t[:], in_=cx.C["da_flr"][h].rearrange("a p f -> p a f"), W=[fl.b])
        for j, cb in enumerate([out1, out2]):
            attn_head(cx, at, j, j * 32, h, 32, h, cb, ki=0, kfull=128)


def phase_s5(cx, W, li):
    fw, nc = cx.fw, cx.nc
    new_phase(cx)
    sc = cx.sc
    V = fw.DVE
    T = CH

    def tt(out, a, b, op, R, Wb):
        fw.op(V, lambda: nc.vector.tensor_tensor(out=out, in0=a, in1=b, op=op), R=R, W=Wb)

    def ts(out, a, s1, op0, R, Wb, s2=None, op1=None):
        if op1 is None:
            fw.op(V, lambda: nc.vector.tensor_scalar(out=out, in0=a, scalar1=s1, scalar2=None, op0=op0), R=R, W=Wb)
        else:
            fw.op(V, lambda: nc.vector.tensor_scalar(out=out, in0=a, scalar1=s1, scalar2=s2, op0=op0, op1=op1), R=R, W=Wb)

    def stt(out, a, sc_, b, op0, op1, R, Wb):
        fw.op(V, lambda: nc.vector.scalar_tensor_tensor(out=out, in0=a, scalar=sc_, in1=b, op0=op0, op1=op1), R=R, W=Wb)

    NP = 40
    pr = SB(cx, [128, NP, 16], F32, "s5p")
    PB = pr.b
    col = lambda i: pr.t[:, i, :]
    AR, AI, LDT, DT, ARDT, TH, SS, CC, M32, ZR, ZI, T1, T2, ABR, ABI, NR, DEN, FR, FI, ATR, ATI, N2 = range(22)
    fw.dma(fw.SP, out=col(AR), in_=W["s5_ar"], W=[PB])
    fw.dma(fw.SP, out=col(AI), in_=W["s5_ai"], W=[PB])
    fw.dma(fw.SP, out=col(LDT), in_=W["s5_ldt"], W=[PB])
    halfpi = SB(cx, [128, 1], F32, "halfpi")
    fw.op(V, lambda: nc.vector.memset(halfpi.t[:], math.pi / 2), W=[halfpi.b])
    fw.op(fw.ACT, lambda: nc.scalar.activation(out=col(DT), in_=col(LDT), func=AF.Exp), R=[PB], W=[PB])
    tt(col(ARDT), col(AR), col(DT), ALU.mult, [PB], [PB])
    tt(col(TH), col(AI), col(DT), ALU.mult, [PB], [PB])
    fw.op(fw.ACT, lambda: nc.scalar.activation(out=col(SS), in_=col(TH), func=AF.Sin, scale=1.0 / 32), R=[PB], W=[PB])
    fw.op(fw.ACT, lambda: nc.scalar.activation(out=col(CC), in_=col(TH), func=AF.Sin, scale=1.0 / 32, bias=halfpi.t[:, 0:1]), R=[PB, halfpi.b], W=[PB])
    fw.op(fw.ACT, lambda: nc.scalar.activation(out=col(M32), in_=col(ARDT), func=AF.Exp, scale=1.0 / 32), R=[PB], W=[PB])
    tt(col(ZR), col(M32), col(CC), ALU.mult, [PB], [PB])
    tt(col(ZI), col(M32), col(SS), ALU.mult, [PB], [PB])
    pw = SB(cx, [128, 10, 2, 16], F32, "s5pw")

    def square():
        tt(col(T1), col(ZR), col(ZR), ALU.mult, [PB], [PB])
        tt(col(T2), col(ZI), col(ZI), ALU.mult, [PB], [PB])
        tt(col(T2), col(T1), col(T2), ALU.subtract, [PB], [PB])
        tt(col(T1), col(ZR), col(ZI), ALU.mult, [PB], [PB])
        ts(col(ZI), col(T1), 2.0, ALU.mult, [PB], [PB])
        fw.op(V, lambda: nc.vector.tensor_copy(out=col(ZR), in_=col(T2)), R=[PB], W=[PB])
    for _ in range(5):
        square()
    for j in range(10):
        fw.op(V, lambda: nc.vector.tensor_copy(out=pw.t[:, j, 0, :], in_=col(ZR)), R=[PB], W=[pw.b])
        fw.op(V, lambda: nc.vector.tensor_copy(out=pw.t[:, j, 1, :], in_=col(ZI)), R=[PB], W=[pw.b])
        if j < 9:
            square()
    ABRa, ABIa = pw.t[:, 0, 0, :], pw.t[:, 0, 1, :]
    ts(col(NR), ABRa, -1.0, ALU.add, [pw.b], [PB])
    tt(col(T1), col(AR), col(AR), ALU.mult, [PB], [PB])
    tt(col(T2), col(AI), col(AI), ALU.mult, [PB], [PB])
    tt(col(DEN), col(T1), col(T2), ALU.add, [PB], [PB])
    fw.op(V, lambda: nc.vector.reciprocal(out=col(DEN), in_=col(DEN)), R=[PB], W=[PB])
    tt(col(T1), col(NR), col(AR), ALU.mult, [PB], [PB])
    tt(col(T2), ABIa, col(AI), ALU.mult, [PB, pw.b], [PB])
    tt(col(T1), col(T1), col(T2), ALU.add, [PB], [PB])
    tt(col(FR), col(T1), col(DEN), ALU.mult, [PB], [PB])
    tt(col(T1), ABIa, col(AR), ALU.mult, [PB, pw.b], [PB])
    tt(col(T2), col(NR), col(AI), ALU.mult, [PB], [PB])
    tt(col(T1), col(T1), col(T2), ALU.subtract, [PB], [PB])
    tt(col(FI), col(T1), col(DEN), ALU.mult, [PB], [PB])
    ts(col(N2), col(ARDT), -2.0, ALU.mult, [PB], [PB])
    bsrc = SB(cx, [128, 16, 2, 16], F32, "s5b")
    csrc = SB(cx, [128, 16, 2, 16], F32, "s5c")
    fw.dma(fw.SP, out=bsrc.t[:], in_=W["s5_b"], W=[bsrc.b])
    fw.dma(fw.SP, out=csrc.t[:], in_=W["s5_c"], W=[csrc.b])
    dsk = SB(cx, [32, 8], F32, "s5d")
    fw.dma(fw.SP, out=dsk.t[:], in_=W["s5_d"], W=[dsk.b])
    tau = SB(cx, [128, T], F32, "tau")
    fw.dma(fw.SP, out=tau.t[:], in_=cx.C["tau"], W=[tau.b])
    ones = SB(cx, [128, T], F32, "ones512")
    fw.op(V, lambda: nc.vector.memset(ones.t[:], 1.0), W=[ones.b])
    Ep = [SB(cx, [128, 2, T], F32, "Ep") for _ in range(2)]
    Em = [SB(cx, [128, 2, T], F32, "Em") for _ in range(2)]
    mg = SB(cx, [128, T], F32, "mg")
    Uf = [SB(cx, [32, L], F32, "Uf") for _ in range(2)]
    Ub = [SB(cx, [32, L], BF16, "Ub") for _ in range(2)]
    Y = SB(cx, [32, L], F32, "Y")
    bpad = SB(cx, [128, 2, 32], F32, "bpad")
    cpad = SB(cx, [128, 2, 32], F32, "cpad")
    bb = SB(cx, [128, 2, 16], F32, "bb")
    BT_ = [SB(cx, [32, 2, 128], BF16, "BTl") for _ in range(2)]
    CT_ = [SB(cx, [128, 2, 32], BF16, "CTl") for _ in range(2)]
    Pt = [SB(cx, [128, 2, T], F32, "Pt") for _ in range(2)]
    St = [SB(cx, [128, 2, T], F32, "St") for _ in range(2)]
    tm = [SB(cx, [128, 2, T], F32, "tm") for _ in range(2)]
    xb = [SB(cx, [128, 2, T], BF16, "xb") for _ in range(2)]
    ax = SB(cx, [128, 4], F32, "ax")
    it = 0
    for k in range(8):
        uf, ub = Uf[k % 2], Ub[k % 2]
        fw.dma(fw.SP, out=uf.t[:], in_=sc["ZS5"].t[32 * k:32 * k + 32, :], W=[uf.b])
        fw.op(fw.ACT, lambda: nc.scalar.copy(out=ub.t[:], in_=uf.t[:]), R=[uf.b], W=[ub.b])
        for dr in range(2):
            cl = dr * 8 + k
            c1 = slice(cl, cl + 1)
            ep, em = Ep[(k * 2 + dr) % 2], Em[(k * 2 + dr) % 2]
            btl, ctl = BT_[(k * 2 + dr) % 2], CT_[(k * 2 + dr) % 2]
            fw.op(V, lambda: nc.vector.memset(ep.t[:, 0, 0:1], 1.0), W=[ep.b])
            fw.op(V, lambda: nc.vector.memset(ep.t[:, 1, 0:1], 0.0), W=[ep.b])
            for j in range(9):
                n_ = 1 << j
                p_r, p_i = pw.t[:, j, 0, c1], pw.t[:, j, 1, c1]
                ts(mg.t[:, 0:n_], ep.t[:, 1, 0:n_], p_i, ALU.mult, [ep.b, pw.b], [mg.b])
                stt(ep.t[:, 0, n_:2 * n_], ep.t[:, 0, 0:n_], p_r, mg.t[:, 0:n_], ALU.mult, ALU.subtract, [ep.b, pw.b, mg.b], [ep.b])
                ts(mg.t[:, 0:n_], ep.t[:, 1, 0:n_], p_r, ALU.mult, [ep.b, pw.b], [mg.b])
                stt(ep.t[:, 1, n_:2 * n_], ep.t[:, 0, 0:n_], p_i, mg.t[:, 0:n_], ALU.mult, ALU.add, [ep.b, pw.b, mg.b], [ep.b])
            fw.op(fw.ACT, lambda: nc.scalar.activation(out=mg.t[:], in_=tau.t[:], func=AF.Exp, scale=pr.t[:, N2, c1]), R=[tau.b, PB, mg.b], W=[mg.b])
            tt(em.t[:, 0, :], ep.t[:, 0, :], mg.t[:], ALU.mult, [ep.b, mg.b], [em.b])
            stt(em.t[:, 1, :], ep.t[:, 1, :], -1.0, mg.t[:], ALU.mult, ALU.mult, [ep.b, mg.b], [em.b])
            f_r, f_i = pr.t[:, FR, c1], pr.t[:, FI, c1]
            b_r, b_i = bsrc.t[:, cl, 0, :], bsrc.t[:, cl, 1, :]
            ts(bb.t[:, 0, :], b_i, f_i, ALU.mult, [bsrc.b, PB], [bb.b])
            stt(bb.t[:, 0, :], b_r, f_r, bb.t[:, 0, :], ALU.mult, ALU.subtract, [bsrc.b, PB, bb.b], [bb.b])
            ts(bb.t[:, 1, :], b_r, f_i, ALU.mult, [bsrc.b, PB], [bb.b])
            stt(bb.t[:, 1, :], b_i, f_r, bb.t[:, 1, :], ALU.mult, ALU.add, [bsrc.b, PB, bb.b], [bb.b])
            fw.op(V, lambda: nc.vector.memset(bpad.t[:], 0.0), W=[bpad.b])
            fw.op(V, lambda: nc.vector.memset(cpad.t[:], 0.0), W=[cpad.b])
            for hf in range(2):
                ps_ = slice(hf * 64, hf * 64 + 64)
                cs_ = slice(hf * 16, hf * 16 + 16)
                fw.op(V, lambda: nc.vector.tensor_copy(out=bpad.t[ps_, :, cs_], in_=bb.t[ps_, :, :]), R=[bb.b], W=[bpad.b])
                fw.op(V, lambda: nc.vector.tensor_copy(out=cpad.t[ps_, 0, cs_], in_=csrc.t[ps_, cl, 0, :]), R=[csrc.b], W=[cpad.b])
                ts(cpad.t[ps_, 1, cs_], csrc.t[ps_, cl, 1, :], -1.0, ALU.mult, [csrc.b], [cpad.b])
            fw.op(V, lambda: nc.vector.tensor_copy(out=ctl.t[:], in_=cpad.t[:]), R=[cpad.b], W=[ctl.b])
            psb = cx.ps[6]
            for ri in range(2):
                fw.op(fw.PE, lambda: nc.tensor.transpose(out=psb.t[0:32, ri * 128:(ri + 1) * 128], in_=bpad.t[:, ri, :], identity=cx.ident_f.t[:]),
                      R=[bpad.b, cx.ident_f.b], W=[psb.b], inc=(ri == 1))
            fw.op(V, lambda: nc.vector.tensor_copy(out=btl.t[:].rearrange("p a b -> p (a b)"), in_=psb.t[0:32, 0:256]), R=[psb.b], W=[btl.b])
            fw.op(V, lambda: nc.vector.memset(ax.t[:], 0.0), W=[ax.b])
            at_r, at_i = pw.t[:, 9, 0, c1], pw.t[:, 9, 1, c1]
            for ci in range(NCH):
                c = ci if dr == 0 else NCH - 1 - ci
                tsl = slice(c * T, (c + 1) * T)
                rv = (lambda a: a) if dr == 0 else (lambda a: a[:, ::-1])
                bur, bui = cx.ps[(it % 2) * 2], cx.ps[(it % 2) * 2 + 1]
                yps = cx.ps[4 + it % 2]
                P_, S_, t_, x_ = Pt[it % 2], St[it % 2], tm[it % 2], xb[it % 2]
                it += 1
                fw.op(fw.PE, lambda: nc.tensor.matmul(bur.t[:, :], lhsT=btl.t[:, 0, :], rhs=ub.t[:, tsl], start=True, stop=True), R=[btl.b, ub.b], W=[bur.b])
                fw.op(fw.PE, lambda: nc.tensor.matmul(bui.t[:, :], lhsT=btl.t[:, 1, :], rhs=ub.t[:, tsl], start=True, stop=True), R=[btl.b, ub.b], W=[bui.b])
                emr, emi = rv(em.t[:, 0, :]), rv(em.t[:, 1, :])
                epr, epi = rv(ep.t[:, 0, :]), rv(ep.t[:, 1, :])
                tt(t_.t[:, 0, :], bui.t[:, :], emi, ALU.mult, [bui.b, em.b], [t_.b])
                tt(P_.t[:, 0, :], bur.t[:, :], emr, ALU.mult, [bur.b, em.b], [P_.b])
                tt(P_.t[:, 0, :], P_.t[:, 0, :], t_.t[:, 0, :], ALU.subtract, [P_.b, t_.b], [P_.b])
                tt(t_.t[:, 1, :], bur.t[:, :], emi, ALU.mult, [bur.b, em.b], [t_.b])
                tt(P_.t[:, 1, :], bui.t[:, :], emr, ALU.mult, [bui.b, em.b], [P_.b])
                tt(P_.t[:, 1, :], P_.t[:, 1, :], t_.t[:, 1, :], ALU.add, [P_.b, t_.b], [P_.b])
                for ri in range(2):
                    fw.op(V, lambda: nc.vector.tensor_tensor_scan(out=rv(S_.t[:, ri, :]), data0=ones.t[:, :], data1=rv(P_.t[:, ri, :]), initial=ax.t[:, ri:ri + 1],
                                                                  op0=ALU.mult, op1=ALU.add), R=[ones.b, P_.b, ax.b], W=[S_.b])
                tt(t_.t[:, 0, :], S_.t[:, 1, :], epi, ALU.mult, [S_.b, ep.b], [t_.b])
                tt(t_.t[:, 1, :], S_.t[:, 0, :], epr, ALU.mult, [S_.b, ep.b], [t_.b])
                tt(x_.t[:, 0, :], t_.t[:, 1, :], t_.t[:, 0, :], ALU.subtract, [t_.b], [x_.b])
                tt(t_.t[:, 0, :], S_.t[:, 0, :], epi, ALU.mult, [S_.b, ep.b], [t_.b])
                tt(t_.t[:, 1, :], S_.t[:, 1, :], epr, ALU.mult, [S_.b, ep.b], [t_.b])
                tt(x_.t[:, 1, :], t_.t[:, 1, :], t_.t[:, 0, :], ALU.add, [t_.b], [x_.b])
                e_ = T - 1 if dr == 0 else 0
                sre, sie = S_.t[:, 0, e_:e_ + 1], S_.t[:, 1, e_:e_ + 1]
                ts(ax.t[:, 2:3], sie, at_i, ALU.mult, [S_.b, pw.b], [ax.b])
                ts(ax.t[:, 3:4], sie, at_r, ALU.mult, [S_.b, pw.b], [ax.b])
                stt(ax.t[:, 0:1], sre, at_r, ax.t[:, 2:3], ALU.mult, ALU.subtract, [S_.b, pw.b, ax.b], [ax.b])
                stt(ax.t[:, 1:2], sre, at_i, ax.t[:, 3:4], ALU.mult, ALU.add, [S_.b, pw.b, ax.b], [ax.b])
                fw.op(fw.PE, lambda: nc.tensor.matmul(yps.t[0:32, :], lhsT=ctl.t[:, 0, :], rhs=x_.t[:, 0, :], start=True, stop=False), R=[ctl.b, x_.b], W=[yps.b], inc=False)
                fw.op(fw.PE, lambda: nc.tensor.matmul(yps.t[0:32, :], lhsT=ctl.t[:, 1, :], rhs=x_.t[:, 1, :], start=False, stop=True), R=[ctl.b, x_.b], W=[yps.b])
                if dr == 0:
                    fw.op(fw.ACT, lambda: nc.scalar.copy(out=Y.t[:, tsl], in_=yps.t[0:32, :]), R=[yps.b], W=[Y.b])
                else:
                    fw.op(fw.POOL, lambda: nc.gpsimd.tensor_copy(out=t_.t[0:32, 0, :], in_=t_.t[0:32, 0, :]), R=[t_.b], W=[t_.b]) if False else None
                    fw.op(fw.ACT, lambda: nc.scalar.copy(out=S_.t[0:32, 0, :], in_=yps.t[0:32, :]), R=[yps.b, S_.b], W=[S_.b])
                    fw.op(fw.POOL, lambda: nc.gpsimd.tensor_tensor(out=Y.t[:, tsl], in0=Y.t[:, tsl], in1=S_.t[0:32, 0, :], op=ALU.add), R=[Y.b, S_.b], W=[Y.b])
        for c in range(NCH):
            tsl = slice(c * T, (c + 1) * T)
            fw.op(V, lambda: nc.vector.scalar_tensor_tensor(out=Y.t[:, tsl], in0=uf.t[:, tsl], scalar=dsk.t[:, k:k + 1], in1=Y.t[:, tsl], op0=ALU.mult, op1=ALU.add),
                  R=[uf.b, dsk.b, Y.b], W=[Y.b])
        fw.op(fw.ACT, lambda: nc.scalar.activation(out=Y.t[:], in_=Y.t[:], func=AF.Gelu), R=[Y.b], W=[Y.b])
        fw.dma(fw.GD, out=sc["GS5"].t[32 * k:32 * k + 32, :], in_=Y.t[:], R=[Y.b], W=sc["GS5"].cb)

    new_phase(cx)
    cx.wstg = [SB(cx, [128, 1024], F32, "wstg") for _ in range(2)]
    wgl = SB(cx, [128, 2, 256], BF16, "wgl")
    load_w_bf16(cx, wgl, W["s5_w_glu"], 2, 256)
    bgl = SB(cx, [128, 2], F32, "bgl")
    fw.dma(fw.SP, out=bgl.t[:], in_=W["s5_b_glu"], W=[bgl.b])
    gts = [SB(cx, [128, 2, CH], F32, "gt") for _ in range(2)]
    gbs = [SB(cx, [128, 2, CH], BF16, "gb") for _ in range(2)]
    sgs = [SB(cx, [128, CH], F32, "sg") for _ in range(2)]
    kq = 0
    for c in range(NCH):
        tsl = slice(c * CH, (c + 1) * CH)
        g, gbf = gts[c % 2], gbs[c % 2]
        fw.dma(fw.SP, out=g.t[:], in_=sc["GS5"].t[:, tsl].rearrange("(j p) t -> p j t", p=128), R=[sc["GS5"].cb[c]], W=[g.b])
        fw.op(fw.ACT, lambda: nc.scalar.copy(out=gbf.t[:], in_=g.t[:]), R=[g.b], W=[gbf.b])
        for m in range(2):
            ps = cx.ps[2 + kq % 6]
            kq += 1
            for j in range(2):
                fw.op(fw.PE, lambda: nc.tensor.matmul(ps.t[:, :], lhsT=wgl.t[:, j, m * 128:(m + 1) * 128], rhs=gbf.t[:, j, :], start=(j == 0), stop=(j == 1)),
                      R=[wgl.b, gbf.b], W=[ps.b], inc=(j == 1))
            sg = sgs[m]
            fw.op(fw.ACT, lambda: nc.scalar.activation(out=sg.t[:], in_=ps.t[:], func=AF.Sigmoid, bias=bgl.t[:, m:m + 1], scale=1.0), R=[ps.b, bgl.b], W=[sg.b])
            fw.op(V, lambda: nc.vector.tensor_tensor(out=g.t[:, m, :], in0=g.t[:, m, :], in1=sg.t[:], op=ALU.mult), R=[g.b, sg.b], W=[g.b])
        fw.dma(fw.GD, out=sc["YS5"].t[:, tsl].rearrange("(j p) t -> p j t", p=128), in_=g.t[:], R=[g.b], W=[sc["YS5"].cb[c]])


NK1 = 65
HC = 32


def fft_fwd_group(cx, hy, lhs_of, K, on_bank):
    fw, nc = cx.fw, cx.nc
    Yb = hy["Ybuf"]
    for c in range(HC):
        ps = cx.ps[hy["pi"] % 4]
        hy["pi"] += 1
        fw.op(fw.PE, lambda: nc.tensor.matmul(ps.t[:, 0:4 * NK1], lhsT=lhs_of(c), rhs=hy["F4"].t[0:K, :], start=True, stop=True), R=hy["lhsR"] + [hy["F4"].b], W=[ps.b])
        src = ps.t[:, 0:4 * NK1].rearrange("p (a k) -> p k a", a=4)
        if c % 2 == 0:
            fw.op(fw.ACT, lambda: nc.scalar.copy(out=Yb.t[:, :, :, c], in_=src), R=[ps.b], W=[Yb.b])
        else:
            fw.op(fw.DVE, lambda: nc.vector.tensor_copy(out=Yb.t[:, :, :, c], in_=src), R=[ps.b], W=[Yb.b])
    G = hy["G"]
    nb = (NK1 + 7) // 8
    for b in range(nb):
        k1s = list(range(b * 8, min(b * 8 + 8, NK1)))
        ps = cx.ps[4 + hy["pj"] % 4]
        hy["pj"] += 1
        for i, k1 in enumerate(k1s):
            o = ps.t[:, i * 2 * HC:(i + 1) * 2 * HC]
            fw.op(fw.PE, lambda: nc.tensor.matmul(o, lhsT=G.t[:, k1, 0, :], rhs=Yb.t[:, k1, 0:2, :].rearrange("p a c -> p (a c)"), start=True, stop=False),
                  R=[G.b, Yb.b], W=[ps.b], inc=False)
            fw.op(fw.PE, lambda: nc.tensor.matmul(o, lhsT=G.t[:, k1, 1, :], rhs=Yb.t[:, k1, 2:4, :].rearrange("p a c -> p (a c)"), start=False, stop=True),
                  R=[G.b, Yb.b], W=[ps.b], inc=(i == len(k1s) - 1))
        on_bank(b, k1s, ps)


def phase_hy_filter(cx, W, li):
    fw, nc = cx.fw, cx.nc
    new_phase(cx)
    sc = cx.sc
    V = fw.DVE
    hy = {"pi": 0, "pj": 0}
    hy["G"] = SB(cx, [128, NK1, 2, 128], BF16, "G")
    for i in range(5):
        fw.dma(fw.SP, out=hy["G"].t[:, i * 13:(i + 1) * 13], in_=cx.C["fftG"][:, i * 13:(i + 1) * 13], W=[hy["G"].b])
    hy["F4"] = SB(cx, [128, 4 * NK1], BF16, "F4")
    fw.dma(fw.SP, out=hy["F4"].t[:], in_=cx.C["fftF4"], W=[hy["F4"].b])
    hy["Ybuf"] = SB(cx, [128, NK1, 4, HC], BF16, "Ybuf")
    w1 = SB(cx, [33, 64], F32, "w1")
    w2 = SB(cx, [64, 64], F32, "w2")
    b12 = SB(cx, [64, 2], F32, "b12")
    fw.dma(fw.SP, out=w1.t[:], in_=W["hy_f_w1"], W=[w1.b])
    fw.dma(fw.SP, out=w2.t[:], in_=W["hy_f_w2"], W=[w2.b])
    fw.dma(fw.SP, out=b12.t[:], in_=W["hy_f_b12"], W=[b12.b])
    w3s = SB(cx, [64, 2, 512], F32, "w3s")
    nr = SB(cx, [128, 512], F32, "nr")
    for half in range(2):
        for o in range(2):
            c0 = o * 512 + half * 256
            fw.dma(fw.SP, out=w3s.t[:, half, o * 256:(o + 1) * 256], in_=W["hy_f_w3"][:, c0:c0 + 256], W=[w3s.b])
            fw.dma(fw.SP, out=nr.t[half * 64:(half + 1) * 64, o * 256:(o + 1) * 256], in_=W["hy_log_decay"][c0:c0 + 256].partition_broadcast(64), W=[nr.b])
    fw.op(fw.ACT, lambda: nc.scalar.activation(out=nr.t[:], in_=nr.t[:], func=AF.Exp), R=[nr.b], W=[nr.b])
    fw.op(V, lambda: nc.vector.tensor_scalar(out=nr.t[:], in0=nr.t[:], scalar1=-1.0, scalar2=None, op0=ALU.mult), R=[nr.b], W=[nr.b])
    tpos = SB(cx, [128, 128], F32, "tpos")
    fw.dma(fw.SP, out=tpos.t[:], in_=cx.C["tpos"], W=[tpos.b])
    h2T = SB(cx, [64, 2 * L], F32, "h2T")
    fts = [SB(cx, [33, CH], F32, "ft")] * 2
    xf = [SB(cx, [64, CH], F32, "xf") for _ in range(2)]
    xi = [SB(cx, [64, CH], mybir.dt.int32, "xi")] * 2
    h1 = [SB(cx, [64, CH], F32, "h1") for _ in range(2)]
    twopi = 2 * math.pi

    def sin_from(ps_ap, psb, bias_ap, out_ap, outb, i):
        x_, xi_ = xf[i % 2], xi[i % 2]
        fw.op(V, lambda: nc.vector.tensor_scalar(out=x_.t[:], in0=ps_ap, scalar1=bias_ap, scalar2=None, op0=ALU.add), R=[psb, b12.b], W=[x_.b])
        fw.op(V, lambda: nc.vector.tensor_scalar(out=xi_.t[:], in0=x_.t[:], scalar1=1.0 / twopi, scalar2=None, op0=ALU.mult), R=[x_.b], W=[xi_.b])
        fw.op(V, lambda: nc.vector.tensor_copy(out=out_ap, in_=xi_.t[:]), R=[xi_.b], W=[outb])
        fw.op(V, lambda: nc.vector.scalar_tensor_tensor(out=x_.t[:], in0=out_ap, scalar=-twopi, in1=x_.t[:], op0=ALU.mult, op1=ALU.add), R=[outb, x_.b], W=[x_.b])
        fw.op(fw.ACT, lambda: nc.scalar.activation(out=out_ap, in_=x_.t[:], func=AF.Sin), R=[x_.b], W=[outb])
    for c in range(32):
        ft = fts[c % 2]
        fw.dma(fw.SP, out=ft.t[:], in_=cx.C["featsT"][:, c * CH:(c + 1) * CH], W=[ft.b])
        ps = cx.ps[c % 2]
        fw.op(fw.PE, lambda: nc.tensor.matmul(ps.t[0:64, :], lhsT=w1.t[:], rhs=ft.t[:], start=True, stop=True), R=[w1.b, ft.b], W=[ps.b])
        sin_from(ps.t[0:64, :], ps.b, b12.t[:, 0:1], h1[c % 2].t[:], h1[c % 2].b, c)
        ps2 = cx.ps[2 + c % 2]
        fw.op(fw.PE, lambda: nc.tensor.matmul(ps2.t[0:64, :], lhsT=w2.t[:], rhs=h1[c % 2].t[:], start=True, stop=True), R=[w2.b, h1[c % 2].b], W=[ps2.b])
        sin_from(ps2.t[0:64, :], ps2.b, b12.t[:, 1:2], h2T.t[:, c * CH:(c + 1) * CH], h2T.b, c)
    HB = SB(cx, [128, 128, 128], BF16, "HB")
    acc4 = SB(cx, [128, 512], F32, "acc4")
    dec = [SB(cx, [128, 512], F32, "dec") for _ in range(2)]
    hh = [SB(cx, [128, 512], F32, "hh") for _ in range(2)]
    ab = [SB(cx, [128, 512], F32, "ab") for _ in range(1)] * 2
    rnt = SB(cx, [128, 512], F32, "rnt")
    hst = [SB(cx, [128, NK1, 2, HC], BF16, "hst") for _ in range(1)]
    hy["lhsR"] = [HB.b]
    for q in range(4):
        qs = slice(q * 128, (q + 1) * 128)
        fw.op(V, lambda: nc.vector.memset(acc4.t[:], 0.0), W=[acc4.b])
        for bt in range(32):
            ps = cx.ps[bt % 2]
            d_, h_, a_ = dec[bt % 2], hh[bt % 2], ab[bt % 2]
            for i in range(4):
                n2 = bt * 4 + i
                fw.op(fw.PE, lambda: nc.tensor.matmul(ps.t[0:64, i * 128:(i + 1) * 128], lhsT=h2T.t[:, n2:L:128], rhs=w3s.t[:, 0, qs], start=True, stop=True),
                      R=[h2T.b, w3s.b], W=[ps.b], inc=False)
                fw.op(fw.PE, lambda: nc.tensor.matmul(ps.t[64:128, i * 128:(i + 1) * 128], lhsT=h2T.t[:, L + n2:2 * L:128], rhs=w3s.t[:, 1, qs], start=True, stop=True),
                      R=[h2T.b, w3s.b], W=[ps.b], inc=(i == 3))
                fw.op(fw.ACT, lambda: nc.scalar.activation(out=d_.t[:, i * 128:(i + 1) * 128], in_=nr.t[:, qs], func=AF.Exp, scale=tpos.t[:, n2:n2 + 1]),
                      R=[nr.b, tpos.b], W=[d_.b])
            fw.op(V, lambda: nc.vector.tensor_tensor(out=h_.t[:], in0=ps.t[:], in1=d_.t[:], op=ALU.mult), R=[ps.b, d_.b], W=[h_.b])
            if bt == 0:
                fw.op(V, lambda: nc.vector.memset(h_.t[64:65, 0:128], 0.0), W=[h_.b])
            fw.op(fw.ACT, lambda: nc.scalar.activation(out=a_.t[:], in_=h_.t[:], func=AF.Abs), R=[h_.b], W=[a_.b])
            fw.op(fw.POOL, lambda: nc.gpsimd.tensor_tensor(out=acc4.t[:], in0=acc4.t[:], in1=a_.t[:], op=ALU.add), R=[acc4.b, a_.b], W=[acc4.b])
            fw.op(V, lambda: nc.vector.tensor_copy(out=HB.t[:, :, bt * 4:(bt + 1) * 4].rearrange("p c n -> p n c"), in_=h_.t[:].rearrange("p (n c) -> p n c", c=128)),
                  R=[h_.b], W=[HB.b])
        ps = cx.ps[2]
        for i in range(4):
            fw.op(fw.PE, lambda: nc.tensor.matmul(ps.t[:, 0:128], lhsT=cx.ones_f.t[:], rhs=acc4.t[:, i * 128:(i + 1) * 128], start=(i == 0), stop=(i == 3)),
                  R=[cx.ones_f.b, acc4.b], W=[ps.b], inc=(i == 3))
        fw.op(V, lambda: nc.vector.reciprocal(out=rnt.t[:, qs], in_=ps.t[:, 0:128]), R=[ps.b], W=[rnt.b])
        for g in range(4):
            gi = q * 4 + g
            hs = hst[0]

            def on_bank(b, k1s, ps, hs=hs):
                src = ps.t[:, 0:len(k1s) * 2 * HC]
                dst = hs.t[:, k1s[0]:k1s[-1] + 1, :, :].rearrange("p k a c -> p (k a c)")
                if b % 2 == 0:
                    fw.op(fw.ACT, lambda: nc.scalar.copy(out=dst, in_=src), R=[ps.b], W=[hs.b])
                else:
                    fw.op(V, lambda: nc.vector.tensor_copy(out=dst, in_=src), R=[ps.b], W=[hs.b])
            fft_fwd_group(cx, hy, lambda c, g=g: HB.t[:, g * HC + c, :], 128, on_bank)
            fw.dma(fw.GD, out=sc["HSPEC"].t[gi], in_=hs.t[:].rearrange("p k a c -> p (k a c)"), R=[hs.b], W=[sc["HSPEC"].cb[gi]])
    fw.dma(fw.GD, out=sc["RN"].t, in_=rnt.t[:], R=[rnt.b], W=sc["RN"].cb)


def phase_hy(cx, W, li):
    fw, nc = cx.fw, cx.nc
    sc = cx.sc
    V = fw.DVE
    new_phase(cx)
    cw = SB(cx, [128, 6, 4], F32, "cw")
    fw.dma(fw.SP, out=cw.t[:], in_=W["hy_conv"], W=[cw.b])
    zt = [SB(cx, [128, L], F32, "zt") for _ in range(2)]
    zo = [SB(cx, [128, L], F32, "zo") for _ in range(2)]
    for j in range(6):
        z, o = zt[j % 2], zo[j % 2]
        fw.dma(fw.SP, out=z.t[:], in_=sc["ZHY"].t[j * 128:(j + 1) * 128, :], W=[z.b])
        fw.op(V, lambda: nc.vector.tensor_scalar(out=o.t[:], in0=z.t[:], scalar1=cw.t[:, j, 1:2], scalar2=cw.t[:, j, 3:4], op0=ALU.mult, op1=ALU.add),
              R=[z.b, cw.b], W=[o.b])
        fw.op(V, lambda: nc.vector.scalar_tensor_tensor(out=o.t[:, 1:L], in0=z.t[:, 0:L - 1], scalar=cw.t[:, j, 0:1], in1=o.t[:, 1:L], op0=ALU.mult, op1=ALU.add),
              R=[z.b, cw.b, o.b], W=[o.b])
        fw.op(V, lambda: nc.vector.scalar_tensor_tensor(out=o.t[:, 0:L - 1], in0=z.t[:, 1:L], scalar=cw.t[:, j, 2:3], in1=o.t[:, 0:L - 1], op0=ALU.mult, op1=ALU.add),
              R=[z.b, cw.b, o.b], W=[o.b])
        fw.dma(fw.GD, out=sc["ZC"].t[j * 128:(j + 1) * 128, :], in_=o.t[:], R=[o.b], W=sc["ZC"].cb)
    if getattr(cx, "hy_sc_only", False):
        return
    new_phase(cx)
    hy = {"pi": 0, "pj": 0}
    hy["G"] = SB(cx, [128, NK1, 2, 128], BF16, "G")
    for i in range(5):
        fw.dma(fw.SP, out=hy["G"].t[:, i * 13:(i + 1) * 13], in_=cx.C["fftG"][:, i * 13:(i + 1) * 13], W=[hy["G"].b])
    hy["F4"] = SB(cx, [128, 4 * NK1], BF16, "F4")
    fw.dma(fw.SP, out=hy["F4"].t[:], in_=cx.C["fftF4"], W=[hy["F4"].b])
    Q = SB(cx, [NK1, 128, 2, 64], BF16, "Q")
    for i in range(4):
        fw.dma(fw.SP, out=Q.t[:, i * 32:(i + 1) * 32], in_=cx.C["fftQ"][:, i * 32:(i + 1) * 32], W=[Q.b])
    FI = SB(cx, [128, 2, 256], BF16, "FI")
    fw.dma(fw.SP, out=FI.t[:], in_=cx.C["fftFI"], W=[FI.b])
    hy["Ybuf"] = SB(cx, [128, NK1, 4, HC], BF16, "Ybuf")
    Yb = hy["Ybuf"]
    Tb_ap = Yb.t[0:NK1].rearrange("p k a c -> p (k a c)")[:, 0:2 * 128 * HC].rearrange("p (a n c) -> p a n c", a=2, c=HC)
    Hg = [SB(cx, [128, NK1, 2, HC], BF16, "Hg") for _ in range(2)]
    Zb = SB(cx, [128, HC, 2, NK1], BF16, "Zb")
    Vf = SB(cx, [64, HC, 128], F32, "Vf")
    X1 = SB(cx, [64, HC, 128], F32, "X1")
    G1 = SB(cx, [64, HC, 128], F32, "G1")
    Ub = SB(cx, [64, HC, 128], BF16, "Ub")
    U2 = SB(cx, [64, HC, 128], BF16, "U2")
    RN = SB(cx, [64, 512], F32, "RN")
    fw.dma(fw.SP, out=RN.t[:], in_=sc["RN"].t[0:64, :], R=sc["RN"].cb, W=[RN.b])
    SK = SB(cx, [64, 512], F32, "SK")
    fw.dma(fw.SP, out=SK.t[:], in_=W["hy_skip"].rearrange("a b -> (a b)").partition_broadcast(64), W=[SK.b])
    tA = [SB(cx, [128, 8 * HC], F32, "tA") for _ in range(2)]
    tB = [SB(cx, [128, 8 * HC], F32, "tB") for _ in range(2)]
    e1 = [SB(cx, [64, 16 * HC], F32, "e1") for _ in range(2)]
    e2 = [SB(cx, [64, 16 * HC], F32, "e2") for _ in range(2)]
    zc = sc["ZC"].t

    def ld(dst, row0):
        src = zc[row0:row0 + HC, :].rearrange("c (a b) -> a c b", b=128)
        for i in range(2):
            fw.dma(fw.SP, out=dst.t[:, i * 16:(i + 1) * 16, :], in_=src[:, i * 16:(i + 1) * 16, :], R=sc["ZC"].cb, W=[dst.b])

    def conv(g, o, ub, v_f, gate, out_f32, out_bf):
        hg = Hg[o]
        fw.dma(fw.SP, out=hg.t[:].rearrange("p k a c -> p (k a c)"), in_=sc["HSPEC"].t[o * 8 + g], R=[sc["HSPEC"].cb[o * 8 + g]], W=[hg.b])
        hy["lhsR"] = [ub.b]

        def on_bank(b, k1s, ps):
            nk = len(k1s)
            ks = slice(k1s[0], k1s[-1] + 1)
            pv = ps.t[:, 0:nk * 2 * HC].rearrange("p (k a c) -> p k a c", a=2, c=HC)
            xr, xi_ = pv[:, :, 0, :], pv[:, :, 1, :]
            hr, hi = hg.t[:, ks, 0, :], hg.t[:, ks, 1, :]
            a_, b_ = tA[b % 2], tB[b % 2]
            av = a_.t[:, 0:nk * HC].rearrange("p (k c) -> p k c", c=HC)
            bv = b_.t[:, 0:nk * HC].rearrange("p (k c) -> p k c", c=HC)
            zr = Zb.t[:, :, 0, ks].rearrange("p c k -> p k c")
            zi = Zb.t[:, :, 1, ks].rearrange("p c k -> p k c")
            fw.op(V, lambda: nc.vector.tensor_tensor(out=av, in0=xr, in1=hr, op=ALU.mult), R=[ps.b, hg.b], W=[a_.b])
            fw.op(V, lambda: nc.vector.tensor_tensor(out=bv, in0=xi_, in1=hi, op=ALU.mult), R=[ps.b, hg.b], W=[b_.b])
            fw.op(fw.POOL, lambda: nc.gpsimd.tensor_tensor(out=zr, in0=av, in1=bv, op=ALU.subtract), R=[a_.b, b_.b], W=[Zb.b])
            a2, b2 = tA[(b + 1) % 2], tB[(b + 1) % 2]
            av2 = a2.t[:, 0:nk * HC].rearrange("p (k c) -> p k c", c=HC)
            bv2 = b2.t[:, 0:nk * HC].rearrange("p (k c) -> p k c", c=HC)
            fw.op(V, lambda: nc.vector.tensor_tensor(out=av2, in0=xr, in1=hi, op=ALU.mult), R=[ps.b, hg.b], W=[a2.b])
            fw.op(V, lambda: nc.vector.tensor_tensor(out=bv2, in0=xi_, in1=hr, op=ALU.mult), R=[ps.b, hg.b], W=[b2.b])
            fw.op(fw.POOL, lambda: nc.gpsimd.tensor_tensor(out=zi, in0=av2, in1=bv2, op=ALU.add), R=[a2.b, b2.b], W=[Zb.b])
        if cx.hy_stop <= 1:
            return
        fft_fwd_group(cx, hy, lambda c: ub.t[:, c, :], 64, on_bank)
        if cx.hy_stop <= 2:
            return
        for c in range(HC):
            ps = cx.ps[hy["pi"] % 4]
            hy["pi"] += 1
            fw.op(fw.PE, lambda: nc.tensor.matmul(ps.t[0:NK1, 0:256], lhsT=Zb.t[:, c, 0, :], rhs=FI.t[:, 0, :], start=True, stop=False), R=[Zb.b, FI.b], W=[ps.b], inc=False)
            fw.op(fw.PE, lambda: nc.tensor.matmul(ps.t[0:NK1, 0:256], lhsT=Zb.t[:, c, 1, :], rhs=FI.t[:, 1, :], start=False, stop=True), R=[Zb.b, FI.b], W=[ps.b])
            src = ps.t[0:NK1, 0:256].rearrange("p (a n) -> p a n", a=2)
            if c % 2 == 0:
                fw.op(fw.ACT, lambda: nc.scalar.copy(out=Tb_ap[:, :, :, c], in_=src), R=[ps.b], W=[Yb.b])
            else:
                fw.op(V, lambda: nc.vector.tensor_copy(out=Tb_ap[:, :, :, c], in_=src), R=[ps.b], W=[Yb.b])
        if cx.hy_stop <= 3:
            return
        cs = slice(o * 256 + g * HC, o * 256 + (g + 1) * HC)
        for nb in range(8):
            ps = cx.ps[4 + hy["pj"] % 4]
            hy["pj"] += 1
            for j in range(16):
                n2 = nb * 16 + j
                oo = ps.t[0:64, j * HC:(j + 1) * HC]
                fw.op(fw.PE, lambda: nc.tensor.matmul(oo, lhsT=Q.t[:, n2, 0, :], rhs=Tb_ap[:, 0, n2, :], start=True, stop=False), R=[Q.b, Yb.b], W=[ps.b], inc=False)
                fw.op(fw.PE, lambda: nc.tensor.matmul(oo, lhsT=Q.t[:, n2, 1, :], rhs=Tb_ap[:, 1, n2, :], start=False, stop=True), R=[Q.b, Yb.b], W=[ps.b], inc=(j == 15))
            pv = ps.t[0:64, 0:16 * HC].rearrange("p (n c) -> p n c", c=HC)
            ns = slice(nb * 16, (nb + 1) * 16)
            a_, b_ = e1[nb % 2], e2[nb % 2]
            av = a_.t[:].rearrange("p (n c) -> p n c", c=HC)
            bv = b_.t[:].rearrange("p (n c) -> p n c", c=HC)
            fw.op(V, lambda: nc.vector.tensor_tensor(out=av, in0=pv, in1=RN.t[:, cs].unsqueeze(1).broadcast_to([64, 16, HC]), op=ALU.mult), R=[ps.b, RN.b], W=[a_.b])
            fw.op(fw.POOL, lambda: nc.gpsimd.tensor_tensor(out=bv, in0=v_f.t[:, :, ns].rearrange("p c n -> p n c"), in1=SK.t[:, cs].unsqueeze(1).broadcast_to([64, 16, HC]), op=ALU.mult),
                  R=[v_f.b, SK.b], W=[b_.b])
            fw.op(V, lambda: nc.vector.tensor_tensor(out=av, in0=av, in1=bv, op=ALU.add), R=[a_.b, b_.b], W=[a_.b])
            fw.op(V, lambda: nc.vector.tensor_tensor(out=out_f32.t[:, :, ns].rearrange("p c n -> p n c"), in0=av, in1=gate.t[:, :, ns].rearrange("p c n -> p n c"), op=ALU.mult),
                  R=[a_.b, gate.b], W=[out_f32.b])
            if out_bf is not None:
                fw.op(fw.ACT, lambda: nc.scalar.copy(out=out_bf.t[:, :, ns], in_=out_f32.t[:, :, ns]), R=[out_f32.b], W=[out_bf.b])

    for g_ in range(cx.hy_lim):
        g = cx.hy_g0 if cx.hy_same else g_
        ld(Vf, g * HC)
        ld(X1, 256 + g * HC)
        fw.op(fw.ACT, lambda: nc.scalar.copy(out=Ub.t[:], in_=Vf.t[:]), R=[Vf.b], W=[Ub.b])
        conv(g, 0, Ub, Vf, X1, G1, U2)
        ld(X1, 512 + g * HC)
        conv(g, 1, U2, G1, X1, Vf, None)
        dst = sc["YHY"].t[g * HC:(g + 1) * HC, :].rearrange("c (a b) -> a c b", b=128)
        for i in range(2):
            fw.dma(fw.GD, out=dst[:, i * 16:(i + 1) * 16, :], in_=Vf.t[:, i * 16:(i + 1) * 16, :], R=[Vf.b], W=sc["YHY"].cb)


def phase_f1(cx, W, li):
    fw, nc = cx.fw, cx.nc
    new_phase(cx)
    sc = cx.sc
    cx.wstg = [SB(cx, [128, 1024], F32, "wstg") for _ in range(2)]
    gb = SB(cx, [128, 8], F32, "gb")
    fw.dma(fw.SP, out=gb.t[:], in_=W["g_branch"], W=[gb.b])
    wout = SB(cx, [128, 8, D], BF16, "wout")
    load_w_bf16(cx, wout, W["w_out"], 8, D, gain=gb)
    ys = [SB(cx, [128, 8, CH], F32, "ys") for _ in range(2)]
    hts = [SB(cx, [128, 8, CH], F32, "ht") for _ in range(2)]
    mixT = SB(cx, [128, 8, CH], BF16, "mixT")
    k = 0
    srcs = ["YHY", "YS5", "YDA", "YMLA"]
    for c in range(NCH):
        tsl = slice(c * CH, (c + 1) * CH)
        y, ht = ys[c % 2], hts[c % 2]
        for bi, nm in enumerate(srcs):
            fw.dma(fw.SP, out=y.t[:, 2 * bi:2 * bi + 2, :], in_=sc[nm].t[:, tsl].rearrange("(j p) t -> p j t", p=128), R=[sc[nm].cb[c]], W=[y.b])
        fw.dma(fw.SP, out=ht.t[:], in_=cx.HT.t[:, tsl].rearrange("(j p) t -> p j t", p=128), R=[cx.HT.cb[c]], W=[ht.b])
        for bi in range(4):
            if bi == 2:
                fw.op(fw.ACT, lambda: nc.scalar.copy(out=mixT.t[:, 4:6, :], in_=y.t[:, 4:6, :]), R=[y.b], W=[mixT.b])
                continue
            r = rstd_chunk(cx, [(y.t[:, 2 * bi + jj, :], 128, [y.b]) for jj in range(2)], 256)
            fw.op(fw.DVE, lambda: nc.vector.tensor_tensor(out=mixT.t[:, 2 * bi:2 * bi + 2, :], in0=y.t[:, 2 * bi:2 * bi + 2, :],
                                                          in1=r.t[:].unsqueeze(1).broadcast_to([128, 2, CH]), op=ALU.mult), R=[y.b, r.b], W=[mixT.b])
        for m in range(8):
            ps = cx.ps[2 + (k % 6)]
            k += 1
            for j in range(8):
                fw.op(fw.PE, lambda: nc.tensor.matmul(ps.t[:, :], lhsT=wout.t[:, j, m * 128:(m + 1) * 128], rhs=mixT.t[:, j, :], start=(j == 0), stop=(j == 7)),
                      R=[wout.b, mixT.b], W=[ps.b], inc=(j == 7))
            fw.op(fw.DVE, lambda: nc.vector.tensor_tensor(out=ht.t[:, m, :], in0=ht.t[:, m, :], in1=ps.t[:, :], op=ALU.add), R=[ht.b, ps.b], W=[ht.b])
        fw.dma(fw.GD, out=cx.HT.t[:, tsl].rearrange("(j p) t -> p j t", p=128), in_=ht.t[:], R=[ht.b], W=[cx.HT.cb[c]])


def phase_f2(cx, W, li):
    fw, nc = cx.fw, cx.nc
    new_phase(cx)
    n = 256
    nchunks = L // n
    cx.wstg = [SB(cx, [128, 1024], F32, "wstg") for _ in range(2)]
    gf = SB(cx, [128, 8], F32, "gf")
    fw.dma(fw.SP, out=gf.t[:], in_=W["g_ffn"], W=[gf.b])
    wg = SB(cx, [128, 8, FFN], BF16, "wg")
    wu = SB(cx, [128, 8, FFN], BF16, "wu")
    wd = SB(cx, [128, 22, D], BF16, "wd")
    kq = 0
    for wt, src in [(wg, W["w_gate"]), (wu, W["w_up"])]:
        for j in range(8):
            for c0 in range(0, FFN, 1024):
                cn = min(1024, FFN - c0)
                stg = cx.wstg[kq % 2]
                fw.dma(fw.SP, out=stg.t[:, 0:cn], in_=src[j * 128:(j + 1) * 128, c0:c0 + cn], W=[stg.b])
                if kq % 2 == 0:
                    fw.op(fw.ACT, lambda: nc.scalar.mul(out=wt.t[:, j, c0:c0 + cn], in_=stg.t[:, 0:cn], mul=gf.t[:, j:j + 1]), R=[stg.b, gf.b], W=[wt.b])
                else:
                    fw.op(fw.DVE, lambda: nc.vector.tensor_scalar(out=wt.t[:, j, c0:c0 + cn], in0=stg.t[:, 0:cn], scalar1=gf.t[:, j:j + 1], scalar2=None, op0=ALU.mult),
                          R=[stg.b, gf.b], W=[wt.b])
                kq += 1
    for i in range(22):
        stg = cx.wstg[kq % 2]
        fw.dma(fw.SP, out=stg.t[:, :], in_=W["w_down"][i * 128:(i + 1) * 128, :], W=[stg.b])
        evac(cx, kq, wd.t[:, i, :], stg.t[:, :], R=[stg.b], W=[wd.b])
        kq += 1
    hts = [SB(cx, [128, 8, n], F32, "ht") for _ in range(2)]
    fT = SB(cx, [128, 8, n], BF16, "fT")
    act = SB(cx, [128, 22, n], BF16, "act")
    sg = [SB(cx, [128, n], F32, "sg") for _ in range(2)]
    k = 0
    for c in range(nchunks):
        tsl = slice(c * n, (c + 1) * n)
        cb = cx.HT.cb[c // 2]
        ht = hts[c % 2]
        fw.dma(fw.SP, out=ht.t[:], in_=cx.HT.t[:, tsl].rearrange("(j p) t -> p j t", p=128), R=[cb], W=[ht.b])
        r = rstd_chunk(cx, [(ht.t[:, j, :], 128, [ht.b]) for j in range(8)], D, n=n)
        fw.op(fw.DVE, lambda: nc.vector.tensor_tensor(out=fT.t[:], in0=ht.t[:], in1=r.t[:, 0:n].unsqueeze(1).broadcast_to([128, 8, n]), op=ALU.mult),
              R=[ht.b, r.b], W=[fT.b])
        for i in range(22):
            ps = cx.ps[2 + (k % 6)]
            k += 1
            for half, wt in enumerate([wg, wu]):
                for j in range(8):
                    fw.op(fw.PE, lambda: nc.tensor.matmul(ps.t[:, half * n:(half + 1) * n], lhsT=wt.t[:, j, i * 128:(i + 1) * 128], rhs=fT.t[:, j, :],
                                                          start=(j == 0), stop=(j == 7)), R=[wt.b, fT.b], W=[ps.b], inc=(j == 7))
            s_ = sg[i % 2]
            fw.op(fw.ACT, lambda: nc.scalar.activation(out=s_.t[:], in_=ps.t[:, 0:n], func=AF.Silu), R=[ps.b], W=[s_.b])
            fw.op(fw.DVE, lambda: nc.vector.tensor_tensor(out=act.t[:, i, :], in0=s_.t[:], in1=ps.t[:, n:2 * n], op=ALU.mult), R=[s_.b, ps.b], W=[act.b])
        for m in range(8):
            ps = cx.ps[2 + (k % 6)]
            k += 1
            for i in range(22):
                fw.op(fw.PE, lambda: nc.tensor.matmul(ps.t[:, 0:n], lhsT=wd.t[:, i, m * 128:(m + 1) * 128], rhs=act.t[:, i, :], start=(i == 0), stop=(i == 21)),
                      R=[wd.b, act.b], W=[ps.b], inc=(i == 21))
            fw.op(fw.DVE, lambda: nc.vector.tensor_tensor(out=ht.t[:, m, :], in0=ht.t[:, m, :], in1=ps.t[:, 0:n], op=ALU.add), R=[ht.b, ps.b], W=[ht.b])
        fw.dma(fw.GD, out=cx.HT.t[:, tsl].rearrange("(j p) t -> p j t", p=128), in_=ht.t[:], R=[ht.b], W=[cb])


def phase_f3(cx, W, li, p_ap):
    fw, nc = cx.fw, cx.nc
    new_phase(cx)
    cx.wstg = [SB(cx, [128, 1024], F32, "wstg") for _ in range(2)]
    gp = SB(cx, [128, 8], F32, "gp")
    fw.dma(fw.SP, out=gp.t[:], in_=W["g_ple"], W=[gp.b])
    wpg = SB(cx, [128, 8, D], BF16, "wpg")
    load_w_bf16(cx, wpg, W["w_ple_gate"], 8, D, gain=gp)
    wpl = SB(cx, [128, 2, D], BF16, "wpl")
    load_w_bf16(cx, wpl, W["w_ple"], 2, D)
    hts = [SB(cx, [128, 8, CH], F32, "ht") for _ in range(2)]
    pcs = [SB(cx, [128, 4, 256], F32, "pc") for _ in range(2)]
    fT = SB(cx, [128, 8, CH], BF16, "fT")
    pT = SB(cx, [128, 2, CH], BF16, "pT")
    sg = [SB(cx, [128, CH], F32, "sg") for _ in range(2)]
    k = 0
    for c in range(NCH):
        tsl = slice(c * CH, (c + 1) * CH)
        ht, pc = hts[c % 2], pcs[c % 2]
        fw.dma(fw.SP, out=ht.t[:], in_=cx.HT.t[:, tsl].rearrange("(j p) t -> p j t", p=128), R=[cx.HT.cb[c]], W=[ht.b])
        fw.dma(fw.SP, out=pc.t[:], in_=p_ap[li, tsl, :].rearrange("(s p) d -> p s d", p=128), W=[pc.b])
        r = rstd_chunk(cx, [(ht.t[:, j, :], 128, [ht.b]) for j in range(8)], D)
        fw.op(fw.DVE, lambda: nc.vector.tensor_tensor(out=fT.t[:], in0=ht.t[:], in1=r.t[:].unsqueeze(1).broadcast_to([128, 8, CH]), op=ALU.mult),
              R=[ht.b, r.b], W=[fT.b])
        for ct in range(2):
            ps = cx.ps[2 + (k % 6)]
            k += 1
            for s in range(4):
                fw.op(fw.PE, lambda: nc.tensor.transpose(out=ps.t[:, s * 128:(s + 1) * 128], in_=pc.t[:, s, ct * 128:(ct + 1) * 128], identity=cx.ident_f.t[:]),
                      R=[pc.b, cx.ident_f.b], W=[ps.b], inc=(s == 3))
            evac(cx, k, pT.t[:, ct, :], ps.t[:], R=[ps.b], W=[pT.b])
        for m in range(8):
            ps = cx.ps[2 + (k % 6)]
            k += 1
            for j in range(8):
                fw.op(fw.PE, lambda: nc.tensor.matmul(ps.t[:, :], lhsT=wpg.t[:, j, m * 128:(m + 1) * 128], rhs=fT.t[:, j, :], start=(j == 0), stop=(j == 7)),
                      R=[wpg.b, fT.b], W=[ps.b], inc=(j == 7))
            ps2 = cx.ps[2 + (k % 6)]
            k += 1
            for ct in range(2):
                fw.op(fw.PE, lambda: nc.tensor.matmul(ps2.t[:, :], lhsT=wpl.t[:, ct, m * 128:(m + 1) * 128], rhs=pT.t[:, ct, :], start=(ct == 0), stop=(ct == 1)),
                      R=[wpl.b, pT.b], W=[ps2.b], inc=(ct == 1))
            s_ = sg[m % 2]
            fw.op(fw.ACT, lambda: nc.scalar.activation(out=s_.t[:], in_=ps.t[:], func=AF.Sigmoid), R=[ps.b], W=[s_.b])
            fw.op(fw.DVE, lambda: nc.vector.tensor_tensor(out=s_.t[:], in0=s_.t[:], in1=ps2.t[:], op=ALU.mult), R=[s_.b, ps2.b], W=[s_.b])
            fw.op(fw.DVE, lambda: nc.vector.tensor_tensor(out=ht.t[:, m, :], in0=ht.t[:, m, :], in1=s_.t[:], op=ALU.add), R=[ht.b, s_.b], W=[ht.b])
        fw.dma(fw.GD, out=cx.HT.t[:, tsl].rearrange("(j p) t -> p j t", p=128), in_=ht.t[:], R=[ht.b], W=[cx.HT.cb[c]])


def host_consts():
    C = {}
    C["ident_f"] = np.eye(128, dtype=np.float32)
    C["ones_f"] = np.ones((128, 128), dtype=np.float32)
    inv = 10000.0 ** (-np.arange(0, 32, 2, dtype=np.float32) / 32)
    ang = (np.arange(L, dtype=np.float32)[:, None] * inv[None, :]).astype(np.float32)
    cos, sin = np.cos(ang).T.astype(np.float32), np.sin(ang).T.astype(np.float32)
    sq = np.float32(96 ** -0.5)
    C["cs_q"] = np.stack([np.tile(cos, (4, 1)) * sq, np.tile(sin, (4, 1)) * sq]).astype(np.float32)
    C["cs_k"] = np.stack([cos, sin]).astype(np.float32)
    sel = np.zeros((65, 64), np.float32)
    sel[64, :] = 1.0
    C["sel"] = sel
    bf = ml_dtypes.bfloat16
    N = 2 * L
    n1 = np.arange(128, dtype=np.float64)
    k1 = np.arange(NK1, dtype=np.float64)
    a1 = 2 * np.pi * np.outer(n1, k1) / 128.0
    Fr, Fi = np.cos(a1), -np.sin(a1)
    C["fftF4"] = np.concatenate([Fr, Fi, -Fi, Fr], axis=1).astype(bf)
    n2 = np.arange(128, dtype=np.float64)
    k2 = np.arange(128, dtype=np.float64)
    kk = k1[None, :, None] + 128.0 * k2[None, None, :]
    ag = 2 * np.pi * ((n2[:, None, None] * kk) % N) / N
    C["fftG"] = np.stack([np.cos(ag), -np.sin(ag)], axis=2).astype(bf)
    ai_ = 2 * np.pi * np.outer(k2, n2) / 128.0
    Fc, Fs = np.cos(ai_), np.sin(ai_)
    C["fftFI"] = np.stack([np.concatenate([Fc, Fs], axis=1), np.concatenate([-Fs, Fc], axis=1)], axis=1).astype(bf)
    wk = np.where((k1 == 0) | (k1 == 64), 1.0, 2.0) / N
    nn = 128.0 * np.arange(64, dtype=np.float64)[None, None, :] + n2[None, :, None]
    aq = 2 * np.pi * ((k1[:, None, None] * nn) % N) / N
    C["fftQ"] = np.stack([np.cos(aq) * wk[:, None, None], -np.sin(aq) * wk[:, None, None]], axis=2).astype(bf)
    n = np.arange(N)
    pos = np.where(n < L, n, N - n).astype(np.float64)
    t = (pos.astype(np.float32) / np.float32(L)).astype(np.float32)
    bands = np.arange(1, 17, dtype=np.float32)
    ang2 = (np.float32(2.0 * math.pi) * t[:, None] * bands[None, :]).astype(np.float32)
    C["featsT"] = np.ascontiguousarray(np.concatenate([t[:, None], np.cos(ang2), np.sin(ang2)], axis=-1).T.astype(np.float32))
    C["tpos"] = np.ascontiguousarray(t.reshape(128, 128))
    C["tau"] = np.tile(np.arange(512, dtype=np.float32)[None, :], (128, 1))
    slopes = [2.0 ** (-8.0 * (h + 1) / 4) for h in range(4)]
    p = np.arange(128, dtype=np.float64)[:, None]
    dl = np.arange(-63, 64, dtype=np.float64)[None, :]
    f = np.arange(512, dtype=np.float64)
    bt, dg, flr = [], [], []
    for sl in slopes:
        left = sl * (p + 128 * dl)
        right = -sl * (p + 128 * dl - 511)
        bt.append(np.where(dl < 0, left, np.where(dl >= 4, right, 0.0)))
        dg.append(np.stack([-sl * np.abs(f[None, :] - 128 * d_ - p) for d_ in range(4)]))
        flr.append(np.stack([np.tile(np.exp(-sl * f)[None, :], (65, 1)), np.tile(np.exp(-sl * (511 - f))[None, :], (65, 1))]))
    C["da_bt"] = np.stack(bt).astype(np.float32)
    C["da_dg"] = np.stack(dg).astype(np.float32)
    C["da_flr"] = np.stack(flr).astype(np.float32)
    return C


PER_LAYER = ["g_mix", "w_in", "w_out", "g_branch", "hy_conv_w", "hy_conv_b", "hy_f_w1", "hy_f_b1", "hy_f_w2", "hy_f_b2", "hy_f_w3",
             "hy_log_decay", "hy_skip", "s5_a_re", "s5_a_im", "s5_log_dt", "s5_b_re", "s5_b_im", "s5_c_re", "s5_c_im", "s5_d",
             "s5_w_glu", "s5_b_glu", "da_lambda", "da_g_head", "mla_g_q", "mla_g_kv", "mla_w_uq", "mla_w_ukv", "g_ffn", "w_gate",
             "w_up", "w_down", "g_ple", "w_ple_gate", "w_ple"]


def pj(v, kt):
    v = np.asarray(v, dtype=np.float32).reshape(-1)
    out = np.zeros((kt * 128,), np.float32)
    out[:v.shape[0]] = v
    return np.ascontiguousarray(out.reshape(kt, 128).T)


def host_layout(inputs, nl):
    Wd = {}
    for li in range(nl):
        g = lambda n: np.asarray(inputs[n][li], dtype=np.float32)
        Wd[f"g_mix__{li}"] = pj(g("g_mix"), 8)
        Wd[f"w_in__{li}"] = g("w_in")
        Wd[f"mla_g_q__{li}"] = pj(g("mla_g_q"), 2)
        Wd[f"mla_g_kv__{li}"] = pj(g("mla_g_kv"), 1)
        Wd[f"mla_w_uq__{li}"] = g("mla_w_uq")
        Wd[f"mla_w_ukv__{li}"] = g("mla_w_ukv")
        Wd[f"da_lambda__{li}"] = g("da_lambda")
        def s5l(a):
            a = np.asarray(a, np.float32)
            rest = a.shape[2:]
            a = a.reshape((2, 8, 128) + rest)
            a = np.moveaxis(a, 2, 0)
            return np.ascontiguousarray(a.reshape((128, 16) + rest))
        Wd[f"s5_ar__{li}"] = s5l(g("s5_a_re").reshape(2, 1024))
        Wd[f"s5_ai__{li}"] = s5l(g("s5_a_im").reshape(2, 1024))
        Wd[f"s5_ldt__{li}"] = s5l(np.repeat(g("s5_log_dt"), 64, axis=1))
        Wd[f"s5_b__{li}"] = s5l(np.stack([g("s5_b_re").reshape(2, 1024, 16), g("s5_b_im").reshape(2, 1024, 16)], axis=2))
        cre = np.transpose(g("s5_c_re"), (0, 1, 3, 2)).reshape(2, 1024, 16)
        cim = np.transpose(g("s5_c_im"), (0, 1, 3, 2)).reshape(2, 1024, 16)
        Wd[f"s5_c__{li}"] = s5l(np.stack([cre, cim], axis=2))
        Wd[f"s5_d__{li}"] = np.ascontiguousarray(g("s5_d").reshape(8, 32).T)
        Wd[f"s5_w_glu__{li}"] = g("s5_w_glu")
        Wd[f"s5_b_glu__{li}"] = pj(g("s5_b_glu"), 2)
        Wd[f"hy_f_w1__{li}"] = g("hy_f_w1")
        Wd[f"hy_f_w2__{li}"] = g("hy_f_w2")
        Wd[f"hy_f_b12__{li}"] = np.ascontiguousarray(np.stack([g("hy_f_b1"), g("hy_f_b2")], axis=1))
        Wd[f"hy_f_w3__{li}"] = g("hy_f_w3")
        Wd[f"hy_log_decay__{li}"] = g("hy_log_decay")
        Wd[f"hy_skip__{li}"] = g("hy_skip")
        cwb = np.concatenate([g("hy_conv_w"), g("hy_conv_b")[None, :]], axis=0)
        Wd[f"hy_conv__{li}"] = np.ascontiguousarray(cwb.T.reshape(6, 128, 4).transpose(1, 0, 2))
        gbr = g("g_branch")
        Wd[f"g_branch__{li}"] = pj(np.concatenate([gbr[0], gbr[1], np.ones(256, np.float32), gbr[2]]), 8)
        Wd[f"w_out__{li}"] = g("w_out")
        Wd[f"g_ffn__{li}"] = pj(g("g_ffn"), 8)
        Wd[f"w_gate__{li}"] = g("w_gate")
        Wd[f"w_up__{li}"] = g("w_up")
        Wd[f"w_down__{li}"] = g("w_down")
        Wd[f"g_ple__{li}"] = pj(g("g_ple"), 8)
        Wd[f"w_ple_gate__{li}"] = g("w_ple_gate")
        Wd[f"w_ple__{li}"] = g("w_ple")
        Wd[f"da_g_head__{li}"] = g("da_g_head").reshape(64, 1)
    Wd["g_final"] = pj(inputs["g_final"], 8)
    return Wd


def build(nl, Wd_shapes, C_shapes, stages, dbg=()):
    nc = bass.Bass("TRN2", target_bir_lowering=False)
    cx = Ctx()
    cx.nc = nc
    x_ap = nc.dram_tensor("x", [L, D], F32, kind="ExternalInput").ap()
    p_ap = nc.dram_tensor("p", [nl, L, 256], F32, kind="ExternalInput").ap()
    out_ap = nc.dram_tensor("out", [L, D], F32, kind="ExternalOutput").ap()
    WA = {k: nc.dram_tensor(k, list(shp), F32, kind="ExternalInput").ap() for k, shp in Wd_shapes.items()}
    cx.C = {k: nc.dram_tensor("c_" + k, list(shp), dt, kind="ExternalInput").ap() for k, (shp, dt) in C_shapes.items()}
    with ExitStack() as es:
        cx.es = es
        cx.phase_es = None
        cx.tcount = 0
        cx.fw = fw = FW(nc, es)
        cx.ps = [Tl(es.enter_context(nc.psum_tensor(f"ps{i}", [128, 512], F32)), f"ps{i}") for i in range(8)]
        P = lambda n, shp, dt: Tl(es.enter_context(nc.sbuf_tensor(n, shp, dt)), n)
        cx.ones_f = P("ones_f", [128, 128], F32)
        cx.ident_f = P("ident_f", [128, 128], F32)
        cx.eps_t = P("eps_t", [128, 1], F32)
        cx.sq = [P(f"sq{i}", [128, CH], F32) for i in range(2)]
        cx.rstd = [P(f"rstd{i}", [128, CH], F32) for i in range(2)]
        cx.rstd_i = 0
        fw.dma(fw.SP, out=cx.ones_f.t[:], in_=cx.C["ones_f"], W=[cx.ones_f.b])
        fw.dma(fw.SP, out=cx.ident_f.t[:], in_=cx.C["ident_f"], W=[cx.ident_f.b])
        fw.op(fw.DVE, lambda: nc.vector.memset(cx.eps_t.t[:], EPS), W=[cx.eps_t.b])
        cx.OUTB = [Buf() for _ in range(NCH)]
        cx.HT = dram(cx, "HT", [D, L], F32)
        sc = cx.sc = {"HT": cx.HT}
        sc["ZHY"] = dram(cx, "ZHY", [768, L], F32)
        sc["ZS5"] = dram(cx, "ZS5", [256, L], F32)
        sc["QDA"] = dram(cx, "QDA", [256, L], BF16)
        sc["KDA"] = dram(cx, "KDA", [256, L], BF16)
        sc["VDA"] = dram(cx, "VDA", [L, 256], BF16)
        sc["QMLA"] = dram(cx, "QMLA", [4, 96, L], BF16)
        sc["KMLA"] = dram(cx, "KMLA", [4, 96, L], BF16)
        sc["VMLA"] = dram(cx, "VMLA", [L, 256], BF16)
        sc["YMLA"] = dram(cx, "YMLA", [256, L], F32)
        sc["YHY"] = dram(cx, "YHY", [256, L], F32)
        sc["YS5"] = dram(cx, "YS5", [256, L], F32)
        sc["GS5"] = dram(cx, "GS5", [256, L], F32)
        sc["ZC"] = dram(cx, "ZC", [768, L], F32)
        sc["HSPEC"] = dram(cx, "HSPEC", [16, 128, NK1 * 2 * HC], BF16)
        sc["RN"] = dram(cx, "RN", [128, 512], F32)
        sc["YDA"] = dram(cx, "YDA", [256, L], F32)
        cx.slopes = [2.0 ** (-8.0 * (h + 1) / 4) for h in range(4)]
        cx.window = 100.0
        cx.dbgt = 'dbgt' in stages
        cx.hy_sc_only = 'hysc' in stages
        cx.hy_lim = 8
        cx.hy_stop = 9
        cx.hy_same = 'hysame' in stages
        cx.hy_g0 = 1 if 'hyg1' in stages else 0
        for st_ in stages:
            if st_.startswith('hystop'):
                cx.hy_stop = int(st_[6:])
                cx.hy_lim = 1
            if st_.startswith('hylim'):
                cx.hy_lim = int(st_[5:])
        if "x0" in stages:
            phase_x0(cx, x_ap)
        for li in range(nl):
            W = {k.rsplit("__", 1)[0]: v for k, v in WA.items() if k.endswith(f"__{li}")}
            if "a" in stages:
                phase_a(cx, W, li)
            if "mla" in stages:
                phase_mla(cx)
            if "da" in stages:
                phase_da(cx, W, li)
            if "s5" in stages:
                phase_s5(cx, W, li)
            if "hyf" in stages:
                phase_hy_filter(cx, W, li)
            if "hy" in stages:
                phase_hy(cx, W, li)
            if "inj" in stages:
                new_phase(cx)
                for nm in ["YHY", "YS5"]:
                    src = nc.dram_tensor("inj_" + nm, [256, L], F32, kind="ExternalInput").ap()
                    fw.dma(fw.SP, out=sc[nm].t, in_=src)
            if "f1" in stages:
                phase_f1(cx, W, li)
            if "f2" in stages:
                phase_f2(cx, W, li)
            if "f3" in stages:
                phase_f3(cx, W, li, p_ap)
        if "out" in stages:
            phase_out(cx, WA["g_final"], out_ap)
        new_phase(cx)
        for name in dbg:
            src = sc[name].t
            dst = nc.dram_tensor("dbg_" + name, list(src.shape), src.dtype, kind="ExternalOutput").ap()
            fw.dma(fw.SP, out=dst, in_=src)
        fw.final_wait()
        fw.check_deadlock()
        cx.phase_es.close()
    print("instructions:", fw.n_inst)
    return nc


ALL_STAGES = ["x0", "a", "mla", "da", "s5", "hyf", "hy", "f1", "f2", "f3", "out"]
_CACHE = {}


def kernel(**inputs):
    nl = 4
    nb = 8
    inputs = {k: np.asarray(v) for k, v in inputs.items()}
    Wd = host_layout(inputs, nl)
    C = host_consts()
    key = "prog"
    if key not in _CACHE:
        _CACHE[key] = build(nl, {k: v.shape for k, v in Wd.items()},
                            {k: (v.shape, BF16 if v.dtype == ml_dtypes.bfloat16 else F32) for k, v in C.items()}, ALL_STAGES)
    nc = _CACHE[key]
    x = np.asarray(inputs["x"], dtype=np.float32)
    p = np.asarray(inputs["p"], dtype=np.float32)
    in_maps = []
    for b in range(nb):
        m = {"x": np.ascontiguousarray(x[b]), "p": np.ascontiguousarray(p[:, b])}
        m.update(Wd)
        m.update({"c_" + k: v for k, v in C.items()})
        in_maps.append(m)
    res = run_bass_kernel_spmd(nc, in_maps, core_ids=list(range(nb)))
    return np.stack([np.asarray(res.results[b]["out"], dtype=np.float32) for b in range(nb)], axis=0)
```

```python
import math
from contextlib import ExitStack
import numpy as np
import ml_dtypes
import concourse.bass as bass
import concourse.mybir as mybir
from concourse.bass_utils import run_bass_kernel_spmd

F32 = mybir.dt.float32
BF16 = mybir.dt.bfloat16
AF = mybir.ActivationFunctionType
ALU = mybir.AluOpType
AX = mybir.AxisListType


class Buf:
    __slots__ = ("lw", "rd", "name")

    def __init__(self, name=""):
        self.lw = None
        self.rd = {}
        self.name = name


class Stream:
    def __init__(self, eng):
        self.eng = eng
        self.seen = {}
        self.ev = []


class CQ:
    def __init__(self, stream, sem):
        self.stream = stream
        self.sem = sem
        self.count = 0
        self.is_dma = False


class DQ:
    def __init__(self, stream, sems):
        self.stream = stream
        self.sems = sems
        self.cnt = [0] * len(sems)
        self.idx = 0
        self.is_dma = True
        self.outst = []


class FW:
    def __init__(self, nc, es):
        self.nc = nc
        self.es = es
        S = lambda n: es.enter_context(nc.semaphore(n))
        self.s_pe = Stream(nc.tensor)
        self.s_act = Stream(nc.scalar)
        self.s_dve = Stream(nc.vector)
        self.s_pool = Stream(nc.gpsimd)
        self.s_sp = Stream(nc.sync)
        self.streams = [self.s_pe, self.s_act, self.s_dve, self.s_pool, self.s_sp]
        self.PE = CQ(self.s_pe, S("q_pe"))
        self.ACT = CQ(self.s_act, S("q_act"))
        self.DVE = CQ(self.s_dve, S("q_dve"))
        self.POOL = CQ(self.s_pool, S("q_pool"))
        self.SP = DQ(self.s_sp, [S(f"q_sp{i}") for i in range(12)])
        self.GD = DQ(self.s_pool, [S(f"q_gd{i}") for i in range(12)])
        self.AD = DQ(self.s_act, [S(f"q_ad{i}") for i in range(4)])
        self.cqs = [self.PE, self.ACT, self.DVE, self.POOL]
        self.dqs = [self.SP, self.GD, self.AD]
        self.n_inst = 0

    def _wait(self, stream, deps, own=None):
        need = {}
        for d in deps:
            if d is None:
                continue
            (sem, val, q), raw = d
            if own is not None and q is own and not own.is_dma:
                if not raw or own is self.PE:
                    continue
            k = id(sem)
            if stream.seen.get(k, 0) >= val:
                continue
            if k not in need or need[k][1] < val:
                need[k] = (sem, val)
        for k, (sem, val) in need.items():
            stream.eng.wait_ge(sem, val)
            stream.seen[k] = val
            self.n_inst += 1
            stream.ev.append(("w", k, val))

    @staticmethod
    def _deps(R, W):
        deps = []
        for b in R:
            if b.lw is not None:
                deps.append((b.lw, True))
        for b in W:
            if b.lw is not None:
                deps.append((b.lw, False))
            deps.extend((t, False) for t in b.rd.values())
        return deps

    @staticmethod
    def _mark(tok, R, W):
        k = id(tok[0])
        for b in R:
            o = b.rd.get(k)
            if o is None or o[1] < tok[1]:
                b.rd[k] = tok
        for b in W:
            b.lw = tok
            b.rd = {}

    def op(self, q, fn, R=(), W=(), inc=True):
        self._wait(q.stream, self._deps(R, W), own=q)
        inst = fn()
        self.n_inst += 1
        if inc:
            inst.then_inc(q.sem, 1)
            q.count += 1
            tok = (q.sem, q.count, q)
            q.stream.ev.append(("i", id(q.sem), 1))
        else:
            tok = (q.sem, q.count + 1, q)
        self._mark(tok, R, W)
        return inst

    def dma(self, q, out, in_, R=(), W=(), **kw):
        st = q.stream
        self._wait(st, self._deps(R, W), own=None)
        nd = 1
        for d_ in tuple(out.shape)[:-1]:
            nd *= int(d_)
        LIM = 1536
        q.outst = [o for o in q.outst if st.seen.get(id(o[0]), 0) < o[1]]
        while q.outst and sum(o[2] for o in q.outst) + nd > LIM:
            osem, oval, _ = q.outst.pop(0)
            if st.seen.get(id(osem), 0) < oval:
                st.eng.wait_ge(osem, oval)
                st.seen[id(osem)] = oval
                st.ev.append(("w", id(osem), oval))
        slot = q.idx % len(q.sems)
        q.idx += 1
        sem = q.sems[slot]
        if q.cnt[slot] > 0 and st.seen.get(id(sem), 0) < q.cnt[slot]:
            st.eng.wait_ge(sem, q.cnt[slot])
            st.seen[id(sem)] = q.cnt[slot]
            st.ev.append(("w", id(sem), q.cnt[slot]))
        inst = st.eng.dma_start(out=out, in_=in_, **kw)
        inst.then_inc(sem, 16)
        st.ev.append(("i", id(sem), 16))
        self.n_inst += 1
        q.cnt[slot] += 16
        tok = (sem, q.cnt[slot], q)
        q.outst.append((sem, q.cnt[slot], nd))
        self._mark(tok, R, W)
        return inst

    def barrier(self, streams=None):
        deps = []
        for q in self.cqs:
            if q.count > 0:
                deps.append(((q.sem, q.count, q), True))
        for q in self.dqs:
            for s, c in zip(q.sems, q.cnt):
                if c > 0:
                    deps.append(((s, c, q), True))
        for st in (streams or self.streams):
            self._wait(st, deps, own=None)

    def check_deadlock(self):
        vals = {}
        pos = [0] * len(self.streams)
        progress = True
        while progress:
            progress = False
            for si, st in enumerate(self.streams):
                while pos[si] < len(st.ev):
                    kind, k, v = st.ev[pos[si]]
                    if kind == "w":
                        if vals.get(k, 0) >= v:
                            pos[si] += 1
                            progress = True
                        else:
                            break
                    else:
                        vals[k] = vals.get(k, 0) + v
                        pos[si] += 1
                        progress = True
        stuck = [(si, pos[si], len(st.ev)) for si, st in enumerate(self.streams) if pos[si] < len(st.ev)]
        if stuck:
            for si, p, n in stuck:
                kind, k, v = self.streams[si].ev[p]
                print("DEADLOCK stream", si, "at", p, "/", n, "waiting sem", k, "val", v, "cur", vals.get(k, 0))
            raise RuntimeError("deadlock detected in emitted program")
        return True

    def final_wait(self):
        self.barrier(streams=[self.s_sp])


L = 8192
D = 1024
NCH = 16
CH = 512
IN_W = 2144
FFN = 2816
EPS = 1e-6


class Tl:
    __slots__ = ("t", "b", "cb")

    def __init__(self, t, name=""):
        self.t = t
        self.b = Buf(name)


class Ctx:
    pass


def new_phase(cx):
    cx.fw.barrier()
    if cx.phase_es is not None:
        cx.phase_es.close()
    cx.phase_es = ExitStack()
    cx.es.callback(lambda e=cx.phase_es: e.close())
    cx.tcount += 1000


def SB(cx, shape, dt, name=None):
    cx.tcount += 1
    nm = f"{name or 't'}_{cx.tcount}"
    return Tl(cx.phase_es.enter_context(cx.nc.sbuf_tensor(nm, list(shape), dt)), nm)


def dram(cx, name, shape, dt):
    t = cx.nc.dram_tensor(name, list(shape), dt, kind="Internal").ap()
    tl = Tl(t, name)
    tl.cb = [Buf(name + str(i)) for i in range(NCH)]
    return tl


def rstd_chunk(cx, parts, dim, n=CH):
    fw, nc = cx.fw, cx.nc
    ps = cx.ps[0]
    np_ = len(parts)
    for i, (ap, K, bufs) in enumerate(parts):
        sq = cx.sq[i % 2]
        fw.op(fw.ACT, lambda: nc.scalar.activation(out=sq.t[0:K, 0:n], in_=ap, func=AF.Square), R=bufs, W=[sq.b])
        fw.op(fw.PE, lambda: nc.tensor.matmul(ps.t[:, 0:n], lhsT=cx.ones_f.t[0:K, :], rhs=sq.t[0:K, 0:n], start=(i == 0), stop=(i == np_ - 1)),
              R=[sq.b, cx.ones_f.b], W=[ps.b], inc=True)
    r = cx.rstd[cx.rstd_i % 2]
    cx.rstd_i += 1
    fw.op(fw.ACT, lambda: nc.scalar.activation(out=r.t[:, 0:n], in_=ps.t[:, 0:n], func=AF.Sqrt, bias=cx.eps_t.t[:, 0:1], scale=1.0 / dim), R=[ps.b, cx.eps_t.b], W=[r.b])
    fw.op(fw.DVE, lambda: nc.vector.reciprocal(out=r.t[:, 0:n], in_=r.t[:, 0:n]), R=[r.b], W=[r.b])
    return r


def evac(cx, k, out_ap, in_ap, R, W, scale=None):
    fw, nc = cx.fw, cx.nc
    if k % 2 == 0:
        if scale is None:
            fw.op(fw.ACT, lambda: nc.scalar.copy(out=out_ap, in_=in_ap), R=R, W=W)
        else:
            fw.op(fw.ACT, lambda: nc.scalar.mul(out=out_ap, in_=in_ap, mul=scale), R=R, W=W)
    else:
        if scale is None:
            fw.op(fw.DVE, lambda: nc.vector.tensor_copy(out=out_ap, in_=in_ap), R=R, W=W)
        else:
            fw.op(fw.DVE, lambda: nc.vector.tensor_scalar(out=out_ap, in0=in_ap, scalar1=scale, scalar2=None, op0=ALU.mult), R=R, W=W)


def load_w_bf16(cx, dst, src_ap, kt, ncols, gain=None, stg_cols=2144):
    fw, nc = cx.fw, cx.nc
    for j in range(kt):
        stg = cx.wstg[j % 2]
        fw.dma(fw.SP, out=stg.t[:, 0:ncols], in_=src_ap[j * 128:(j + 1) * 128, :], W=[stg.b])
        if gain is None:
            evac(cx, j, dst.t[:, j, :], stg.t[:, 0:ncols], R=[stg.b], W=[dst.b])
        else:
            if j % 2 == 0:
                fw.op(fw.ACT, lambda: nc.scalar.mul(out=dst.t[:, j, :], in_=stg.t[:, 0:ncols], mul=gain.t[:, j:j + 1]), R=[stg.b, gain.b], W=[dst.b])
            else:
                fw.op(fw.DVE, lambda: nc.vector.tensor_scalar(out=dst.t[:, j, :], in0=stg.t[:, 0:ncols], scalar1=gain.t[:, j:j + 1], scalar2=None, op0=ALU.mult),
                      R=[stg.b, gain.b], W=[dst.b])


def phase_x0(cx, x_ap):
    fw, nc = cx.fw, cx.nc
    new_phase(cx)
    xts = [SB(cx, [128, 4, D], F32, "xt") for _ in range(2)]
    hts = [SB(cx, [128, 8, CH], F32, "ht") for _ in range(2)]
    k = 0
    for c in range(NCH):
        xt, ht = xts[c % 2], hts[c % 2]
        fw.dma(fw.SP, out=xt.t[:], in_=x_ap[c * CH:(c + 1) * CH, :].rearrange("(s p) d -> p s d", p=128), W=[xt.b])
        for j in range(8):
            ps = cx.ps[2 + (k % 6)]
            for s in range(4):
                fw.op(fw.PE, lambda: nc.tensor.transpose(out=ps.t[:, s * 128:(s + 1) * 128], in_=xt.t[:, s, j * 128:(j + 1) * 128], identity=cx.ident_f.t[:]),
                      R=[xt.b, cx.ident_f.b], W=[ps.b], inc=(s == 3))
            evac(cx, k, ht.t[:, j, :], ps.t[:], R=[ps.b], W=[ht.b])
            k += 1
        fw.dma(fw.GD, out=cx.HT.t[:, c * CH:(c + 1) * CH].rearrange("(j p) t -> p j t", p=128), in_=ht.t[:], R=[ht.b], W=[cx.HT.cb[c]])


def phase_out(cx, gfin_ap, out_ap):
    fw, nc = cx.fw, cx.nc
    new_phase(cx)
    gf = SB(cx, [128, 8], F32, "gf")
    fw.dma(fw.SP, out=gf.t[:], in_=gfin_ap, W=[gf.b])
    hts = [SB(cx, [128, 8, CH], F32, "ht") for _ in range(2)]
    ots = [SB(cx, [128, 4, D], F32, "ot") for _ in range(2)]
    k = 0
    for c in range(NCH):
        ht, ot = hts[c % 2], ots[c % 2]
        fw.dma(fw.SP, out=ht.t[:], in_=cx.HT.t[:, c * CH:(c + 1) * CH].rearrange("(j p) t -> p j t", p=128), R=[cx.HT.cb[c]], W=[ht.b])
        r = rstd_chunk(cx, [(ht.t[:, j, :], 128, [ht.b]) for j in range(8)], D)
        for j in range(8):
            fw.op(fw.DVE, lambda: nc.vector.scalar_tensor_tensor(out=ht.t[:, j, :], in0=ht.t[:, j, :], scalar=gf.t[:, j:j + 1], in1=r.t[:],
                                                                 op0=ALU.mult, op1=ALU.mult), R=[ht.b, r.b, gf.b], W=[ht.b])
        for s in range(4):
            for jj in range(2):
                ps = cx.ps[2 + (k % 6)]
                for j4 in range(4):
                    j = jj * 4 + j4
                    fw.op(fw.PE, lambda: nc.tensor.transpose(out=ps.t[:, j4 * 128:(j4 + 1) * 128], in_=ht.t[:, j, s * 128:(s + 1) * 128], identity=cx.ident_f.t[:]),
                          R=[ht.b, cx.ident_f.b], W=[ps.b], inc=(j4 == 3))
                evac(cx, k, ot.t[:, s, jj * 512:(jj + 1) * 512], ps.t[:], R=[ps.b], W=[ot.b])
                k += 1
        fw.dma(fw.GD, out=out_ap[c * CH:(c + 1) * CH, :].rearrange("(s p) d -> p s d", p=128), in_=ot.t[:], R=[ot.b], W=[cx.OUTB[c]])


def phase_a(cx, W, li):
    fw, nc = cx.fw, cx.nc
    new_phase(cx)
    sc = cx.sc
    cx.wstg = [SB(cx, [128, IN_W], F32, "wstg") for _ in range(2)]
    gmix = SB(cx, [128, 8], F32, "gmix")
    fw.dma(fw.SP, out=gmix.t[:], in_=W["g_mix"], W=[gmix.b])
    win = SB(cx, [128, 8, IN_W], BF16, "win")
    load_w_bf16(cx, win, W["w_in"], 8, IN_W, gain=gmix)
    gq = SB(cx, [128, 2], F32, "gq")
    fw.dma(fw.SP, out=gq.t[:], in_=W["mla_g_q"], W=[gq.b])
    gkv = SB(cx, [128, 1], F32, "gkv")
    fw.dma(fw.SP, out=gkv.t[:], in_=W["mla_g_kv"], W=[gkv.b])
    wuq = SB(cx, [128, 2, 384], BF16, "wuq")
    wuq_src = W["mla_w_uq"]
    for kt, (r0, rn) in enumerate([(0, 128), (128, 64)]):
        stg = cx.wstg[kt % 2]
        src = wuq_src[r0:r0 + rn, :].rearrange("r (h c) -> r h c", c=96)
        fw.dma(fw.SP, out=stg.t[0:rn, 0:256].rearrange("r (h c) -> r h c", c=64), in_=src[:, :, 0:64], W=[stg.b])
        fw.dma(fw.SP, out=stg.t[0:rn, 256:320].rearrange("r (h c) -> r h c", c=16), in_=src[:, :, 64:80], W=[stg.b])
        fw.dma(fw.SP, out=stg.t[0:rn, 320:384].rearrange("r (h c) -> r h c", c=16), in_=src[:, :, 80:96], W=[stg.b])
        fw.op(fw.DVE, lambda: nc.vector.tensor_scalar(out=wuq.t[0:rn, kt, :], in0=stg.t[0:rn, 0:384], scalar1=gq.t[0:rn, kt:kt + 1], scalar2=None, op0=ALU.mult),
              R=[stg.b, gq.b], W=[wuq.b])
    wukv = SB(cx, [128, 512], BF16, "wukv")
    stg = cx.wstg[0]
    src = W["mla_w_ukv"].rearrange("r (h c) -> r h c", c=128)
    fw.dma(fw.SP, out=stg.t[:, 0:256].rearrange("r (h c) -> r h c", c=64), in_=src[:, :, 0:64], W=[stg.b])
    fw.dma(fw.SP, out=stg.t[:, 256:512].rearrange("r (h c) -> r h c", c=64), in_=src[:, :, 64:128], W=[stg.b])
    fw.op(fw.DVE, lambda: nc.vector.tensor_scalar(out=wukv.t[:], in0=stg.t[:, 0:512], scalar1=gkv.t[:, 0:1], scalar2=None, op0=ALU.mult),
          R=[stg.b, gkv.b], W=[wukv.b])

    hts = [SB(cx, [128, 8, CH], F32, "ht") for _ in range(2)]
    aT = SB(cx, [128, 8, CH], BF16, "aT")
    zhy = [SB(cx, [128, 6, CH], F32, "zhy") for _ in range(2)]
    zs5 = [SB(cx, [128, 2, CH], F32, "zs5") for _ in range(2)]
    qda = [SB(cx, [128, 2, CH], BF16, "qda") for _ in range(2)]
    kda = [SB(cx, [128, 2, CH], BF16, "kda") for _ in range(2)]
    vda = [SB(cx, [128, 4, 256], BF16, "vda") for _ in range(2)]
    cq = SB(cx, [128, 2, CH], F32, "cq")
    ckv = SB(cx, [128, CH], F32, "ckv")
    cqn = SB(cx, [128, 2, CH], BF16, "cqn")
    ckvn = SB(cx, [128, CH], BF16, "ckvn")
    qn_st = [SB(cx, [128, 2, CH], BF16, "qnst") for _ in range(2)]
    kn_st = [SB(cx, [128, 2, CH], BF16, "knst") for _ in range(2)]
    vm_st = [SB(cx, [128, 4, 256], BF16, "vmst") for _ in range(2)]
    csq = [SB(cx, [64, 2, CH], F32, "csq") for _ in range(2)]
    csk = [SB(cx, [16, 2, CH], F32, "csk") for _ in range(2)]
    t1 = SB(cx, [64, CH], F32, "t1")
    t2 = SB(cx, [64, CH], F32, "t2")
    qr_st = [SB(cx, [64, 2, CH], BF16, "qrst") for _ in range(2)]
    kr_st = [SB(cx, [16, 2, CH], BF16, "krst") for _ in range(2)]

    kk = [0]

    def nextps():
        p = cx.ps[2 + (kk[0] % 6)]
        kk[0] += 1
        return p

    def mm_group(ps_ap, psb, lhs_list, rhs_list, Rb):
        n = len(lhs_list)
        for i in range(n):
            fw.op(fw.PE, lambda: nc.tensor.matmul(ps_ap, lhsT=lhs_list[i], rhs=rhs_list[i], start=(i == 0), stop=(i == n - 1)),
                  R=Rb, W=[psb], inc=(i == n - 1))

    for c in range(NCH):
        tsl = slice(c * CH, (c + 1) * CH)
        ht = hts[c % 2]
        fw.dma(fw.SP, out=ht.t[:], in_=cx.HT.t[:, tsl].rearrange("(j p) t -> p j t", p=128), R=[cx.HT.cb[c]], W=[ht.b])
        fw.dma(fw.SP, out=csq[c % 2].t[:], in_=cx.C["cs_q"][:, :, tsl].rearrange("a p t -> p a t"), W=[csq[c % 2].b])
        fw.dma(fw.SP, out=csk[c % 2].t[:], in_=cx.C["cs_k"][:, :, tsl].rearrange("a p t -> p a t"), W=[csk[c % 2].b])
        r = rstd_chunk(cx, [(ht.t[:, j, :], 128, [ht.b]) for j in range(8)], D)
        fw.op(fw.DVE, lambda: nc.vector.tensor_tensor(out=aT.t[:], in0=ht.t[:], in1=r.t[:].unsqueeze(1).broadcast_to([128, 8, CH]), op=ALU.mult),
              R=[ht.b, r.b], W=[aT.b])
        rhs8 = [aT.t[:, j, :] for j in range(8)]

        def colmm(c0, m):
            ps = nextps()
            mm_group(ps.t[0:m, :], ps.b, [win.t[:, j, c0:c0 + m] for j in range(8)], rhs8, [win.b, aT.b])
            return ps
        for i in range(6):
            ps = colmm(i * 128, 128)
            evac(cx, kk[0], zhy[c % 2].t[:, i, :], ps.t[:], R=[ps.b], W=[zhy[c % 2].b])
        fw.dma(fw.GD, out=sc["ZHY"].t[:, tsl].rearrange("(j p) t -> p j t", p=128), in_=zhy[c % 2].t[:], R=[zhy[c % 2].b], W=[sc["ZHY"].cb[c]])
        for i in range(2):
            ps = colmm(768 + i * 128, 128)
            evac(cx, kk[0], zs5[c % 2].t[:, i, :], ps.t[:], R=[ps.b], W=[zs5[c % 2].b])
        fw.dma(fw.GD, out=sc["ZS5"].t[:, tsl].rearrange("(j p) t -> p j t", p=128), in_=zs5[c % 2].t[:], R=[zs5[c % 2].b], W=[sc["ZS5"].cb[c]])
        for i in range(2):
            ps = colmm(1024 + i * 128, 128)
            evac(cx, kk[0], qda[c % 2].t[:, i, :], ps.t[:], R=[ps.b], W=[qda[c % 2].b], scale=32 ** -0.5)
        fw.dma(fw.GD, out=sc["QDA"].t[:, tsl].rearrange("(j p) t -> p j t", p=128), in_=qda[c % 2].t[:], R=[qda[c % 2].b], W=[sc["QDA"].cb[c]])
        for i in range(2):
            ps = colmm(1280 + i * 128, 128)
            evac(cx, kk[0], kda[c % 2].t[:, i, :], ps.t[:], R=[ps.b], W=[kda[c % 2].b])
        fw.dma(fw.GD, out=sc["KDA"].t[:, tsl].rearrange("(j p) t -> p j t", p=128), in_=kda[c % 2].t[:], R=[kda[c % 2].b], W=[sc["KDA"].cb[c]])
        for s in range(4):
            ps = nextps()
            mm_group(ps.t[:, 0:256], ps.b, [aT.t[:, j, s * 128:(s + 1) * 128] for j in range(8)], [win.t[:, j, 1536:1792] for j in range(8)], [win.b, aT.b])
            evac(cx, kk[0], vda[c % 2].t[:, s, :], ps.t[:, 0:256], R=[ps.b], W=[vda[c % 2].b])
        fw.dma(fw.GD, out=sc["VDA"].t[tsl, :].rearrange("(s p) d -> p s d", p=128), in_=vda[c % 2].t[:], R=[vda[c % 2].b], W=[sc["VDA"].cb[c]])
        ps = colmm(1792, 128)
        evac(cx, kk[0], cq.t[:, 0, :], ps.t[:], R=[ps.b], W=[cq.b])
        ps = colmm(1920, 64)
        evac(cx, kk[0], cq.t[0:64, 1, :], ps.t[0:64, :], R=[ps.b], W=[cq.b])
        ps = colmm(1984, 128)
        evac(cx, kk[0], ckv.t[:], ps.t[:], R=[ps.b], W=[ckv.b])
        rq = rstd_chunk(cx, [(cq.t[:, 0, :], 128, [cq.b]), (cq.t[0:64, 1, :], 64, [cq.b])], 192)
        fw.op(fw.DVE, lambda: nc.vector.tensor_tensor(out=cqn.t[:, 0, :], in0=cq.t[:, 0, :], in1=rq.t[:], op=ALU.mult), R=[cq.b, rq.b], W=[cqn.b])
        fw.op(fw.DVE, lambda: nc.vector.tensor_tensor(out=cqn.t[0:64, 1, :], in0=cq.t[0:64, 1, :], in1=rq.t[0:64, :], op=ALU.mult), R=[cq.b, rq.b], W=[cqn.b])
        rkv = rstd_chunk(cx, [(ckv.t[:], 128, [ckv.b])], 128)
        fw.op(fw.DVE, lambda: nc.vector.tensor_tensor(out=ckvn.t[:], in0=ckv.t[:], in1=rkv.t[:], op=ALU.mult), R=[ckv.b, rkv.b], W=[ckvn.b])
        qrhs = [cqn.t[:, 0, :], cqn.t[0:64, 1, :]]
        for i in range(2):
            ps = nextps()
            mm_group(ps.t[:, :], ps.b, [wuq.t[:, 0, i * 128:(i + 1) * 128], wuq.t[0:64, 1, i * 128:(i + 1) * 128]], qrhs, [wuq.b, cqn.b])
            evac(cx, kk[0], qn_st[c % 2].t[:, i, :], ps.t[:], R=[ps.b], W=[qn_st[c % 2].b], scale=96 ** -0.5)
        for h in range(4):
            fw.dma(fw.GD, out=sc["QMLA"].t[h, 0:64, tsl], in_=qn_st[c % 2].t[(h % 2) * 64:(h % 2) * 64 + 64, h // 2, :], R=[qn_st[c % 2].b], W=[sc["QMLA"].cb[c]])
        ps1 = nextps()
        mm_group(ps1.t[0:64, :], ps1.b, [wuq.t[:, 0, 256:320], wuq.t[0:64, 1, 256:320]], qrhs, [wuq.b, cqn.b])
        ps2 = nextps()
        mm_group(ps2.t[0:64, :], ps2.b, [wuq.t[:, 0, 320:384], wuq.t[0:64, 1, 320:384]], qrhs, [wuq.b, cqn.b])
        cs = csq[c % 2]
        qr = qr_st[c % 2]
        fw.op(fw.DVE, lambda: nc.vector.tensor_tensor(out=t1.t[:], in0=ps1.t[0:64, :], in1=cs.t[:, 0, :], op=ALU.mult), R=[ps1.b, cs.b], W=[t1.b])
        fw.op(fw.DVE, lambda: nc.vector.tensor_tensor(out=t2.t[:], in0=ps2.t[0:64, :], in1=cs.t[:, 1, :], op=ALU.mult), R=[ps2.b, cs.b], W=[t2.b])
        fw.op(fw.DVE, lambda: nc.vector.tensor_tensor(out=qr.t[:, 0, :], in0=t1.t[:], in1=t2.t[:], op=ALU.subtract), R=[t1.b, t2.b], W=[qr.b])
        fw.op(fw.DVE, lambda: nc.vector.tensor_tensor(out=t1.t[:], in0=ps2.t[0:64, :], in1=cs.t[:, 0, :], op=ALU.mult), R=[ps2.b, cs.b], W=[t1.b])
        fw.op(fw.DVE, lambda: nc.vector.tensor_tensor(out=t2.t[:], in0=ps1.t[0:64, :], in1=cs.t[:, 1, :], op=ALU.mult), R=[ps1.b, cs.b], W=[t2.b])
        fw.op(fw.DVE, lambda: nc.vector.tensor_tensor(out=qr.t[:, 1, :], in0=t1.t[:], in1=t2.t[:], op=ALU.add), R=[t1.b, t2.b], W=[qr.b])
        for h in range(4):
            fw.dma(fw.GD, out=sc["QMLA"].t[h, 64:80, tsl], in_=qr.t[h * 16:(h + 1) * 16, 0, :], R=[qr.b], W=[sc["QMLA"].cb[c]])
            fw.dma(fw.GD, out=sc["QMLA"].t[h, 80:96, tsl], in_=qr.t[h * 16:(h + 1) * 16, 1, :], R=[qr.b], W=[sc["QMLA"].cb[c]])
        for i in range(2):
            ps = nextps()
            mm_group(ps.t[:, :], ps.b, [wukv.t[:, i * 128:(i + 1) * 128]], [ckvn.t[:]], [wukv.b, ckvn.b])
            evac(cx, kk[0], kn_st[c % 2].t[:, i, :], ps.t[:], R=[ps.b], W=[kn_st[c % 2].b])
        for h in range(4):
            fw.dma(fw.GD, out=sc["KMLA"].t[h, 0:64, tsl], in_=kn_st[c % 2].t[(h % 2) * 64:(h % 2) * 64 + 64, h // 2, :], R=[kn_st[c % 2].b], W=[sc["KMLA"].cb[c]])
        for s in range(4):
            ps = nextps()
            mm_group(ps.t[:, 0:256], ps.b, [ckvn.t[:, s * 128:(s + 1) * 128]], [wukv.t[:, 256:512]], [wukv.b, ckvn.b])
            evac(cx, kk[0], vm_st[c % 2].t[:, s, :], ps.t[:, 0:256], R=[ps.b], W=[vm_st[c % 2].b])
        fw.dma(fw.GD, out=sc["VMLA"].t[tsl, :].rearrange("(s p) d -> p s d", p=128), in_=vm_st[c % 2].t[:], R=[vm_st[c % 2].b], W=[sc["VMLA"].cb[c]])
        ps1 = colmm(2112, 16)
        ps2 = colmm(2128, 16)
        ck = csk[c % 2]
        kr = kr_st[c % 2]
        fw.op(fw.DVE, lambda: nc.vector.tensor_tensor(out=t1.t[0:16, :], in0=ps1.t[0:16, :], in1=ck.t[:, 0, :], op=ALU.mult), R=[ps1.b, ck.b], W=[t1.b])
        fw.op(fw.DVE, lambda: nc.vector.tensor_tensor(out=t2.t[0:16, :], in0=ps2.t[0:16, :], in1=ck.t[:, 1, :], op=ALU.mult), R=[ps2.b, ck.b], W=[t2.b])
        fw.op(fw.DVE, lambda: nc.vector.tensor_tensor(out=kr.t[:, 0, :], in0=t1.t[0:16, :], in1=t2.t[0:16, :], op=ALU.subtract), R=[t1.b, t2.b], W=[kr.b])
        fw.op(fw.DVE, lambda: nc.vector.tensor_tensor(out=t1.t[0:16, :], in0=ps2.t[0:16, :], in1=ck.t[:, 0, :], op=ALU.mult), R=[ps2.b, ck.b], W=[t1.b])
        fw.op(fw.DVE, lambda: nc.vector.tensor_tensor(out=t2.t[0:16, :], in0=ps1.t[0:16, :], in1=ck.t[:, 1, :], op=ALU.mult), R=[ps1.b, ck.b], W=[t2.b])
        fw.op(fw.DVE, lambda: nc.vector.tensor_tensor(out=kr.t[:, 1, :], in0=t1.t[0:16, :], in1=t2.t[0:16, :], op=ALU.add), R=[t1.b, t2.b], W=[kr.b])
        for h in range(4):
            fw.dma(fw.GD, out=sc["KMLA"].t[h, 64:80, tsl], in_=kr.t[:, 0, :], R=[kr.b], W=[sc["KMLA"].cb[c]])
            fw.dma(fw.GD, out=sc["KMLA"].t[h, 80:96, tsl], in_=kr.t[:, 1, :], R=[kr.b], W=[sc["KMLA"].cb[c]])


def attn_load_qk(cx, at, i, QT_src, KT_src):
    fw = cx.fw
    QT, KT = at["QT"][i], at["KT"][i]
    for (ap, r0, nr) in QT_src:
        fw.dma(fw.SP, out=QT.t[r0:r0 + nr, :], in_=ap, W=[QT.b])
    for (ap, r0, nr) in KT_src:
        fw.dma(fw.SP, out=KT.t[r0:r0 + nr, :], in_=ap, W=[KT.b])


def attn_head(cx, at, qi, r0, V_src, dk, slope_idx, on_out, ki=None, kfull=None):
    fw, nc = cx.fw, cx.nc
    i = at["n"] % 2
    at["n"] += 1
    QT, KT, V1 = at["QT"][qi], at["KT"][qi if ki is None else ki], at["V1"][i]
    r1 = r0 + dk
    m0, m1 = (r0, r1) if kfull is None else (0, kfull)
    VALL = at["VALL"]
    fw.op(fw.POOL, lambda: nc.gpsimd.tensor_copy(out=V1.t[:, :, 0:64], in_=VALL.t[:, :, V_src * 64:(V_src + 1) * 64]), R=[VALL.b], W=[V1.b])
    fw.op(fw.POOL, lambda: nc.gpsimd.memset(V1.t[:, :, 64:65], 1.0), W=[V1.b])
    mx = at["mx"]
    for which, T_ in enumerate([QT, KT]):
        for c in range(NCH):
            sq = cx.sq[c % 2]
            ps = cx.ps[c % 2]
            fw.op(fw.ACT, lambda: nc.scalar.activation(out=sq.t[0:dk, :], in_=T_.t[r0:r1, c * CH:(c + 1) * CH], func=AF.Square), R=[T_.b], W=[sq.b])
            fw.op(fw.PE, lambda: nc.tensor.matmul(ps.t[:, :], lhsT=cx.ones_f.t[0:dk, :], rhs=sq.t[0:dk, :], start=True, stop=True), R=[sq.b, cx.ones_f.b], W=[ps.b])
            fw.op(fw.DVE, lambda: nc.vector.reduce_max(out=mx.t[:, which * 16 + c:which * 16 + c + 1], in_=ps.t[:, :], axis=AX.X), R=[ps.b], W=[mx.b])
    m2 = at["m2"]
    fw.op(fw.DVE, lambda: nc.vector.reduce_max(out=m2.t[:, 0:1], in_=mx.t[:, 0:16], axis=AX.X), R=[mx.b], W=[m2.b])
    fw.op(fw.DVE, lambda: nc.vector.reduce_max(out=m2.t[:, 1:2], in_=mx.t[:, 16:32], axis=AX.X), R=[mx.b], W=[m2.b])
    fw.op(fw.DVE, lambda: nc.vector.tensor_tensor(out=m2.t[:, 2:3], in0=m2.t[:, 0:1], in1=m2.t[:, 1:2], op=ALU.mult), R=[m2.b], W=[m2.b])
    fw.op(fw.ACT, lambda: nc.scalar.activation(out=m2.t[:, 3:4], in_=m2.t[:, 2:3], func=AF.Sqrt), R=[m2.b], W=[m2.b])
    negC = at["negC"][i]
    fw.op(fw.DVE, lambda: nc.vector.tensor_scalar(out=negC.t[:, 0:1], in0=m2.t[:, 3:4], scalar1=-1.0, scalar2=None, op0=ALU.mult), R=[m2.b], W=[negC.b])
    if slope_idx is not None:
        BT = at["BT"][i]
        fw.dma(fw.SP, out=BT.t[:], in_=cx.C["da_bt"][slope_idx], W=[BT.b])
        fw.op(fw.DVE, lambda: nc.vector.tensor_scalar(out=BT.t[:], in0=BT.t[:], scalar1=negC.t[:, 0:1], scalar2=None, op0=ALU.add), R=[BT.b, negC.b], W=[BT.b])
        DG, FLR = at["DG"][slope_idx % 2], at["FLR"][slope_idx % 2]
    for qc in range(NCH):
        qsl = slice(qc * CH, (qc + 1) * CH)
        aset = at["acc_i"] % 2
        at["acc_i"] += 1
        if slope_idx is None:
            accs = [None, cx.ps[at["acc_banks"][aset % len(at["acc_banks"])]], None]
        else:
            accs = [cx.ps[4 + r] for r in range(3)]
        def region(kc):
            if slope_idx is None:
                return 1
            dl = kc - 4 * qc
            return 0 if dl < 0 else (1 if dl < 4 else 2)
        kcs = list(range(64))
        if slope_idx is not None and at.get("window") is not None:
            thr = at["window"] / cx.slopes[slope_idx]
            keep = []
            for kc in kcs:
                dl = kc - 4 * qc
                if dl < 0:
                    dmin = qc * CH - (kc * 128 + 127)
                elif dl >= 4:
                    dmin = kc * 128 - (qc * CH + 511)
                else:
                    dmin = 0
                if dmin <= thr:
                    keep.append(kc)
            kcs = keep
        regs = [region(kc) for kc in kcs]
        first = {}
        last = {}
        for idx, rg in enumerate(regs):
            first.setdefault(rg, idx)
            last[rg] = idx
        nst = at["nst"]

        def emit_qk(idx):
            kc = kcs[idx]
            st = cx.ps[at["st_banks"][at["st_i"] % nst]]
            at["st_i"] += 1
            fw.op(fw.PE, lambda: nc.tensor.matmul(st.t[:, :], lhsT=KT.t[m0:m1, kc * 128:(kc + 1) * 128], rhs=QT.t[m0:m1, qsl], start=True, stop=True),
                  R=[KT.b, QT.b], W=[st.b])
            return st

        def emit_exp(idx, st):
            kc = kcs[idx]
            rg = regs[idx]
            pt = at["PT"][at["pt_i"] % len(at["PT"])]
            at["pt_i"] += 1
            if slope_idx is None:
                fw.op(fw.ACT, lambda: nc.scalar.activation(out=pt.t[:], in_=st.t[:], func=AF.Exp, bias=negC.t[:, 0:1], scale=1.0), R=[st.b, negC.b], W=[pt.b])
            elif rg == 1:
                dl = kc - 4 * qc
                tmp = at["tmp"][at["tmp_i"] % 2]
                at["tmp_i"] += 1
                fw.op(fw.DVE, lambda: nc.vector.tensor_tensor(out=tmp.t[:], in0=st.t[:], in1=DG.t[:, dl, :], op=ALU.add), R=[st.b, DG.b], W=[tmp.b])
                fw.op(fw.ACT, lambda: nc.scalar.activation(out=pt.t[:], in_=tmp.t[:], func=AF.Exp, bias=negC.t[:, 0:1], scale=1.0), R=[tmp.b, negC.b], W=[pt.b])
            else:
                dl = kc - 4 * qc
                fw.op(fw.ACT, lambda: nc.scalar.activation(out=pt.t[:], in_=st.t[:], func=AF.Exp, bias=BT.t[:, dl + 63:dl + 64], scale=1.0), R=[st.b, BT.b], W=[pt.b])
            return pt

        def emit_pv(idx, pt):
            kc = kcs[idx]
            rg = regs[idx]
            acc = accs[rg]
            fw.op(fw.PE, lambda: nc.tensor.matmul(acc.t[0:65, :], lhsT=V1.t[:, kc, :], rhs=pt.t[:], start=(first[rg] == idx), stop=(last[rg] == idx)),
                  R=[V1.b, pt.b], W=[acc.b], inc=True)
        look = nst - 1
        sts = {}
        for idx in range(min(look, len(kcs))):
            sts[idx] = emit_qk(idx)
        for idx in range(len(kcs)):
            if idx + look < len(kcs):
                sts[idx + look] = emit_qk(idx + look)
            pt = emit_exp(idx, sts.pop(idx))
            emit_pv(idx, pt)
            if idx % 4 == 3:
                yield
        comb = at["comb"][at["comb_i"] % 2]
        at["comb_i"] += 1
        if slope_idx is None:
            fw.op(fw.DVE, lambda: nc.vector.tensor_copy(out=comb.t[:], in_=accs[1].t[0:65, :]), R=[accs[1].b], W=[comb.b])
        else:
            fw.op(fw.DVE, lambda: nc.vector.tensor_copy(out=comb.t[:], in_=accs[1].t[0:65, :]), R=[accs[1].b], W=[comb.b])
            t65 = at["t65"]
            if 0 in first:
                fw.op(fw.DVE, lambda: nc.vector.tensor_tensor(out=t65.t[:], in0=accs[0].t[0:65, :], in1=FLR.t[:, 0, :], op=ALU.mult), R=[accs[0].b, FLR.b], W=[t65.b])
                fw.op(fw.DVE, lambda: nc.vector.tensor_tensor(out=comb.t[:], in0=comb.t[:], in1=t65.t[:], op=ALU.add), R=[comb.b, t65.b], W=[comb.b])
            if 2 in first:
                fw.op(fw.DVE, lambda: nc.vector.tensor_tensor(out=t65.t[:], in0=accs[2].t[0:65, :], in1=FLR.t[:, 1, :], op=ALU.mult), R=[accs[2].b, FLR.b], W=[t65.b])
                fw.op(fw.DVE, lambda: nc.vector.tensor_tensor(out=comb.t[:], in0=comb.t[:], in1=t65.t[:], op=ALU.add), R=[comb.b, t65.b], W=[comb.b])
        if slope_idx is None:
            if at["bs_banks"] == at["st_banks"]:
                bs = cx.ps[at["st_banks"][at["st_i"] % at["nst"]]]
                at["st_i"] += 1
            else:
                bs = cx.ps[at["bs_banks"][at["bs_i"] % len(at["bs_banks"])]]
                at["bs_i"] += 1
        else:
            bs = cx.ps[7]
        fw.op(fw.PE, lambda: nc.tensor.matmul(bs.t[0:64, :], lhsT=at["sel"].t[0:65, 0:64], rhs=comb.t[0:65, :], start=True, stop=True), R=[at["sel"].b, comb.b], W=[bs.b])
        rs = at["rs"]
        fw.op(fw.DVE, lambda: nc.vector.reciprocal(out=rs.t[:], in_=bs.t[0:64, :]), R=[bs.b], W=[rs.b])
        on = at["on"][at["on_i"] % 2]
        at["on_i"] += 1
        fw.op(fw.DVE, lambda: nc.vector.tensor_tensor(out=on.t[:], in0=comb.t[0:64, :], in1=rs.t[:], op=ALU.mult), R=[comb.b, rs.b], W=[on.b])
        on_out(qc, on)


def attn_setup(cx, alibi, V_all, shared=False):
    fw, nc = cx.fw, cx.nc
    at = {"n": 0, "acc_i": 0, "st_i": 0, "pt_i": 0, "tmp_i": 0, "comb_i": 0, "on_i": 0, "bs_i": 0}
    at["st_banks"] = [0, 1, 2, 3]
    at["acc_banks"] = [4, 5]
    at["bs_banks"] = [6, 7]
    at["nst"] = 4
    if shared:
        at["st_banks"] = [0, 1]
        at["acc_banks"] = [2]
        at["bs_banks"] = [0, 1]
        at["nst"] = 2
    if alibi:
        at["QT"] = [SB(cx, [128, L], BF16, "QT") for _ in range(2)]
        at["KT"] = [SB(cx, [128, L], BF16, "KT") for _ in range(1)]
        for t_ in at["QT"] + at["KT"]:
            fw.op(fw.POOL, lambda: nc.gpsimd.memset(t_.t[:], 0.0), W=[t_.b])
    else:
        nq_ = 1 if shared else 2
        at["QT"] = [SB(cx, [96, L], BF16, "QT") for _ in range(nq_)] * (2 // nq_)
        at["KT"] = [SB(cx, [96, L], BF16, "KT") for _ in range(nq_)] * (2 // nq_)
    at["V1"] = [SB(cx, [128, 64, 65], BF16, "V1") for _ in range(2)]
    at["PT"] = [SB(cx, [128, CH], BF16, "PT") for _ in range(4 if shared else 6)]
    at["mx"] = SB(cx, [128, 32], F32, "mx")
    at["m2"] = SB(cx, [128, 4], F32, "m2")
    at["negC"] = [SB(cx, [128, 1], F32, "negC") for _ in range(2)]
    at["comb"] = [SB(cx, [65, CH], F32, "comb") for _ in range(2)]
    at["rs"] = SB(cx, [64, CH], F32, "rs")
    at["on"] = [SB(cx, [64, CH], F32, "on") for _ in range(2)]
    at["sel"] = SB(cx, [65, 64], F32, "sel")
    at["VALL"] = SB(cx, [128, 64, 256], BF16, "VALL")
    vsrc = V_all.rearrange("(kc p) d -> p kc d", p=128)
    for i in range(8):
        fw.dma(fw.SP, out=at["VALL"].t[:, i * 8:(i + 1) * 8, :], in_=vsrc[:, i * 8:(i + 1) * 8, :], W=[at["VALL"].b])
    fw.dma(fw.SP, out=at["sel"].t[:], in_=cx.C["sel"], W=[at["sel"].b])
    if alibi:
        at["BT"] = [SB(cx, [128, 127], F32, "BT") for _ in range(2)]
        at["tmp"] = [SB(cx, [128, CH], F32, "tmp") for _ in range(2)]
        at["t65"] = SB(cx, [65, CH], F32, "t65")
        at["DG"] = [SB(cx, [128, 4, CH], F32, "DG") for _ in range(2)]
        at["FLR"] = [SB(cx, [65, 2, CH], F32, "FLR") for _ in range(2)]
    return at


def mla_gen(cx, at):
    fw, nc = cx.fw, cx.nc
    sc = cx.sc
    for h in range(4):
        def on_out(qc, on, h=h):
            fw.dma(fw.GD, out=sc["YMLA"].t[h * 64:(h + 1) * 64, qc * CH:(qc + 1) * CH], in_=on.t[:], R=[on.b], W=[sc["YMLA"].cb[qc]])
        attn_load_qk(cx, at, h % 2, [(sc["QMLA"].t[h], 0, 96)], [(sc["KMLA"].t[h], 0, 96)])
        yield from attn_head(cx, at, h % 2, 0, h, 96, None, on_out)


def phase_mla(cx):
    new_phase(cx)
    at = attn_setup(cx, False, cx.sc["VMLA"].t)
    for _ in mla_gen(cx, at):
        pass


def phase_mla_s5(cx, W, li):
    new_phase(cx)
    at = attn_setup(cx, False, cx.sc["VMLA"].t, shared=True)
    g1 = mla_gen(cx, at)
    g2 = s5_main(cx, W, li, shared=True)
    a1 = a2 = True
    while a1 or a2:
        if a1:
            for _ in range(4):
                try:
                    next(g1)
                except StopIteration:
                    a1 = False
                    break
        if a2:
            try:
                next(g2)
            except StopIteration:
                a2 = False
    s5_glu(cx, W, li)


def phase_da(cx, W, li):
    fw, nc = cx.fw, cx.nc
    new_phase(cx)
    sc = cx.sc
    at = attn_setup(cx, True, sc["VDA"].t)
    at["window"] = cx.window
    lam_init = 0.8 - 0.6 * math.exp(-0.3 * li)
    lv = SB(cx, [64, 128], F32, "lv")
    fw.dma(fw.SP, out=lv.t[:], in_=W["da_lambda"].rearrange("a b -> (a b)").partition_broadcast(64), W=[lv.b])
    lp = SB(cx, [64, 64], F32, "lp")
    fw.op(fw.DVE, lambda: nc.vector.tensor_tensor(out=lp.t[:, 0:32], in0=lv.t[:, 0:32], in1=lv.t[:, 32:64], op=ALU.mult), R=[lv.b], W=[lp.b])
    fw.op(fw.DVE, lambda: nc.vector.tensor_tensor(out=lp.t[:, 32:64], in0=lv.t[:, 64:96], in1=lv.t[:, 96:128], op=ALU.mult), R=[lv.b], W=[lp.b])
    ls = SB(cx, [64, 4], F32, "ls")
    fw.op(fw.DVE, lambda: nc.vector.reduce_sum(out=ls.t[:, 0:1], in_=lp.t[:, 0:32], axis=AX.X), R=[lp.b], W=[ls.b])
    fw.op(fw.DVE, lambda: nc.vector.reduce_sum(out=ls.t[:, 1:2], in_=lp.t[:, 32:64], axis=AX.X), R=[lp.b], W=[ls.b])
    fw.op(fw.ACT, lambda: nc.scalar.activation(out=ls.t[:, 0:2], in_=ls.t[:, 0:2], func=AF.Exp), R=[ls.b], W=[ls.b])
    fw.op(fw.DVE, lambda: nc.vector.tensor_tensor(out=ls.t[:, 2:3], in0=ls.t[:, 1:2], in1=ls.t[:, 0:1], op=ALU.subtract), R=[ls.b], W=[ls.b])
    fw.op(fw.DVE, lambda: nc.vector.tensor_scalar(out=ls.t[:, 3:4], in0=ls.t[:, 2:3], scalar1=-lam_init, scalar2=None, op0=ALU.add), R=[ls.b], W=[ls.b])
    gh = SB(cx, [64, 1], F32, "gh")
    fw.dma(fw.SP, out=gh.t[:], in_=W["da_g_head"], W=[gh.b])
    fw.op(fw.DVE, lambda: nc.vector.tensor_scalar(out=gh.t[:], in0=gh.t[:], scalar1=1.0 - lam_init, scalar2=None, op0=ALU.mult), R=[gh.b], W=[gh.b])
    O1 = SB(cx, [64, L], F32, "O1")
    dd = [SB(cx, [64, CH], F32, "dd") for _ in range(2)]
    for h in range(4):
        def out1(qc, on):
            fw.op(fw.ACT, lambda: nc.scalar.copy(out=O1.t[:, qc * CH:(qc + 1) * CH], in_=on.t[:]), R=[on.b], W=[O1.b])

        def out2(qc, on, h=h):
            d_ = dd[qc % 2]
            fw.op(fw.DVE, lambda: nc.vector.scalar_tensor_tensor(out=d_.t[:], in0=on.t[:], scalar=ls.t[:, 3:4], in1=O1.t[:, qc * CH:(qc + 1) * CH],
                                                                 op0=ALU.mult, op1=ALU.add), R=[on.b, ls.b, O1.b], W=[d_.b])
            r = rstd_chunk(cx, [(d_.t[:], 64, [d_.b])], 64)
            fw.op(fw.DVE, lambda: nc.vector.scalar_tensor_tensor(out=d_.t[:], in0=d_.t[:], scalar=gh.t[:, 0:1], in1=r.t[0:64, :], op0=ALU.mult, op1=ALU.mult),
                  R=[d_.b, gh.b, r.b], W=[d_.b])
            fw.dma(fw.GD, out=sc["YDA"].t[h * 64:(h + 1) * 64, qc * CH:(qc + 1) * CH], in_=d_.t[:], R=[d_.b], W=[sc["YDA"].cb[qc]])
        fw.dma(fw.SP, out=at["QT"][0].t[0:32, :], in_=sc["QDA"].t[h * 64:h * 64 + 32, :], W=[at["QT"][0].b])
        fw.dma(fw.SP, out=at["QT"][1].t[32:64, :], in_=sc["QDA"].t[h * 64 + 32:h * 64 + 64, :], W=[at["QT"][1].b])
        fw.dma(fw.SP, out=at["KT"][0].t[0:64, :], in_=sc["KDA"].t[h * 64:(h + 1) * 64, :], W=[at["KT"][0].b])
        dg, fl = at["DG"][h % 2], at["FLR"][h % 2]
        fw.dma(fw.SP, out=dg.t[:], in_=cx.C["da_dg"][h].rearrange("a p f -> p a f"), W=[dg.b])
        fw.dma(fw.SP, out=fl.t[:], in_=cx.C["da_flr"][h].rearrange("a p f -> p a f"), W=[fl.b])
        for j, cb in enumerate([out1, out2]):
            for _ in attn_head(cx, at, j, j * 32, h, 32, h, cb, ki=0, kfull=128):
                pass


def phase_s5(cx, W, li):
    new_phase(cx)
    for _ in s5_main(cx, W, li, shared=False):
        pass
    s5_glu(cx, W, li)


def s5_main(cx, W, li, shared=False):
    fw, nc = cx.fw, cx.nc
    sc = cx.sc
    V = fw.DVE
    T = CH
    nb_ = 1 if shared else 2

    def tt(out, a, b, op, R, Wb):
        fw.op(V, lambda: nc.vector.tensor_tensor(out=out, in0=a, in1=b, op=op), R=R, W=Wb)

    def ts(out, a, s1, op0, R, Wb, s2=None, op1=None):
        if op1 is None:
            fw.op(V, lambda: nc.vector.tensor_scalar(out=out, in0=a, scalar1=s1, scalar2=None, op0=op0), R=R, W=Wb)
        else:
            fw.op(V, lambda: nc.vector.tensor_scalar(out=out, in0=a, scalar1=s1, scalar2=s2, op0=op0, op1=op1), R=R, W=Wb)

    def stt(out, a, sc_, b, op0, op1, R, Wb):
        fw.op(V, lambda: nc.vector.scalar_tensor_tensor(out=out, in0=a, scalar=sc_, in1=b, op0=op0, op1=op1), R=R, W=Wb)

    NP = 40
    pr = SB(cx, [128, NP, 16], F32, "s5p")
    PB = pr.b
    col = lambda i: pr.t[:, i, :]
    AR, AI, LDT, DT, ARDT, TH, SS, CC, M32, ZR, ZI, T1, T2, ABR, ABI, NR, DEN, FR, FI, ATR, ATI, N2 = range(22)
    fw.dma(fw.SP, out=col(AR), in_=W["s5_ar"], W=[PB])
    fw.dma(fw.SP, out=col(AI), in_=W["s5_ai"], W=[PB])
    fw.dma(fw.SP, out=col(LDT), in_=W["s5_ldt"], W=[PB])
    halfpi = SB(cx, [128, 1], F32, "halfpi")
    fw.op(V, lambda: nc.vector.memset(halfpi.t[:], math.pi / 2), W=[halfpi.b])
    fw.op(fw.ACT, lambda: nc.scalar.activation(out=col(DT), in_=col(LDT), func=AF.Exp), R=[PB], W=[PB])
    tt(col(ARDT), col(AR), col(DT), ALU.mult, [PB], [PB])
    tt(col(TH), col(AI), col(DT), ALU.mult, [PB], [PB])
    fw.op(fw.ACT, lambda: nc.scalar.activation(out=col(SS), in_=col(TH), func=AF.Sin, scale=1.0 / 32), R=[PB], W=[PB])
    fw.op(fw.ACT, lambda: nc.scalar.activation(out=col(CC), in_=col(TH), func=AF.Sin, scale=1.0 / 32, bias=halfpi.t[:, 0:1]), R=[PB, halfpi.b], W=[PB])
    fw.op(fw.ACT, lambda: nc.scalar.activation(out=col(M32), in_=col(ARDT), func=AF.Exp, scale=1.0 / 32), R=[PB], W=[PB])
    tt(col(ZR), col(M32), col(CC), ALU.mult, [PB], [PB])
    tt(col(ZI), col(M32), col(SS), ALU.mult, [PB], [PB])
    pw = SB(cx, [128, 10, 2, 16], F32, "s5pw")

    def square():
        tt(col(T1), col(ZR), col(ZR), ALU.mult, [PB], [PB])
        tt(col(T2), col(ZI), col(ZI), ALU.mult, [PB], [PB])
        tt(col(T2), col(T1), col(T2), ALU.subtract, [PB], [PB])
        tt(col(T1), col(ZR), col(ZI), ALU.mult, [PB], [PB])
        ts(col(ZI), col(T1), 2.0, ALU.mult, [PB], [PB])
        fw.op(V, lambda: nc.vector.tensor_copy(out=col(ZR), in_=col(T2)), R=[PB], W=[PB])
    for _ in range(5):
        square()
    for j in range(10):
        fw.op(V, lambda: nc.vector.tensor_copy(out=pw.t[:, j, 0, :], in_=col(ZR)), R=[PB], W=[pw.b])
        fw.op(V, lambda: nc.vector.tensor_copy(out=pw.t[:, j, 1, :], in_=col(ZI)), R=[PB], W=[pw.b])
        if j < 9:
            square()
    ABRa, ABIa = pw.t[:, 0, 0, :], pw.t[:, 0, 1, :]
    ts(col(NR), ABRa, -1.0, ALU.add, [pw.b], [PB])
    tt(col(T1), col(AR), col(AR), ALU.mult, [PB], [PB])
    tt(col(T2), col(AI), col(AI), ALU.mult, [PB], [PB])
    tt(col(DEN), col(T1), col(T2), ALU.add, [PB], [PB])
    fw.op(V, lambda: nc.vector.reciprocal(out=col(DEN), in_=col(DEN)), R=[PB], W=[PB])
    tt(col(T1), col(NR), col(AR), ALU.mult, [PB], [PB])
    tt(col(T2), ABIa, col(AI), ALU.mult, [PB, pw.b], [PB])
    tt(col(T1), col(T1), col(T2), ALU.add, [PB], [PB])
    tt(col(FR), col(T1), col(DEN), ALU.mult, [PB], [PB])
    tt(col(T1), ABIa, col(AR), ALU.mult, [PB, pw.b], [PB])
    tt(col(T2), col(NR), col(AI), ALU.mult, [PB], [PB])
    tt(col(T1), col(T1), col(T2), ALU.subtract, [PB], [PB])
    tt(col(FI), col(T1), col(DEN), ALU.mult, [PB], [PB])
    ts(col(N2), col(ARDT), -2.0, ALU.mult, [PB], [PB])
    bsrc = SB(cx, [128, 16, 2, 16], F32, "s5b")
    csrc = SB(cx, [128, 16, 2, 16], F32, "s5c")
    fw.dma(fw.SP, out=bsrc.t[:], in_=W["s5_b"], W=[bsrc.b])
    fw.dma(fw.SP, out=csrc.t[:], in_=W["s5_c"], W=[csrc.b])
    dsk = SB(cx, [32, 8], F32, "s5d")
    fw.dma(fw.SP, out=dsk.t[:], in_=W["s5_d"], W=[dsk.b])
    tau = SB(cx, [128, T], F32, "tau")
    fw.dma(fw.SP, out=tau.t[:], in_=cx.C["tau"], W=[tau.b])
    ones = SB(cx, [128, T], F32, "ones512")
    fw.op(V, lambda: nc.vector.memset(ones.t[:], 1.0), W=[ones.b])
    Ep = [SB(cx, [128, 2, T], F32, "Ep") for _ in range(nb_)] * (2 // nb_)
    Em = [SB(cx, [128, 2, T], F32, "Em") for _ in range(nb_)] * (2 // nb_)
    mg = SB(cx, [128, T], F32, "mg")
    Ub = [SB(cx, [32, L], BF16, "Ub") for _ in range(nb_)] * (2 // nb_)
    Y = SB(cx, [32, L], F32, "Y")
    bpad = SB(cx, [128, 2, 32], F32, "bpad")
    cpad = SB(cx, [128, 2, 32], F32, "cpad")
    bb = SB(cx, [128, 2, 16], F32, "bb")
    BT_ = [SB(cx, [32, 2, 128], BF16, "BTl") for _ in range(2)]
    CT_ = [SB(cx, [128, 2, 32], BF16, "CTl") for _ in range(2)]
    Pt = [SB(cx, [128, 2, T], F32, "Pt") for _ in range(2)]
    St = [SB(cx, [128, 2, T], F32, "St") for _ in range(2)]
    tm = [SB(cx, [128, 2, T], F32, "tm") for _ in range(2)]
    xb = [SB(cx, [128, 2, T], BF16, "xb") for _ in range(2)]
    ax = SB(cx, [128, 4], F32, "ax")
    yev = [SB(cx, [32, T], F32, "yev") for _ in range(2)]
    it = 0
    for k in range(8):
        ub = Ub[k % 2]
        fw.dma(fw.SP, out=Y.t[:], in_=sc["ZS5"].t[32 * k:32 * k + 32, :], W=[Y.b])
        fw.op(fw.ACT, lambda: nc.scalar.copy(out=ub.t[:], in_=Y.t[:]), R=[Y.b], W=[ub.b])
        fw.op(fw.ACT, lambda: nc.scalar.mul(out=Y.t[:], in_=Y.t[:], mul=dsk.t[:, k:k + 1]), R=[Y.b, dsk.b], W=[Y.b])
        for dr in range(2):
            cl = dr * 8 + k
            c1 = slice(cl, cl + 1)
            ep, em = Ep[(k * 2 + dr) % 2], Em[(k * 2 + dr) % 2]
            btl, ctl = BT_[(k * 2 + dr) % 2], CT_[(k * 2 + dr) % 2]
            fw.op(V, lambda: nc.vector.memset(ep.t[:, 0, 0:1], 1.0), W=[ep.b])
            fw.op(V, lambda: nc.vector.memset(ep.t[:, 1, 0:1], 0.0), W=[ep.b])
            for j in range(9):
                n_ = 1 << j
                p_r, p_i = pw.t[:, j, 0, c1], pw.t[:, j, 1, c1]
                ts(mg.t[:, 0:n_], ep.t[:, 1, 0:n_], p_i, ALU.mult, [ep.b, pw.b], [mg.b])
                stt(ep.t[:, 0, n_:2 * n_], ep.t[:, 0, 0:n_], p_r, mg.t[:, 0:n_], ALU.mult, ALU.subtract, [ep.b, pw.b, mg.b], [ep.b])
                ts(mg.t[:, 0:n_], ep.t[:, 1, 0:n_], p_r, ALU.mult, [ep.b, pw.b], [mg.b])
                stt(ep.t[:, 1, n_:2 * n_], ep.t[:, 0, 0:n_], p_i, mg.t[:, 0:n_], ALU.mult, ALU.add, [ep.b, pw.b, mg.b], [ep.b])
            fw.op(fw.ACT, lambda: nc.scalar.activation(out=mg.t[:], in_=tau.t[:], func=AF.Exp, scale=pr.t[:, N2, c1]), R=[tau.b, PB, mg.b], W=[mg.b])
            tt(em.t[:, 0, :], ep.t[:, 0, :], mg.t[:], ALU.mult, [ep.b, mg.b], [em.b])
            stt(em.t[:, 1, :], ep.t[:, 1, :], -1.0, mg.t[:], ALU.mult, ALU.mult, [ep.b, mg.b], [em.b])
            f_r, f_i = pr.t[:, FR, c1], pr.t[:, FI, c1]
            b_r, b_i = bsrc.t[:, cl, 0, :], bsrc.t[:, cl, 1, :]
            ts(bb.t[:, 0, :], b_i, f_i, ALU.mult, [bsrc.b, PB], [bb.b])
            stt(bb.t[:, 0, :], b_r, f_r, bb.t[:, 0, :], ALU.mult, ALU.subtract, [bsrc.b, PB, bb.b], [bb.b])
            ts(bb.t[:, 1, :], b_r, f_i, ALU.mult, [bsrc.b, PB], [bb.b])
            stt(bb.t[:, 1, :], b_i, f_r, bb.t[:, 1, :], ALU.mult, ALU.add, [bsrc.b, PB, bb.b], [bb.b])
            fw.op(V, lambda: nc.vector.memset(bpad.t[:], 0.0), W=[bpad.b])
            fw.op(V, lambda: nc.vector.memset(cpad.t[:], 0.0), W=[cpad.b])
            for hf in range(2):
                ps_ = slice(hf * 64, hf * 64 + 64)
                cs_ = slice(hf * 16, hf * 16 + 16)
                fw.op(V, lambda: nc.vector.tensor_copy(out=bpad.t[ps_, :, cs_], in_=bb.t[ps_, :, :]), R=[bb.b], W=[bpad.b])
                fw.op(V, lambda: nc.vector.tensor_copy(out=cpad.t[ps_, 0, cs_], in_=csrc.t[ps_, cl, 0, :]), R=[csrc.b], W=[cpad.b])
                ts(cpad.t[ps_, 1, cs_], csrc.t[ps_, cl, 1, :], -1.0, ALU.mult, [csrc.b], [cpad.b])
            fw.op(V, lambda: nc.vector.tensor_copy(out=ctl.t[:], in_=cpad.t[:]), R=[cpad.b], W=[ctl.b])
            psb = cx.ps[7] if shared else cx.ps[6]
            for ri in range(2):
                fw.op(fw.PE, lambda: nc.tensor.transpose(out=psb.t[0:32, ri * 128:(ri + 1) * 128], in_=bpad.t[:, ri, :], identity=cx.ident_f.t[:]),
                      R=[bpad.b, cx.ident_f.b], W=[psb.b], inc=(ri == 1))
            fw.op(V, lambda: nc.vector.tensor_copy(out=btl.t[:].rearrange("p a b -> p (a b)"), in_=psb.t[0:32, 0:256]), R=[psb.b], W=[btl.b])
            fw.op(V, lambda: nc.vector.memset(ax.t[:], 0.0), W=[ax.b])
            at_r, at_i = pw.t[:, 9, 0, c1], pw.t[:, 9, 1, c1]
            rv = (lambda a_: a_) if dr == 0 else (lambda a_: a_[:, ::-1])
            emr, emi = rv(em.t[:, 0, :]), rv(em.t[:, 1, :])
            epr, epi = rv(ep.t[:, 0, :]), rv(ep.t[:, 1, :])
            tiles = {}

            def tsl_of(ci):
                c = ci if dr == 0 else NCH - 1 - ci
                return slice(c * T, (c + 1) * T)

            def pe_in(ci):
                nonlocal it
                tsl = tsl_of(ci)
                if shared:
                    bur, bui = cx.ps[3 + (it % 2) * 2], cx.ps[4 + (it % 2) * 2]
                    yps = cx.ps[7]
                else:
                    bur, bui = cx.ps[(it % 2) * 2], cx.ps[(it % 2) * 2 + 1]
                    yps = cx.ps[4 + it % 2]
                tiles[ci] = (bur, bui, yps, Pt[it % 2], St[it % 2], tm[it % 2], xb[it % 2], yev[it % 2])
                it += 1
                fw.op(fw.PE, lambda: nc.tensor.matmul(bur.t[:, :], lhsT=btl.t[:, 0, :], rhs=ub.t[:, tsl], start=True, stop=True), R=[btl.b, ub.b], W=[bur.b])
                fw.op(fw.PE, lambda: nc.tensor.matmul(bui.t[:, :], lhsT=btl.t[:, 1, :], rhs=ub.t[:, tsl], start=True, stop=True), R=[btl.b, ub.b], W=[bui.b])

            def dve(ci):
                bur, bui, yps, P_, S_, t_, x_, ye = tiles[ci]
                tt(t_.t[:, 0, :], bui.t[:, :], emi, ALU.mult, [bui.b, em.b], [t_.b])
                tt(t_.t[:, 1, :], bur.t[:, :], emi, ALU.mult, [bur.b, em.b], [t_.b])
                tt(P_.t[:, 0, :], bur.t[:, :], emr, ALU.mult, [bur.b, em.b], [P_.b])
                tt(P_.t[:, 1, :], bui.t[:, :], emr, ALU.mult, [bui.b, em.b], [P_.b])
                tt(P_.t[:, 0, :], P_.t[:, 0, :], t_.t[:, 0, :], ALU.subtract, [P_.b, t_.b], [P_.b])
                tt(P_.t[:, 1, :], P_.t[:, 1, :], t_.t[:, 1, :], ALU.add, [P_.b, t_.b], [P_.b])
                for ri in range(2):
                    fw.op(V, lambda: nc.vector.tensor_tensor_scan(out=rv(S_.t[:, ri, :]), data0=ones.t[:, :], data1=rv(P_.t[:, ri, :]), initial=ax.t[:, ri:ri + 1],
                                                                  op0=ALU.mult, op1=ALU.add), R=[ones.b, P_.b, ax.b], W=[S_.b])
                e_ = T - 1 if dr == 0 else 0
                sre, sie = S_.t[:, 0, e_:e_ + 1], S_.t[:, 1, e_:e_ + 1]
                ts(ax.t[:, 2:3], sie, at_i, ALU.mult, [S_.b, pw.b], [ax.b])
                ts(ax.t[:, 3:4], sie, at_r, ALU.mult, [S_.b, pw.b], [ax.b])
                stt(ax.t[:, 0:1], sre, at_r, ax.t[:, 2:3], ALU.mult, ALU.subtract, [S_.b, pw.b, ax.b], [ax.b])
                stt(ax.t[:, 1:2], sre, at_i, ax.t[:, 3:4], ALU.mult, ALU.add, [S_.b, pw.b, ax.b], [ax.b])
                tt(t_.t[:, 0, :], S_.t[:, 1, :], epi, ALU.mult, [S_.b, ep.b], [t_.b])
                tt(t_.t[:, 1, :], S_.t[:, 0, :], epi, ALU.mult, [S_.b, ep.b], [t_.b])
                tt(P_.t[:, 0, :], S_.t[:, 0, :], epr, ALU.mult, [S_.b, ep.b], [P_.b])
                tt(P_.t[:, 1, :], S_.t[:, 1, :], epr, ALU.mult, [S_.b, ep.b], [P_.b])
                tt(x_.t[:, 0, :], P_.t[:, 0, :], t_.t[:, 0, :], ALU.subtract, [P_.b, t_.b], [x_.b])
                tt(x_.t[:, 1, :], P_.t[:, 1, :], t_.t[:, 1, :], ALU.add, [P_.b, t_.b], [x_.b])

            def pe_out(ci):
                bur, bui, yps, P_, S_, t_, x_, ye = tiles.pop(ci)
                tsl = tsl_of(ci)
                fw.op(fw.PE, lambda: nc.tensor.matmul(yps.t[0:32, :], lhsT=ctl.t[:, 0, :], rhs=x_.t[:, 0, :], start=True, stop=False), R=[ctl.b, x_.b], W=[yps.b], inc=False)
                fw.op(fw.PE, lambda: nc.tensor.matmul(yps.t[0:32, :], lhsT=ctl.t[:, 1, :], rhs=x_.t[:, 1, :], start=False, stop=True), R=[ctl.b, x_.b], W=[yps.b])
                fw.op(fw.ACT, lambda: nc.scalar.copy(out=ye.t[:], in_=yps.t[0:32, :]), R=[yps.b], W=[ye.b])
                fw.op(fw.POOL, lambda: nc.gpsimd.tensor_tensor(out=Y.t[:, tsl], in0=Y.t[:, tsl], in1=ye.t[:], op=ALU.add), R=[Y.b, ye.b], W=[Y.b])

            pe_in(0)
            for ci in range(NCH):
                if ci + 1 < NCH:
                    pe_in(ci + 1)
                if ci >= 1:
                    pe_out(ci - 1)
                yield
                dve(ci)
            pe_out(NCH - 1)
        fw.op(fw.ACT, lambda: nc.scalar.activation(out=Y.t[:], in_=Y.t[:], func=AF.Gelu), R=[Y.b], W=[Y.b])
        fw.dma(fw.GD, out=sc["GS5"].t[32 * k:32 * k + 32, :], in_=Y.t[:], R=[Y.b], W=sc["GS5"].cb)


def s5_glu(cx, W, li):
    fw, nc = cx.fw, cx.nc
    sc = cx.sc
    V = fw.DVE
    new_phase(cx)
    cx.wstg = [SB(cx, [128, 1024], F32, "wstg") for _ in range(2)]
    wgl = SB(cx, [128, 2, 256], BF16, "wgl")
    load_w_bf16(cx, wgl, W["s5_w_glu"], 2, 256)
    bgl = SB(cx, [128, 2], F32, "bgl")
    fw.dma(fw.SP, out=bgl.t[:], in_=W["s5_b_glu"], W=[bgl.b])
    gts = [SB(cx, [128, 2, CH], F32, "gt") for _ in range(2)]
    gbs = [SB(cx, [128, 2, CH], BF16, "gb") for _ in range(2)]
    sgs = [SB(cx, [128, CH], F32, "sg") for _ in range(2)]
    kq = 0
    for c in range(NCH):
        tsl = slice(c * CH, (c + 1) * CH)
        g, gbf = gts[c % 2], gbs[c % 2]
        fw.dma(fw.SP, out=g.t[:], in_=sc["GS5"].t[:, tsl].rearrange("(j p) t -> p j t", p=128), R=[sc["GS5"].cb[c]], W=[g.b])
        fw.op(fw.ACT, lambda: nc.scalar.copy(out=gbf.t[:], in_=g.t[:]), R=[g.b], W=[gbf.b])
        for m in range(2):
            ps = cx.ps[2 + kq % 6]
            kq += 1
            for j in range(2):
                fw.op(fw.PE, lambda: nc.tensor.matmul(ps.t[:, :], lhsT=wgl.t[:, j, m * 128:(m + 1) * 128], rhs=gbf.t[:, j, :], start=(j == 0), stop=(j == 1)),
                      R=[wgl.b, gbf.b], W=[ps.b], inc=(j == 1))
            sg = sgs[m]
            fw.op(fw.ACT, lambda: nc.scalar.activation(out=sg.t[:], in_=ps.t[:], func=AF.Sigmoid, bias=bgl.t[:, m:m + 1], scale=1.0), R=[ps.b, bgl.b], W=[sg.b])
            fw.op(V, lambda: nc.vector.tensor_tensor(out=g.t[:, m, :], in0=g.t[:, m, :], in1=sg.t[:], op=ALU.mult), R=[g.b, sg.b], W=[g.b])
        fw.dma(fw.GD, out=sc["YS5"].t[:, tsl].rearrange("(j p) t -> p j t", p=128), in_=g.t[:], R=[g.b], W=[sc["YS5"].cb[c]])


NK1 = 65
HC = 32


def fft_fwd_group(cx, hy, lhs_of, K, on_bank):
    fw, nc = cx.fw, cx.nc
    Yb = hy["Ybuf"]
    for c in range(HC):
        ps = cx.ps[hy["pi"] % 4]
        hy["pi"] += 1
        fw.op(fw.PE, lambda: nc.tensor.matmul(ps.t[:, 0:4 * NK1], lhsT=lhs_of(c), rhs=hy["F4"].t[0:K, :], start=True, stop=True), R=hy["lhsR"] + [hy["F4"].b], W=[ps.b])
        src = ps.t[:, 0:4 * NK1].rearrange("p (a k) -> p k a", a=4)
        if c % 2 == 0:
            fw.op(fw.ACT, lambda: nc.scalar.copy(out=Yb.t[:, :, :, c], in_=src), R=[ps.b], W=[Yb.b])
        else:
            fw.op(fw.DVE, lambda: nc.vector.tensor_copy(out=Yb.t[:, :, :, c], in_=src), R=[ps.b], W=[Yb.b])
    G = hy["G"]
    nb = (NK1 + 7) // 8
    for b in range(nb):
        k1s = list(range(b * 8, min(b * 8 + 8, NK1)))
        ps = cx.ps[4 + hy["pj"] % 4]
        hy["pj"] += 1
        for i, k1 in enumerate(k1s):
            o = ps.t[:, i * 2 * HC:(i + 1) * 2 * HC]
            fw.op(fw.PE, lambda: nc.tensor.matmul(o, lhsT=G.t[:, k1, 0, :], rhs=Yb.t[:, k1, 0:2, :].rearrange("p a c -> p (a c)"), start=True, stop=False),
                  R=[G.b, Yb.b], W=[ps.b], inc=False)
            fw.op(fw.PE, lambda: nc.tensor.matmul(o, lhsT=G.t[:, k1, 1, :], rhs=Yb.t[:, k1, 2:4, :].rearrange("p a c -> p (a c)"), start=False, stop=True),
                  R=[G.b, Yb.b], W=[ps.b], inc=(i == len(k1s) - 1))
        on_bank(b, k1s, ps)


def phase_hy_filter(cx, W, li):
    fw, nc = cx.fw, cx.nc
    new_phase(cx)
    sc = cx.sc
    V = fw.DVE
    hy = {"pi": 0, "pj": 0}
    hy["G"] = SB(cx, [128, NK1, 2, 128], BF16, "G")
    for i in range(5):
        fw.dma(fw.SP, out=hy["G"].t[:, i * 13:(i + 1) * 13], in_=cx.C["fftG"][:, i * 13:(i + 1) * 13], W=[hy["G"].b])
    hy["F4"] = SB(cx, [128, 4 * NK1], BF16, "F4")
    fw.dma(fw.SP, out=hy["F4"].t[:], in_=cx.C["fftF4"], W=[hy["F4"].b])
    hy["Ybuf"] = SB(cx, [128, NK1, 4, HC], BF16, "Ybuf")
    w1 = SB(cx, [33, 64], F32, "w1")
    w2 = SB(cx, [64, 64], F32, "w2")
    b12 = SB(cx, [64, 2], F32, "b12")
    fw.dma(fw.SP, out=w1.t[:], in_=W["hy_f_w1"], W=[w1.b])
    fw.dma(fw.SP, out=w2.t[:], in_=W["hy_f_w2"], W=[w2.b])
    fw.dma(fw.SP, out=b12.t[:], in_=W["hy_f_b12"], W=[b12.b])
    w3s = SB(cx, [64, 2, 512], F32, "w3s")
    nr = SB(cx, [128, 512], F32, "nr")
    for half in range(2):
        for o in range(2):
            c0 = o * 512 + half * 256
            fw.dma(fw.SP, out=w3s.t[:, half, o * 256:(o + 1) * 256], in_=W["hy_f_w3"][:, c0:c0 + 256], W=[w3s.b])
            fw.dma(fw.SP, out=nr.t[half * 64:(half + 1) * 64, o * 256:(o + 1) * 256], in_=W["hy_log_decay"][c0:c0 + 256].partition_broadcast(64), W=[nr.b])
    fw.op(fw.ACT, lambda: nc.scalar.activation(out=nr.t[:], in_=nr.t[:], func=AF.Exp), R=[nr.b], W=[nr.b])
    fw.op(V, lambda: nc.vector.tensor_scalar(out=nr.t[:], in0=nr.t[:], scalar1=-1.0, scalar2=None, op0=ALU.mult), R=[nr.b], W=[nr.b])
    tpos = SB(cx, [128, 128], F32, "tpos")
    fw.dma(fw.SP, out=tpos.t[:], in_=cx.C["tpos"], W=[tpos.b])
    h2T = SB(cx, [64, 2 * L], F32, "h2T")
    fts = [SB(cx, [33, CH], F32, "ft")] * 2
    xf = [SB(cx, [64, CH], F32, "xf") for _ in range(2)]
    xi = [SB(cx, [64, CH], mybir.dt.int32, "xi")] * 2
    h1 = [SB(cx, [64, CH], F32, "h1") for _ in range(2)]
    twopi = 2 * math.pi

    def sin_from(ps_ap, psb, bias_ap, out_ap, outb, i):
        x_, xi_ = xf[i % 2], xi[i % 2]
        fw.op(V, lambda: nc.vector.tensor_scalar(out=x_.t[:], in0=ps_ap, scalar1=bias_ap, scalar2=None, op0=ALU.add), R=[psb, b12.b], W=[x_.b])
        fw.op(V, lambda: nc.vector.tensor_scalar(out=xi_.t[:], in0=x_.t[:], scalar1=1.0 / twopi, scalar2=None, op0=ALU.mult), R=[x_.b], W=[xi_.b])
        fw.op(V, lambda: nc.vector.tensor_copy(out=out_ap, in_=xi_.t[:]), R=[xi_.b], W=[outb])
        fw.op(V, lambda: nc.vector.scalar_tensor_tensor(out=x_.t[:], in0=out_ap, scalar=-twopi, in1=x_.t[:], op0=ALU.mult, op1=ALU.add), R=[outb, x_.b], W=[x_.b])
        fw.op(fw.ACT, lambda: nc.scalar.activation(out=out_ap, in_=x_.t[:], func=AF.Sin), R=[x_.b], W=[outb])
    for c in range(32):
        ft = fts[c % 2]
        fw.dma(fw.SP, out=ft.t[:], in_=cx.C["featsT"][:, c * CH:(c + 1) * CH], W=[ft.b])
        ps = cx.ps[c % 2]
        fw.op(fw.PE, lambda: nc.tensor.matmul(ps.t[0:64, :], lhsT=w1.t[:], rhs=ft.t[:], start=True, stop=True), R=[w1.b, ft.b], W=[ps.b])
        sin_from(ps.t[0:64, :], ps.b, b12.t[:, 0:1], h1[c % 2].t[:], h1[c % 2].b, c)
        ps2 = cx.ps[2 + c % 2]
        fw.op(fw.PE, lambda: nc.tensor.matmul(ps2.t[0:64, :], lhsT=w2.t[:], rhs=h1[c % 2].t[:], start=True, stop=True), R=[w2.b, h1[c % 2].b], W=[ps2.b])
        sin_from(ps2.t[0:64, :], ps2.b, b12.t[:, 1:2], h2T.t[:, c * CH:(c + 1) * CH], h2T.b, c)
    HB = SB(cx, [128, 128, 128], BF16, "HB")
    acc4 = SB(cx, [128, 512], F32, "acc4")
    dec = [SB(cx, [128, 512], F32, "dec") for _ in range(2)]
    hh = [SB(cx, [128, 512], F32, "hh") for _ in range(2)]
    ab = [SB(cx, [128, 512], F32, "ab") for _ in range(1)] * 2
    rnt = SB(cx, [128, 512], F32, "rnt")
    hst = [SB(cx, [128, NK1, 2, HC], BF16, "hst") for _ in range(1)]
    hy["lhsR"] = [HB.b]
    for q in range(4):
        qs = slice(q * 128, (q + 1) * 128)
        fw.op(V, lambda: nc.vector.memset(acc4.t[:], 0.0), W=[acc4.b])
        for bt in range(32):
            ps = cx.ps[bt % 2]
            d_, h_, a_ = dec[bt % 2], hh[bt % 2], ab[bt % 2]
            for i in range(4):
                n2 = bt * 4 + i
                fw.op(fw.PE, lambda: nc.tensor.matmul(ps.t[0:64, i * 128:(i + 1) * 128], lhsT=h2T.t[:, n2:L:128], rhs=w3s.t[:, 0, qs], start=True, stop=True),
                      R=[h2T.b, w3s.b], W=[ps.b], inc=False)
                fw.op(fw.PE, lambda: nc.tensor.matmul(ps.t[64:128, i * 128:(i + 1) * 128], lhsT=h2T.t[:, L + n2:2 * L:128], rhs=w3s.t[:, 1, qs], start=True, stop=True),
                      R=[h2T.b, w3s.b], W=[ps.b], inc=(i == 3))
                fw.op(fw.ACT, lambda: nc.scalar.activation(out=d_.t[:, i * 128:(i + 1) * 128], in_=nr.t[:, qs], func=AF.Exp, scale=tpos.t[:, n2:n2 + 1]),
                      R=[nr.b, tpos.b], W=[d_.b])
            fw.op(V, lambda: nc.vector.tensor_tensor(out=h_.t[:], in0=ps.t[:], in1=d_.t[:], op=ALU.mult), R=[ps.b, d_.b], W=[h_.b])
            if bt == 0:
                fw.op(V, lambda: nc.vector.memset(h_.t[64:65, 0:128], 0.0), W=[h_.b])
            fw.op(fw.ACT, lambda: nc.scalar.activation(out=a_.t[:], in_=h_.t[:], func=AF.Abs), R=[h_.b], W=[a_.b])
            fw.op(fw.POOL, lambda: nc.gpsimd.tensor_tensor(out=acc4.t[:], in0=acc4.t[:], in1=a_.t[:], op=ALU.add), R=[acc4.b, a_.b], W=[acc4.b])
            fw.op(V, lambda: nc.vector.tensor_copy(out=HB.t[:, :, bt * 4:(bt + 1) * 4].rearrange("p c n -> p n c"), in_=h_.t[:].rearrange("p (n c) -> p n c", c=128)),
                  R=[h_.b], W=[HB.b])
        ps = cx.ps[2]
        for i in range(4):
            fw.op(fw.PE, lambda: nc.tensor.matmul(ps.t[:, 0:128], lhsT=cx.ones_f.t[:], rhs=acc4.t[:, i * 128:(i + 1) * 128], start=(i == 0), stop=(i == 3)),
                  R=[cx.ones_f.b, acc4.b], W=[ps.b], inc=(i == 3))
        fw.op(V, lambda: nc.vector.reciprocal(out=rnt.t[:, qs], in_=ps.t[:, 0:128]), R=[ps.b], W=[rnt.b])
        for g in range(4):
            gi = q * 4 + g
            hs = hst[0]

            def on_bank(b, k1s, ps, hs=hs):
                src = ps.t[:, 0:len(k1s) * 2 * HC]
                dst = hs.t[:, k1s[0]:k1s[-1] + 1, :, :].rearrange("p k a c -> p (k a c)")
                if b % 2 == 0:
                    fw.op(fw.ACT, lambda: nc.scalar.copy(out=dst, in_=src), R=[ps.b], W=[hs.b])
                else:
                    fw.op(V, lambda: nc.vector.tensor_copy(out=dst, in_=src), R=[ps.b], W=[hs.b])
            fft_fwd_group(cx, hy, lambda c, g=g: HB.t[:, g * HC + c, :], 128, on_bank)
            fw.dma(fw.GD, out=sc["HSPEC"].t[gi], in_=hs.t[:].rearrange("p k a c -> p (k a c)"), R=[hs.b], W=[sc["HSPEC"].cb[gi]])
    fw.dma(fw.GD, out=sc["RN"].t, in_=rnt.t[:], R=[rnt.b], W=sc["RN"].cb)


def phase_hy(cx, W, li):
    fw, nc = cx.fw, cx.nc
    sc = cx.sc
    V = fw.DVE
    new_phase(cx)
    cw = SB(cx, [128, 6, 4], F32, "cw")
    fw.dma(fw.SP, out=cw.t[:], in_=W["hy_conv"], W=[cw.b])
    zt = [SB(cx, [128, L], F32, "zt") for _ in range(2)]
    zo = [SB(cx, [128, L], F32, "zo") for _ in range(2)]
    for j in range(6):
        z, o = zt[j % 2], zo[j % 2]
        fw.dma(fw.SP, out=z.t[:], in_=sc["ZHY"].t[j * 128:(j + 1) * 128, :], W=[z.b])
        fw.op(V, lambda: nc.vector.tensor_scalar(out=o.t[:], in0=z.t[:], scalar1=cw.t[:, j, 1:2], scalar2=cw.t[:, j, 3:4], op0=ALU.mult, op1=ALU.add),
              R=[z.b, cw.b], W=[o.b])
        fw.op(V, lambda: nc.vector.scalar_tensor_tensor(out=o.t[:, 1:L], in0=z.t[:, 0:L - 1], scalar=cw.t[:, j, 0:1], in1=o.t[:, 1:L], op0=ALU.mult, op1=ALU.add),
              R=[z.b, cw.b, o.b], W=[o.b])
        fw.op(V, lambda: nc.vector.scalar_tensor_tensor(out=o.t[:, 0:L - 1], in0=z.t[:, 1:L], scalar=cw.t[:, j, 2:3], in1=o.t[:, 0:L - 1], op0=ALU.mult, op1=ALU.add),
              R=[z.b, cw.b, o.b], W=[o.b])
        fw.dma(fw.GD, out=sc["ZC"].t[j * 128:(j + 1) * 128, :], in_=o.t[:], R=[o.b], W=sc["ZC"].cb)
    if getattr(cx, "hy_sc_only", False):
        return
    new_phase(cx)
    hy = {"pi": 0, "pj": 0}
    hy["G"] = SB(cx, [128, NK1, 2, 128], BF16, "G")
    for i in range(5):
        fw.dma(fw.SP, out=hy["G"].t[:, i * 13:(i + 1) * 13], in_=cx.C["fftG"][:, i * 13:(i + 1) * 13], W=[hy["G"].b])
    hy["F4"] = SB(cx, [128, 4 * NK1], BF16, "F4")
    fw.dma(fw.SP, out=hy["F4"].t[:], in_=cx.C["fftF4"], W=[hy["F4"].b])
    Q = SB(cx, [NK1, 128, 2, 64], BF16, "Q")
    for i in range(4):
        fw.dma(fw.SP, out=Q.t[:, i * 32:(i + 1) * 32], in_=cx.C["fftQ"][:, i * 32:(i + 1) * 32], W=[Q.b])
    FI = SB(cx, [128, 2, 256], BF16, "FI")
    fw.dma(fw.SP, out=FI.t[:], in_=cx.C["fftFI"], W=[FI.b])
    hy["Ybuf"] = SB(cx, [128, NK1, 4, HC], BF16, "Ybuf")
    Yb = hy["Ybuf"]
    Tb_ap = Yb.t[0:NK1].rearrange("p k a c -> p (k a c)")[:, 0:2 * 128 * HC].rearrange("p (a n c) -> p a n c", a=2, c=HC)
    Hg = [SB(cx, [128, NK1, 2, HC], BF16, "Hg") for _ in range(2)]
    Zb = SB(cx, [128, HC, 2, NK1], BF16, "Zb")
    Vf = SB(cx, [64, HC, 128], F32, "Vf")
    X1 = SB(cx, [64, HC, 128], F32, "X1")
    G1 = SB(cx, [64, HC, 128], F32, "G1")
    Ub = SB(cx, [64, HC, 128], BF16, "Ub")
    U2 = SB(cx, [64, HC, 128], BF16, "U2")
    RN = SB(cx, [64, 512], F32, "RN")
    fw.dma(fw.SP, out=RN.t[:], in_=sc["RN"].t[0:64, :], R=sc["RN"].cb, W=[RN.b])
    SK = SB(cx, [64, 512], F32, "SK")
    fw.dma(fw.SP, out=SK.t[:], in_=W["hy_skip"].rearrange("a b -> (a b)").partition_broadcast(64), W=[SK.b])
    tA = [SB(cx, [128, 8 * HC], F32, "tA") for _ in range(2)]
    tB = [SB(cx, [128, 8 * HC], F32, "tB") for _ in range(2)]
    e1 = [SB(cx, [64, 16 * HC], F32, "e1") for _ in range(2)]
    e2 = [SB(cx, [64, 16 * HC], F32, "e2") for _ in range(2)]
    zc = sc["ZC"].t

    def ld(dst, row0):
        src = zc[row0:row0 + HC, :].rearrange("c (a b) -> a c b", b=128)
        for i in range(2):
            fw.dma(fw.SP, out=dst.t[:, i * 16:(i + 1) * 16, :], in_=src[:, i * 16:(i + 1) * 16, :], R=sc["ZC"].cb, W=[dst.b])

    def conv(g, o, ub, v_f, gate, out_f32, out_bf):
        hg = Hg[o]
        fw.dma(fw.SP, out=hg.t[:].rearrange("p k a c -> p (k a c)"), in_=sc["HSPEC"].t[o * 8 + g], R=[sc["HSPEC"].cb[o * 8 + g]], W=[hg.b])
        hy["lhsR"] = [ub.b]

        def on_bank(b, k1s, ps):
            nk = len(k1s)
            ks = slice(k1s[0], k1s[-1] + 1)
            pv = ps.t[:, 0:nk * 2 * HC].rearrange("p (k a c) -> p k a c", a=2, c=HC)
            xr, xi_ = pv[:, :, 0, :], pv[:, :, 1, :]
            hr, hi = hg.t[:, ks, 0, :], hg.t[:, ks, 1, :]
            a_, b_ = tA[b % 2], tB[b % 2]
            av = a_.t[:, 0:nk * HC].rearrange("p (k c) -> p k c", c=HC)
            bv = b_.t[:, 0:nk * HC].rearrange("p (k c) -> p k c", c=HC)
            zr = Zb.t[:, :, 0, ks].rearrange("p c k -> p k c")
            zi = Zb.t[:, :, 1, ks].rearrange("p c k -> p k c")
            fw.op(V, lambda: nc.vector.tensor_tensor(out=av, in0=xr, in1=hr, op=ALU.mult), R=[ps.b, hg.b], W=[a_.b])
            fw.op(V, lambda: nc.vector.tensor_tensor(out=bv, in0=xi_, in1=hi, op=ALU.mult), R=[ps.b, hg.b], W=[b_.b])
            fw.op(fw.POOL, lambda: nc.gpsimd.tensor_tensor(out=zr, in0=av, in1=bv, op=ALU.subtract), R=[a_.b, b_.b], W=[Zb.b])
            a2, b2 = tA[(b + 1) % 2], tB[(b + 1) % 2]
            av2 = a2.t[:, 0:nk * HC].rearrange("p (k c) -> p k c", c=HC)
            bv2 = b2.t[:, 0:nk * HC].rearrange("p (k c) -> p k c", c=HC)
            fw.op(V, lambda: nc.vector.tensor_tensor(out=av2, in0=xr, in1=hi, op=ALU.mult), R=[ps.b, hg.b], W=[a2.b])
            fw.op(V, lambda: nc.vector.tensor_tensor(out=bv2, in0=xi_, in1=hr, op=ALU.mult), R=[ps.b, hg.b], W=[b2.b])
            fw.op(fw.POOL, lambda: nc.gpsimd.tensor_tensor(out=zi, in0=av2, in1=bv2, op=ALU.add), R=[a2.b, b2.b], W=[Zb.b])
        if cx.hy_stop <= 1:
            return
        fft_fwd_group(cx, hy, lambda c: ub.t[:, c, :], 64, on_bank)
        if cx.hy_stop <= 2:
            return
        for c in range(HC):
            ps = cx.ps[hy["pi"] % 4]
            hy["pi"] += 1
            fw.op(fw.PE, lambda: nc.tensor.matmul(ps.t[0:NK1, 0:256], lhsT=Zb.t[:, c, 0, :], rhs=FI.t[:, 0, :], start=True, stop=False), R=[Zb.b, FI.b], W=[ps.b], inc=False)
            fw.op(fw.PE, lambda: nc.tensor.matmul(ps.t[0:NK1, 0:256], lhsT=Zb.t[:, c, 1, :], rhs=FI.t[:, 1, :], start=False, stop=True), R=[Zb.b, FI.b], W=[ps.b])
            src = ps.t[0:NK1, 0:256].rearrange("p (a n) -> p a n", a=2)
            if c % 2 == 0:
                fw.op(fw.ACT, lambda: nc.scalar.copy(out=Tb_ap[:, :, :, c], in_=src), R=[ps.b], W=[Yb.b])
            else:
                fw.op(V, lambda: nc.vector.tensor_copy(out=Tb_ap[:, :, :, c], in_=src), R=[ps.b], W=[Yb.b])
        if cx.hy_stop <= 3:
            return
        cs = slice(o * 256 + g * HC, o * 256 + (g + 1) * HC)
        for nb in range(8):
            ps = cx.ps[4 + hy["pj"] % 4]
            hy["pj"] += 1
            for j in range(16):
                n2 = nb * 16 + j
                oo = ps.t[0:64, j * HC:(j + 1) * HC]
                fw.op(fw.PE, lambda: nc.tensor.matmul(oo, lhsT=Q.t[:, n2, 0, :], rhs=Tb_ap[:, 0, n2, :], start=True, stop=False), R=[Q.b, Yb.b], W=[ps.b], inc=False)
                fw.op(fw.PE, lambda: nc.tensor.matmul(oo, lhsT=Q.t[:, n2, 1, :], rhs=Tb_ap[:, 1, n2, :], start=False, stop=True), R=[Q.b, Yb.b], W=[ps.b], inc=(j == 15))
            pv = ps.t[0:64, 0:16 * HC].rearrange("p (n c) -> p n c", c=HC)
            ns = slice(nb * 16, (nb + 1) * 16)
            a_, b_ = e1[nb % 2], e2[nb % 2]
            av = a_.t[:].rearrange("p (n c) -> p n c", c=HC)
            bv = b_.t[:].rearrange("p (n c) -> p n c", c=HC)
            fw.op(V, lambda: nc.vector.tensor_tensor(out=av, in0=pv, in1=RN.t[:, cs].unsqueeze(1).broadcast_to([64, 16, HC]), op=ALU.mult), R=[ps.b, RN.b], W=[a_.b])
            fw.op(fw.POOL, lambda: nc.gpsimd.tensor_tensor(out=bv, in0=v_f.t[:, :, ns].rearrange("p c n -> p n c"), in1=SK.t[:, cs].unsqueeze(1).broadcast_to([64, 16, HC]), op=ALU.mult),
                  R=[v_f.b, SK.b], W=[b_.b])
            fw.op(V, lambda: nc.vector.tensor_tensor(out=av, in0=av, in1=bv, op=ALU.add), R=[a_.b, b_.b], W=[a_.b])
            fw.op(V, lambda: nc.vector.tensor_tensor(out=out_f32.t[:, :, ns].rearrange("p c n -> p n c"), in0=av, in1=gate.t[:, :, ns].rearrange("p c n -> p n c"), op=ALU.mult),
                  R=[a_.b, gate.b], W=[out_f32.b])
            if out_bf is not None:
                fw.op(fw.ACT, lambda: nc.scalar.copy(out=out_bf.t[:, :, ns], in_=out_f32.t[:, :, ns]), R=[out_f32.b], W=[out_bf.b])

    for g_ in range(cx.hy_lim):
        g = cx.hy_g0 if cx.hy_same else g_
        ld(Vf, g * HC)
        ld(X1, 256 + g * HC)
        fw.op(fw.ACT, lambda: nc.scalar.copy(out=Ub.t[:], in_=Vf.t[:]), R=[Vf.b], W=[Ub.b])
        conv(g, 0, Ub, Vf, X1, G1, U2)
        ld(X1, 512 + g * HC)
        conv(g, 1, U2, G1, X1, Vf, None)
        dst = sc["YHY"].t[g * HC:(g + 1) * HC, :].rearrange("c (a b) -> a c b", b=128)
        for i in range(2):
            fw.dma(fw.GD, out=dst[:, i * 16:(i + 1) * 16, :], in_=Vf.t[:, i * 16:(i + 1) * 16, :], R=[Vf.b], W=sc["YHY"].cb)


def phase_f1(cx, W, li):
    fw, nc = cx.fw, cx.nc
    new_phase(cx)
    sc = cx.sc
    cx.wstg = [SB(cx, [128, 1024], F32, "wstg") for _ in range(2)]
    gb = SB(cx, [128, 8], F32, "gb")
    fw.dma(fw.SP, out=gb.t[:], in_=W["g_branch"], W=[gb.b])
    wout = SB(cx, [128, 8, D], BF16, "wout")
    load_w_bf16(cx, wout, W["w_out"], 8, D, gain=gb)
    ys = [SB(cx, [128, 8, CH], F32, "ys") for _ in range(2)]
    hts = [SB(cx, [128, 8, CH], F32, "ht") for _ in range(2)]
    mixT = SB(cx, [128, 8, CH], BF16, "mixT")
    k = 0
    srcs = ["YHY", "YS5", "YDA", "YMLA"]
    for c in range(NCH):
        tsl = slice(c * CH, (c + 1) * CH)
        y, ht = ys[c % 2], hts[c % 2]
        for bi, nm in enumerate(srcs):
            fw.dma(fw.SP, out=y.t[:, 2 * bi:2 * bi + 2, :], in_=sc[nm].t[:, tsl].rearrange("(j p) t -> p j t", p=128), R=[sc[nm].cb[c]], W=[y.b])
        fw.dma(fw.SP, out=ht.t[:], in_=cx.HT.t[:, tsl].rearrange("(j p) t -> p j t", p=128), R=[cx.HT.cb[c]], W=[ht.b])
        for bi in range(4):
            if bi == 2:
                fw.op(fw.ACT, lambda: nc.scalar.copy(out=mixT.t[:, 4:6, :], in_=y.t[:, 4:6, :]), R=[y.b], W=[mixT.b])
                continue
            r = rstd_chunk(cx, [(y.t[:, 2 * bi + jj, :], 128, [y.b]) for jj in range(2)], 256)
            fw.op(fw.DVE, lambda: nc.vector.tensor_tensor(out=mixT.t[:, 2 * bi:2 * bi + 2, :], in0=y.t[:, 2 * bi:2 * bi + 2, :],
                                                          in1=r.t[:].unsqueeze(1).broadcast_to([128, 2, CH]), op=ALU.mult), R=[y.b, r.b], W=[mixT.b])
        for m in range(8):
            ps = cx.ps[2 + (k % 6)]
            k += 1
            for j in range(8):
                fw.op(fw.PE, lambda: nc.tensor.matmul(ps.t[:, :], lhsT=wout.t[:, j, m * 128:(m + 1) * 128], rhs=mixT.t[:, j, :], start=(j == 0), stop=(j == 7)),
                      R=[wout.b, mixT.b], W=[ps.b], inc=(j == 7))
            fw.op(fw.DVE, lambda: nc.vector.tensor_tensor(out=ht.t[:, m, :], in0=ht.t[:, m, :], in1=ps.t[:, :], op=ALU.add), R=[ht.b, ps.b], W=[ht.b])
        fw.dma(fw.GD, out=cx.HT.t[:, tsl].rearrange("(j p) t -> p j t", p=128), in_=ht.t[:], R=[ht.b], W=[cx.HT.cb[c]])


def phase_f2(cx, W, li):
    fw, nc = cx.fw, cx.nc
    new_phase(cx)
    n = 256
    nchunks = L // n
    cx.wstg = [SB(cx, [128, 1024], F32, "wstg") for _ in range(2)]
    gf = SB(cx, [128, 8], F32, "gf")
    fw.dma(fw.SP, out=gf.t[:], in_=W["g_ffn"], W=[gf.b])
    wg = SB(cx, [128, 8, FFN], BF16, "wg")
    wu = SB(cx, [128, 8, FFN], BF16, "wu")
    wd = SB(cx, [128, 22, D], BF16, "wd")
    kq = 0
    for wt, src in [(wg, W["w_gate"]), (wu, W["w_up"])]:
        for j in range(8):
            for c0 in range(0, FFN, 1024):
                cn = min(1024, FFN - c0)
                stg = cx.wstg[kq % 2]
                fw.dma(fw.SP, out=stg.t[:, 0:cn], in_=src[j * 128:(j + 1) * 128, c0:c0 + cn], W=[stg.b])
                if kq % 2 == 0:
                    fw.op(fw.ACT, lambda: nc.scalar.mul(out=wt.t[:, j, c0:c0 + cn], in_=stg.t[:, 0:cn], mul=gf.t[:, j:j + 1]), R=[stg.b, gf.b], W=[wt.b])
                else:
                    fw.op(fw.DVE, lambda: nc.vector.tensor_scalar(out=wt.t[:, j, c0:c0 + cn], in0=stg.t[:, 0:cn], scalar1=gf.t[:, j:j + 1], scalar2=None, op0=ALU.mult),
                          R=[stg.b, gf.b], W=[wt.b])
                kq += 1
    for i in range(22):
        stg = cx.wstg[kq % 2]
        fw.dma(fw.SP, out=stg.t[:, :], in_=W["w_down"][i * 128:(i + 1) * 128, :], W=[stg.b])
        evac(cx, kq, wd.t[:, i, :], stg.t[:, :], R=[stg.b], W=[wd.b])
        kq += 1
    hts = [SB(cx, [128, 8, n], F32, "ht") for _ in range(2)]
    fT = SB(cx, [128, 8, n], BF16, "fT")
    act = SB(cx, [128, 22, n], BF16, "act")
    sg = [SB(cx, [128, n], F32, "sg") for _ in range(2)]
    k = 0
    for c in range(nchunks):
        tsl = slice(c * n, (c + 1) * n)
        cb = cx.HT.cb[c // 2]
        ht = hts[c % 2]
        fw.dma(fw.SP, out=ht.t[:], in_=cx.HT.t[:, tsl].rearrange("(j p) t -> p j t", p=128), R=[cb], W=[ht.b])
        r = rstd_chunk(cx, [(ht.t[:, j, :], 128, [ht.b]) for j in range(8)], D, n=n)
        fw.op(fw.DVE, lambda: nc.vector.tensor_tensor(out=fT.t[:], in0=ht.t[:], in1=r.t[:, 0:n].unsqueeze(1).broadcast_to([128, 8, n]), op=ALU.mult),
              R=[ht.b, r.b], W=[fT.b])
        for i in range(22):
            ps = cx.ps[2 + (k % 6)]
            k += 1
            for half, wt in enumerate([wg, wu]):
                for j in range(8):
                    fw.op(fw.PE, lambda: nc.tensor.matmul(ps.t[:, half * n:(half + 1) * n], lhsT=wt.t[:, j, i * 128:(i + 1) * 128], rhs=fT.t[:, j, :],
                                                          start=(j == 0), stop=(j == 7)), R=[wt.b, fT.b], W=[ps.b], inc=(j == 7))
            s_ = sg[i % 2]
            fw.op(fw.ACT, lambda: nc.scalar.activation(out=s_.t[:], in_=ps.t[:, 0:n], func=AF.Silu), R=[ps.b], W=[s_.b])
            fw.op(fw.DVE, lambda: nc.vector.tensor_tensor(out=act.t[:, i, :], in0=s_.t[:], in1=ps.t[:, n:2 * n], op=ALU.mult), R=[s_.b, ps.b], W=[act.b])
        for m in range(8):
            ps = cx.ps[2 + (k % 6)]
            k += 1
            for i in range(22):
                fw.op(fw.PE, lambda: nc.tensor.matmul(ps.t[:, 0:n], lhsT=wd.t[:, i, m * 128:(m + 1) * 128], rhs=act.t[:, i, :], start=(i == 0), stop=(i == 21)),
                      R=[wd.b, act.b], W=[ps.b], inc=(i == 21))
            fw.op(fw.DVE, lambda: nc.vector.tensor_tensor(out=ht.t[:, m, :], in0=ht.t[:, m, :], in1=ps.t[:, 0:n], op=ALU.add), R=[ht.b, ps.b], W=[ht.b])
        fw.dma(fw.GD, out=cx.HT.t[:, tsl].rearrange("(j p) t -> p j t", p=128), in_=ht.t[:], R=[ht.b], W=[cb])


def phase_f3(cx, W, li, p_ap):
    fw, nc = cx.fw, cx.nc
    new_phase(cx)
    cx.wstg = [SB(cx, [128, 1024], F32, "wstg") for _ in range(2)]
    gp = SB(cx, [128, 8], F32, "gp")
    fw.dma(fw.SP, out=gp.t[:], in_=W["g_ple"], W=[gp.b])
    wpg = SB(cx, [128, 8, D], BF16, "wpg")
    load_w_bf16(cx, wpg, W["w_ple_gate"], 8, D, gain=gp)
    wpl = SB(cx, [128, 2, D], BF16, "wpl")
    load_w_bf16(cx, wpl, W["w_ple"], 2, D)
    hts = [SB(cx, [128, 8, CH], F32, "ht") for _ in range(2)]
    pcs = [SB(cx, [128, 4, 256], F32, "pc") for _ in range(2)]
    fT = SB(cx, [128, 8, CH], BF16, "fT")
    pT = SB(cx, [128, 2, CH], BF16, "pT")
    sg = [SB(cx, [128, CH], F32, "sg") for _ in range(2)]
    k = 0
    for c in range(NCH):
        tsl = slice(c * CH, (c + 1) * CH)
        ht, pc = hts[c % 2], pcs[c % 2]
        fw.dma(fw.SP, out=ht.t[:], in_=cx.HT.t[:, tsl].rearrange("(j p) t -> p j t", p=128), R=[cx.HT.cb[c]], W=[ht.b])
        fw.dma(fw.SP, out=pc.t[:], in_=p_ap[li, tsl, :].rearrange("(s p) d -> p s d", p=128), W=[pc.b])
        r = rstd_chunk(cx, [(ht.t[:, j, :], 128, [ht.b]) for j in range(8)], D)
        fw.op(fw.DVE, lambda: nc.vector.tensor_tensor(out=fT.t[:], in0=ht.t[:], in1=r.t[:].unsqueeze(1).broadcast_to([128, 8, CH]), op=ALU.mult),
              R=[ht.b, r.b], W=[fT.b])
        for ct in range(2):
            ps = cx.ps[2 + (k % 6)]
            k += 1
            for s in range(4):
                fw.op(fw.PE, lambda: nc.tensor.transpose(out=ps.t[:, s * 128:(s + 1) * 128], in_=pc.t[:, s, ct * 128:(ct + 1) * 128], identity=cx.ident_f.t[:]),
                      R=[pc.b, cx.ident_f.b], W=[ps.b], inc=(s == 3))
            evac(cx, k, pT.t[:, ct, :], ps.t[:], R=[ps.b], W=[pT.b])
        for m in range(8):
            ps = cx.ps[2 + (k % 6)]
            k += 1
            for j in range(8):
                fw.op(fw.PE, lambda: nc.tensor.matmul(ps.t[:, :], lhsT=wpg.t[:, j, m * 128:(m + 1) * 128], rhs=fT.t[:, j, :], start=(j == 0), stop=(j == 7)),
                      R=[wpg.b, fT.b], W=[ps.b], inc=(j == 7))
            ps2 = cx.ps[2 + (k % 6)]
            k += 1
            for ct in range(2):
                fw.op(fw.PE, lambda: nc.tensor.matmul(ps2.t[:, :], lhsT=wpl.t[:, ct, m * 128:(m + 1) * 128], rhs=pT.t[:, ct, :], start=(ct == 0), stop=(ct == 1)),
                      R=[wpl.b, pT.b], W=[ps2.b], inc=(ct == 1))
            s_ = sg[m % 2]
            fw.op(fw.ACT, lambda: nc.scalar.activation(out=s_.t[:], in_=ps.t[:], func=AF.Sigmoid), R=[ps.b], W=[s_.b])
            fw.op(fw.DVE, lambda: nc.vector.tensor_tensor(out=s_.t[:], in0=s_.t[:], in1=ps2.t[:], op=ALU.mult), R=[s_.b, ps2.b], W=[s_.b])
            fw.op(fw.DVE, lambda: nc.vector.tensor_tensor(out=ht.t[:, m, :], in0=ht.t[:, m, :], in1=s_.t[:], op=ALU.add), R=[ht.b, s_.b], W=[ht.b])
        fw.dma(fw.GD, out=cx.HT.t[:, tsl].rearrange("(j p) t -> p j t", p=128), in_=ht.t[:], R=[ht.b], W=[cx.HT.cb[c]])


def host_consts():
    C = {}
    C["ident_f"] = np.eye(128, dtype=np.float32)
    C["ones_f"] = np.ones((128, 128), dtype=np.float32)
    inv = 10000.0 ** (-np.arange(0, 32, 2, dtype=np.float32) / 32)
    ang = (np.arange(L, dtype=np.float32)[:, None] * inv[None, :]).astype(np.float32)
    cos, sin = np.cos(ang).T.astype(np.float32), np.sin(ang).T.astype(np.float32)
    sq = np.float32(96 ** -0.5)
    C["cs_q"] = np.stack([np.tile(cos, (4, 1)) * sq, np.tile(sin, (4, 1)) * sq]).astype(np.float32)
    C["cs_k"] = np.stack([cos, sin]).astype(np.float32)
    sel = np.zeros((65, 64), np.float32)
    sel[64, :] = 1.0
    C["sel"] = sel
    bf = ml_dtypes.bfloat16
    N = 2 * L
    n1 = np.arange(128, dtype=np.float64)
    k1 = np.arange(NK1, dtype=np.float64)
    a1 = 2 * np.pi * np.outer(n1, k1) / 128.0
    Fr, Fi = np.cos(a1), -np.sin(a1)
    C["fftF4"] = np.concatenate([Fr, Fi, -Fi, Fr], axis=1).astype(bf)
    n2 = np.arange(128, dtype=np.float64)
    k2 = np.arange(128, dtype=np.float64)
    kk = k1[None, :, None] + 128.0 * k2[None, None, :]
    ag = 2 * np.pi * ((n2[:, None, None] * kk) % N) / N
    C["fftG"] = np.stack([np.cos(ag), -np.sin(ag)], axis=2).astype(bf)
    ai_ = 2 * np.pi * np.outer(k2, n2) / 128.0
    Fc, Fs = np.cos(ai_), np.sin(ai_)
    C["fftFI"] = np.stack([np.concatenate([Fc, Fs], axis=1), np.concatenate([-Fs, Fc], axis=1)], axis=1).astype(bf)
    wk = np.where((k1 == 0) | (k1 == 64), 1.0, 2.0) / N
    nn = 128.0 * np.arange(64, dtype=np.float64)[None, None, :] + n2[None, :, None]
    aq = 2 * np.pi * ((k1[:, None, None] * nn) % N) / N
    C["fftQ"] = np.stack([np.cos(aq) * wk[:, None, None], -np.sin(aq) * wk[:, None, None]], axis=2).astype(bf)
    n = np.arange(N)
    pos = np.where(n < L, n, N - n).astype(np.float64)
    t = (pos.astype(np.float32) / np.float32(L)).astype(np.float32)
    bands = np.arange(1, 17, dtype=np.float32)
    ang2 = (np.float32(2.0 * math.pi) * t[:, None] * bands[None, :]).astype(np.float32)
    C["featsT"] = np.ascontiguousarray(np.concatenate([t[:, None], np.cos(ang2), np.sin(ang2)], axis=-1).T.astype(np.float32))
    C["tpos"] = np.ascontiguousarray(t.reshape(128, 128))
    C["tau"] = np.tile(np.arange(512, dtype=np.float32)[None, :], (128, 1))
    slopes = [2.0 ** (-8.0 * (h + 1) / 4) for h in range(4)]
    p = np.arange(128, dtype=np.float64)[:, None]
    dl = np.arange(-63, 64, dtype=np.float64)[None, :]
    f = np.arange(512, dtype=np.float64)
    bt, dg, flr = [], [], []
    for sl in slopes:
        left = sl * (p + 128 * dl)
        right = -sl * (p + 128 * dl - 511)
        bt.append(np.where(dl < 0, left, np.where(dl >= 4, right, 0.0)))
        dg.append(np.stack([-sl * np.abs(f[None, :] - 128 * d_ - p) for d_ in range(4)]))
        flr.append(np.stack([np.tile(np.exp(-sl * f)[None, :], (65, 1)), np.tile(np.exp(-sl * (511 - f))[None, :], (65, 1))]))
    C["da_bt"] = np.stack(bt).astype(np.float32)
    C["da_dg"] = np.stack(dg).astype(np.float32)
    C["da_flr"] = np.stack(flr).astype(np.float32)
    return C


PER_LAYER = ["g_mix", "w_in", "w_out", "g_branch", "hy_conv_w", "hy_conv_b", "hy_f_w1", "hy_f_b1", "hy_f_w2", "hy_f_b2", "hy_f_w3",
             "hy_log_decay", "hy_skip", "s5_a_re", "s5_a_im", "s5_log_dt", "s5_b_re", "s5_b_im", "s5_c_re", "s5_c_im", "s5_d",
             "s5_w_glu", "s5_b_glu", "da_lambda", "da_g_head", "mla_g_q", "mla_g_kv", "mla_w_uq", "mla_w_ukv", "g_ffn", "w_gate",
             "w_up", "w_down", "g_ple", "w_ple_gate", "w_ple"]


def pj(v, kt):
    v = np.asarray(v, dtype=np.float32).reshape(-1)
    out = np.zeros((kt * 128,), np.float32)
    out[:v.shape[0]] = v
    return np.ascontiguousarray(out.reshape(kt, 128).T)


def host_layout(inputs, nl):
    Wd = {}
    for li in range(nl):
        g = lambda n: np.asarray(inputs[n][li], dtype=np.float32)
        Wd[f"g_mix__{li}"] = pj(g("g_mix"), 8)
        Wd[f"w_in__{li}"] = g("w_in")
        Wd[f"mla_g_q__{li}"] = pj(g("mla_g_q"), 2)
        Wd[f"mla_g_kv__{li}"] = pj(g("mla_g_kv"), 1)
        Wd[f"mla_w_uq__{li}"] = g("mla_w_uq")
        Wd[f"mla_w_ukv__{li}"] = g("mla_w_ukv")
        Wd[f"da_lambda__{li}"] = g("da_lambda")
        def s5l(a):
            a = np.asarray(a, np.float32)
            rest = a.shape[2:]
            a = a.reshape((2, 8, 128) + rest)
            a = np.moveaxis(a, 2, 0)
            return np.ascontiguousarray(a.reshape((128, 16) + rest))
        Wd[f"s5_ar__{li}"] = s5l(g("s5_a_re").reshape(2, 1024))
        Wd[f"s5_ai__{li}"] = s5l(g("s5_a_im").reshape(2, 1024))
        Wd[f"s5_ldt__{li}"] = s5l(np.repeat(g("s5_log_dt"), 64, axis=1))
        Wd[f"s5_b__{li}"] = s5l(np.stack([g("s5_b_re").reshape(2, 1024, 16), g("s5_b_im").reshape(2, 1024, 16)], axis=2))
        cre = np.transpose(g("s5_c_re"), (0, 1, 3, 2)).reshape(2, 1024, 16)
        cim = np.transpose(g("s5_c_im"), (0, 1, 3, 2)).reshape(2, 1024, 16)
        Wd[f"s5_c__{li}"] = s5l(np.stack([cre, cim], axis=2))
        Wd[f"s5_d__{li}"] = np.ascontiguousarray(g("s5_d").reshape(8, 32).T)
        Wd[f"s5_w_glu__{li}"] = g("s5_w_glu")
        Wd[f"s5_b_glu__{li}"] = pj(g("s5_b_glu"), 2)
        Wd[f"hy_f_w1__{li}"] = g("hy_f_w1")
        Wd[f"hy_f_w2__{li}"] = g("hy_f_w2")
        Wd[f"hy_f_b12__{li}"] = np.ascontiguousarray(np.stack([g("hy_f_b1"), g("hy_f_b2")], axis=1))
        Wd[f"hy_f_w3__{li}"] = g("hy_f_w3")
        Wd[f"hy_log_decay__{li}"] = g("hy_log_decay")
        Wd[f"hy_skip__{li}"] = g("hy_skip")
        cwb = np.concatenate([g("hy_conv_w"), g("hy_conv_b")[None, :]], axis=0)
        Wd[f"hy_conv__{li}"] = np.ascontiguousarray(cwb.T.reshape(6, 128, 4).transpose(1, 0, 2))
        gbr = g("g_branch")
        Wd[f"g_branch__{li}"] = pj(np.concatenate([gbr[0], gbr[1], np.ones(256, np.float32), gbr[2]]), 8)
        Wd[f"w_out__{li}"] = g("w_out")
        Wd[f"g_ffn__{li}"] = pj(g("g_ffn"), 8)
        Wd[f"w_gate__{li}"] = g("w_gate")
        Wd[f"w_up__{li}"] = g("w_up")
        Wd[f"w_down__{li}"] = g("w_down")
        Wd[f"g_ple__{li}"] = pj(g("g_ple"), 8)
        Wd[f"w_ple_gate__{li}"] = g("w_ple_gate")
        Wd[f"w_ple__{li}"] = g("w_ple")
        Wd[f"da_g_head__{li}"] = g("da_g_head").reshape(64, 1)
    Wd["g_final"] = pj(inputs["g_final"], 8)
    return Wd


def build(nl, Wd_shapes, C_shapes, stages, dbg=()):
    nc = bass.Bass("TRN2", target_bir_lowering=False)
    cx = Ctx()
    cx.nc = nc
    x_ap = nc.dram_tensor("x", [L, D], F32, kind="ExternalInput").ap()
    p_ap = nc.dram_tensor("p", [nl, L, 256], F32, kind="ExternalInput").ap()
    out_ap = nc.dram_tensor("out", [L, D], F32, kind="ExternalOutput").ap()
    WA = {k: nc.dram_tensor(k, list(shp), F32, kind="ExternalInput").ap() for k, shp in Wd_shapes.items()}
    cx.C = {k: nc.dram_tensor("c_" + k, list(shp), dt, kind="ExternalInput").ap() for k, (shp, dt) in C_shapes.items()}
    with ExitStack() as es:
        cx.es = es
        cx.phase_es = None
        cx.tcount = 0
        cx.fw = fw = FW(nc, es)
        cx.ps = [Tl(es.enter_context(nc.psum_tensor(f"ps{i}", [128, 512], F32)), f"ps{i}") for i in range(8)]
        P = lambda n, shp, dt: Tl(es.enter_context(nc.sbuf_tensor(n, shp, dt)), n)
        cx.ones_f = P("ones_f", [128, 128], F32)
        cx.ident_f = P("ident_f", [128, 128], F32)
        cx.eps_t = P("eps_t", [128, 1], F32)
        cx.sq = [P(f"sq{i}", [128, CH], F32) for i in range(2)]
        cx.rstd = [P(f"rstd{i}", [128, CH], F32) for i in range(2)]
        cx.rstd_i = 0
        fw.dma(fw.SP, out=cx.ones_f.t[:], in_=cx.C["ones_f"], W=[cx.ones_f.b])
        fw.dma(fw.SP, out=cx.ident_f.t[:], in_=cx.C["ident_f"], W=[cx.ident_f.b])
        fw.op(fw.DVE, lambda: nc.vector.memset(cx.eps_t.t[:], EPS), W=[cx.eps_t.b])
        cx.OUTB = [Buf() for _ in range(NCH)]
        cx.HT = dram(cx, "HT", [D, L], F32)
        sc = cx.sc = {"HT": cx.HT}
        sc["ZHY"] = dram(cx, "ZHY", [768, L], F32)
        sc["ZS5"] = dram(cx, "ZS5", [256, L], F32)
        sc["QDA"] = dram(cx, "QDA", [256, L], BF16)
        sc["KDA"] = dram(cx, "KDA", [256, L], BF16)
        sc["VDA"] = dram(cx, "VDA", [L, 256], BF16)
        sc["QMLA"] = dram(cx, "QMLA", [4, 96, L], BF16)
        sc["KMLA"] = dram(cx, "KMLA", [4, 96, L], BF16)
        sc["VMLA"] = dram(cx, "VMLA", [L, 256], BF16)
        sc["YMLA"] = dram(cx, "YMLA", [256, L], F32)
        sc["YHY"] = dram(cx, "YHY", [256, L], F32)
        sc["YS5"] = dram(cx, "YS5", [256, L], F32)
        sc["GS5"] = dram(cx, "GS5", [256, L], F32)
        sc["ZC"] = dram(cx, "ZC", [768, L], F32)
        sc["HSPEC"] = dram(cx, "HSPEC", [16, 128, NK1 * 2 * HC], BF16)
        sc["RN"] = dram(cx, "RN", [128, 512], F32)
        sc["YDA"] = dram(cx, "YDA", [256, L], F32)
        cx.slopes = [2.0 ** (-8.0 * (h + 1) / 4) for h in range(4)]
        cx.window = 100.0
        cx.dbgt = 'dbgt' in stages
        cx.hy_sc_only = 'hysc' in stages
        cx.hy_lim = 8
        cx.hy_stop = 9
        cx.hy_same = 'hysame' in stages
        cx.hy_g0 = 1 if 'hyg1' in stages else 0
        for st_ in stages:
            if st_.startswith('hystop'):
                cx.hy_stop = int(st_[6:])
                cx.hy_lim = 1
            if st_.startswith('hylim'):
                cx.hy_lim = int(st_[5:])
        if "x0" in stages:
            phase_x0(cx, x_ap)
        for li in range(nl):
            W = {k.rsplit("__", 1)[0]: v for k, v in WA.items() if k.endswith(f"__{li}")}
            if "a" in stages:
                phase_a(cx, W, li)
            if "mlas5" in stages:
                phase_mla_s5(cx, W, li)
            if "mla" in stages:
                phase_mla(cx)
            if "da" in stages:
                phase_da(cx, W, li)
            if "s5" in stages:
                phase_s5(cx, W, li)
            if "hyf" in stages:
                phase_hy_filter(cx, W, li)
            if "hy" in stages:
                phase_hy(cx, W, li)
            if "inj" in stages:
                new_phase(cx)
                for nm in ["YHY", "YS5"]:
                    src = nc.dram_tensor("inj_" + nm, [256, L], F32, kind="ExternalInput").ap()
                    fw.dma(fw.SP, out=sc[nm].t, in_=src)
            if "f1" in stages:
                phase_f1(cx, W, li)
            if "f2" in stages:
                phase_f2(cx, W, li)
            if "f3" in stages:
                phase_f3(cx, W, li, p_ap)
        if "out" in stages:
            phase_out(cx, WA["g_final"], out_ap)
        new_phase(cx)
        for name in dbg:
            src = sc[name].t
            dst = nc.dram_tensor("dbg_" + name, list(src.shape), src.dtype, kind="ExternalOutput").ap()
            fw.dma(fw.SP, out=dst, in_=src)
        fw.final_wait()
        fw.check_deadlock()
        cx.phase_es.close()
    print("instructions:", fw.n_inst)
    return nc


ALL_STAGES = ["x0", "a", "mlas5", "da", "hyf", "hy", "f1", "f2", "f3", "out"]
_CACHE = {}


def kernel(**inputs):
    nl = 4
    nb = 8
    inputs = {k: np.asarray(v) for k, v in inputs.items()}
    Wd = host_layout(inputs, nl)
    C = host_consts()
    key = "prog"
    if key not in _CACHE:
        _CACHE[key] = build(nl, {k: v.shape for k, v in Wd.items()},
                            {k: (v.shape, BF16 if v.dtype == ml_dtypes.bfloat16 else F32) for k, v in C.items()}, ALL_STAGES)
    nc = _CACHE[key]
    x = np.asarray(inputs["x"], dtype=np.float32)
    p = np.asarray(inputs["p"], dtype=np.float32)
    in_maps = []
    for b in range(nb):
        m = {"x": np.ascontiguousarray(x[b]), "p": np.ascontiguousarray(p[:, b])}
        m.update(Wd)
        m.update({"c_" + k: v for k, v in C.items()})
        in_maps.append(m)
    res = run_bass_kernel_spmd(nc, in_maps, core_ids=list(range(nb)))
    return np.stack([np.asarray(res.results[b]["out"], dtype=np.float32) for b in range(nb)], axis=0)
```

```python
import math
from contextlib import ExitStack
import numpy as np
import ml_dtypes
import concourse.bass as bass
import concourse.mybir as mybir
from concourse.bass_utils import run_bass_kernel_spmd

F32 = mybir.dt.float32
BF16 = mybir.dt.bfloat16
AF = mybir.ActivationFunctionType
ALU = mybir.AluOpType
AX = mybir.AxisListType


class Buf:
    __slots__ = ("lw", "rd", "name")

    def __init__(self, name=""):
        self.lw = None
        self.rd = {}
        self.name = name


class Stream:
    def __init__(self, eng):
        self.eng = eng
        self.seen = {}
        self.ev = []


class CQ:
    def __init__(self, stream, sem):
        self.stream = stream
        self.sem = sem
        self.count = 0
        self.is_dma = False


class DQ:
    def __init__(self, stream, sems):
        self.stream = stream
        self.sems = sems
        self.cnt = [0] * len(sems)
        self.idx = 0
        self.is_dma = True
        self.outst = []


class FW:
    def __init__(self, nc, es):
        self.nc = nc
        self.es = es
        S = lambda n: es.enter_context(nc.semaphore(n))
        self.s_pe = Stream(nc.tensor)
        self.s_act = Stream(nc.scalar)
        self.s_dve = Stream(nc.vector)
        self.s_pool = Stream(nc.gpsimd)
        self.s_sp = Stream(nc.sync)
        self.streams = [self.s_pe, self.s_act, self.s_dve, self.s_pool, self.s_sp]
        self.PE = CQ(self.s_pe, S("q_pe"))
        self.ACT = CQ(self.s_act, S("q_act"))
        self.DVE = CQ(self.s_dve, S("q_dve"))
        self.POOL = CQ(self.s_pool, S("q_pool"))
        self.SP = DQ(self.s_sp, [S(f"q_sp{i}") for i in range(12)])
        self.GD = DQ(self.s_pool, [S(f"q_gd{i}") for i in range(12)])
        self.AD = DQ(self.s_act, [S(f"q_ad{i}") for i in range(4)])
        self.cqs = [self.PE, self.ACT, self.DVE, self.POOL]
        self.dqs = [self.SP, self.GD, self.AD]
        self.n_inst = 0

    def _wait(self, stream, deps, own=None):
        need = {}
        for d in deps:
            if d is None:
                continue
            (sem, val, q), raw = d
            if own is not None and q is own and not own.is_dma:
                if not raw or own is self.PE:
                    continue
            k = id(sem)
            if stream.seen.get(k, 0) >= val:
                continue
            if k not in need or need[k][1] < val:
                need[k] = (sem, val)
        for k, (sem, val) in need.items():
            stream.eng.wait_ge(sem, val)
            stream.seen[k] = val
            self.n_inst += 1
            stream.ev.append(("w", k, val))

    @staticmethod
    def _deps(R, W):
        deps = []
        for b in R:
            if b.lw is not None:
                deps.append((b.lw, True))
        for b in W:
            if b.lw is not None:
                deps.append((b.lw, False))
            deps.extend((t, False) for t in b.rd.values())
        return deps

    @staticmethod
    def _mark(tok, R, W):
        k = id(tok[0])
        for b in R:
            o = b.rd.get(k)
            if o is None or o[1] < tok[1]:
                b.rd[k] = tok
        for b in W:
            b.lw = tok
            b.rd = {}

    def op(self, q, fn, R=(), W=(), inc=True):
        self._wait(q.stream, self._deps(R, W), own=q)
        inst = fn()
        self.n_inst += 1
        if inc:
            inst.then_inc(q.sem, 1)
            q.count += 1
            tok = (q.sem, q.count, q)
            q.stream.ev.append(("i", id(q.sem), 1))
        else:
            tok = (q.sem, q.count + 1, q)
        self._mark(tok, R, W)
        return inst

    def dma(self, q, out, in_, R=(), W=(), **kw):
        st = q.stream
        self._wait(st, self._deps(R, W), own=None)
        nd = 1
        for d_ in tuple(out.shape)[:-1]:
            nd *= int(d_)
        LIM = 1536
        q.outst = [o for o in q.outst if st.seen.get(id(o[0]), 0) < o[1]]
        while q.outst and sum(o[2] for o in q.outst) + nd > LIM:
            osem, oval, _ = q.outst.pop(0)
            if st.seen.get(id(osem), 0) < oval:
                st.eng.wait_ge(osem, oval)
                st.seen[id(osem)] = oval
                st.ev.append(("w", id(osem), oval))
        slot = q.idx % len(q.sems)
        q.idx += 1
        sem = q.sems[slot]
        if q.cnt[slot] > 0 and st.seen.get(id(sem), 0) < q.cnt[slot]:
            st.eng.wait_ge(sem, q.cnt[slot])
            st.seen[id(sem)] = q.cnt[slot]
            st.ev.append(("w", id(sem), q.cnt[slot]))
        inst = st.eng.dma_start(out=out, in_=in_, **kw)
        inst.then_inc(sem, 16)
        st.ev.append(("i", id(sem), 16))
        self.n_inst += 1
        q.cnt[slot] += 16
        tok = (sem, q.cnt[slot], q)
        q.outst.append((sem, q.cnt[slot], nd))
        self._mark(tok, R, W)
        return inst

    def barrier(self, streams=None):
        deps = []
        for q in self.cqs:
            if q.count > 0:
                deps.append(((q.sem, q.count, q), True))
        for q in self.dqs:
            for s, c in zip(q.sems, q.cnt):
                if c > 0:
                    deps.append(((s, c, q), True))
        for st in (streams or self.streams):
            self._wait(st, deps, own=None)

    def check_deadlock(self):
        vals = {}
        pos = [0] * len(self.streams)
        progress = True
        while progress:
            progress = False
            for si, st in enumerate(self.streams):
                while pos[si] < len(st.ev):
                    kind, k, v = st.ev[pos[si]]
                    if kind == "w":
                        if vals.get(k, 0) >= v:
                            pos[si] += 1
                            progress = True
                        else:
                            break
                    else:
                        vals[k] = vals.get(k, 0) + v
                        pos[si] += 1
                        progress = True
        stuck = [(si, pos[si], len(st.ev)) for si, st in enumerate(self.streams) if pos[si] < len(st.ev)]
        if stuck:
            for si, p, n in stuck:
                kind, k, v = self.streams[si].ev[p]
                print("DEADLOCK stream", si, "at", p, "/", n, "waiting sem", k, "val", v, "cur", vals.get(k, 0))
            raise RuntimeError("deadlock detected in emitted program")
        return True

    def final_wait(self):
        self.barrier(streams=[self.s_sp])


L = 8192
D = 1024
NCH = 16
CH = 512
IN_W = 2144
FFN = 2816
EPS = 1e-6


class Tl:
    __slots__ = ("t", "b", "cb")

    def __init__(self, t, name=""):
        self.t = t
        self.b = Buf(name)


class Ctx:
    pass


def new_phase(cx):
    cx.fw.barrier()
    if cx.phase_es is not None:
        cx.phase_es.close()
    cx.phase_es = ExitStack()
    cx.es.callback(lambda e=cx.phase_es: e.close())
    cx.tcount += 1000


def SB(cx, shape, dt, name=None):
    cx.tcount += 1
    nm = f"{name or 't'}_{cx.tcount}"
    return Tl(cx.phase_es.enter_context(cx.nc.sbuf_tensor(nm, list(shape), dt)), nm)


def dram(cx, name, shape, dt):
    t = cx.nc.dram_tensor(name, list(shape), dt, kind="Internal").ap()
    tl = Tl(t, name)
    tl.cb = [Buf(name + str(i)) for i in range(NCH)]
    return tl


def rstd_chunk(cx, parts, dim, n=CH):
    fw, nc = cx.fw, cx.nc
    ps = cx.ps[0]
    np_ = len(parts)
    for i, (ap, K, bufs) in enumerate(parts):
        sq = cx.sq[i % 2]
        fw.op(fw.ACT, lambda: nc.scalar.activation(out=sq.t[0:K, 0:n], in_=ap, func=AF.Square), R=bufs, W=[sq.b])
        fw.op(fw.PE, lambda: nc.tensor.matmul(ps.t[:, 0:n], lhsT=cx.ones_f.t[0:K, :], rhs=sq.t[0:K, 0:n], start=(i == 0), stop=(i == np_ - 1)),
              R=[sq.b, cx.ones_f.b], W=[ps.b], inc=True)
    r = cx.rstd[cx.rstd_i % 2]
    cx.rstd_i += 1
    fw.op(fw.ACT, lambda: nc.scalar.activation(out=r.t[:, 0:n], in_=ps.t[:, 0:n], func=AF.Sqrt, bias=cx.eps_t.t[:, 0:1], scale=1.0 / dim), R=[ps.b, cx.eps_t.b], W=[r.b])
    fw.op(fw.DVE, lambda: nc.vector.reciprocal(out=r.t[:, 0:n], in_=r.t[:, 0:n]), R=[r.b], W=[r.b])
    return r


def evac(cx, k, out_ap, in_ap, R, W, scale=None):
    fw, nc = cx.fw, cx.nc
    if k % 2 == 0:
        if scale is None:
            fw.op(fw.ACT, lambda: nc.scalar.copy(out=out_ap, in_=in_ap), R=R, W=W)
        else:
            fw.op(fw.ACT, lambda: nc.scalar.mul(out=out_ap, in_=in_ap, mul=scale), R=R, W=W)
    else:
        if scale is None:
            fw.op(fw.DVE, lambda: nc.vector.tensor_copy(out=out_ap, in_=in_ap), R=R, W=W)
        else:
            fw.op(fw.DVE, lambda: nc.vector.tensor_scalar(out=out_ap, in0=in_ap, scalar1=scale, scalar2=None, op0=ALU.mult), R=R, W=W)


def load_w_bf16(cx, dst, src_ap, kt, ncols, gain=None, stg_cols=2144):
    fw, nc = cx.fw, cx.nc
    for j in range(kt):
        stg = cx.wstg[j % 2]
        fw.dma(fw.SP, out=stg.t[:, 0:ncols], in_=src_ap[j * 128:(j + 1) * 128, :], W=[stg.b])
        if gain is None:
            evac(cx, j, dst.t[:, j, :], stg.t[:, 0:ncols], R=[stg.b], W=[dst.b])
        else:
            if j % 2 == 0:
                fw.op(fw.ACT, lambda: nc.scalar.mul(out=dst.t[:, j, :], in_=stg.t[:, 0:ncols], mul=gain.t[:, j:j + 1]), R=[stg.b, gain.b], W=[dst.b])
            else:
                fw.op(fw.DVE, lambda: nc.vector.tensor_scalar(out=dst.t[:, j, :], in0=stg.t[:, 0:ncols], scalar1=gain.t[:, j:j + 1], scalar2=None, op0=ALU.mult),
                      R=[stg.b, gain.b], W=[dst.b])


def phase_x0(cx, x_ap):
    fw, nc = cx.fw, cx.nc
    new_phase(cx)
    xts = [SB(cx, [128, 4, D], F32, "xt") for _ in range(2)]
    hts = [SB(cx, [128, 8, CH], F32, "ht") for _ in range(2)]
    k = 0
    for c in range(NCH):
        xt, ht = xts[c % 2], hts[c % 2]
        fw.dma(fw.SP, out=xt.t[:], in_=x_ap[c * CH:(c + 1) * CH, :].rearrange("(s p) d -> p s d", p=128), W=[xt.b])
        for j in range(8):
            ps = cx.ps[2 + (k % 6)]
            for s in range(4):
                fw.op(fw.PE, lambda: nc.tensor.transpose(out=ps.t[:, s * 128:(s + 1) * 128], in_=xt.t[:, s, j * 128:(j + 1) * 128], identity=cx.ident_f.t[:]),
                      R=[xt.b, cx.ident_f.b], W=[ps.b], inc=(s == 3))
            evac(cx, k, ht.t[:, j, :], ps.t[:], R=[ps.b], W=[ht.b])
            k += 1
        fw.dma(fw.GD, out=cx.HT.t[:, c * CH:(c + 1) * CH].rearrange("(j p) t -> p j t", p=128), in_=ht.t[:], R=[ht.b], W=[cx.HT.cb[c]])


def phase_out(cx, gfin_ap, out_ap):
    fw, nc = cx.fw, cx.nc
    new_phase(cx)
    gf = SB(cx, [128, 8], F32, "gf")
    fw.dma(fw.SP, out=gf.t[:], in_=gfin_ap, W=[gf.b])
    hts = [SB(cx, [128, 8, CH], F32, "ht") for _ in range(2)]
    ots = [SB(cx, [128, 4, D], F32, "ot") for _ in range(2)]
    k = 0
    for c in range(NCH):
        ht, ot = hts[c % 2], ots[c % 2]
        fw.dma(fw.SP, out=ht.t[:], in_=cx.HT.t[:, c * CH:(c + 1) * CH].rearrange("(j p) t -> p j t", p=128), R=[cx.HT.cb[c]], W=[ht.b])
        r = rstd_chunk(cx, [(ht.t[:, j, :], 128, [ht.b]) for j in range(8)], D)
        for j in range(8):
            fw.op(fw.DVE, lambda: nc.vector.scalar_tensor_tensor(out=ht.t[:, j, :], in0=ht.t[:, j, :], scalar=gf.t[:, j:j + 1], in1=r.t[:],
                                                                 op0=ALU.mult, op1=ALU.mult), R=[ht.b, r.b, gf.b], W=[ht.b])
        for s in range(4):
            for jj in range(2):
                ps = cx.ps[2 + (k % 6)]
                for j4 in range(4):
                    j = jj * 4 + j4
                    fw.op(fw.PE, lambda: nc.tensor.transpose(out=ps.t[:, j4 * 128:(j4 + 1) * 128], in_=ht.t[:, j, s * 128:(s + 1) * 128], identity=cx.ident_f.t[:]),
                          R=[ht.b, cx.ident_f.b], W=[ps.b], inc=(j4 == 3))
                evac(cx, k, ot.t[:, s, jj * 512:(jj + 1) * 512], ps.t[:], R=[ps.b], W=[ot.b])
                k += 1
        fw.dma(fw.GD, out=out_ap[c * CH:(c + 1) * CH, :].rearrange("(s p) d -> p s d", p=128), in_=ot.t[:], R=[ot.b], W=[cx.OUTB[c]])


def phase_a(cx, W, li):
    fw, nc = cx.fw, cx.nc
    new_phase(cx)
    sc = cx.sc
    cx.wstg = [SB(cx, [128, IN_W], F32, "wstg") for _ in range(2)]
    gmix = SB(cx, [128, 8], F32, "gmix")
    fw.dma(fw.SP, out=gmix.t[:], in_=W["g_mix"], W=[gmix.b])
    win = SB(cx, [128, 8, IN_W], BF16, "win")
    load_w_bf16(cx, win, W["w_in"], 8, IN_W, gain=gmix)
    gq = SB(cx, [128, 2], F32, "gq")
    fw.dma(fw.SP, out=gq.t[:], in_=W["mla_g_q"], W=[gq.b])
    gkv = SB(cx, [128, 1], F32, "gkv")
    fw.dma(fw.SP, out=gkv.t[:], in_=W["mla_g_kv"], W=[gkv.b])
    wuq = SB(cx, [128, 2, 384], BF16, "wuq")
    wuq_src = W["mla_w_uq"]
    for kt, (r0, rn) in enumerate([(0, 128), (128, 64)]):
        stg = cx.wstg[kt % 2]
        src = wuq_src[r0:r0 + rn, :].rearrange("r (h c) -> r h c", c=96)
        fw.dma(fw.SP, out=stg.t[0:rn, 0:256].rearrange("r (h c) -> r h c", c=64), in_=src[:, :, 0:64], W=[stg.b])
        fw.dma(fw.SP, out=stg.t[0:rn, 256:320].rearrange("r (h c) -> r h c", c=16), in_=src[:, :, 64:80], W=[stg.b])
        fw.dma(fw.SP, out=stg.t[0:rn, 320:384].rearrange("r (h c) -> r h c", c=16), in_=src[:, :, 80:96], W=[stg.b])
        fw.op(fw.DVE, lambda: nc.vector.tensor_scalar(out=wuq.t[0:rn, kt, :], in0=stg.t[0:rn, 0:384], scalar1=gq.t[0:rn, kt:kt + 1], scalar2=None, op0=ALU.mult),
              R=[stg.b, gq.b], W=[wuq.b])
    wukv = SB(cx, [128, 512], BF16, "wukv")
    stg = cx.wstg[0]
    src = W["mla_w_ukv"].rearrange("r (h c) -> r h c", c=128)
    fw.dma(fw.SP, out=stg.t[:, 0:256].rearrange("r (h c) -> r h c", c=64), in_=src[:, :, 0:64], W=[stg.b])
    fw.dma(fw.SP, out=stg.t[:, 256:512].rearrange("r (h c) -> r h c", c=64), in_=src[:, :, 64:128], W=[stg.b])
    fw.op(fw.DVE, lambda: nc.vector.tensor_scalar(out=wukv.t[:], in0=stg.t[:, 0:512], scalar1=gkv.t[:, 0:1], scalar2=None, op0=ALU.mult),
          R=[stg.b, gkv.b], W=[wukv.b])

    hts = [SB(cx, [128, 8, CH], F32, "ht") for _ in range(2)]
    aT = SB(cx, [128, 8, CH], BF16, "aT")
    zhy = [SB(cx, [128, 6, CH], F32, "zhy") for _ in range(2)]
    zs5 = [SB(cx, [128, 2, CH], F32, "zs5") for _ in range(2)]
    qda = [SB(cx, [128, 2, CH], BF16, "qda") for _ in range(2)]
    kda = [SB(cx, [128, 2, CH], BF16, "kda") for _ in range(2)]
    vda = [SB(cx, [128, 4, 256], BF16, "vda") for _ in range(2)]
    cq = SB(cx, [128, 2, CH], F32, "cq")
    ckv = SB(cx, [128, CH], F32, "ckv")
    cqn = SB(cx, [128, 2, CH], BF16, "cqn")
    ckvn = SB(cx, [128, CH], BF16, "ckvn")
    qn_st = [SB(cx, [128, 2, CH], BF16, "qnst") for _ in range(2)]
    kn_st = [SB(cx, [128, 2, CH], BF16, "knst") for _ in range(2)]
    vm_st = [SB(cx, [128, 4, 256], BF16, "vmst") for _ in range(2)]
    csq = [SB(cx, [64, 2, CH], F32, "csq") for _ in range(2)]
    csk = [SB(cx, [16, 2, CH], F32, "csk") for _ in range(2)]
    t1 = SB(cx, [64, CH], F32, "t1")
    t2 = SB(cx, [64, CH], F32, "t2")
    qr_st = [SB(cx, [64, 2, CH], BF16, "qrst") for _ in range(2)]
    kr_st = [SB(cx, [16, 2, CH], BF16, "krst") for _ in range(2)]

    kk = [0]

    def nextps():
        p = cx.ps[2 + (kk[0] % 6)]
        kk[0] += 1
        return p

    def mm_group(ps_ap, psb, lhs_list, rhs_list, Rb):
        n = len(lhs_list)
        for i in range(n):
            fw.op(fw.PE, lambda: nc.tensor.matmul(ps_ap, lhsT=lhs_list[i], rhs=rhs_list[i], start=(i == 0), stop=(i == n - 1)),
                  R=Rb, W=[psb], inc=(i == n - 1))

    for c in range(NCH):
        tsl = slice(c * CH, (c + 1) * CH)
        ht = hts[c % 2]
        fw.dma(fw.SP, out=ht.t[:], in_=cx.HT.t[:, tsl].rearrange("(j p) t -> p j t", p=128), R=[cx.HT.cb[c]], W=[ht.b])
        fw.dma(fw.SP, out=csq[c % 2].t[:], in_=cx.C["cs_q"][:, :, tsl].rearrange("a p t -> p a t"), W=[csq[c % 2].b])
        fw.dma(fw.SP, out=csk[c % 2].t[:], in_=cx.C["cs_k"][:, :, tsl].rearrange("a p t -> p a t"), W=[csk[c % 2].b])
        r = rstd_chunk(cx, [(ht.t[:, j, :], 128, [ht.b]) for j in range(8)], D)
        fw.op(fw.DVE, lambda: nc.vector.tensor_tensor(out=aT.t[:], in0=ht.t[:], in1=r.t[:].unsqueeze(1).broadcast_to([128, 8, CH]), op=ALU.mult),
              R=[ht.b, r.b], W=[aT.b])
        rhs8 = [aT.t[:, j, :] for j in range(8)]

        def colmm(c0, m):
            ps = nextps()
            mm_group(ps.t[0:m, :], ps.b, [win.t[:, j, c0:c0 + m] for j in range(8)], rhs8, [win.b, aT.b])
            return ps
        for i in range(6):
            ps = colmm(i * 128, 128)
            evac(cx, kk[0], zhy[c % 2].t[:, i, :], ps.t[:], R=[ps.b], W=[zhy[c % 2].b])
        fw.dma(fw.GD, out=sc["ZHY"].t[:, tsl].rearrange("(j p) t -> p j t", p=128), in_=zhy[c % 2].t[:], R=[zhy[c % 2].b], W=[sc["ZHY"].cb[c]])
        for i in range(2):
            ps = colmm(768 + i * 128, 128)
            evac(cx, kk[0], zs5[c % 2].t[:, i, :], ps.t[:], R=[ps.b], W=[zs5[c % 2].b])
        fw.dma(fw.GD, out=sc["ZS5"].t[:, tsl].rearrange("(j p) t -> p j t", p=128), in_=zs5[c % 2].t[:], R=[zs5[c % 2].b], W=[sc["ZS5"].cb[c]])
        for i in range(2):
            ps = colmm(1024 + i * 128, 128)
            evac(cx, kk[0], qda[c % 2].t[:, i, :], ps.t[:], R=[ps.b], W=[qda[c % 2].b], scale=32 ** -0.5)
        fw.dma(fw.GD, out=sc["QDA"].t[:, tsl].rearrange("(j p) t -> p j t", p=128), in_=qda[c % 2].t[:], R=[qda[c % 2].b], W=[sc["QDA"].cb[c]])
        for i in range(2):
            ps = colmm(1280 + i * 128, 128)
            evac(cx, kk[0], kda[c % 2].t[:, i, :], ps.t[:], R=[ps.b], W=[kda[c % 2].b])
        fw.dma(fw.GD, out=sc["KDA"].t[:, tsl].rearrange("(j p) t -> p j t", p=128), in_=kda[c % 2].t[:], R=[kda[c % 2].b], W=[sc["KDA"].cb[c]])
        for s in range(4):
            ps = nextps()
            mm_group(ps.t[:, 0:256], ps.b, [aT.t[:, j, s * 128:(s + 1) * 128] for j in range(8)], [win.t[:, j, 1536:1792] for j in range(8)], [win.b, aT.b])
            evac(cx, kk[0], vda[c % 2].t[:, s, :], ps.t[:, 0:256], R=[ps.b], W=[vda[c % 2].b])
        fw.dma(fw.GD, out=sc["VDA"].t[tsl, :].rearrange("(s p) d -> p s d", p=128), in_=vda[c % 2].t[:], R=[vda[c % 2].b], W=[sc["VDA"].cb[c]])
        ps = colmm(1792, 128)
        evac(cx, kk[0], cq.t[:, 0, :], ps.t[:], R=[ps.b], W=[cq.b])
        ps = colmm(1920, 64)
        evac(cx, kk[0], cq.t[0:64, 1, :], ps.t[0:64, :], R=[ps.b], W=[cq.b])
        ps = colmm(1984, 128)
        evac(cx, kk[0], ckv.t[:], ps.t[:], R=[ps.b], W=[ckv.b])
        rq = rstd_chunk(cx, [(cq.t[:, 0, :], 128, [cq.b]), (cq.t[0:64, 1, :], 64, [cq.b])], 192)
        fw.op(fw.DVE, lambda: nc.vector.tensor_tensor(out=cqn.t[:, 0, :], in0=cq.t[:, 0, :], in1=rq.t[:], op=ALU.mult), R=[cq.b, rq.b], W=[cqn.b])
        fw.op(fw.DVE, lambda: nc.vector.tensor_tensor(out=cqn.t[0:64, 1, :], in0=cq.t[0:64, 1, :], in1=rq.t[0:64, :], op=ALU.mult), R=[cq.b, rq.b], W=[cqn.b])
        rkv = rstd_chunk(cx, [(ckv.t[:], 128, [ckv.b])], 128)
        fw.op(fw.DVE, lambda: nc.vector.tensor_tensor(out=ckvn.t[:], in0=ckv.t[:], in1=rkv.t[:], op=ALU.mult), R=[ckv.b, rkv.b], W=[ckvn.b])
        qrhs = [cqn.t[:, 0, :], cqn.t[0:64, 1, :]]
        for i in range(2):
            ps = nextps()
            mm_group(ps.t[:, :], ps.b, [wuq.t[:, 0, i * 128:(i + 1) * 128], wuq.t[0:64, 1, i * 128:(i + 1) * 128]], qrhs, [wuq.b, cqn.b])
            evac(cx, kk[0], qn_st[c % 2].t[:, i, :], ps.t[:], R=[ps.b], W=[qn_st[c % 2].b], scale=96 ** -0.5)
        for h in range(4):
            fw.dma(fw.GD, out=sc["QMLA"].t[h, 0:64, tsl], in_=qn_st[c % 2].t[(h % 2) * 64:(h % 2) * 64 + 64, h // 2, :], R=[qn_st[c % 2].b], W=[sc["QMLA"].cb[c]])
        ps1 = nextps()
        mm_group(ps1.t[0:64, :], ps1.b, [wuq.t[:, 0, 256:320], wuq.t[0:64, 1, 256:320]], qrhs, [wuq.b, cqn.b])
        ps2 = nextps()
        mm_group(ps2.t[0:64, :], ps2.b, [wuq.t[:, 0, 320:384], wuq.t[0:64, 1, 320:384]], qrhs, [wuq.b, cqn.b])
        cs = csq[c % 2]
        qr = qr_st[c % 2]
        fw.op(fw.DVE, lambda: nc.vector.tensor_tensor(out=t1.t[:], in0=ps1.t[0:64, :], in1=cs.t[:, 0, :], op=ALU.mult), R=[ps1.b, cs.b], W=[t1.b])
        fw.op(fw.DVE, lambda: nc.vector.tensor_tensor(out=t2.t[:], in0=ps2.t[0:64, :], in1=cs.t[:, 1, :], op=ALU.mult), R=[ps2.b, cs.b], W=[t2.b])
        fw.op(fw.DVE, lambda: nc.vector.tensor_tensor(out=qr.t[:, 0, :], in0=t1.t[:], in1=t2.t[:], op=ALU.subtract), R=[t1.b, t2.b], W=[qr.b])
        fw.op(fw.DVE, lambda: nc.vector.tensor_tensor(out=t1.t[:], in0=ps2.t[0:64, :], in1=cs.t[:, 0, :], op=ALU.mult), R=[ps2.b, cs.b], W=[t1.b])
        fw.op(fw.DVE, lambda: nc.vector.tensor_tensor(out=t2.t[:], in0=ps1.t[0:64, :], in1=cs.t[:, 1, :], op=ALU.mult), R=[ps1.b, cs.b], W=[t2.b])
        fw.op(fw.DVE, lambda: nc.vector.tensor_tensor(out=qr.t[:, 1, :], in0=t1.t[:], in1=t2.t[:], op=ALU.add), R=[t1.b, t2.b], W=[qr.b])
        for h in range(4):
            fw.dma(fw.GD, out=sc["QMLA"].t[h, 64:80, tsl], in_=qr.t[h * 16:(h + 1) * 16, 0, :], R=[qr.b], W=[sc["QMLA"].cb[c]])
            fw.dma(fw.GD, out=sc["QMLA"].t[h, 80:96, tsl], in_=qr.t[h * 16:(h + 1) * 16, 1, :], R=[qr.b], W=[sc["QMLA"].cb[c]])
        for i in range(2):
            ps = nextps()
            mm_group(ps.t[:, :], ps.b, [wukv.t[:, i * 128:(i + 1) * 128]], [ckvn.t[:]], [wukv.b, ckvn.b])
            evac(cx, kk[0], kn_st[c % 2].t[:, i, :], ps.t[:], R=[ps.b], W=[kn_st[c % 2].b])
        for h in range(4):
            fw.dma(fw.GD, out=sc["KMLA"].t[h, 0:64, tsl], in_=kn_st[c % 2].t[(h % 2) * 64:(h % 2) * 64 + 64, h // 2, :], R=[kn_st[c % 2].b], W=[sc["KMLA"].cb[c]])
        for s in range(4):
            ps = nextps()
            mm_group(ps.t[:, 0:256], ps.b, [ckvn.t[:, s * 128:(s + 1) * 128]], [wukv.t[:, 256:512]], [wukv.b, ckvn.b])
            evac(cx, kk[0], vm_st[c % 2].t[:, s, :], ps.t[:, 0:256], R=[ps.b], W=[vm_st[c % 2].b])
        fw.dma(fw.GD, out=sc["VMLA"].t[tsl, :].rearrange("(s p) d -> p s d", p=128), in_=vm_st[c % 2].t[:], R=[vm_st[c % 2].b], W=[sc["VMLA"].cb[c]])
        ps1 = colmm(2112, 16)
        ps2 = colmm(2128, 16)
        ck = csk[c % 2]
        kr = kr_st[c % 2]
        fw.op(fw.DVE, lambda: nc.vector.tensor_tensor(out=t1.t[0:16, :], in0=ps1.t[0:16, :], in1=ck.t[:, 0, :], op=ALU.mult), R=[ps1.b, ck.b], W=[t1.b])
        fw.op(fw.DVE, lambda: nc.vector.tensor_tensor(out=t2.t[0:16, :], in0=ps2.t[0:16, :], in1=ck.t[:, 1, :], op=ALU.mult), R=[ps2.b, ck.b], W=[t2.b])
        fw.op(fw.DVE, lambda: nc.vector.tensor_tensor(out=kr.t[:, 0, :], in0=t1.t[0:16, :], in1=t2.t[0:16, :], op=ALU.subtract), R=[t1.b, t2.b], W=[kr.b])
        fw.op(fw.DVE, lambda: nc.vector.tensor_tensor(out=t1.t[0:16, :], in0=ps2.t[0:16, :], in1=ck.t[:, 0, :], op=ALU.mult), R=[ps2.b, ck.b], W=[t1.b])
        fw.op(fw.DVE, lambda: nc.vector.tensor_tensor(out=t2.t[0:16, :], in0=ps1.t[0:16, :], in1=ck.t[:, 1, :], op=ALU.mult), R=[ps1.b, ck.b], W=[t2.b])
        fw.op(fw.DVE, lambda: nc.vector.tensor_tensor(out=kr.t[:, 1, :], in0=t1.t[0:16, :], in1=t2.t[0:16, :], op=ALU.add), R=[t1.b, t2.b], W=[kr.b])
        for h in range(4):
            fw.dma(fw.GD, out=sc["KMLA"].t[h, 64:80, tsl], in_=kr.t[:, 0, :], R=[kr.b], W=[sc["KMLA"].cb[c]])
            fw.dma(fw.GD, out=sc["KMLA"].t[h, 80:96, tsl], in_=kr.t[:, 1, :], R=[kr.b], W=[sc["KMLA"].cb[c]])


def attn_load_qk(cx, at, i, QT_src, KT_src):
    fw = cx.fw
    QT, KT = at["QT"][i], at["KT"][i]
    for (ap, r0, nr) in QT_src:
        fw.dma(fw.SP, out=QT.t[r0:r0 + nr, :], in_=ap, W=[QT.b])
    for (ap, r0, nr) in KT_src:
        fw.dma(fw.SP, out=KT.t[r0:r0 + nr, :], in_=ap, W=[KT.b])


def attn_head(cx, at, qi, r0, V_src, dk, slope_idx, on_out, ki=None, kfull=None):
    fw, nc = cx.fw, cx.nc
    i = at["n"] % 2
    at["n"] += 1
    QT, KT, V1 = at["QT"][qi], at["KT"][qi if ki is None else ki], at["V1"][i]
    r1 = r0 + dk
    m0, m1 = (r0, r1) if kfull is None else (0, kfull)
    VALL = at["VALL"]
    fw.op(fw.POOL, lambda: nc.gpsimd.tensor_copy(out=V1.t[:, :, 0:64], in_=VALL.t[:, :, V_src * 64:(V_src + 1) * 64]), R=[VALL.b], W=[V1.b])
    fw.op(fw.POOL, lambda: nc.gpsimd.memset(V1.t[:, :, 64:65], 1.0), W=[V1.b])
    mx = at["mx"]
    for which, T_ in enumerate([QT, KT]):
        for c in range(NCH):
            sq = cx.sq[c % 2]
            ps = cx.ps[c % 2]
            fw.op(fw.ACT, lambda: nc.scalar.activation(out=sq.t[0:dk, :], in_=T_.t[r0:r1, c * CH:(c + 1) * CH], func=AF.Square), R=[T_.b], W=[sq.b])
            fw.op(fw.PE, lambda: nc.tensor.matmul(ps.t[:, :], lhsT=cx.ones_f.t[0:dk, :], rhs=sq.t[0:dk, :], start=True, stop=True), R=[sq.b, cx.ones_f.b], W=[ps.b])
            fw.op(fw.DVE, lambda: nc.vector.reduce_max(out=mx.t[:, which * 16 + c:which * 16 + c + 1], in_=ps.t[:, :], axis=AX.X), R=[ps.b], W=[mx.b])
    m2 = at["m2"]
    fw.op(fw.DVE, lambda: nc.vector.reduce_max(out=m2.t[:, 0:1], in_=mx.t[:, 0:16], axis=AX.X), R=[mx.b], W=[m2.b])
    fw.op(fw.DVE, lambda: nc.vector.reduce_max(out=m2.t[:, 1:2], in_=mx.t[:, 16:32], axis=AX.X), R=[mx.b], W=[m2.b])
    fw.op(fw.DVE, lambda: nc.vector.tensor_tensor(out=m2.t[:, 2:3], in0=m2.t[:, 0:1], in1=m2.t[:, 1:2], op=ALU.mult), R=[m2.b], W=[m2.b])
    fw.op(fw.ACT, lambda: nc.scalar.activation(out=m2.t[:, 3:4], in_=m2.t[:, 2:3], func=AF.Sqrt), R=[m2.b], W=[m2.b])
    negC = at["negC"][i]
    fw.op(fw.DVE, lambda: nc.vector.tensor_scalar(out=negC.t[:, 0:1], in0=m2.t[:, 3:4], scalar1=-1.0, scalar2=None, op0=ALU.mult), R=[m2.b], W=[negC.b])
    if slope_idx is not None:
        BT = at["BT"][i]
        fw.dma(fw.SP, out=BT.t[:], in_=cx.C["da_bt"][slope_idx], W=[BT.b])
        fw.op(fw.DVE, lambda: nc.vector.tensor_scalar(out=BT.t[:], in0=BT.t[:], scalar1=negC.t[:, 0:1], scalar2=None, op0=ALU.add), R=[BT.b, negC.b], W=[BT.b])
        DG, FLR = at["DG"][slope_idx % 2], at["FLR"][slope_idx % 2]
    for qc in range(NCH):
        qsl = slice(qc * CH, (qc + 1) * CH)
        aset = at["acc_i"] % 2
        at["acc_i"] += 1
        if slope_idx is None:
            accs = [None, cx.ps[at["acc_banks"][aset % len(at["acc_banks"])]], None]
        else:
            accs = [cx.ps[4 + r] for r in range(3)]
        def region(kc):
            if slope_idx is None:
                return 1
            dl = kc - 4 * qc
            return 0 if dl < 0 else (1 if dl < 4 else 2)
        kcs = list(range(64))
        if slope_idx is not None and at.get("window") is not None:
            thr = at["window"] / cx.slopes[slope_idx]
            keep = []
            for kc in kcs:
                dl = kc - 4 * qc
                if dl < 0:
                    dmin = qc * CH - (kc * 128 + 127)
                elif dl >= 4:
                    dmin = kc * 128 - (qc * CH + 511)
                else:
                    dmin = 0
                if dmin <= thr:
                    keep.append(kc)
            kcs = keep
        regs = [region(kc) for kc in kcs]
        first = {}
        last = {}
        for idx, rg in enumerate(regs):
            first.setdefault(rg, idx)
            last[rg] = idx
        nst = at["nst"]

        def emit_qk(idx):
            kc = kcs[idx]
            st = cx.ps[at["st_banks"][at["st_i"] % nst]]
            at["st_i"] += 1
            fw.op(fw.PE, lambda: nc.tensor.matmul(st.t[:, :], lhsT=KT.t[m0:m1, kc * 128:(kc + 1) * 128], rhs=QT.t[m0:m1, qsl], start=True, stop=True),
                  R=[KT.b, QT.b], W=[st.b])
            return st

        def emit_exp(idx, st):
            kc = kcs[idx]
            rg = regs[idx]
            pt = at["PT"][at["pt_i"] % len(at["PT"])]
            at["pt_i"] += 1
            if slope_idx is None:
                fw.op(fw.ACT, lambda: nc.scalar.activation(out=pt.t[:], in_=st.t[:], func=AF.Exp, bias=negC.t[:, 0:1], scale=1.0), R=[st.b, negC.b], W=[pt.b])
            elif rg == 1:
                dl = kc - 4 * qc
                tmp = at["tmp"][at["tmp_i"] % 2]
                at["tmp_i"] += 1
                fw.op(fw.DVE, lambda: nc.vector.tensor_tensor(out=tmp.t[:], in0=st.t[:], in1=DG.t[:, dl, :], op=ALU.add), R=[st.b, DG.b], W=[tmp.b])
                fw.op(fw.ACT, lambda: nc.scalar.activation(out=pt.t[:], in_=tmp.t[:], func=AF.Exp, bias=negC.t[:, 0:1], scale=1.0), R=[tmp.b, negC.b], W=[pt.b])
            else:
                dl = kc - 4 * qc
                fw.op(fw.ACT, lambda: nc.scalar.activation(out=pt.t[:], in_=st.t[:], func=AF.Exp, bias=BT.t[:, dl + 63:dl + 64], scale=1.0), R=[st.b, BT.b], W=[pt.b])
            return pt

        def emit_pv(idx, pt):
            kc = kcs[idx]
            rg = regs[idx]
            acc = accs[rg]
            fw.op(fw.PE, lambda: nc.tensor.matmul(acc.t[0:65, :], lhsT=V1.t[:, kc, :], rhs=pt.t[:], start=(first[rg] == idx), stop=(last[rg] == idx)),
                  R=[V1.b, pt.b], W=[acc.b], inc=True)
        look = nst - 1
        sts = {}
        for idx in range(min(look, len(kcs))):
            sts[idx] = emit_qk(idx)
        for idx in range(len(kcs)):
            if idx + look < len(kcs):
                sts[idx + look] = emit_qk(idx + look)
            pt = emit_exp(idx, sts.pop(idx))
            emit_pv(idx, pt)
            if idx % 4 == 3:
                yield
        comb = at["comb"][at["comb_i"] % 2]
        at["comb_i"] += 1
        if slope_idx is None:
            fw.op(fw.DVE, lambda: nc.vector.tensor_copy(out=comb.t[:], in_=accs[1].t[0:65, :]), R=[accs[1].b], W=[comb.b])
        else:
            t65, t65b = at["t65"], at["t65b"]
            if 0 in first:
                fw.op(fw.DVE, lambda: nc.vector.tensor_tensor(out=t65.t[:], in0=accs[0].t[0:65, :], in1=FLR.t[:, 0, :], op=ALU.mult), R=[accs[0].b, FLR.b], W=[t65.b])
            fw.op(fw.ACT, lambda: nc.scalar.copy(out=comb.t[:], in_=accs[1].t[0:65, :]), R=[accs[1].b], W=[comb.b])
            if 2 in first:
                fw.op(fw.DVE, lambda: nc.vector.tensor_tensor(out=t65b.t[:], in0=accs[2].t[0:65, :], in1=FLR.t[:, 1, :], op=ALU.mult), R=[accs[2].b, FLR.b], W=[t65b.b])
            if 0 in first:
                fw.op(fw.DVE, lambda: nc.vector.tensor_tensor(out=comb.t[:], in0=comb.t[:], in1=t65.t[:], op=ALU.add), R=[comb.b, t65.b], W=[comb.b])
            if 2 in first:
                fw.op(fw.DVE, lambda: nc.vector.tensor_tensor(out=comb.t[:], in0=comb.t[:], in1=t65b.t[:], op=ALU.add), R=[comb.b, t65b.b], W=[comb.b])
        if slope_idx is None:
            if at["bs_banks"] == at["st_banks"]:
                bs = cx.ps[at["st_banks"][at["st_i"] % at["nst"]]]
                at["st_i"] += 1
            else:
                bs = cx.ps[at["bs_banks"][at["bs_i"] % len(at["bs_banks"])]]
                at["bs_i"] += 1
        else:
            bs = cx.ps[7]
        fw.op(fw.PE, lambda: nc.tensor.matmul(bs.t[0:64, :], lhsT=at["sel"].t[0:65, 0:64], rhs=comb.t[0:65, :], start=True, stop=True), R=[at["sel"].b, comb.b], W=[bs.b])
        rs = at["rs"]
        fw.op(fw.DVE, lambda: nc.vector.reciprocal(out=rs.t[:], in_=bs.t[0:64, :]), R=[bs.b], W=[rs.b])
        on = at["on"][at["on_i"] % 2]
        at["on_i"] += 1
        fw.op(fw.DVE, lambda: nc.vector.tensor_tensor(out=on.t[:], in0=comb.t[0:64, :], in1=rs.t[:], op=ALU.mult), R=[comb.b, rs.b], W=[on.b])
        on_out(qc, on)


def attn_setup(cx, alibi, V_all, shared=False):
    fw, nc = cx.fw, cx.nc
    at = {"n": 0, "acc_i": 0, "st_i": 0, "pt_i": 0, "tmp_i": 0, "comb_i": 0, "on_i": 0, "bs_i": 0}
    at["st_banks"] = [0, 1, 2, 3]
    at["acc_banks"] = [4, 5]
    at["bs_banks"] = [6, 7]
    at["nst"] = 4
    if shared:
        at["st_banks"] = [0, 1]
        at["acc_banks"] = [2]
        at["bs_banks"] = [0, 1]
        at["nst"] = 2
    if alibi:
        at["QT"] = [SB(cx, [128, L], BF16, "QT") for _ in range(2)]
        at["KT"] = [SB(cx, [128, L], BF16, "KT") for _ in range(1)]
        for t_ in at["QT"] + at["KT"]:
            fw.op(fw.POOL, lambda: nc.gpsimd.memset(t_.t[:], 0.0), W=[t_.b])
    else:
        nq_ = 1 if shared else 2
        at["QT"] = [SB(cx, [96, L], BF16, "QT") for _ in range(nq_)] * (2 // nq_)
        at["KT"] = [SB(cx, [96, L], BF16, "KT") for _ in range(nq_)] * (2 // nq_)
    at["V1"] = [SB(cx, [128, 64, 65], BF16, "V1") for _ in range(2)]
    at["PT"] = [SB(cx, [128, CH], BF16, "PT") for _ in range(4 if shared else 6)]
    at["mx"] = SB(cx, [128, 32], F32, "mx")
    at["m2"] = SB(cx, [128, 4], F32, "m2")
    at["negC"] = [SB(cx, [128, 1], F32, "negC") for _ in range(2)]
    at["comb"] = [SB(cx, [65, CH], F32, "comb") for _ in range(2)]
    at["rs"] = SB(cx, [64, CH], F32, "rs")
    at["on"] = [SB(cx, [64, CH], F32, "on") for _ in range(2)]
    at["sel"] = SB(cx, [65, 64], F32, "sel")
    at["VALL"] = SB(cx, [128, 64, 256], BF16, "VALL")
    vsrc = V_all.rearrange("(kc p) d -> p kc d", p=128)
    for i in range(8):
        fw.dma(fw.SP, out=at["VALL"].t[:, i * 8:(i + 1) * 8, :], in_=vsrc[:, i * 8:(i + 1) * 8, :], W=[at["VALL"].b])
    fw.dma(fw.SP, out=at["sel"].t[:], in_=cx.C["sel"], W=[at["sel"].b])
    if alibi:
        at["BT"] = [SB(cx, [128, 127], F32, "BT") for _ in range(2)]
        at["tmp"] = [SB(cx, [128, CH], F32, "tmp") for _ in range(2)]
        at["t65"] = SB(cx, [65, CH], F32, "t65")
        at["t65b"] = SB(cx, [65, CH], F32, "t65b")
        at["DG"] = [SB(cx, [128, 4, CH], F32, "DG") for _ in range(2)]
        at["FLR"] = [SB(cx, [65, 2, CH], F32, "FLR") for _ in range(2)]
    return at


def mla_gen(cx, at):
    fw, nc = cx.fw, cx.nc
    sc = cx.sc
    for h in range(4):
        def on_out(qc, on, h=h):
            fw.dma(fw.GD, out=sc["YMLA"].t[h * 64:(h + 1) * 64, qc * CH:(qc + 1) * CH], in_=on.t[:], R=[on.b], W=[sc["YMLA"].cb[qc]])
        attn_load_qk(cx, at, h % 2, [(sc["QMLA"].t[h], 0, 96)], [(sc["KMLA"].t[h], 0, 96)])
        yield from attn_head(cx, at, h % 2, 0, h, 96, None, on_out)


def phase_mla(cx):
    new_phase(cx)
    at = attn_setup(cx, False, cx.sc["VMLA"].t)
    for _ in mla_gen(cx, at):
        pass


def phase_mla_s5(cx, W, li):
    new_phase(cx)
    at = attn_setup(cx, False, cx.sc["VMLA"].t, shared=True)
    g1 = mla_gen(cx, at)
    g2 = s5_main(cx, W, li, shared=True)
    a1 = a2 = True
    while a1 or a2:
        if a1:
            for _ in range(4):
                try:
                    next(g1)
                except StopIteration:
                    a1 = False
                    break
        if a2:
            try:
                next(g2)
            except StopIteration:
                a2 = False
    s5_glu(cx, W, li)


def phase_da(cx, W, li):
    fw, nc = cx.fw, cx.nc
    new_phase(cx)
    sc = cx.sc
    at = attn_setup(cx, True, sc["VDA"].t)
    at["window"] = cx.window
    lam_init = 0.8 - 0.6 * math.exp(-0.3 * li)
    lv = SB(cx, [64, 128], F32, "lv")
    fw.dma(fw.SP, out=lv.t[:], in_=W["da_lambda"].rearrange("a b -> (a b)").partition_broadcast(64), W=[lv.b])
    lp = SB(cx, [64, 64], F32, "lp")
    fw.op(fw.DVE, lambda: nc.vector.tensor_tensor(out=lp.t[:, 0:32], in0=lv.t[:, 0:32], in1=lv.t[:, 32:64], op=ALU.mult), R=[lv.b], W=[lp.b])
    fw.op(fw.DVE, lambda: nc.vector.tensor_tensor(out=lp.t[:, 32:64], in0=lv.t[:, 64:96], in1=lv.t[:, 96:128], op=ALU.mult), R=[lv.b], W=[lp.b])
    ls = SB(cx, [64, 4], F32, "ls")
    fw.op(fw.DVE, lambda: nc.vector.reduce_sum(out=ls.t[:, 0:1], in_=lp.t[:, 0:32], axis=AX.X), R=[lp.b], W=[ls.b])
    fw.op(fw.DVE, lambda: nc.vector.reduce_sum(out=ls.t[:, 1:2], in_=lp.t[:, 32:64], axis=AX.X), R=[lp.b], W=[ls.b])
    fw.op(fw.ACT, lambda: nc.scalar.activation(out=ls.t[:, 0:2], in_=ls.t[:, 0:2], func=AF.Exp), R=[ls.b], W=[ls.b])
    fw.op(fw.DVE, lambda: nc.vector.tensor_tensor(out=ls.t[:, 2:3], in0=ls.t[:, 1:2], in1=ls.t[:, 0:1], op=ALU.subtract), R=[ls.b], W=[ls.b])
    fw.op(fw.DVE, lambda: nc.vector.tensor_scalar(out=ls.t[:, 3:4], in0=ls.t[:, 2:3], scalar1=-lam_init, scalar2=None, op0=ALU.add), R=[ls.b], W=[ls.b])
    gh = SB(cx, [64, 1], F32, "gh")
    fw.dma(fw.SP, out=gh.t[:], in_=W["da_g_head"], W=[gh.b])
    fw.op(fw.DVE, lambda: nc.vector.tensor_scalar(out=gh.t[:], in0=gh.t[:], scalar1=1.0 - lam_init, scalar2=None, op0=ALU.mult), R=[gh.b], W=[gh.b])
    O1 = SB(cx, [64, L], F32, "O1")
    dd = [SB(cx, [64, CH], F32, "dd") for _ in range(2)]
    for h in range(4):
        def out1(qc, on):
            fw.op(fw.ACT, lambda: nc.scalar.copy(out=O1.t[:, qc * CH:(qc + 1) * CH], in_=on.t[:]), R=[on.b], W=[O1.b])

        def out2(qc, on, h=h):
            d_ = dd[qc % 2]
            fw.op(fw.DVE, lambda: nc.vector.scalar_tensor_tensor(out=d_.t[:], in0=on.t[:], scalar=ls.t[:, 3:4], in1=O1.t[:, qc * CH:(qc + 1) * CH],
                                                                 op0=ALU.mult, op1=ALU.add), R=[on.b, ls.b, O1.b], W=[d_.b])
            r = rstd_chunk(cx, [(d_.t[:], 64, [d_.b])], 64)
            fw.op(fw.DVE, lambda: nc.vector.scalar_tensor_tensor(out=d_.t[:], in0=d_.t[:], scalar=gh.t[:, 0:1], in1=r.t[0:64, :], op0=ALU.mult, op1=ALU.mult),
                  R=[d_.b, gh.b, r.b], W=[d_.b])
            fw.dma(fw.GD, out=sc["YDA"].t[h * 64:(h + 1) * 64, qc * CH:(qc + 1) * CH], in_=d_.t[:], R=[d_.b], W=[sc["YDA"].cb[qc]])
        fw.dma(fw.SP, out=at["QT"][0].t[0:32, :], in_=sc["QDA"].t[h * 64:h * 64 + 32, :], W=[at["QT"][0].b])
        fw.dma(fw.SP, out=at["QT"][1].t[32:64, :], in_=sc["QDA"].t[h * 64 + 32:h * 64 + 64, :], W=[at["QT"][1].b])
        fw.dma(fw.SP, out=at["KT"][0].t[0:64, :], in_=sc["KDA"].t[h * 64:(h + 1) * 64, :], W=[at["KT"][0].b])
        dg, fl = at["DG"][h % 2], at["FLR"][h % 2]
        fw.dma(fw.SP, out=dg.t[:], in_=cx.C["da_dg"][h].rearrange("a p f -> p a f"), W=[dg.b])
        fw.dma(fw.SP, out=fl.t[:], in_=cx.C["da_flr"][h].rearrange("a p f -> p a f"), W=[fl.b])
        for j, cb in enumerate([out1, out2]):
            for _ in attn_head(cx, at, j, j * 32, h, 32, h, cb, ki=0, kfull=128):
                pass


def phase_s5(cx, W, li):
    new_phase(cx)
    for _ in s5_main(cx, W, li, shared=False):
        pass
    s5_glu(cx, W, li)


def s5_main(cx, W, li, shared=False):
    fw, nc = cx.fw, cx.nc
    sc = cx.sc
    V = fw.DVE
    T = CH
    nb_ = 1 if shared else 2

    def tt(out, a, b, op, R, Wb):
        fw.op(V, lambda: nc.vector.tensor_tensor(out=out, in0=a, in1=b, op=op), R=R, W=Wb)

    def ts(out, a, s1, op0, R, Wb, s2=None, op1=None):
        if op1 is None:
            fw.op(V, lambda: nc.vector.tensor_scalar(out=out, in0=a, scalar1=s1, scalar2=None, op0=op0), R=R, W=Wb)
        else:
            fw.op(V, lambda: nc.vector.tensor_scalar(out=out, in0=a, scalar1=s1, scalar2=s2, op0=op0, op1=op1), R=R, W=Wb)

    def stt(out, a, sc_, b, op0, op1, R, Wb):
        fw.op(V, lambda: nc.vector.scalar_tensor_tensor(out=out, in0=a, scalar=sc_, in1=b, op0=op0, op1=op1), R=R, W=Wb)

    NP = 40
    pr = SB(cx, [128, NP, 16], F32, "s5p")
    PB = pr.b
    col = lambda i: pr.t[:, i, :]
    AR, AI, LDT, DT, ARDT, TH, SS, CC, M32, ZR, ZI, T1, T2, ABR, ABI, NR, DEN, FR, FI, ATR, ATI, N2 = range(22)
    fw.dma(fw.SP, out=col(AR), in_=W["s5_ar"], W=[PB])
    fw.dma(fw.SP, out=col(AI), in_=W["s5_ai"], W=[PB])
    fw.dma(fw.SP, out=col(LDT), in_=W["s5_ldt"], W=[PB])
    halfpi = SB(cx, [128, 1], F32, "halfpi")
    fw.op(V, lambda: nc.vector.memset(halfpi.t[:], math.pi / 2), W=[halfpi.b])
    fw.op(fw.ACT, lambda: nc.scalar.activation(out=col(DT), in_=col(LDT), func=AF.Exp), R=[PB], W=[PB])
    tt(col(ARDT), col(AR), col(DT), ALU.mult, [PB], [PB])
    tt(col(TH), col(AI), col(DT), ALU.mult, [PB], [PB])
    fw.op(fw.ACT, lambda: nc.scalar.activation(out=col(SS), in_=col(TH), func=AF.Sin, scale=1.0 / 32), R=[PB], W=[PB])
    fw.op(fw.ACT, lambda: nc.scalar.activation(out=col(CC), in_=col(TH), func=AF.Sin, scale=1.0 / 32, bias=halfpi.t[:, 0:1]), R=[PB, halfpi.b], W=[PB])
    fw.op(fw.ACT, lambda: nc.scalar.activation(out=col(M32), in_=col(ARDT), func=AF.Exp, scale=1.0 / 32), R=[PB], W=[PB])
    tt(col(ZR), col(M32), col(CC), ALU.mult, [PB], [PB])
    tt(col(ZI), col(M32), col(SS), ALU.mult, [PB], [PB])
    pw = SB(cx, [128, 10, 2, 16], F32, "s5pw")

    def square():
        tt(col(T1), col(ZR), col(ZR), ALU.mult, [PB], [PB])
        tt(col(T2), col(ZI), col(ZI), ALU.mult, [PB], [PB])
        tt(col(T2), col(T1), col(T2), ALU.subtract, [PB], [PB])
        tt(col(T1), col(ZR), col(ZI), ALU.mult, [PB], [PB])
        ts(col(ZI), col(T1), 2.0, ALU.mult, [PB], [PB])
        fw.op(V, lambda: nc.vector.tensor_copy(out=col(ZR), in_=col(T2)), R=[PB], W=[PB])
    for _ in range(5):
        square()
    for j in range(10):
        fw.op(V, lambda: nc.vector.tensor_copy(out=pw.t[:, j, 0, :], in_=col(ZR)), R=[PB], W=[pw.b])
        fw.op(V, lambda: nc.vector.tensor_copy(out=pw.t[:, j, 1, :], in_=col(ZI)), R=[PB], W=[pw.b])
        if j < 9:
            square()
    ABRa, ABIa = pw.t[:, 0, 0, :], pw.t[:, 0, 1, :]
    ts(col(NR), ABRa, -1.0, ALU.add, [pw.b], [PB])
    tt(col(T1), col(AR), col(AR), ALU.mult, [PB], [PB])
    tt(col(T2), col(AI), col(AI), ALU.mult, [PB], [PB])
    tt(col(DEN), col(T1), col(T2), ALU.add, [PB], [PB])
    fw.op(V, lambda: nc.vector.reciprocal(out=col(DEN), in_=col(DEN)), R=[PB], W=[PB])
    tt(col(T1), col(NR), col(AR), ALU.mult, [PB], [PB])
    tt(col(T2), ABIa, col(AI), ALU.mult, [PB, pw.b], [PB])
    tt(col(T1), col(T1), col(T2), ALU.add, [PB], [PB])
    tt(col(FR), col(T1), col(DEN), ALU.mult, [PB], [PB])
    tt(col(T1), ABIa, col(AR), ALU.mult, [PB, pw.b], [PB])
    tt(col(T2), col(NR), col(AI), ALU.mult, [PB], [PB])
    tt(col(T1), col(T1), col(T2), ALU.subtract, [PB], [PB])
    tt(col(FI), col(T1), col(DEN), ALU.mult, [PB], [PB])
    ts(col(N2), col(ARDT), -2.0, ALU.mult, [PB], [PB])
    bsrc = SB(cx, [128, 16, 2, 16], F32, "s5b")
    csrc = SB(cx, [128, 16, 2, 16], F32, "s5c")
    fw.dma(fw.SP, out=bsrc.t[:], in_=W["s5_b"], W=[bsrc.b])
    fw.dma(fw.SP, out=csrc.t[:], in_=W["s5_c"], W=[csrc.b])
    dsk = SB(cx, [32, 8], F32, "s5d")
    fw.dma(fw.SP, out=dsk.t[:], in_=W["s5_d"], W=[dsk.b])
    tau = SB(cx, [128, T], F32, "tau")
    fw.dma(fw.SP, out=tau.t[:], in_=cx.C["tau"], W=[tau.b])
    ones = SB(cx, [128, T], F32, "ones512")
    fw.op(V, lambda: nc.vector.memset(ones.t[:], 1.0), W=[ones.b])
    Ep = [SB(cx, [128, 2, T], F32, "Ep") for _ in range(nb_)] * (2 // nb_)
    Em = [SB(cx, [128, 2, T], F32, "Em") for _ in range(nb_)] * (2 // nb_)
    mg = SB(cx, [128, T], F32, "mg")
    Ub = [SB(cx, [32, L], BF16, "Ub") for _ in range(nb_)] * (2 // nb_)
    Y = SB(cx, [32, L], F32, "Y")
    bpad = SB(cx, [128, 2, 32], F32, "bpad")
    cpad = SB(cx, [128, 2, 32], F32, "cpad")
    bb = SB(cx, [128, 2, 16], F32, "bb")
    BT_ = [SB(cx, [32, 2, 128], BF16, "BTl") for _ in range(2)]
    CT_ = [SB(cx, [128, 2, 32], BF16, "CTl") for _ in range(2)]
    Pt = [SB(cx, [128, 2, T], F32, "Pt") for _ in range(2)]
    St = [SB(cx, [128, 2, T], F32, "St") for _ in range(2)]
    tm = [SB(cx, [128, 2, T], F32, "tm") for _ in range(2)]
    xb = [SB(cx, [128, 2, T], BF16, "xb") for _ in range(2)]
    ax = SB(cx, [128, 4], F32, "ax")
    yev = [SB(cx, [32, T], F32, "yev") for _ in range(2)]
    it = 0
    for k in range(8):
        ub = Ub[k % 2]
        fw.dma(fw.SP, out=Y.t[:], in_=sc["ZS5"].t[32 * k:32 * k + 32, :], W=[Y.b])
        fw.op(fw.ACT, lambda: nc.scalar.copy(out=ub.t[:], in_=Y.t[:]), R=[Y.b], W=[ub.b])
        fw.op(fw.ACT, lambda: nc.scalar.mul(out=Y.t[:], in_=Y.t[:], mul=dsk.t[:, k:k + 1]), R=[Y.b, dsk.b], W=[Y.b])
        for dr in range(2):
            cl = dr * 8 + k
            c1 = slice(cl, cl + 1)
            ep, em = Ep[(k * 2 + dr) % 2], Em[(k * 2 + dr) % 2]
            btl, ctl = BT_[(k * 2 + dr) % 2], CT_[(k * 2 + dr) % 2]
            fw.op(V, lambda: nc.vector.memset(ep.t[:, 0, 0:1], 1.0), W=[ep.b])
            fw.op(V, lambda: nc.vector.memset(ep.t[:, 1, 0:1], 0.0), W=[ep.b])
            for j in range(9):
                n_ = 1 << j
                p_r, p_i = pw.t[:, j, 0, c1], pw.t[:, j, 1, c1]
                ts(mg.t[:, 0:n_], ep.t[:, 1, 0:n_], p_i, ALU.mult, [ep.b, pw.b], [mg.b])
                stt(ep.t[:, 0, n_:2 * n_], ep.t[:, 0, 0:n_], p_r, mg.t[:, 0:n_], ALU.mult, ALU.subtract, [ep.b, pw.b, mg.b], [ep.b])
                ts(mg.t[:, 0:n_], ep.t[:, 1, 0:n_], p_r, ALU.mult, [ep.b, pw.b], [mg.b])
                stt(ep.t[:, 1, n_:2 * n_], ep.t[:, 0, 0:n_], p_i, mg.t[:, 0:n_], ALU.mult, ALU.add, [ep.b, pw.b, mg.b], [ep.b])
            fw.op(fw.ACT, lambda: nc.scalar.activation(out=mg.t[:], in_=tau.t[:], func=AF.Exp, scale=pr.t[:, N2, c1]), R=[tau.b, PB, mg.b], W=[mg.b])
            tt(em.t[:, 0, :], ep.t[:, 0, :], mg.t[:], ALU.mult, [ep.b, mg.b], [em.b])
            stt(em.t[:, 1, :], ep.t[:, 1, :], -1.0, mg.t[:], ALU.mult, ALU.mult, [ep.b, mg.b], [em.b])
            f_r, f_i = pr.t[:, FR, c1], pr.t[:, FI, c1]
            b_r, b_i = bsrc.t[:, cl, 0, :], bsrc.t[:, cl, 1, :]
            ts(bb.t[:, 0, :], b_i, f_i, ALU.mult, [bsrc.b, PB], [bb.b])
            stt(bb.t[:, 0, :], b_r, f_r, bb.t[:, 0, :], ALU.mult, ALU.subtract, [bsrc.b, PB, bb.b], [bb.b])
            ts(bb.t[:, 1, :], b_r, f_i, ALU.mult, [bsrc.b, PB], [bb.b])
            stt(bb.t[:, 1, :], b_i, f_r, bb.t[:, 1, :], ALU.mult, ALU.add, [bsrc.b, PB, bb.b], [bb.b])
            fw.op(V, lambda: nc.vector.memset(bpad.t[:], 0.0), W=[bpad.b])
            fw.op(V, lambda: nc.vector.memset(cpad.t[:], 0.0), W=[cpad.b])
            for hf in range(2):
                ps_ = slice(hf * 64, hf * 64 + 64)
                cs_ = slice(hf * 16, hf * 16 + 16)
                fw.op(V, lambda: nc.vector.tensor_copy(out=bpad.t[ps_, :, cs_], in_=bb.t[ps_, :, :]), R=[bb.b], W=[bpad.b])
                fw.op(V, lambda: nc.vector.tensor_copy(out=cpad.t[ps_, 0, cs_], in_=csrc.t[ps_, cl, 0, :]), R=[csrc.b], W=[cpad.b])
                ts(cpad.t[ps_, 1, cs_], csrc.t[ps_, cl, 1, :], -1.0, ALU.mult, [csrc.b], [cpad.b])
            fw.op(V, lambda: nc.vector.tensor_copy(out=ctl.t[:], in_=cpad.t[:]), R=[cpad.b], W=[ctl.b])
            psb = cx.ps[7] if shared else cx.ps[6]
            for ri in range(2):
                fw.op(fw.PE, lambda: nc.tensor.transpose(out=psb.t[0:32, ri * 128:(ri + 1) * 128], in_=bpad.t[:, ri, :], identity=cx.ident_f.t[:]),
                      R=[bpad.b, cx.ident_f.b], W=[psb.b], inc=(ri == 1))
            fw.op(V, lambda: nc.vector.tensor_copy(out=btl.t[:].rearrange("p a b -> p (a b)"), in_=psb.t[0:32, 0:256]), R=[psb.b], W=[btl.b])
            fw.op(V, lambda: nc.vector.memset(ax.t[:], 0.0), W=[ax.b])
            at_r, at_i = pw.t[:, 9, 0, c1], pw.t[:, 9, 1, c1]
            rv = (lambda a_: a_) if dr == 0 else (lambda a_: a_[:, ::-1])
            emr, emi = rv(em.t[:, 0, :]), rv(em.t[:, 1, :])
            epr, epi = rv(ep.t[:, 0, :]), rv(ep.t[:, 1, :])
            tiles = {}

            def tsl_of(ci):
                c = ci if dr == 0 else NCH - 1 - ci
                return slice(c * T, (c + 1) * T)

            def pe_in(ci):
                nonlocal it
                tsl = tsl_of(ci)
                if shared:
                    bur, bui = cx.ps[3 + (it % 2) * 2], cx.ps[4 + (it % 2) * 2]
                    yps = cx.ps[7]
                else:
                    bur, bui = cx.ps[(it % 2) * 2], cx.ps[(it % 2) * 2 + 1]
                    yps = cx.ps[4 + it % 2]
                tiles[ci] = (bur, bui, yps, Pt[it % 2], St[it % 2], tm[it % 2], xb[it % 2], yev[it % 2])
                it += 1
                fw.op(fw.PE, lambda: nc.tensor.matmul(bur.t[:, :], lhsT=btl.t[:, 0, :], rhs=ub.t[:, tsl], start=True, stop=True), R=[btl.b, ub.b], W=[bur.b])
                fw.op(fw.PE, lambda: nc.tensor.matmul(bui.t[:, :], lhsT=btl.t[:, 1, :], rhs=ub.t[:, tsl], start=True, stop=True), R=[btl.b, ub.b], W=[bui.b])

            def dve(ci):
                bur, bui, yps, P_, S_, t_, x_, ye = tiles[ci]
                tt(t_.t[:, 0, :], bui.t[:, :], emi, ALU.mult, [bui.b, em.b], [t_.b])
                tt(t_.t[:, 1, :], bur.t[:, :], emi, ALU.mult, [bur.b, em.b], [t_.b])
                tt(P_.t[:, 0, :], bur.t[:, :], emr, ALU.mult, [bur.b, em.b], [P_.b])
                tt(P_.t[:, 1, :], bui.t[:, :], emr, ALU.mult, [bui.b, em.b], [P_.b])
                tt(P_.t[:, 0, :], P_.t[:, 0, :], t_.t[:, 0, :], ALU.subtract, [P_.b, t_.b], [P_.b])
                tt(P_.t[:, 1, :], P_.t[:, 1, :], t_.t[:, 1, :], ALU.add, [P_.b, t_.b], [P_.b])
                for ri in range(2):
                    fw.op(V, lambda: nc.vector.tensor_tensor_scan(out=rv(S_.t[:, ri, :]), data0=ones.t[:, :], data1=rv(P_.t[:, ri, :]), initial=ax.t[:, ri:ri + 1],
                                                                  op0=ALU.mult, op1=ALU.add), R=[ones.b, P_.b, ax.b], W=[S_.b])
                e_ = T - 1 if dr == 0 else 0
                sre, sie = S_.t[:, 0, e_:e_ + 1], S_.t[:, 1, e_:e_ + 1]
                ts(ax.t[:, 2:3], sie, at_i, ALU.mult, [S_.b, pw.b], [ax.b])
                ts(ax.t[:, 3:4], sie, at_r, ALU.mult, [S_.b, pw.b], [ax.b])
                stt(ax.t[:, 0:1], sre, at_r, ax.t[:, 2:3], ALU.mult, ALU.subtract, [S_.b, pw.b, ax.b], [ax.b])
                stt(ax.t[:, 1:2], sre, at_i, ax.t[:, 3:4], ALU.mult, ALU.add, [S_.b, pw.b, ax.b], [ax.b])
                tt(t_.t[:, 0, :], S_.t[:, 1, :], epi, ALU.mult, [S_.b, ep.b], [t_.b])
                tt(t_.t[:, 1, :], S_.t[:, 0, :], epi, ALU.mult, [S_.b, ep.b], [t_.b])
                tt(P_.t[:, 0, :], S_.t[:, 0, :], epr, ALU.mult, [S_.b, ep.b], [P_.b])
                tt(P_.t[:, 1, :], S_.t[:, 1, :], epr, ALU.mult, [S_.b, ep.b], [P_.b])
                tt(x_.t[:, 0, :], P_.t[:, 0, :], t_.t[:, 0, :], ALU.subtract, [P_.b, t_.b], [x_.b])
                tt(x_.t[:, 1, :], P_.t[:, 1, :], t_.t[:, 1, :], ALU.add, [P_.b, t_.b], [x_.b])

            def pe_out(ci):
                bur, bui, yps, P_, S_, t_, x_, ye = tiles.pop(ci)
                tsl = tsl_of(ci)
                fw.op(fw.PE, lambda: nc.tensor.matmul(yps.t[0:32, :], lhsT=ctl.t[:, 0, :], rhs=x_.t[:, 0, :], start=True, stop=False), R=[ctl.b, x_.b], W=[yps.b], inc=False)
                fw.op(fw.PE, lambda: nc.tensor.matmul(yps.t[0:32, :], lhsT=ctl.t[:, 1, :], rhs=x_.t[:, 1, :], start=False, stop=True), R=[ctl.b, x_.b], W=[yps.b])
                fw.op(fw.ACT, lambda: nc.scalar.copy(out=ye.t[:], in_=yps.t[0:32, :]), R=[yps.b], W=[ye.b])
                fw.op(fw.POOL, lambda: nc.gpsimd.tensor_tensor(out=Y.t[:, tsl], in0=Y.t[:, tsl], in1=ye.t[:], op=ALU.add), R=[Y.b, ye.b], W=[Y.b])

            pe_in(0)
            for ci in range(NCH):
                if ci + 1 < NCH:
                    pe_in(ci + 1)
                if ci >= 1:
                    pe_out(ci - 1)
                yield
                dve(ci)
            pe_out(NCH - 1)
        fw.op(fw.ACT, lambda: nc.scalar.activation(out=Y.t[:], in_=Y.t[:], func=AF.Gelu), R=[Y.b], W=[Y.b])
        fw.dma(fw.GD, out=sc["GS5"].t[32 * k:32 * k + 32, :], in_=Y.t[:], R=[Y.b], W=sc["GS5"].cb)


def s5_glu(cx, W, li):
    fw, nc = cx.fw, cx.nc
    sc = cx.sc
    V = fw.DVE
    new_phase(cx)
    cx.wstg = [SB(cx, [128, 1024], F32, "wstg") for _ in range(2)]
    wgl = SB(cx, [128, 2, 256], BF16, "wgl")
    load_w_bf16(cx, wgl, W["s5_w_glu"], 2, 256)
    bgl = SB(cx, [128, 2], F32, "bgl")
    fw.dma(fw.SP, out=bgl.t[:], in_=W["s5_b_glu"], W=[bgl.b])
    gts = [SB(cx, [128, 2, CH], F32, "gt") for _ in range(2)]
    gbs = [SB(cx, [128, 2, CH], BF16, "gb") for _ in range(2)]
    sgs = [SB(cx, [128, CH], F32, "sg") for _ in range(2)]
    kq = 0
    for c in range(NCH):
        tsl = slice(c * CH, (c + 1) * CH)
        g, gbf = gts[c % 2], gbs[c % 2]
        fw.dma(fw.SP, out=g.t[:], in_=sc["GS5"].t[:, tsl].rearrange("(j p) t -> p j t", p=128), R=[sc["GS5"].cb[c]], W=[g.b])
        fw.op(fw.ACT, lambda: nc.scalar.copy(out=gbf.t[:], in_=g.t[:]), R=[g.b], W=[gbf.b])
        for m in range(2):
            ps = cx.ps[2 + kq % 6]
            kq += 1
            for j in range(2):
                fw.op(fw.PE, lambda: nc.tensor.matmul(ps.t[:, :], lhsT=wgl.t[:, j, m * 128:(m + 1) * 128], rhs=gbf.t[:, j, :], start=(j == 0), stop=(j == 1)),
                      R=[wgl.b, gbf.b], W=[ps.b], inc=(j == 1))
            sg = sgs[m]
            fw.op(fw.ACT, lambda: nc.scalar.activation(out=sg.t[:], in_=ps.t[:], func=AF.Sigmoid, bias=bgl.t[:, m:m + 1], scale=1.0), R=[ps.b, bgl.b], W=[sg.b])
            fw.op(V, lambda: nc.vector.tensor_tensor(out=g.t[:, m, :], in0=g.t[:, m, :], in1=sg.t[:], op=ALU.mult), R=[g.b, sg.b], W=[g.b])
        fw.dma(fw.GD, out=sc["YS5"].t[:, tsl].rearrange("(j p) t -> p j t", p=128), in_=g.t[:], R=[g.b], W=[sc["YS5"].cb[c]])


NK1 = 65
HC = 32


def fft_fwd_group(cx, hy, lhs_of, K, on_bank):
    fw, nc = cx.fw, cx.nc
    Yb = hy["Ybuf"]
    for c in range(HC):
        ps = cx.ps[hy["pi"] % 4]
        hy["pi"] += 1
        fw.op(fw.PE, lambda: nc.tensor.matmul(ps.t[:, 0:4 * NK1], lhsT=lhs_of(c), rhs=hy["F4"].t[0:K, :], start=True, stop=True), R=hy["lhsR"] + [hy["F4"].b], W=[ps.b])
        src = ps.t[:, 0:4 * NK1].rearrange("p (a k) -> p k a", a=4)
        if c % 2 == 0:
            fw.op(fw.ACT, lambda: nc.scalar.copy(out=Yb.t[:, :, :, c], in_=src), R=[ps.b], W=[Yb.b])
        else:
            fw.op(fw.DVE, lambda: nc.vector.tensor_copy(out=Yb.t[:, :, :, c], in_=src), R=[ps.b], W=[Yb.b])
    G = hy["G"]
    nb = (NK1 + 7) // 8
    for b in range(nb):
        k1s = list(range(b * 8, min(b * 8 + 8, NK1)))
        ps = cx.ps[4 + hy["pj"] % 4]
        hy["pj"] += 1
        for i, k1 in enumerate(k1s):
            o = ps.t[:, i * 2 * HC:(i + 1) * 2 * HC]
            fw.op(fw.PE, lambda: nc.tensor.matmul(o, lhsT=G.t[:, k1, 0, :], rhs=Yb.t[:, k1, 0:2, :].rearrange("p a c -> p (a c)"), start=True, stop=False),
                  R=[G.b, Yb.b], W=[ps.b], inc=False)
            fw.op(fw.PE, lambda: nc.tensor.matmul(o, lhsT=G.t[:, k1, 1, :], rhs=Yb.t[:, k1, 2:4, :].rearrange("p a c -> p (a c)"), start=False, stop=True),
                  R=[G.b, Yb.b], W=[ps.b], inc=(i == len(k1s) - 1))
        on_bank(b, k1s, ps)


def phase_hy_filter(cx, W, li):
    fw, nc = cx.fw, cx.nc
    new_phase(cx)
    sc = cx.sc
    V = fw.DVE
    hy = {"pi": 0, "pj": 0}
    hy["G"] = SB(cx, [128, NK1, 2, 128], BF16, "G")
    for i in range(5):
        fw.dma(fw.SP, out=hy["G"].t[:, i * 13:(i + 1) * 13], in_=cx.C["fftG"][:, i * 13:(i + 1) * 13], W=[hy["G"].b])
    hy["F4"] = SB(cx, [128, 4 * NK1], BF16, "F4")
    fw.dma(fw.SP, out=hy["F4"].t[:], in_=cx.C["fftF4"], W=[hy["F4"].b])
    hy["Ybuf"] = SB(cx, [128, NK1, 4, HC], BF16, "Ybuf")
    w1 = SB(cx, [33, 64], F32, "w1")
    w2 = SB(cx, [64, 64], F32, "w2")
    b12 = SB(cx, [64, 2], F32, "b12")
    fw.dma(fw.SP, out=w1.t[:], in_=W["hy_f_w1"], W=[w1.b])
    fw.dma(fw.SP, out=w2.t[:], in_=W["hy_f_w2"], W=[w2.b])
    fw.dma(fw.SP, out=b12.t[:], in_=W["hy_f_b12"], W=[b12.b])
    w3s = SB(cx, [64, 2, 512], F32, "w3s")
    nr = SB(cx, [128, 512], F32, "nr")
    for half in range(2):
        for o in range(2):
            c0 = o * 512 + half * 256
            fw.dma(fw.SP, out=w3s.t[:, half, o * 256:(o + 1) * 256], in_=W["hy_f_w3"][:, c0:c0 + 256], W=[w3s.b])
            fw.dma(fw.SP, out=nr.t[half * 64:(half + 1) * 64, o * 256:(o + 1) * 256], in_=W["hy_log_decay"][c0:c0 + 256].partition_broadcast(64), W=[nr.b])
    fw.op(fw.ACT, lambda: nc.scalar.activation(out=nr.t[:], in_=nr.t[:], func=AF.Exp), R=[nr.b], W=[nr.b])
    fw.op(V, lambda: nc.vector.tensor_scalar(out=nr.t[:], in0=nr.t[:], scalar1=-1.0, scalar2=None, op0=ALU.mult), R=[nr.b], W=[nr.b])
    tpos = SB(cx, [128, 128], F32, "tpos")
    fw.dma(fw.SP, out=tpos.t[:], in_=cx.C["tpos"], W=[tpos.b])
    h2T = SB(cx, [64, 2 * L], F32, "h2T")
    fts = [SB(cx, [33, CH], F32, "ft")] * 2
    xf = [SB(cx, [64, CH], F32, "xf") for _ in range(2)]
    xi = [SB(cx, [64, CH], mybir.dt.int32, "xi")] * 2
    h1 = [SB(cx, [64, CH], F32, "h1") for _ in range(2)]
    twopi = 2 * math.pi

    def sin_from(ps_ap, psb, bias_ap, out_ap, outb, i):
        x_, xi_ = xf[i % 2], xi[i % 2]
        fw.op(V, lambda: nc.vector.tensor_scalar(out=x_.t[:], in0=ps_ap, scalar1=bias_ap, scalar2=None, op0=ALU.add), R=[psb, b12.b], W=[x_.b])
        fw.op(V, lambda: nc.vector.tensor_scalar(out=xi_.t[:], in0=x_.t[:], scalar1=1.0 / twopi, scalar2=None, op0=ALU.mult), R=[x_.b], W=[xi_.b])
        fw.op(V, lambda: nc.vector.tensor_copy(out=out_ap, in_=xi_.t[:]), R=[xi_.b], W=[outb])
        fw.op(V, lambda: nc.vector.scalar_tensor_tensor(out=x_.t[:], in0=out_ap, scalar=-twopi, in1=x_.t[:], op0=ALU.mult, op1=ALU.add), R=[outb, x_.b], W=[x_.b])
        fw.op(fw.ACT, lambda: nc.scalar.activation(out=out_ap, in_=x_.t[:], func=AF.Sin), R=[x_.b], W=[outb])
    for c in range(32):
        ft = fts[c % 2]
        fw.dma(fw.SP, out=ft.t[:], in_=cx.C["featsT"][:, c * CH:(c + 1) * CH], W=[ft.b])
        ps = cx.ps[c % 2]
        fw.op(fw.PE, lambda: nc.tensor.matmul(ps.t[0:64, :], lhsT=w1.t[:], rhs=ft.t[:], start=True, stop=True), R=[w1.b, ft.b], W=[ps.b])
        sin_from(ps.t[0:64, :], ps.b, b12.t[:, 0:1], h1[c % 2].t[:], h1[c % 2].b, c)
        ps2 = cx.ps[2 + c % 2]
        fw.op(fw.PE, lambda: nc.tensor.matmul(ps2.t[0:64, :], lhsT=w2.t[:], rhs=h1[c % 2].t[:], start=True, stop=True), R=[w2.b, h1[c % 2].b], W=[ps2.b])
        sin_from(ps2.t[0:64, :], ps2.b, b12.t[:, 1:2], h2T.t[:, c * CH:(c + 1) * CH], h2T.b, c)
    HB = SB(cx, [128, 128, 128], BF16, "HB")
    acc4 = SB(cx, [128, 512], F32, "acc4")
    dec = [SB(cx, [128, 512], F32, "dec") for _ in range(2)]
    hh = [SB(cx, [128, 512], F32, "hh") for _ in range(2)]
    ab = [SB(cx, [128, 512], F32, "ab") for _ in range(1)] * 2
    rnt = SB(cx, [128, 512], F32, "rnt")
    hst = [SB(cx, [128, NK1, 2, HC], BF16, "hst") for _ in range(1)]
    hy["lhsR"] = [HB.b]
    for q in range(4):
        qs = slice(q * 128, (q + 1) * 128)
        fw.op(V, lambda: nc.vector.memset(acc4.t[:], 0.0), W=[acc4.b])
        for bt in range(32):
            ps = cx.ps[bt % 2]
            d_, h_, a_ = dec[bt % 2], hh[bt % 2], ab[bt % 2]
            for i in range(4):
                n2 = bt * 4 + i
                fw.op(fw.PE, lambda: nc.tensor.matmul(ps.t[0:64, i * 128:(i + 1) * 128], lhsT=h2T.t[:, n2:L:128], rhs=w3s.t[:, 0, qs], start=True, stop=True),
                      R=[h2T.b, w3s.b], W=[ps.b], inc=False)
                fw.op(fw.PE, lambda: nc.tensor.matmul(ps.t[64:128, i * 128:(i + 1) * 128], lhsT=h2T.t[:, L + n2:2 * L:128], rhs=w3s.t[:, 1, qs], start=True, stop=True),
                      R=[h2T.b, w3s.b], W=[ps.b], inc=(i == 3))
                fw.op(fw.ACT, lambda: nc.scalar.activation(out=d_.t[:, i * 128:(i + 1) * 128], in_=nr.t[:, qs], func=AF.Exp, scale=tpos.t[:, n2:n2 + 1]),
                      R=[nr.b, tpos.b], W=[d_.b])
            fw.op(V, lambda: nc.vector.tensor_tensor(out=h_.t[:], in0=ps.t[:], in1=d_.t[:], op=ALU.mult), R=[ps.b, d_.b], W=[h_.b])
            if bt == 0:
                fw.op(V, lambda: nc.vector.memset(h_.t[64:65, 0:128], 0.0), W=[h_.b])
            fw.op(fw.ACT, lambda: nc.scalar.activation(out=a_.t[:], in_=h_.t[:], func=AF.Abs), R=[h_.b], W=[a_.b])
            fw.op(fw.POOL, lambda: nc.gpsimd.tensor_tensor(out=acc4.t[:], in0=acc4.t[:], in1=a_.t[:], op=ALU.add), R=[acc4.b, a_.b], W=[acc4.b])
            fw.op(V, lambda: nc.vector.tensor_copy(out=HB.t[:, :, bt * 4:(bt + 1) * 4].rearrange("p c n -> p n c"), in_=h_.t[:].rearrange("p (n c) -> p n c", c=128)),
                  R=[h_.b], W=[HB.b])
        ps = cx.ps[2]
        for i in range(4):
            fw.op(fw.PE, lambda: nc.tensor.matmul(ps.t[:, 0:128], lhsT=cx.ones_f.t[:], rhs=acc4.t[:, i * 128:(i + 1) * 128], start=(i == 0), stop=(i == 3)),
                  R=[cx.ones_f.b, acc4.b], W=[ps.b], inc=(i == 3))
        fw.op(V, lambda: nc.vector.reciprocal(out=rnt.t[:, qs], in_=ps.t[:, 0:128]), R=[ps.b], W=[rnt.b])
        for g in range(4):
            gi = q * 4 + g
            hs = hst[0]

            def on_bank(b, k1s, ps, hs=hs):
                src = ps.t[:, 0:len(k1s) * 2 * HC]
                dst = hs.t[:, k1s[0]:k1s[-1] + 1, :, :].rearrange("p k a c -> p (k a c)")
                if b % 2 == 0:
                    fw.op(fw.ACT, lambda: nc.scalar.copy(out=dst, in_=src), R=[ps.b], W=[hs.b])
                else:
                    fw.op(V, lambda: nc.vector.tensor_copy(out=dst, in_=src), R=[ps.b], W=[hs.b])
            fft_fwd_group(cx, hy, lambda c, g=g: HB.t[:, g * HC + c, :], 128, on_bank)
            fw.dma(fw.GD, out=sc["HSPEC"].t[gi], in_=hs.t[:].rearrange("p k a c -> p (k a c)"), R=[hs.b], W=[sc["HSPEC"].cb[gi]])
    fw.dma(fw.GD, out=sc["RN"].t, in_=rnt.t[:], R=[rnt.b], W=sc["RN"].cb)


def phase_hy(cx, W, li):
    fw, nc = cx.fw, cx.nc
    sc = cx.sc
    V = fw.DVE
    new_phase(cx)
    cw = SB(cx, [128, 6, 4], F32, "cw")
    fw.dma(fw.SP, out=cw.t[:], in_=W["hy_conv"], W=[cw.b])
    zt = [SB(cx, [128, L], F32, "zt") for _ in range(2)]
    zo = [SB(cx, [128, L], F32, "zo") for _ in range(2)]
    for j in range(6):
        z, o = zt[j % 2], zo[j % 2]
        fw.dma(fw.SP, out=z.t[:], in_=sc["ZHY"].t[j * 128:(j + 1) * 128, :], W=[z.b])
        fw.op(V, lambda: nc.vector.tensor_scalar(out=o.t[:], in0=z.t[:], scalar1=cw.t[:, j, 1:2], scalar2=cw.t[:, j, 3:4], op0=ALU.mult, op1=ALU.add),
              R=[z.b, cw.b], W=[o.b])
        fw.op(V, lambda: nc.vector.scalar_tensor_tensor(out=o.t[:, 1:L], in0=z.t[:, 0:L - 1], scalar=cw.t[:, j, 0:1], in1=o.t[:, 1:L], op0=ALU.mult, op1=ALU.add),
              R=[z.b, cw.b, o.b], W=[o.b])
        fw.op(V, lambda: nc.vector.scalar_tensor_tensor(out=o.t[:, 0:L - 1], in0=z.t[:, 1:L], scalar=cw.t[:, j, 2:3], in1=o.t[:, 0:L - 1], op0=ALU.mult, op1=ALU.add),
              R=[z.b, cw.b, o.b], W=[o.b])
        fw.dma(fw.GD, out=sc["ZC"].t[j * 128:(j + 1) * 128, :], in_=o.t[:], R=[o.b], W=sc["ZC"].cb)
    if getattr(cx, "hy_sc_only", False):
        return
    new_phase(cx)
    hy = {"pi": 0, "pj": 0}
    hy["G"] = SB(cx, [128, NK1, 2, 128], BF16, "G")
    for i in range(5):
        fw.dma(fw.SP, out=hy["G"].t[:, i * 13:(i + 1) * 13], in_=cx.C["fftG"][:, i * 13:(i + 1) * 13], W=[hy["G"].b])
    hy["F4"] = SB(cx, [128, 4 * NK1], BF16, "F4")
    fw.dma(fw.SP, out=hy["F4"].t[:], in_=cx.C["fftF4"], W=[hy["F4"].b])
    Q = SB(cx, [NK1, 128, 2, 64], BF16, "Q")
    for i in range(4):
        fw.dma(fw.SP, out=Q.t[:, i * 32:(i + 1) * 32], in_=cx.C["fftQ"][:, i * 32:(i + 1) * 32], W=[Q.b])
    FI = SB(cx, [128, 2, 256], BF16, "FI")
    fw.dma(fw.SP, out=FI.t[:], in_=cx.C["fftFI"], W=[FI.b])
    hy["Ybuf"] = SB(cx, [128, NK1, 4, HC], BF16, "Ybuf")
    Yb = hy["Ybuf"]
    Tb_ap = Yb.t[0:NK1].rearrange("p k a c -> p (k a c)")[:, 0:2 * 128 * HC].rearrange("p (a n c) -> p a n c", a=2, c=HC)
    Hg = [SB(cx, [128, NK1, 2, HC], BF16, "Hg") for _ in range(2)]
    Zb = SB(cx, [128, HC, 2, NK1], BF16, "Zb")
    Vf = SB(cx, [64, HC, 128], F32, "Vf")
    X1 = SB(cx, [64, HC, 128], F32, "X1")
    G1 = SB(cx, [64, HC, 128], F32, "G1")
    Ub = SB(cx, [64, HC, 128], BF16, "Ub")
    U2 = SB(cx, [64, HC, 128], BF16, "U2")
    RN = SB(cx, [64, 512], F32, "RN")
    fw.dma(fw.SP, out=RN.t[:], in_=sc["RN"].t[0:64, :], R=sc["RN"].cb, W=[RN.b])
    SK = SB(cx, [64, 512], F32, "SK")
    fw.dma(fw.SP, out=SK.t[:], in_=W["hy_skip"].rearrange("a b -> (a b)").partition_broadcast(64), W=[SK.b])
    tA = [SB(cx, [128, 8 * HC], F32, "tA") for _ in range(2)]
    tB = [SB(cx, [128, 8 * HC], F32, "tB") for _ in range(2)]
    e1 = [SB(cx, [64, 16 * HC], F32, "e1") for _ in range(2)]
    e2 = [SB(cx, [64, 16 * HC], F32, "e2") for _ in range(2)]
    zc = sc["ZC"].t

    def ld(dst, row0):
        src = zc[row0:row0 + HC, :].rearrange("c (a b) -> a c b", b=128)
        for i in range(2):
            fw.dma(fw.SP, out=dst.t[:, i * 16:(i + 1) * 16, :], in_=src[:, i * 16:(i + 1) * 16, :], R=sc["ZC"].cb, W=[dst.b])

    def conv(g, o, ub, v_f, gate, out_f32, out_bf):
        hg = Hg[o]
        fw.dma(fw.SP, out=hg.t[:].rearrange("p k a c -> p (k a c)"), in_=sc["HSPEC"].t[o * 8 + g], R=[sc["HSPEC"].cb[o * 8 + g]], W=[hg.b])
        hy["lhsR"] = [ub.b]

        def on_bank(b, k1s, ps):
            nk = len(k1s)
            ks = slice(k1s[0], k1s[-1] + 1)
            pv = ps.t[:, 0:nk * 2 * HC].rearrange("p (k a c) -> p k a c", a=2, c=HC)
            xr, xi_ = pv[:, :, 0, :], pv[:, :, 1, :]
            hr, hi = hg.t[:, ks, 0, :], hg.t[:, ks, 1, :]
            a_, b_ = tA[b % 2], tB[b % 2]
            av = a_.t[:, 0:nk * HC].rearrange("p (k c) -> p k c", c=HC)
            bv = b_.t[:, 0:nk * HC].rearrange("p (k c) -> p k c", c=HC)
            zr = Zb.t[:, :, 0, ks].rearrange("p c k -> p k c")
            zi = Zb.t[:, :, 1, ks].rearrange("p c k -> p k c")
            fw.op(V, lambda: nc.vector.tensor_tensor(out=av, in0=xr, in1=hr, op=ALU.mult), R=[ps.b, hg.b], W=[a_.b])
            fw.op(V, lambda: nc.vector.tensor_tensor(out=bv, in0=xi_, in1=hi, op=ALU.mult), R=[ps.b, hg.b], W=[b_.b])
            fw.op(fw.POOL, lambda: nc.gpsimd.tensor_tensor(out=zr, in0=av, in1=bv, op=ALU.subtract), R=[a_.b, b_.b], W=[Zb.b])
            a2, b2 = tA[(b + 1) % 2], tB[(b + 1) % 2]
            av2 = a2.t[:, 0:nk * HC].rearrange("p (k c) -> p k c", c=HC)
            bv2 = b2.t[:, 0:nk * HC].rearrange("p (k c) -> p k c", c=HC)
            fw.op(V, lambda: nc.vector.tensor_tensor(out=av2, in0=xr, in1=hi, op=ALU.mult), R=[ps.b, hg.b], W=[a2.b])
            fw.op(V, lambda: nc.vector.tensor_tensor(out=bv2, in0=xi_, in1=hr, op=ALU.mult), R=[ps.b, hg.b], W=[b2.b])
            fw.op(fw.POOL, lambda: nc.gpsimd.tensor_tensor(out=zi, in0=av2, in1=bv2, op=ALU.add), R=[a2.b, b2.b], W=[Zb.b])
        if cx.hy_stop <= 1:
            return
        fft_fwd_group(cx, hy, lambda c: ub.t[:, c, :], 64, on_bank)
        if cx.hy_stop <= 2:
            return
        for c in range(HC):
            ps = cx.ps[hy["pi"] % 4]
            hy["pi"] += 1
            fw.op(fw.PE, lambda: nc.tensor.matmul(ps.t[0:NK1, 0:256], lhsT=Zb.t[:, c, 0, :], rhs=FI.t[:, 0, :], start=True, stop=False), R=[Zb.b, FI.b], W=[ps.b], inc=False)
            fw.op(fw.PE, lambda: nc.tensor.matmul(ps.t[0:NK1, 0:256], lhsT=Zb.t[:, c, 1, :], rhs=FI.t[:, 1, :], start=False, stop=True), R=[Zb.b, FI.b], W=[ps.b])
            src = ps.t[0:NK1, 0:256].rearrange("p (a n) -> p a n", a=2)
            if c % 2 == 0:
                fw.op(fw.ACT, lambda: nc.scalar.copy(out=Tb_ap[:, :, :, c], in_=src), R=[ps.b], W=[Yb.b])
            else:
                fw.op(V, lambda: nc.vector.tensor_copy(out=Tb_ap[:, :, :, c], in_=src), R=[ps.b], W=[Yb.b])
        if cx.hy_stop <= 3:
            return
        cs = slice(o * 256 + g * HC, o * 256 + (g + 1) * HC)
        for nb in range(8):
            ps = cx.ps[4 + hy["pj"] % 4]
            hy["pj"] += 1
            for j in range(16):
                n2 = nb * 16 + j
                oo = ps.t[0:64, j * HC:(j + 1) * HC]
                fw.op(fw.PE, lambda: nc.tensor.matmul(oo, lhsT=Q.t[:, n2, 0, :], rhs=Tb_ap[:, 0, n2, :], start=True, stop=False), R=[Q.b, Yb.b], W=[ps.b], inc=False)
                fw.op(fw.PE, lambda: nc.tensor.matmul(oo, lhsT=Q.t[:, n2, 1, :], rhs=Tb_ap[:, 1, n2, :], start=False, stop=True), R=[Q.b, Yb.b], W=[ps.b], inc=(j == 15))
            pv = ps.t[0:64, 0:16 * HC].rearrange("p (n c) -> p n c", c=HC)
            ns = slice(nb * 16, (nb + 1) * 16)
            a_, b_ = e1[nb % 2], e2[nb % 2]
            av = a_.t[:].rearrange("p (n c) -> p n c", c=HC)
            bv = b_.t[:].rearrange("p (n c) -> p n c", c=HC)
            fw.op(V, lambda: nc.vector.tensor_tensor(out=av, in0=pv, in1=RN.t[:, cs].unsqueeze(1).broadcast_to([64, 16, HC]), op=ALU.mult), R=[ps.b, RN.b], W=[a_.b])
            fw.op(fw.POOL, lambda: nc.gpsimd.tensor_tensor(out=bv, in0=v_f.t[:, :, ns].rearrange("p c n -> p n c"), in1=SK.t[:, cs].unsqueeze(1).broadcast_to([64, 16, HC]), op=ALU.mult),
                  R=[v_f.b, SK.b], W=[b_.b])
            fw.op(V, lambda: nc.vector.tensor_tensor(out=av, in0=av, in1=bv, op=ALU.add), R=[a_.b, b_.b], W=[a_.b])
            fw.op(V, lambda: nc.vector.tensor_tensor(out=out_f32.t[:, :, ns].rearrange("p c n -> p n c"), in0=av, in1=gate.t[:, :, ns].rearrange("p c n -> p n c"), op=ALU.mult),
                  R=[a_.b, gate.b], W=[out_f32.b])
            if out_bf is not None:
                fw.op(fw.ACT, lambda: nc.scalar.copy(out=out_bf.t[:, :, ns], in_=out_f32.t[:, :, ns]), R=[out_f32.b], W=[out_bf.b])

    for g_ in range(cx.hy_lim):
        g = cx.hy_g0 if cx.hy_same else g_
        ld(Vf, g * HC)
        ld(X1, 256 + g * HC)
        fw.op(fw.ACT, lambda: nc.scalar.copy(out=Ub.t[:], in_=Vf.t[:]), R=[Vf.b], W=[Ub.b])
        conv(g, 0, Ub, Vf, X1, G1, U2)
        ld(X1, 512 + g * HC)
        conv(g, 1, U2, G1, X1, Vf, None)
        dst = sc["YHY"].t[g * HC:(g + 1) * HC, :].rearrange("c (a b) -> a c b", b=128)
        for i in range(2):
            fw.dma(fw.GD, out=dst[:, i * 16:(i + 1) * 16, :], in_=Vf.t[:, i * 16:(i + 1) * 16, :], R=[Vf.b], W=sc["YHY"].cb)


def phase_f1(cx, W, li):
    fw, nc = cx.fw, cx.nc
    new_phase(cx)
    sc = cx.sc
    cx.wstg = [SB(cx, [128, 1024], F32, "wstg") for _ in range(2)]
    gb = SB(cx, [128, 8], F32, "gb")
    fw.dma(fw.SP, out=gb.t[:], in_=W["g_branch"], W=[gb.b])
    wout = SB(cx, [128, 8, D], BF16, "wout")
    load_w_bf16(cx, wout, W["w_out"], 8, D, gain=gb)
    ys = [SB(cx, [128, 8, CH], F32, "ys") for _ in range(2)]
    hts = [SB(cx, [128, 8, CH], F32, "ht") for _ in range(2)]
    mixT = SB(cx, [128, 8, CH], BF16, "mixT")
    k = 0
    srcs = ["YHY", "YS5", "YDA", "YMLA"]
    for c in range(NCH):
        tsl = slice(c * CH, (c + 1) * CH)
        y, ht = ys[c % 2], hts[c % 2]
        for bi, nm in enumerate(srcs):
            fw.dma(fw.SP, out=y.t[:, 2 * bi:2 * bi + 2, :], in_=sc[nm].t[:, tsl].rearrange("(j p) t -> p j t", p=128), R=[sc[nm].cb[c]], W=[y.b])
        fw.dma(fw.SP, out=ht.t[:], in_=cx.HT.t[:, tsl].rearrange("(j p) t -> p j t", p=128), R=[cx.HT.cb[c]], W=[ht.b])
        for bi in range(4):
            if bi == 2:
                fw.op(fw.ACT, lambda: nc.scalar.copy(out=mixT.t[:, 4:6, :], in_=y.t[:, 4:6, :]), R=[y.b], W=[mixT.b])
                continue
            r = rstd_chunk(cx, [(y.t[:, 2 * bi + jj, :], 128, [y.b]) for jj in range(2)], 256)
            fw.op(fw.DVE, lambda: nc.vector.tensor_tensor(out=mixT.t[:, 2 * bi:2 * bi + 2, :], in0=y.t[:, 2 * bi:2 * bi + 2, :],
                                                          in1=r.t[:].unsqueeze(1).broadcast_to([128, 2, CH]), op=ALU.mult), R=[y.b, r.b], W=[mixT.b])
        for m in range(8):
            ps = cx.ps[2 + (k % 6)]
            k += 1
            for j in range(8):
                fw.op(fw.PE, lambda: nc.tensor.matmul(ps.t[:, :], lhsT=wout.t[:, j, m * 128:(m + 1) * 128], rhs=mixT.t[:, j, :], start=(j == 0), stop=(j == 7)),
                      R=[wout.b, mixT.b], W=[ps.b], inc=(j == 7))
            fw.op(fw.DVE, lambda: nc.vector.tensor_tensor(out=ht.t[:, m, :], in0=ht.t[:, m, :], in1=ps.t[:, :], op=ALU.add), R=[ht.b, ps.b], W=[ht.b])
        fw.dma(fw.GD, out=cx.HT.t[:, tsl].rearrange("(j p) t -> p j t", p=128), in_=ht.t[:], R=[ht.b], W=[cx.HT.cb[c]])


def phase_f2(cx, W, li):
    fw, nc = cx.fw, cx.nc
    new_phase(cx)
    n = 256
    nchunks = L // n
    cx.wstg = [SB(cx, [128, 1024], F32, "wstg") for _ in range(2)]
    gf = SB(cx, [128, 8], F32, "gf")
    fw.dma(fw.SP, out=gf.t[:], in_=W["g_ffn"], W=[gf.b])
    wg = SB(cx, [128, 8, FFN], BF16, "wg")
    wu = SB(cx, [128, 8, FFN], BF16, "wu")
    wd = SB(cx, [128, 22, D], BF16, "wd")
    kq = 0
    for wt, src in [(wg, W["w_gate"]), (wu, W["w_up"])]:
        for j in range(8):
            for c0 in range(0, FFN, 1024):
                cn = min(1024, FFN - c0)
                stg = cx.wstg[kq % 2]
                fw.dma(fw.SP, out=stg.t[:, 0:cn], in_=src[j * 128:(j + 1) * 128, c0:c0 + cn], W=[stg.b])
                if kq % 2 == 0:
                    fw.op(fw.ACT, lambda: nc.scalar.mul(out=wt.t[:, j, c0:c0 + cn], in_=stg.t[:, 0:cn], mul=gf.t[:, j:j + 1]), R=[stg.b, gf.b], W=[wt.b])
                else:
                    fw.op(fw.DVE, lambda: nc.vector.tensor_scalar(out=wt.t[:, j, c0:c0 + cn], in0=stg.t[:, 0:cn], scalar1=gf.t[:, j:j + 1], scalar2=None, op0=ALU.mult),
                          R=[stg.b, gf.b], W=[wt.b])
                kq += 1
    for i in range(22):
        stg = cx.wstg[kq % 2]
        fw.dma(fw.SP, out=stg.t[:, :], in_=W["w_down"][i * 128:(i + 1) * 128, :], W=[stg.b])
        evac(cx, kq, wd.t[:, i, :], stg.t[:, :], R=[stg.b], W=[wd.b])
        kq += 1
    hts = [SB(cx, [128, 8, n], F32, "ht") for _ in range(2)]
    fT = SB(cx, [128, 8, n], BF16, "fT")
    act = SB(cx, [128, 22, n], BF16, "act")
    sg = [SB(cx, [128, n], F32, "sg") for _ in range(2)]
    k = 0
    for c in range(nchunks):
        tsl = slice(c * n, (c + 1) * n)
        cb = cx.HT.cb[c // 2]
        ht = hts[c % 2]
        fw.dma(fw.SP, out=ht.t[:], in_=cx.HT.t[:, tsl].rearrange("(j p) t -> p j t", p=128), R=[cb], W=[ht.b])
        r = rstd_chunk(cx, [(ht.t[:, j, :], 128, [ht.b]) for j in range(8)], D, n=n)
        fw.op(fw.DVE, lambda: nc.vector.tensor_tensor(out=fT.t[:], in0=ht.t[:], in1=r.t[:, 0:n].unsqueeze(1).broadcast_to([128, 8, n]), op=ALU.mult),
              R=[ht.b, r.b], W=[fT.b])
        for i in range(22):
            ps = cx.ps[2 + (k % 6)]
            k += 1
            for half, wt in enumerate([wg, wu]):
                for j in range(8):
                    fw.op(fw.PE, lambda: nc.tensor.matmul(ps.t[:, half * n:(half + 1) * n], lhsT=wt.t[:, j, i * 128:(i + 1) * 128], rhs=fT.t[:, j, :],
                                                          start=(j == 0), stop=(j == 7)), R=[wt.b, fT.b], W=[ps.b], inc=(j == 7))
            s_ = sg[i % 2]
            fw.op(fw.ACT, lambda: nc.scalar.activation(out=s_.t[:], in_=ps.t[:, 0:n], func=AF.Silu), R=[ps.b], W=[s_.b])
            fw.op(fw.DVE, lambda: nc.vector.tensor_tensor(out=act.t[:, i, :], in0=s_.t[:], in1=ps.t[:, n:2 * n], op=ALU.mult), R=[s_.b, ps.b], W=[act.b])
        for m in range(8):
            ps = cx.ps[2 + (k % 6)]
            k += 1
            for i in range(22):
                fw.op(fw.PE, lambda: nc.tensor.matmul(ps.t[:, 0:n], lhsT=wd.t[:, i, m * 128:(m + 1) * 128], rhs=act.t[:, i, :], start=(i == 0), stop=(i == 21)),
                      R=[wd.b, act.b], W=[ps.b], inc=(i == 21))
            fw.op(fw.DVE, lambda: nc.vector.tensor_tensor(out=ht.t[:, m, :], in0=ht.t[:, m, :], in1=ps.t[:, 0:n], op=ALU.add), R=[ht.b, ps.b], W=[ht.b])
        fw.dma(fw.GD, out=cx.HT.t[:, tsl].rearrange("(j p) t -> p j t", p=128), in_=ht.t[:], R=[ht.b], W=[cb])


def phase_f3(cx, W, li, p_ap):
    fw, nc = cx.fw, cx.nc
    new_phase(cx)
    cx.wstg = [SB(cx, [128, 1024], F32, "wstg") for _ in range(2)]
    gp = SB(cx, [128, 8], F32, "gp")
    fw.dma(fw.SP, out=gp.t[:], in_=W["g_ple"], W=[gp.b])
    wpg = SB(cx, [128, 8, D], BF16, "wpg")
    load_w_bf16(cx, wpg, W["w_ple_gate"], 8, D, gain=gp)
    wpl = SB(cx, [128, 2, D], BF16, "wpl")
    load_w_bf16(cx, wpl, W["w_ple"], 2, D)
    hts = [SB(cx, [128, 8, CH], F32, "ht") for _ in range(2)]
    pcs = [SB(cx, [128, 4, 256], F32, "pc") for _ in range(2)]
    fT = SB(cx, [128, 8, CH], BF16, "fT")
    pT = SB(cx, [128, 2, CH], BF16, "pT")
    sg = [SB(cx, [128, CH], F32, "sg") for _ in range(2)]
    k = 0
    for c in range(NCH):
        tsl = slice(c * CH, (c + 1) * CH)
        ht, pc = hts[c % 2], pcs[c % 2]
        fw.dma(fw.SP, out=ht.t[:], in_=cx.HT.t[:, tsl].rearrange("(j p) t -> p j t", p=128), R=[cx.HT.cb[c]], W=[ht.b])
        fw.dma(fw.SP, out=pc.t[:], in_=p_ap[li, tsl, :].rearrange("(s p) d -> p s d", p=128), W=[pc.b])
        r = rstd_chunk(cx, [(ht.t[:, j, :], 128, [ht.b]) for j in range(8)], D)
        fw.op(fw.DVE, lambda: nc.vector.tensor_tensor(out=fT.t[:], in0=ht.t[:], in1=r.t[:].unsqueeze(1).broadcast_to([128, 8, CH]), op=ALU.mult),
              R=[ht.b, r.b], W=[fT.b])
        for ct in range(2):
            ps = cx.ps[2 + (k % 6)]
            k += 1
            for s in range(4):
                fw.op(fw.PE, lambda: nc.tensor.transpose(out=ps.t[:, s * 128:(s + 1) * 128], in_=pc.t[:, s, ct * 128:(ct + 1) * 128], identity=cx.ident_f.t[:]),
                      R=[pc.b, cx.ident_f.b], W=[ps.b], inc=(s == 3))
            evac(cx, k, pT.t[:, ct, :], ps.t[:], R=[ps.b], W=[pT.b])
        for m in range(8):
            ps = cx.ps[2 + (k % 6)]
            k += 1
            for j in range(8):
                fw.op(fw.PE, lambda: nc.tensor.matmul(ps.t[:, :], lhsT=wpg.t[:, j, m * 128:(m + 1) * 128], rhs=fT.t[:, j, :], start=(j == 0), stop=(j == 7)),
                      R=[wpg.b, fT.b], W=[ps.b], inc=(j == 7))
            ps2 = cx.ps[2 + (k % 6)]
            k += 1
            for ct in range(2):
                fw.op(fw.PE, lambda: nc.tensor.matmul(ps2.t[:, :], lhsT=wpl.t[:, ct, m * 128:(m + 1) * 128], rhs=pT.t[:, ct, :], start=(ct == 0), stop=(ct == 1)),
                      R=[wpl.b, pT.b], W=[ps2.b], inc=(ct == 1))
            s_ = sg[m % 2]
            fw.op(fw.ACT, lambda: nc.scalar.activation(out=s_.t[:], in_=ps.t[:], func=AF.Sigmoid), R=[ps.b], W=[s_.b])
            fw.op(fw.DVE, lambda: nc.vector.tensor_tensor(out=s_.t[:], in0=s_.t[:], in1=ps2.t[:], op=ALU.mult), R=[s_.b, ps2.b], W=[s_.b])
            fw.op(fw.DVE, lambda: nc.vector.tensor_tensor(out=ht.t[:, m, :], in0=ht.t[:, m, :], in1=s_.t[:], op=ALU.add), R=[ht.b, s_.b], W=[ht.b])
        fw.dma(fw.GD, out=cx.HT.t[:, tsl].rearrange("(j p) t -> p j t", p=128), in_=ht.t[:], R=[ht.b], W=[cx.HT.cb[c]])


def host_consts():
    C = {}
    C["ident_f"] = np.eye(128, dtype=np.float32)
    C["ones_f"] = np.ones((128, 128), dtype=np.float32)
    inv = 10000.0 ** (-np.arange(0, 32, 2, dtype=np.float32) / 32)
    ang = (np.arange(L, dtype=np.float32)[:, None] * inv[None, :]).astype(np.float32)
    cos, sin = np.cos(ang).T.astype(np.float32), np.sin(ang).T.astype(np.float32)
    sq = np.float32(96 ** -0.5)
    C["cs_q"] = np.stack([np.tile(cos, (4, 1)) * sq, np.tile(sin, (4, 1)) * sq]).astype(np.float32)
    C["cs_k"] = np.stack([cos, sin]).astype(np.float32)
    sel = np.zeros((65, 64), np.float32)
    sel[64, :] = 1.0
    C["sel"] = sel
    bf = ml_dtypes.bfloat16
    N = 2 * L
    n1 = np.arange(128, dtype=np.float64)
    k1 = np.arange(NK1, dtype=np.float64)
    a1 = 2 * np.pi * np.outer(n1, k1) / 128.0
    Fr, Fi = np.cos(a1), -np.sin(a1)
    C["fftF4"] = np.concatenate([Fr, Fi, -Fi, Fr], axis=1).astype(bf)
    n2 = np.arange(128, dtype=np.float64)
    k2 = np.arange(128, dtype=np.float64)
    kk = k1[None, :, None] + 128.0 * k2[None, None, :]
    ag = 2 * np.pi * ((n2[:, None, None] * kk) % N) / N
    C["fftG"] = np.stack([np.cos(ag), -np.sin(ag)], axis=2).astype(bf)
    ai_ = 2 * np.pi * np.outer(k2, n2) / 128.0
    Fc, Fs = np.cos(ai_), np.sin(ai_)
    C["fftFI"] = np.stack([np.concatenate([Fc, Fs], axis=1), np.concatenate([-Fs, Fc], axis=1)], axis=1).astype(bf)
    wk = np.where((k1 == 0) | (k1 == 64), 1.0, 2.0) / N
    nn = 128.0 * np.arange(64, dtype=np.float64)[None, None, :] + n2[None, :, None]
    aq = 2 * np.pi * ((k1[:, None, None] * nn) % N) / N
    C["fftQ"] = np.stack([np.cos(aq) * wk[:, None, None], -np.sin(aq) * wk[:, None, None]], axis=2).astype(bf)
    n = np.arange(N)
    pos = np.where(n < L, n, N - n).astype(np.float64)
    t = (pos.astype(np.float32) / np.float32(L)).astype(np.float32)
    bands = np.arange(1, 17, dtype=np.float32)
    ang2 = (np.float32(2.0 * math.pi) * t[:, None] * bands[None, :]).astype(np.float32)
    C["featsT"] = np.ascontiguousarray(np.concatenate([t[:, None], np.cos(ang2), np.sin(ang2)], axis=-1).T.astype(np.float32))
    C["tpos"] = np.ascontiguousarray(t.reshape(128, 128))
    C["tau"] = np.tile(np.arange(512, dtype=np.float32)[None, :], (128, 1))
    slopes = [2.0 ** (-8.0 * (h + 1) / 4) for h in range(4)]
    p = np.arange(128, dtype=np.float64)[:, None]
    dl = np.arange(-63, 64, dtype=np.float64)[None, :]
    f = np.arange(512, dtype=np.float64)
    bt, dg, flr = [], [], []
    for sl in slopes:
        left = sl * (p + 128 * dl)
        right = -sl * (p + 128 * dl - 511)
        bt.append(np.where(dl < 0, left, np.where(dl >= 4, right, 0.0)))
        dg.append(np.stack([-sl * np.abs(f[None, :] - 128 * d_ - p) for d_ in range(4)]))
        flr.append(np.stack([np.tile(np.exp(-sl * f)[None, :], (65, 1)), np.tile(np.exp(-sl * (511 - f))[None, :], (65, 1))]))
    C["da_bt"] = np.stack(bt).astype(np.float32)
    C["da_dg"] = np.stack(dg).astype(np.float32)
    C["da_flr"] = np.stack(flr).astype(np.float32)
    return C


PER_LAYER = ["g_mix", "w_in", "w_out", "g_branch", "hy_conv_w", "hy_conv_b", "hy_f_w1", "hy_f_b1", "hy_f_w2", "hy_f_b2", "hy_f_w3",
             "hy_log_decay", "hy_skip", "s5_a_re", "s5_a_im", "s5_log_dt", "s5_b_re", "s5_b_im", "s5_c_re", "s5_c_im", "s5_d",
             "s5_w_glu", "s5_b_glu", "da_lambda", "da_g_head", "mla_g_q", "mla_g_kv", "mla_w_uq", "mla_w_ukv", "g_ffn", "w_gate",
             "w_up", "w_down", "g_ple", "w_ple_gate", "w_ple"]


def pj(v, kt):
    v = np.asarray(v, dtype=np.float32).reshape(-1)
    out = np.zeros((kt * 128,), np.float32)
    out[:v.shape[0]] = v
    return np.ascontiguousarray(out.reshape(kt, 128).T)


def host_layout(inputs, nl):
    Wd = {}
    for li in range(nl):
        g = lambda n: np.asarray(inputs[n][li], dtype=np.float32)
        Wd[f"g_mix__{li}"] = pj(g("g_mix"), 8)
        Wd[f"w_in__{li}"] = g("w_in")
        Wd[f"mla_g_q__{li}"] = pj(g("mla_g_q"), 2)
        Wd[f"mla_g_kv__{li}"] = pj(g("mla_g_kv"), 1)
        Wd[f"mla_w_uq__{li}"] = g("mla_w_uq")
        Wd[f"mla_w_ukv__{li}"] = g("mla_w_ukv")
        Wd[f"da_lambda__{li}"] = g("da_lambda")
        def s5l(a):
            a = np.asarray(a, np.float32)
            rest = a.shape[2:]
            a = a.reshape((2, 8, 128) + rest)
            a = np.moveaxis(a, 2, 0)
            return np.ascontiguousarray(a.reshape((128, 16) + rest))
        Wd[f"s5_ar__{li}"] = s5l(g("s5_a_re").reshape(2, 1024))
        Wd[f"s5_ai__{li}"] = s5l(g("s5_a_im").reshape(2, 1024))
        Wd[f"s5_ldt__{li}"] = s5l(np.repeat(g("s5_log_dt"), 64, axis=1))
        Wd[f"s5_b__{li}"] = s5l(np.stack([g("s5_b_re").reshape(2, 1024, 16), g("s5_b_im").reshape(2, 1024, 16)], axis=2))
        cre = np.transpose(g("s5_c_re"), (0, 1, 3, 2)).reshape(2, 1024, 16)
        cim = np.transpose(g("s5_c_im"), (0, 1, 3, 2)).reshape(2, 1024, 16)
        Wd[f"s5_c__{li}"] = s5l(np.stack([cre, cim], axis=2))
        Wd[f"s5_d__{li}"] = np.ascontiguousarray(g("s5_d").reshape(8, 32).T)
        Wd[f"s5_w_glu__{li}"] = g("s5_w_glu")
        Wd[f"s5_b_glu__{li}"] = pj(g("s5_b_glu"), 2)
        Wd[f"hy_f_w1__{li}"] = g("hy_f_w1")
        Wd[f"hy_f_w2__{li}"] = g("hy_f_w2")
        Wd[f"hy_f_b12__{li}"] = np.ascontiguousarray(np.stack([g("hy_f_b1"), g("hy_f_b2")], axis=1))
        Wd[f"hy_f_w3__{li}"] = g("hy_f_w3")
        Wd[f"hy_log_decay__{li}"] = g("hy_log_decay")
        Wd[f"hy_skip__{li}"] = g("hy_skip")
        cwb = np.concatenate([g("hy_conv_w"), g("hy_conv_b")[None, :]], axis=0)
        Wd[f"hy_conv__{li}"] = np.ascontiguousarray(cwb.T.reshape(6, 128, 4).transpose(1, 0, 2))
        gbr = g("g_branch")
        Wd[f"g_branch__{li}"] = pj(np.concatenate([gbr[0], gbr[1], np.ones(256, np.float32), gbr[2]]), 8)
        Wd[f"w_out__{li}"] = g("w_out")
        Wd[f"g_ffn__{li}"] = pj(g("g_ffn"), 8)
        Wd[f"w_gate__{li}"] = g("w_gate")
        Wd[f"w_up__{li}"] = g("w_up")
        Wd[f"w_down__{li}"] = g("w_down")
        Wd[f"g_ple__{li}"] = pj(g("g_ple"), 8)
        Wd[f"w_ple_gate__{li}"] = g("w_ple_gate")
        Wd[f"w_ple__{li}"] = g("w_ple")
        Wd[f"da_g_head__{li}"] = g("da_g_head").reshape(64, 1)
    Wd["g_final"] = pj(inputs["g_final"], 8)
    return Wd


def build(nl, Wd_shapes, C_shapes, stages, dbg=()):
    nc = bass.Bass("TRN2", target_bir_lowering=False)
    cx = Ctx()
    cx.nc = nc
    x_ap = nc.dram_tensor("x", [L, D], F32, kind="ExternalInput").ap()
    p_ap = nc.dram_tensor("p", [nl, L, 256], F32, kind="ExternalInput").ap()
    out_ap = nc.dram_tensor("out", [L, D], F32, kind="ExternalOutput").ap()
    WA = {k: nc.dram_tensor(k, list(shp), F32, kind="ExternalInput").ap() for k, shp in Wd_shapes.items()}
    cx.C = {k: nc.dram_tensor("c_" + k, list(shp), dt, kind="ExternalInput").ap() for k, (shp, dt) in C_shapes.items()}
    with ExitStack() as es:
        cx.es = es
        cx.phase_es = None
        cx.tcount = 0
        cx.fw = fw = FW(nc, es)
        cx.ps = [Tl(es.enter_context(nc.psum_tensor(f"ps{i}", [128, 512], F32)), f"ps{i}") for i in range(8)]
        P = lambda n, shp, dt: Tl(es.enter_context(nc.sbuf_tensor(n, shp, dt)), n)
        cx.ones_f = P("ones_f", [128, 128], F32)
        cx.ident_f = P("ident_f", [128, 128], F32)
        cx.eps_t = P("eps_t", [128, 1], F32)
        cx.sq = [P(f"sq{i}", [128, CH], F32) for i in range(2)]
        cx.rstd = [P(f"rstd{i}", [128, CH], F32) for i in range(2)]
        cx.rstd_i = 0
        fw.dma(fw.SP, out=cx.ones_f.t[:], in_=cx.C["ones_f"], W=[cx.ones_f.b])
        fw.dma(fw.SP, out=cx.ident_f.t[:], in_=cx.C["ident_f"], W=[cx.ident_f.b])
        fw.op(fw.DVE, lambda: nc.vector.memset(cx.eps_t.t[:], EPS), W=[cx.eps_t.b])
        cx.OUTB = [Buf() for _ in range(NCH)]
        cx.HT = dram(cx, "HT", [D, L], F32)
        sc = cx.sc = {"HT": cx.HT}
        sc["ZHY"] = dram(cx, "ZHY", [768, L], F32)
        sc["ZS5"] = dram(cx, "ZS5", [256, L], F32)
        sc["QDA"] = dram(cx, "QDA", [256, L], BF16)
        sc["KDA"] = dram(cx, "KDA", [256, L], BF16)
        sc["VDA"] = dram(cx, "VDA", [L, 256], BF16)
        sc["QMLA"] = dram(cx, "QMLA", [4, 96, L], BF16)
        sc["KMLA"] = dram(cx, "KMLA", [4, 96, L], BF16)
        sc["VMLA"] = dram(cx, "VMLA", [L, 256], BF16)
        sc["YMLA"] = dram(cx, "YMLA", [256, L], F32)
        sc["YHY"] = dram(cx, "YHY", [256, L], F32)
        sc["YS5"] = dram(cx, "YS5", [256, L], F32)
        sc["GS5"] = dram(cx, "GS5", [256, L], F32)
        sc["ZC"] = dram(cx, "ZC", [768, L], F32)
        sc["HSPEC"] = dram(cx, "HSPEC", [16, 128, NK1 * 2 * HC], BF16)
        sc["RN"] = dram(cx, "RN", [128, 512], F32)
        sc["YDA"] = dram(cx, "YDA", [256, L], F32)
        cx.slopes = [2.0 ** (-8.0 * (h + 1) / 4) for h in range(4)]
        cx.window = 80.0
        cx.dbgt = 'dbgt' in stages
        cx.hy_sc_only = 'hysc' in stages
        cx.hy_lim = 8
        cx.hy_stop = 9
        cx.hy_same = 'hysame' in stages
        cx.hy_g0 = 1 if 'hyg1' in stages else 0
        for st_ in stages:
            if st_.startswith('hystop'):
                cx.hy_stop = int(st_[6:])
                cx.hy_lim = 1
            if st_.startswith('hylim'):
                cx.hy_lim = int(st_[5:])
        if "x0" in stages:
            phase_x0(cx, x_ap)
        for li in range(nl):
            W = {k.rsplit("__", 1)[0]: v for k, v in WA.items() if k.endswith(f"__{li}")}
            if "a" in stages:
                phase_a(cx, W, li)
            if "mlas5" in stages:
                phase_mla_s5(cx, W, li)
            if "mla" in stages:
                phase_mla(cx)
            if "da" in stages:
                phase_da(cx, W, li)
            if "s5" in stages:
                phase_s5(cx, W, li)
            if "hyf" in stages:
                phase_hy_filter(cx, W, li)
            if "hy" in stages:
                phase_hy(cx, W, li)
            if "inj" in stages:
                new_phase(cx)
                for nm in ["YHY", "YS5"]:
                    src = nc.dram_tensor("inj_" + nm, [256, L], F32, kind="ExternalInput").ap()
                    fw.dma(fw.SP, out=sc[nm].t, in_=src)
            if "f1" in stages:
                phase_f1(cx, W, li)
            if "f2" in stages:
                phase_f2(cx, W, li)
            if "f3" in stages:
                phase_f3(cx, W, li, p_ap)
        if "out" in stages:
            phase_out(cx, WA["g_final"], out_ap)
        new_phase(cx)
        for name in dbg:
            src = sc[name].t
            dst = nc.dram_tensor("dbg_" + name, list(src.shape), src.dtype, kind="ExternalOutput").ap()
            fw.dma(fw.SP, out=dst, in_=src)
        fw.final_wait()
        fw.check_deadlock()
        cx.phase_es.close()
    print("instructions:", fw.n_inst)
    return nc


ALL_STAGES = ["x0", "a", "mlas5", "da", "hyf", "hy", "f1", "f2", "f3", "out"]
_CACHE = {}


def kernel(**inputs):
    nl = 4
    nb = 8
    inputs = {k: np.asarray(v) for k, v in inputs.items()}
    Wd = host_layout(inputs, nl)
    C = host_consts()
    key = "prog"
    if key not in _CACHE:
        _CACHE[key] = build(nl, {k: v.shape for k, v in Wd.items()},
                            {k: (v.shape, BF16 if v.dtype == ml_dtypes.bfloat16 else F32) for k, v in C.items()}, ALL_STAGES)
    nc = _CACHE[key]
    x = np.asarray(inputs["x"], dtype=np.float32)
    p = np.asarray(inputs["p"], dtype=np.float32)
    in_maps = []
    for b in range(nb):
        m = {"x": np.ascontiguousarray(x[b]), "p": np.ascontiguousarray(p[:, b])}
        m.update(Wd)
        m.update({"c_" + k: v for k, v in C.items()})
        in_maps.append(m)
    res = run_bass_kernel_spmd(nc, in_maps, core_ids=list(range(nb)))
    return np.stack([np.asarray(res.results[b]["out"], dtype=np.float32) for b in range(nb)], axis=0)
```

```python
import math
from contextlib import ExitStack
import numpy as np
import ml_dtypes
import concourse.bass as bass
import concourse.mybir as mybir
from concourse.bass_utils import run_bass_kernel_spmd

F32 = mybir.dt.float32
BF16 = mybir.dt.bfloat16
AF = mybir.ActivationFunctionType
ALU = mybir.AluOpType
AX = mybir.AxisListType


class Buf:
    __slots__ = ("lw", "rd", "name")

    def __init__(self, name=""):
        self.lw = None
        self.rd = {}
        self.name = name


class Stream:
    def __init__(self, eng):
        self.eng = eng
        self.seen = {}
        self.ev = []


class CQ:
    def __init__(self, stream, sem):
        self.stream = stream
        self.sem = sem
        self.count = 0
        self.is_dma = False


class DQ:
    def __init__(self, stream, sems):
        self.stream = stream
        self.sems = sems
        self.cnt = [0] * len(sems)
        self.idx = 0
        self.is_dma = True
        self.outst = []


class FW:
    def __init__(self, nc, es):
        self.nc = nc
        self.es = es
        S = lambda n: es.enter_context(nc.semaphore(n))
        self.s_pe = Stream(nc.tensor)
        self.s_act = Stream(nc.scalar)
        self.s_dve = Stream(nc.vector)
        self.s_pool = Stream(nc.gpsimd)
        self.s_sp = Stream(nc.sync)
        self.streams = [self.s_pe, self.s_act, self.s_dve, self.s_pool, self.s_sp]
        self.PE = CQ(self.s_pe, S("q_pe"))
        self.ACT = CQ(self.s_act, S("q_act"))
        self.DVE = CQ(self.s_dve, S("q_dve"))
        self.POOL = CQ(self.s_pool, S("q_pool"))
        self.SP = DQ(self.s_sp, [S(f"q_sp{i}") for i in range(12)])
        self.GD = DQ(self.s_pool, [S(f"q_gd{i}") for i in range(12)])
        self.AD = DQ(self.s_act, [S(f"q_ad{i}") for i in range(4)])
        self.cqs = [self.PE, self.ACT, self.DVE, self.POOL]
        self.dqs = [self.SP, self.GD, self.AD]
        self.n_inst = 0

    def _wait(self, stream, deps, own=None):
        need = {}
        for d in deps:
            if d is None:
                continue
            (sem, val, q), raw = d
            if own is not None and q is own and not own.is_dma:
                if not raw or own is self.PE:
                    continue
            k = id(sem)
            if stream.seen.get(k, 0) >= val:
                continue
            if k not in need or need[k][1] < val:
                need[k] = (sem, val)
        for k, (sem, val) in need.items():
            stream.eng.wait_ge(sem, val)
            stream.seen[k] = val
            self.n_inst += 1
            stream.ev.append(("w", k, val))

    @staticmethod
    def _deps(R, W):
        deps = []
        for b in R:
            if b.lw is not None:
                deps.append((b.lw, True))
        for b in W:
            if b.lw is not None:
                deps.append((b.lw, False))
            deps.extend((t, False) for t in b.rd.values())
        return deps

    @staticmethod
    def _mark(tok, R, W):
        k = id(tok[0])
        for b in R:
            o = b.rd.get(k)
            if o is None or o[1] < tok[1]:
                b.rd[k] = tok
        for b in W:
            b.lw = tok
            b.rd = {}

    def op(self, q, fn, R=(), W=(), inc=True):
        self._wait(q.stream, self._deps(R, W), own=q)
        inst = fn()
        self.n_inst += 1
        if inc:
            inst.then_inc(q.sem, 1)
            q.count += 1
            tok = (q.sem, q.count, q)
            q.stream.ev.append(("i", id(q.sem), 1))
        else:
            tok = (q.sem, q.count + 1, q)
        self._mark(tok, R, W)
        return inst

    def dma(self, q, out, in_, R=(), W=(), **kw):
        st = q.stream
        self._wait(st, self._deps(R, W), own=None)
        nd = 1
        for d_ in tuple(out.shape)[:-1]:
            nd *= int(d_)
        LIM = 1536
        q.outst = [o for o in q.outst if st.seen.get(id(o[0]), 0) < o[1]]
        while q.outst and sum(o[2] for o in q.outst) + nd > LIM:
            osem, oval, _ = q.outst.pop(0)
            if st.seen.get(id(osem), 0) < oval:
                st.eng.wait_ge(osem, oval)
                st.seen[id(osem)] = oval
                st.ev.append(("w", id(osem), oval))
        slot = q.idx % len(q.sems)
        q.idx += 1
        sem = q.sems[slot]
        if q.cnt[slot] > 0 and st.seen.get(id(sem), 0) < q.cnt[slot]:
            st.eng.wait_ge(sem, q.cnt[slot])
            st.seen[id(sem)] = q.cnt[slot]
            st.ev.append(("w", id(sem), q.cnt[slot]))
        inst = st.eng.dma_start(out=out, in_=in_, **kw)
        inst.then_inc(sem, 16)
        st.ev.append(("i", id(sem), 16))
        self.n_inst += 1
        q.cnt[slot] += 16
        tok = (sem, q.cnt[slot], q)
        q.outst.append((sem, q.cnt[slot], nd))
        self._mark(tok, R, W)
        return inst

    def barrier(self, streams=None):
        deps = []
        for q in self.cqs:
            if q.count > 0:
                deps.append(((q.sem, q.count, q), True))
        for q in self.dqs:
            for s, c in zip(q.sems, q.cnt):
                if c > 0:
                    deps.append(((s, c, q), True))
        for st in (streams or self.streams):
            self._wait(st, deps, own=None)

    def check_deadlock(self):
        vals = {}
        pos = [0] * len(self.streams)
        progress = True
        while progress:
            progress = False
            for si, st in enumerate(self.streams):
                while pos[si] < len(st.ev):
                    kind, k, v = st.ev[pos[si]]
                    if kind == "w":
                        if vals.get(k, 0) >= v:
                            pos[si] += 1
                            progress = True
                        else:
                            break
                    else:
                        vals[k] = vals.get(k, 0) + v
                        pos[si] += 1
                        progress = True
        stuck = [(si, pos[si], len(st.ev)) for si, st in enumerate(self.streams) if pos[si] < len(st.ev)]
        if stuck:
            for si, p, n in stuck:
                kind, k, v = self.streams[si].ev[p]
                print("DEADLOCK stream", si, "at", p, "/", n, "waiting sem", k, "val", v, "cur", vals.get(k, 0))
            raise RuntimeError("deadlock detected in emitted program")
        return True

    def final_wait(self):
        self.barrier(streams=[self.s_sp])


L = 8192
D = 1024
NCH = 16
CH = 512
IN_W = 2144
FFN = 2816
EPS = 1e-6


class Tl:
    __slots__ = ("t", "b", "cb")

    def __init__(self, t, name=""):
        self.t = t
        self.b = Buf(name)


class Ctx:
    pass


def new_phase(cx):
    cx.fw.barrier()
    if cx.phase_es is not None:
        cx.phase_es.close()
    cx.phase_es = ExitStack()
    cx.es.callback(lambda e=cx.phase_es: e.close())
    cx.tcount += 1000


def SB(cx, shape, dt, name=None):
    cx.tcount += 1
    nm = f"{name or 't'}_{cx.tcount}"
    return Tl(cx.phase_es.enter_context(cx.nc.sbuf_tensor(nm, list(shape), dt)), nm)


def dram(cx, name, shape, dt):
    t = cx.nc.dram_tensor(name, list(shape), dt, kind="Internal").ap()
    tl = Tl(t, name)
    tl.cb = [Buf(name + str(i)) for i in range(NCH)]
    return tl


def rstd_chunk(cx, parts, dim, n=CH):
    fw, nc = cx.fw, cx.nc
    ps = cx.ps[0]
    np_ = len(parts)
    for i, (ap, K, bufs) in enumerate(parts):
        sq = cx.sq[i % 2]
        fw.op(fw.ACT, lambda: nc.scalar.activation(out=sq.t[0:K, 0:n], in_=ap, func=AF.Square), R=bufs, W=[sq.b])
        fw.op(fw.PE, lambda: nc.tensor.matmul(ps.t[:, 0:n], lhsT=cx.ones_f.t[0:K, :], rhs=sq.t[0:K, 0:n], start=(i == 0), stop=(i == np_ - 1)),
              R=[sq.b, cx.ones_f.b], W=[ps.b], inc=True)
    r = cx.rstd[cx.rstd_i % 2]
    cx.rstd_i += 1
    fw.op(fw.ACT, lambda: nc.scalar.activation(out=r.t[:, 0:n], in_=ps.t[:, 0:n], func=AF.Sqrt, bias=cx.eps_t.t[:, 0:1], scale=1.0 / dim), R=[ps.b, cx.eps_t.b], W=[r.b])
    fw.op(fw.DVE, lambda: nc.vector.reciprocal(out=r.t[:, 0:n], in_=r.t[:, 0:n]), R=[r.b], W=[r.b])
    return r


def evac(cx, k, out_ap, in_ap, R, W, scale=None):
    fw, nc = cx.fw, cx.nc
    if k % 2 == 0:
        if scale is None:
            fw.op(fw.ACT, lambda: nc.scalar.copy(out=out_ap, in_=in_ap), R=R, W=W)
        else:
            fw.op(fw.ACT, lambda: nc.scalar.mul(out=out_ap, in_=in_ap, mul=scale), R=R, W=W)
    else:
        if scale is None:
            fw.op(fw.DVE, lambda: nc.vector.tensor_copy(out=out_ap, in_=in_ap), R=R, W=W)
        else:
            fw.op(fw.DVE, lambda: nc.vector.tensor_scalar(out=out_ap, in0=in_ap, scalar1=scale, scalar2=None, op0=ALU.mult), R=R, W=W)


def load_w_bf16(cx, dst, src_ap, kt, ncols, gain=None, stg_cols=2144):
    fw, nc = cx.fw, cx.nc
    for j in range(kt):
        stg = cx.wstg[j % 2]
        fw.dma(fw.SP, out=stg.t[:, 0:ncols], in_=src_ap[j * 128:(j + 1) * 128, :], W=[stg.b])
        if gain is None:
            evac(cx, j, dst.t[:, j, :], stg.t[:, 0:ncols], R=[stg.b], W=[dst.b])
        else:
            if j % 2 == 0:
                fw.op(fw.ACT, lambda: nc.scalar.mul(out=dst.t[:, j, :], in_=stg.t[:, 0:ncols], mul=gain.t[:, j:j + 1]), R=[stg.b, gain.b], W=[dst.b])
            else:
                fw.op(fw.DVE, lambda: nc.vector.tensor_scalar(out=dst.t[:, j, :], in0=stg.t[:, 0:ncols], scalar1=gain.t[:, j:j + 1], scalar2=None, op0=ALU.mult),
                      R=[stg.b, gain.b], W=[dst.b])


def phase_x0(cx, x_ap):
    fw, nc = cx.fw, cx.nc
    new_phase(cx)
    xts = [SB(cx, [128, 4, D], F32, "xt") for _ in range(2)]
    hts = [SB(cx, [128, 8, CH], F32, "ht") for _ in range(2)]
    k = 0
    for c in range(NCH):
        xt, ht = xts[c % 2], hts[c % 2]
        fw.dma(fw.SP, out=xt.t[:], in_=x_ap[c * CH:(c + 1) * CH, :].rearrange("(s p) d -> p s d", p=128), W=[xt.b])
        for j in range(8):
            ps = cx.ps[2 + (k % 6)]
            for s in range(4):
                fw.op(fw.PE, lambda: nc.tensor.transpose(out=ps.t[:, s * 128:(s + 1) * 128], in_=xt.t[:, s, j * 128:(j + 1) * 128], identity=cx.ident_f.t[:]),
                      R=[xt.b, cx.ident_f.b], W=[ps.b], inc=(s == 3))
            evac(cx, k, ht.t[:, j, :], ps.t[:], R=[ps.b], W=[ht.b])
            k += 1
        fw.dma(fw.GD, out=cx.HT.t[:, c * CH:(c + 1) * CH].rearrange("(j p) t -> p j t", p=128), in_=ht.t[:], R=[ht.b], W=[cx.HT.cb[c]])


def phase_out(cx, gfin_ap, out_ap):
    fw, nc = cx.fw, cx.nc
    new_phase(cx)
    gf = SB(cx, [128, 8], F32, "gf")
    fw.dma(fw.SP, out=gf.t[:], in_=gfin_ap, W=[gf.b])
    hts = [SB(cx, [128, 8, CH], F32, "ht") for _ in range(2)]
    ots = [SB(cx, [128, 4, D], F32, "ot") for _ in range(2)]
    k = 0
    for c in range(NCH):
        ht, ot = hts[c % 2], ots[c % 2]
        fw.dma(fw.SP, out=ht.t[:], in_=cx.HT.t[:, c * CH:(c + 1) * CH].rearrange("(j p) t -> p j t", p=128), R=[cx.HT.cb[c]], W=[ht.b])
        r = rstd_chunk(cx, [(ht.t[:, j, :], 128, [ht.b]) for j in range(8)], D)
        for j in range(8):
            fw.op(fw.DVE, lambda: nc.vector.scalar_tensor_tensor(out=ht.t[:, j, :], in0=ht.t[:, j, :], scalar=gf.t[:, j:j + 1], in1=r.t[:],
                                                                 op0=ALU.mult, op1=ALU.mult), R=[ht.b, r.b, gf.b], W=[ht.b])
        for s in range(4):
            for jj in range(2):
                ps = cx.ps[2 + (k % 6)]
                for j4 in range(4):
                    j = jj * 4 + j4
                    fw.op(fw.PE, lambda: nc.tensor.transpose(out=ps.t[:, j4 * 128:(j4 + 1) * 128], in_=ht.t[:, j, s * 128:(s + 1) * 128], identity=cx.ident_f.t[:]),
                          R=[ht.b, cx.ident_f.b], W=[ps.b], inc=(j4 == 3))
                evac(cx, k, ot.t[:, s, jj * 512:(jj + 1) * 512], ps.t[:], R=[ps.b], W=[ot.b])
                k += 1
        fw.dma(fw.GD, out=out_ap[c * CH:(c + 1) * CH, :].rearrange("(s p) d -> p s d", p=128), in_=ot.t[:], R=[ot.b], W=[cx.OUTB[c]])


def phase_a(cx, W, li):
    fw, nc = cx.fw, cx.nc
    new_phase(cx)
    sc = cx.sc
    cx.wstg = [SB(cx, [128, IN_W], F32, "wstg") for _ in range(2)]
    gmix = SB(cx, [128, 8], F32, "gmix")
    fw.dma(fw.SP, out=gmix.t[:], in_=W["g_mix"], W=[gmix.b])
    win = SB(cx, [128, 8, IN_W], BF16, "win")
    load_w_bf16(cx, win, W["w_in"], 8, IN_W, gain=gmix)
    gq = SB(cx, [128, 2], F32, "gq")
    fw.dma(fw.SP, out=gq.t[:], in_=W["mla_g_q"], W=[gq.b])
    gkv = SB(cx, [128, 1], F32, "gkv")
    fw.dma(fw.SP, out=gkv.t[:], in_=W["mla_g_kv"], W=[gkv.b])
    wuq = SB(cx, [128, 2, 384], BF16, "wuq")
    wuq_src = W["mla_w_uq"]
    for kt, (r0, rn) in enumerate([(0, 128), (128, 64)]):
        stg = cx.wstg[kt % 2]
        src = wuq_src[r0:r0 + rn, :].rearrange("r (h c) -> r h c", c=96)
        fw.dma(fw.SP, out=stg.t[0:rn, 0:256].rearrange("r (h c) -> r h c", c=64), in_=src[:, :, 0:64], W=[stg.b])
        fw.dma(fw.SP, out=stg.t[0:rn, 256:320].rearrange("r (h c) -> r h c", c=16), in_=src[:, :, 64:80], W=[stg.b])
        fw.dma(fw.SP, out=stg.t[0:rn, 320:384].rearrange("r (h c) -> r h c", c=16), in_=src[:, :, 80:96], W=[stg.b])
        fw.op(fw.DVE, lambda: nc.vector.tensor_scalar(out=wuq.t[0:rn, kt, :], in0=stg.t[0:rn, 0:384], scalar1=gq.t[0:rn, kt:kt + 1], scalar2=None, op0=ALU.mult),
              R=[stg.b, gq.b], W=[wuq.b])
    wukv = SB(cx, [128, 512], BF16, "wukv")
    stg = cx.wstg[0]
    src = W["mla_w_ukv"].rearrange("r (h c) -> r h c", c=128)
    fw.dma(fw.SP, out=stg.t[:, 0:256].rearrange("r (h c) -> r h c", c=64), in_=src[:, :, 0:64], W=[stg.b])
    fw.dma(fw.SP, out=stg.t[:, 256:512].rearrange("r (h c) -> r h c", c=64), in_=src[:, :, 64:128], W=[stg.b])
    fw.op(fw.DVE, lambda: nc.vector.tensor_scalar(out=wukv.t[:], in0=stg.t[:, 0:512], scalar1=gkv.t[:, 0:1], scalar2=None, op0=ALU.mult),
          R=[stg.b, gkv.b], W=[wukv.b])

    hts = [SB(cx, [128, 8, CH], F32, "ht") for _ in range(2)]
    aT = SB(cx, [128, 8, CH], BF16, "aT")
    zhy = [SB(cx, [128, 6, CH], F32, "zhy") for _ in range(2)]
    zs5 = [SB(cx, [128, 2, CH], F32, "zs5") for _ in range(2)]
    qda = [SB(cx, [128, 2, CH], BF16, "qda") for _ in range(2)]
    kda = [SB(cx, [128, 2, CH], BF16, "kda") for _ in range(2)]
    vda = [SB(cx, [128, 4, 256], BF16, "vda") for _ in range(2)]
    cq = SB(cx, [128, 2, CH], F32, "cq")
    ckv = SB(cx, [128, CH], F32, "ckv")
    cqn = SB(cx, [128, 2, CH], BF16, "cqn")
    ckvn = SB(cx, [128, CH], BF16, "ckvn")
    qn_st = [SB(cx, [128, 2, CH], BF16, "qnst") for _ in range(2)]
    kn_st = [SB(cx, [128, 2, CH], BF16, "knst") for _ in range(2)]
    vm_st = [SB(cx, [128, 4, 256], BF16, "vmst") for _ in range(2)]
    csq = [SB(cx, [64, 2, CH], F32, "csq") for _ in range(2)]
    csk = [SB(cx, [16, 2, CH], F32, "csk") for _ in range(2)]
    t1 = SB(cx, [64, CH], F32, "t1")
    t2 = SB(cx, [64, CH], F32, "t2")
    qr_st = [SB(cx, [64, 2, CH], BF16, "qrst") for _ in range(2)]
    kr_st = [SB(cx, [16, 2, CH], BF16, "krst") for _ in range(2)]

    kk = [0]

    def nextps():
        p = cx.ps[2 + (kk[0] % 6)]
        kk[0] += 1
        return p

    def mm_group(ps_ap, psb, lhs_list, rhs_list, Rb):
        n = len(lhs_list)
        for i in range(n):
            fw.op(fw.PE, lambda: nc.tensor.matmul(ps_ap, lhsT=lhs_list[i], rhs=rhs_list[i], start=(i == 0), stop=(i == n - 1)),
                  R=Rb, W=[psb], inc=(i == n - 1))

    for c in range(NCH):
        tsl = slice(c * CH, (c + 1) * CH)
        ht = hts[c % 2]
        fw.dma(fw.SP, out=ht.t[:], in_=cx.HT.t[:, tsl].rearrange("(j p) t -> p j t", p=128), R=[cx.HT.cb[c]], W=[ht.b])
        fw.dma(fw.SP, out=csq[c % 2].t[:], in_=cx.C["cs_q"][:, :, tsl].rearrange("a p t -> p a t"), W=[csq[c % 2].b])
        fw.dma(fw.SP, out=csk[c % 2].t[:], in_=cx.C["cs_k"][:, :, tsl].rearrange("a p t -> p a t"), W=[csk[c % 2].b])
        r = rstd_chunk(cx, [(ht.t[:, j, :], 128, [ht.b]) for j in range(8)], D)
        fw.op(fw.DVE, lambda: nc.vector.tensor_tensor(out=aT.t[:], in0=ht.t[:], in1=r.t[:].unsqueeze(1).broadcast_to([128, 8, CH]), op=ALU.mult),
              R=[ht.b, r.b], W=[aT.b])
        rhs8 = [aT.t[:, j, :] for j in range(8)]

        def colmm(c0, m):
            ps = nextps()
            mm_group(ps.t[0:m, :], ps.b, [win.t[:, j, c0:c0 + m] for j in range(8)], rhs8, [win.b, aT.b])
            return ps
        for i in range(6):
            ps = colmm(i * 128, 128)
            evac(cx, kk[0], zhy[c % 2].t[:, i, :], ps.t[:], R=[ps.b], W=[zhy[c % 2].b])
        fw.dma(fw.GD, out=sc["ZHY"].t[:, tsl].rearrange("(j p) t -> p j t", p=128), in_=zhy[c % 2].t[:], R=[zhy[c % 2].b], W=[sc["ZHY"].cb[c]])
        for i in range(2):
            ps = colmm(768 + i * 128, 128)
            evac(cx, kk[0], zs5[c % 2].t[:, i, :], ps.t[:], R=[ps.b], W=[zs5[c % 2].b])
        fw.dma(fw.GD, out=sc["ZS5"].t[:, tsl].rearrange("(j p) t -> p j t", p=128), in_=zs5[c % 2].t[:], R=[zs5[c % 2].b], W=[sc["ZS5"].cb[c]])
        for i in range(2):
            ps = colmm(1024 + i * 128, 128)
            evac(cx, kk[0], qda[c % 2].t[:, i, :], ps.t[:], R=[ps.b], W=[qda[c % 2].b], scale=32 ** -0.5)
        fw.dma(fw.GD, out=sc["QDA"].t[:, tsl].rearrange("(j p) t -> p j t", p=128), in_=qda[c % 2].t[:], R=[qda[c % 2].b], W=[sc["QDA"].cb[c]])
        for i in range(2):
            ps = colmm(1280 + i * 128, 128)
            evac(cx, kk[0], kda[c % 2].t[:, i, :], ps.t[:], R=[ps.b], W=[kda[c % 2].b])
        fw.dma(fw.GD, out=sc["KDA"].t[:, tsl].rearrange("(j p) t -> p j t", p=128), in_=kda[c % 2].t[:], R=[kda[c % 2].b], W=[sc["KDA"].cb[c]])
        for s in range(4):
            ps = nextps()
            mm_group(ps.t[:, 0:256], ps.b, [aT.t[:, j, s * 128:(s + 1) * 128] for j in range(8)], [win.t[:, j, 1536:1792] for j in range(8)], [win.b, aT.b])
            evac(cx, kk[0], vda[c % 2].t[:, s, :], ps.t[:, 0:256], R=[ps.b], W=[vda[c % 2].b])
        fw.dma(fw.GD, out=sc["VDA"].t[tsl, :].rearrange("(s p) d -> p s d", p=128), in_=vda[c % 2].t[:], R=[vda[c % 2].b], W=[sc["VDA"].cb[c]])
        ps = colmm(1792, 128)
        evac(cx, kk[0], cq.t[:, 0, :], ps.t[:], R=[ps.b], W=[cq.b])
        ps = colmm(1920, 64)
        evac(cx, kk[0], cq.t[0:64, 1, :], ps.t[0:64, :], R=[ps.b], W=[cq.b])
        ps = colmm(1984, 128)
        evac(cx, kk[0], ckv.t[:], ps.t[:], R=[ps.b], W=[ckv.b])
        rq = rstd_chunk(cx, [(cq.t[:, 0, :], 128, [cq.b]), (cq.t[0:64, 1, :], 64, [cq.b])], 192)
        fw.op(fw.DVE, lambda: nc.vector.tensor_tensor(out=cqn.t[:, 0, :], in0=cq.t[:, 0, :], in1=rq.t[:], op=ALU.mult), R=[cq.b, rq.b], W=[cqn.b])
        fw.op(fw.DVE, lambda: nc.vector.tensor_tensor(out=cqn.t[0:64, 1, :], in0=cq.t[0:64, 1, :], in1=rq.t[0:64, :], op=ALU.mult), R=[cq.b, rq.b], W=[cqn.b])
        rkv = rstd_chunk(cx, [(ckv.t[:], 128, [ckv.b])], 128)
        fw.op(fw.DVE, lambda: nc.vector.tensor_tensor(out=ckvn.t[:], in0=ckv.t[:], in1=rkv.t[:], op=ALU.mult), R=[ckv.b, rkv.b], W=[ckvn.b])
        qrhs = [cqn.t[:, 0, :], cqn.t[0:64, 1, :]]
        for i in range(2):
            ps = nextps()
            mm_group(ps.t[:, :], ps.b, [wuq.t[:, 0, i * 128:(i + 1) * 128], wuq.t[0:64, 1, i * 128:(i + 1) * 128]], qrhs, [wuq.b, cqn.b])
            evac(cx, kk[0], qn_st[c % 2].t[:, i, :], ps.t[:], R=[ps.b], W=[qn_st[c % 2].b], scale=96 ** -0.5)
        for h in range(4):
            fw.dma(fw.GD, out=sc["QMLA"].t[h, 0:64, tsl], in_=qn_st[c % 2].t[(h % 2) * 64:(h % 2) * 64 + 64, h // 2, :], R=[qn_st[c % 2].b], W=[sc["QMLA"].cb[c]])
        ps1 = nextps()
        mm_group(ps1.t[0:64, :], ps1.b, [wuq.t[:, 0, 256:320], wuq.t[0:64, 1, 256:320]], qrhs, [wuq.b, cqn.b])
        ps2 = nextps()
        mm_group(ps2.t[0:64, :], ps2.b, [wuq.t[:, 0, 320:384], wuq.t[0:64, 1, 320:384]], qrhs, [wuq.b, cqn.b])
        cs = csq[c % 2]
        qr = qr_st[c % 2]
        fw.op(fw.DVE, lambda: nc.vector.tensor_tensor(out=t1.t[:], in0=ps1.t[0:64, :], in1=cs.t[:, 0, :], op=ALU.mult), R=[ps1.b, cs.b], W=[t1.b])
        fw.op(fw.DVE, lambda: nc.vector.tensor_tensor(out=t2.t[:], in0=ps2.t[0:64, :], in1=cs.t[:, 1, :], op=ALU.mult), R=[ps2.b, cs.b], W=[t2.b])
        fw.op(fw.DVE, lambda: nc.vector.tensor_tensor(out=qr.t[:, 0, :], in0=t1.t[:], in1=t2.t[:], op=ALU.subtract), R=[t1.b, t2.b], W=[qr.b])
        fw.op(fw.DVE, lambda: nc.vector.tensor_tensor(out=t1.t[:], in0=ps2.t[0:64, :], in1=cs.t[:, 0, :], op=ALU.mult), R=[ps2.b, cs.b], W=[t1.b])
        fw.op(fw.DVE, lambda: nc.vector.tensor_tensor(out=t2.t[:], in0=ps1.t[0:64, :], in1=cs.t[:, 1, :], op=ALU.mult), R=[ps1.b, cs.b], W=[t2.b])
        fw.op(fw.DVE, lambda: nc.vector.tensor_tensor(out=qr.t[:, 1, :], in0=t1.t[:], in1=t2.t[:], op=ALU.add), R=[t1.b, t2.b], W=[qr.b])
        for h in range(4):
            fw.dma(fw.GD, out=sc["QMLA"].t[h, 64:80, tsl], in_=qr.t[h * 16:(h + 1) * 16, 0, :], R=[qr.b], W=[sc["QMLA"].cb[c]])
            fw.dma(fw.GD, out=sc["QMLA"].t[h, 80:96, tsl], in_=qr.t[h * 16:(h + 1) * 16, 1, :], R=[qr.b], W=[sc["QMLA"].cb[c]])
        for i in range(2):
            ps = nextps()
            mm_group(ps.t[:, :], ps.b, [wukv.t[:, i * 128:(i + 1) * 128]], [ckvn.t[:]], [wukv.b, ckvn.b])
            evac(cx, kk[0], kn_st[c % 2].t[:, i, :], ps.t[:], R=[ps.b], W=[kn_st[c % 2].b])
        for h in range(4):
            fw.dma(fw.GD, out=sc["KMLA"].t[h, 0:64, tsl], in_=kn_st[c % 2].t[(h % 2) * 64:(h % 2) * 64 + 64, h // 2, :], R=[kn_st[c % 2].b], W=[sc["KMLA"].cb[c]])
        for s in range(4):
            ps = nextps()
            mm_group(ps.t[:, 0:256], ps.b, [ckvn.t[:, s * 128:(s + 1) * 128]], [wukv.t[:, 256:512]], [wukv.b, ckvn.b])
            evac(cx, kk[0], vm_st[c % 2].t[:, s, :], ps.t[:, 0:256], R=[ps.b], W=[vm_st[c % 2].b])
        fw.dma(fw.GD, out=sc["VMLA"].t[tsl, :].rearrange("(s p) d -> p s d", p=128), in_=vm_st[c % 2].t[:], R=[vm_st[c % 2].b], W=[sc["VMLA"].cb[c]])
        ps1 = colmm(2112, 16)
        ps2 = colmm(2128, 16)
        ck = csk[c % 2]
        kr = kr_st[c % 2]
        fw.op(fw.DVE, lambda: nc.vector.tensor_tensor(out=t1.t[0:16, :], in0=ps1.t[0:16, :], in1=ck.t[:, 0, :], op=ALU.mult), R=[ps1.b, ck.b], W=[t1.b])
        fw.op(fw.DVE, lambda: nc.vector.tensor_tensor(out=t2.t[0:16, :], in0=ps2.t[0:16, :], in1=ck.t[:, 1, :], op=ALU.mult), R=[ps2.b, ck.b], W=[t2.b])
        fw.op(fw.DVE, lambda: nc.vector.tensor_tensor(out=kr.t[:, 0, :], in0=t1.t[0:16, :], in1=t2.t[0:16, :], op=ALU.subtract), R=[t1.b, t2.b], W=[kr.b])
        fw.op(fw.DVE, lambda: nc.vector.tensor_tensor(out=t1.t[0:16, :], in0=ps2.t[0:16, :], in1=ck.t[:, 0, :], op=ALU.mult), R=[ps2.b, ck.b], W=[t1.b])
        fw.op(fw.DVE, lambda: nc.vector.tensor_tensor(out=t2.t[0:16, :], in0=ps1.t[0:16, :], in1=ck.t[:, 1, :], op=ALU.mult), R=[ps1.b, ck.b], W=[t2.b])
        fw.op(fw.DVE, lambda: nc.vector.tensor_tensor(out=kr.t[:, 1, :], in0=t1.t[0:16, :], in1=t2.t[0:16, :], op=ALU.add), R=[t1.b, t2.b], W=[kr.b])
        for h in range(4):
            fw.dma(fw.GD, out=sc["KMLA"].t[h, 64:80, tsl], in_=kr.t[:, 0, :], R=[kr.b], W=[sc["KMLA"].cb[c]])
            fw.dma(fw.GD, out=sc["KMLA"].t[h, 80:96, tsl], in_=kr.t[:, 1, :], R=[kr.b], W=[sc["KMLA"].cb[c]])


def attn_load_qk(cx, at, i, QT_src, KT_src):
    fw = cx.fw
    QT, KT = at["QT"][i], at["KT"][i]
    for (ap, r0, nr) in QT_src:
        fw.dma(fw.SP, out=QT.t[r0:r0 + nr, :], in_=ap, W=[QT.b])
    for (ap, r0, nr) in KT_src:
        fw.dma(fw.SP, out=KT.t[r0:r0 + nr, :], in_=ap, W=[KT.b])


def attn_head(cx, at, qi, r0, V_src, dk, slope_idx, on_out, ki=None, kfull=None):
    fw, nc = cx.fw, cx.nc
    i = at["n"] % 2
    at["n"] += 1
    QT, KT, V1 = at["QT"][qi], at["KT"][qi if ki is None else ki], at["V1"][i]
    r1 = r0 + dk
    m0, m1 = (r0, r1) if kfull is None else (0, kfull)
    VALL = at["VALL"]
    fw.op(fw.POOL, lambda: nc.gpsimd.tensor_copy(out=V1.t[:, :, 0:64], in_=VALL.t[:, :, V_src * 64:(V_src + 1) * 64]), R=[VALL.b], W=[V1.b])
    fw.op(fw.POOL, lambda: nc.gpsimd.memset(V1.t[:, :, 64:65], 1.0), W=[V1.b])
    mx = at["mx"]
    for which, T_ in enumerate([QT, KT]):
        for c in range(NCH):
            sq = cx.sq[c % 2]
            ps = cx.ps[c % 2]
            fw.op(fw.ACT, lambda: nc.scalar.activation(out=sq.t[0:dk, :], in_=T_.t[r0:r1, c * CH:(c + 1) * CH], func=AF.Square), R=[T_.b], W=[sq.b])
            fw.op(fw.PE, lambda: nc.tensor.matmul(ps.t[:, :], lhsT=cx.ones_f.t[0:dk, :], rhs=sq.t[0:dk, :], start=True, stop=True), R=[sq.b, cx.ones_f.b], W=[ps.b])
            fw.op(fw.DVE, lambda: nc.vector.reduce_max(out=mx.t[:, which * 16 + c:which * 16 + c + 1], in_=ps.t[:, :], axis=AX.X), R=[ps.b], W=[mx.b])
    m2 = at["m2"]
    fw.op(fw.DVE, lambda: nc.vector.reduce_max(out=m2.t[:, 0:1], in_=mx.t[:, 0:16], axis=AX.X), R=[mx.b], W=[m2.b])
    fw.op(fw.DVE, lambda: nc.vector.reduce_max(out=m2.t[:, 1:2], in_=mx.t[:, 16:32], axis=AX.X), R=[mx.b], W=[m2.b])
    fw.op(fw.DVE, lambda: nc.vector.tensor_tensor(out=m2.t[:, 2:3], in0=m2.t[:, 0:1], in1=m2.t[:, 1:2], op=ALU.mult), R=[m2.b], W=[m2.b])
    fw.op(fw.ACT, lambda: nc.scalar.activation(out=m2.t[:, 3:4], in_=m2.t[:, 2:3], func=AF.Sqrt), R=[m2.b], W=[m2.b])
    negC = at["negC"][i]
    fw.op(fw.DVE, lambda: nc.vector.tensor_scalar(out=negC.t[:, 0:1], in0=m2.t[:, 3:4], scalar1=-1.0, scalar2=None, op0=ALU.mult), R=[m2.b], W=[negC.b])
    if slope_idx is not None:
        BT = at["BT"][i]
        fw.dma(fw.SP, out=BT.t[:], in_=cx.C["da_bt"][slope_idx], W=[BT.b])
        fw.op(fw.DVE, lambda: nc.vector.tensor_scalar(out=BT.t[:], in0=BT.t[:], scalar1=negC.t[:, 0:1], scalar2=None, op0=ALU.add), R=[BT.b, negC.b], W=[BT.b])
        DG, FLR = at["DG"][slope_idx % 2], at["FLR"][slope_idx % 2]
    for qc in range(NCH):
        qsl = slice(qc * CH, (qc + 1) * CH)
        aset = at["acc_i"] % 2
        at["acc_i"] += 1
        if slope_idx is None:
            accs = [None, cx.ps[at["acc_banks"][aset % len(at["acc_banks"])]], None]
        else:
            accs = [cx.ps[4 + r] for r in range(3)]
        def region(kc):
            if slope_idx is None:
                return 1
            dl = kc - 4 * qc
            return 0 if dl < 0 else (1 if dl < 4 else 2)
        kcs = list(range(64))
        if slope_idx is not None and at.get("window") is not None:
            thr = at["window"] / cx.slopes[slope_idx]
            keep = []
            for kc in kcs:
                dl = kc - 4 * qc
                if dl < 0:
                    dmin = qc * CH - (kc * 128 + 127)
                elif dl >= 4:
                    dmin = kc * 128 - (qc * CH + 511)
                else:
                    dmin = 0
                if dmin <= thr:
                    keep.append(kc)
            kcs = keep
        regs = [region(kc) for kc in kcs]
        first = {}
        last = {}
        for idx, rg in enumerate(regs):
            first.setdefault(rg, idx)
            last[rg] = idx
        nst = at["nst"]

        def emit_qk(idx):
            kc = kcs[idx]
            st = cx.ps[at["st_banks"][at["st_i"] % nst]]
            at["st_i"] += 1
            fw.op(fw.PE, lambda: nc.tensor.matmul(st.t[:, :], lhsT=KT.t[m0:m1, kc * 128:(kc + 1) * 128], rhs=QT.t[m0:m1, qsl], start=True, stop=True),
                  R=[KT.b, QT.b], W=[st.b])
            return st

        def emit_exp(idx, st):
            kc = kcs[idx]
            rg = regs[idx]
            pt = at["PT"][at["pt_i"] % len(at["PT"])]
            at["pt_i"] += 1
            if slope_idx is None:
                fw.op(fw.ACT, lambda: nc.scalar.activation(out=pt.t[:], in_=st.t[:], func=AF.Exp, bias=negC.t[:, 0:1], scale=1.0), R=[st.b, negC.b], W=[pt.b])
            elif rg == 1:
                dl = kc - 4 * qc
                tmp = at["tmp"][at["tmp_i"] % 2]
                at["tmp_i"] += 1
                fw.op(fw.DVE, lambda: nc.vector.tensor_tensor(out=tmp.t[:], in0=st.t[:], in1=DG.t[:, dl, :], op=ALU.add), R=[st.b, DG.b], W=[tmp.b])
                fw.op(fw.ACT, lambda: nc.scalar.activation(out=pt.t[:], in_=tmp.t[:], func=AF.Exp, bias=negC.t[:, 0:1], scale=1.0), R=[tmp.b, negC.b], W=[pt.b])
            else:
                dl = kc - 4 * qc
                fw.op(fw.ACT, lambda: nc.scalar.activation(out=pt.t[:], in_=st.t[:], func=AF.Exp, bias=BT.t[:, dl + 63:dl + 64], scale=1.0), R=[st.b, BT.b], W=[pt.b])
            return pt

        def emit_pv(idx, pt):
            kc = kcs[idx]
            rg = regs[idx]
            acc = accs[rg]
            fw.op(fw.PE, lambda: nc.tensor.matmul(acc.t[0:65, :], lhsT=V1.t[:, kc, :], rhs=pt.t[:], start=(first[rg] == idx), stop=(last[rg] == idx)),
                  R=[V1.b, pt.b], W=[acc.b], inc=True)
        look = nst - 1
        sts = {}
        for idx in range(min(look, len(kcs))):
            sts[idx] = emit_qk(idx)
        for idx in range(len(kcs)):
            if idx + look < len(kcs):
                sts[idx + look] = emit_qk(idx + look)
            pt = emit_exp(idx, sts.pop(idx))
            emit_pv(idx, pt)
            if idx % 4 == 3:
                yield
        comb = at["comb"][at["comb_i"] % 2]
        at["comb_i"] += 1
        if slope_idx is None:
            fw.op(fw.DVE, lambda: nc.vector.tensor_copy(out=comb.t[:], in_=accs[1].t[0:65, :]), R=[accs[1].b], W=[comb.b])
        else:
            t65, t65b = at["t65"], at["t65b"]
            if 0 in first:
                fw.op(fw.DVE, lambda: nc.vector.tensor_tensor(out=t65.t[:], in0=accs[0].t[0:65, :], in1=FLR.t[:, 0, :], op=ALU.mult), R=[accs[0].b, FLR.b], W=[t65.b])
            fw.op(fw.ACT, lambda: nc.scalar.copy(out=comb.t[:], in_=accs[1].t[0:65, :]), R=[accs[1].b], W=[comb.b])
            if 2 in first:
                fw.op(fw.DVE, lambda: nc.vector.tensor_tensor(out=t65b.t[:], in0=accs[2].t[0:65, :], in1=FLR.t[:, 1, :], op=ALU.mult), R=[accs[2].b, FLR.b], W=[t65b.b])
            if 0 in first:
                fw.op(fw.DVE, lambda: nc.vector.tensor_tensor(out=comb.t[:], in0=comb.t[:], in1=t65.t[:], op=ALU.add), R=[comb.b, t65.b], W=[comb.b])
            if 2 in first:
                fw.op(fw.DVE, lambda: nc.vector.tensor_tensor(out=comb.t[:], in0=comb.t[:], in1=t65b.t[:], op=ALU.add), R=[comb.b, t65b.b], W=[comb.b])
        if slope_idx is None:
            if at["bs_banks"] == at["st_banks"]:
                bs = cx.ps[at["st_banks"][at["st_i"] % at["nst"]]]
                at["st_i"] += 1
            else:
                bs = cx.ps[at["bs_banks"][at["bs_i"] % len(at["bs_banks"])]]
                at["bs_i"] += 1
        else:
            bs = cx.ps[7]
        fw.op(fw.PE, lambda: nc.tensor.matmul(bs.t[0:64, :], lhsT=at["sel"].t[0:65, 0:64], rhs=comb.t[0:65, :], start=True, stop=True), R=[at["sel"].b, comb.b], W=[bs.b])
        rs = at["rs"]
        fw.op(fw.DVE, lambda: nc.vector.reciprocal(out=rs.t[:], in_=bs.t[0:64, :]), R=[bs.b], W=[rs.b])
        on = at["on"][at["on_i"] % 2]
        at["on_i"] += 1
        fw.op(fw.DVE, lambda: nc.vector.tensor_tensor(out=on.t[:], in0=comb.t[0:64, :], in1=rs.t[:], op=ALU.mult), R=[comb.b, rs.b], W=[on.b])
        on_out(qc, on)


def attn_setup(cx, alibi, V_all, shared=False):
    fw, nc = cx.fw, cx.nc
    at = {"n": 0, "acc_i": 0, "st_i": 0, "pt_i": 0, "tmp_i": 0, "comb_i": 0, "on_i": 0, "bs_i": 0}
    at["st_banks"] = [0, 1, 2, 3]
    at["acc_banks"] = [4, 5]
    at["bs_banks"] = [6, 7]
    at["nst"] = 4
    if shared:
        at["st_banks"] = [0, 1]
        at["acc_banks"] = [2]
        at["bs_banks"] = [0, 1]
        at["nst"] = 2
    if alibi:
        at["QT"] = [SB(cx, [128, L], BF16, "QT") for _ in range(2)]
        at["KT"] = [SB(cx, [128, L], BF16, "KT") for _ in range(1)]
        for t_ in at["QT"] + at["KT"]:
            fw.op(fw.POOL, lambda: nc.gpsimd.memset(t_.t[:], 0.0), W=[t_.b])
    else:
        nq_ = 1 if shared else 2
        at["QT"] = [SB(cx, [96, L], BF16, "QT") for _ in range(nq_)] * (2 // nq_)
        at["KT"] = [SB(cx, [96, L], BF16, "KT") for _ in range(nq_)] * (2 // nq_)
    at["V1"] = [SB(cx, [128, 64, 65], BF16, "V1") for _ in range(2)]
    at["PT"] = [SB(cx, [128, CH], BF16, "PT") for _ in range(4 if shared else 6)]
    at["mx"] = SB(cx, [128, 32], F32, "mx")
    at["m2"] = SB(cx, [128, 4], F32, "m2")
    at["negC"] = [SB(cx, [128, 1], F32, "negC") for _ in range(2)]
    at["comb"] = [SB(cx, [65, CH], F32, "comb") for _ in range(2)]
    at["rs"] = SB(cx, [64, CH], F32, "rs")
    at["on"] = [SB(cx, [64, CH], F32, "on") for _ in range(2)]
    at["sel"] = SB(cx, [65, 64], F32, "sel")
    at["VALL"] = SB(cx, [128, 64, 256], BF16, "VALL")
    vsrc = V_all.rearrange("(kc p) d -> p kc d", p=128)
    for i in range(8):
        fw.dma(fw.SP, out=at["VALL"].t[:, i * 8:(i + 1) * 8, :], in_=vsrc[:, i * 8:(i + 1) * 8, :], W=[at["VALL"].b])
    fw.dma(fw.SP, out=at["sel"].t[:], in_=cx.C["sel"], W=[at["sel"].b])
    if alibi:
        at["BT"] = [SB(cx, [128, 127], F32, "BT") for _ in range(2)]
        at["tmp"] = [SB(cx, [128, CH], F32, "tmp") for _ in range(2)]
        at["t65"] = SB(cx, [65, CH], F32, "t65")
        at["t65b"] = SB(cx, [65, CH], F32, "t65b")
        at["DG"] = [SB(cx, [128, 4, CH], F32, "DG") for _ in range(2)]
        at["FLR"] = [SB(cx, [65, 2, CH], F32, "FLR") for _ in range(2)]
    return at


def mla_gen(cx, at):
    fw, nc = cx.fw, cx.nc
    sc = cx.sc
    for h in range(4):
        def on_out(qc, on, h=h):
            fw.dma(fw.GD, out=sc["YMLA"].t[h * 64:(h + 1) * 64, qc * CH:(qc + 1) * CH], in_=on.t[:], R=[on.b], W=[sc["YMLA"].cb[qc]])
        attn_load_qk(cx, at, h % 2, [(sc["QMLA"].t[h], 0, 96)], [(sc["KMLA"].t[h], 0, 96)])
        yield from attn_head(cx, at, h % 2, 0, h, 96, None, on_out)


def phase_mla(cx):
    new_phase(cx)
    at = attn_setup(cx, False, cx.sc["VMLA"].t)
    for _ in mla_gen(cx, at):
        pass


def phase_mla_s5(cx, W, li):
    new_phase(cx)
    at = attn_setup(cx, False, cx.sc["VMLA"].t, shared=True)
    g1 = mla_gen(cx, at)
    g2 = s5_main(cx, W, li, shared=True)
    a1 = a2 = True
    while a1 or a2:
        if a1:
            for _ in range(4):
                try:
                    next(g1)
                except StopIteration:
                    a1 = False
                    break
        if a2:
            try:
                next(g2)
            except StopIteration:
                a2 = False
    s5_glu(cx, W, li)


def phase_da(cx, W, li):
    fw, nc = cx.fw, cx.nc
    new_phase(cx)
    sc = cx.sc
    at = attn_setup(cx, True, sc["VDA"].t)
    at["window"] = cx.window
    lam_init = 0.8 - 0.6 * math.exp(-0.3 * li)
    lv = SB(cx, [64, 128], F32, "lv")
    fw.dma(fw.SP, out=lv.t[:], in_=W["da_lambda"].rearrange("a b -> (a b)").partition_broadcast(64), W=[lv.b])
    lp = SB(cx, [64, 64], F32, "lp")
    fw.op(fw.DVE, lambda: nc.vector.tensor_tensor(out=lp.t[:, 0:32], in0=lv.t[:, 0:32], in1=lv.t[:, 32:64], op=ALU.mult), R=[lv.b], W=[lp.b])
    fw.op(fw.DVE, lambda: nc.vector.tensor_tensor(out=lp.t[:, 32:64], in0=lv.t[:, 64:96], in1=lv.t[:, 96:128], op=ALU.mult), R=[lv.b], W=[lp.b])
    ls = SB(cx, [64, 4], F32, "ls")
    fw.op(fw.DVE, lambda: nc.vector.reduce_sum(out=ls.t[:, 0:1], in_=lp.t[:, 0:32], axis=AX.X), R=[lp.b], W=[ls.b])
    fw.op(fw.DVE, lambda: nc.vector.reduce_sum(out=ls.t[:, 1:2], in_=lp.t[:, 32:64], axis=AX.X), R=[lp.b], W=[ls.b])
    fw.op(fw.ACT, lambda: nc.scalar.activation(out=ls.t[:, 0:2], in_=ls.t[:, 0:2], func=AF.Exp), R=[ls.b], W=[ls.b])
    fw.op(fw.DVE, lambda: nc.vector.tensor_tensor(out=ls.t[:, 2:3], in0=ls.t[:, 1:2], in1=ls.t[:, 0:1], op=ALU.subtract), R=[ls.b], W=[ls.b])
    fw.op(fw.DVE, lambda: nc.vector.tensor_scalar(out=ls.t[:, 3:4], in0=ls.t[:, 2:3], scalar1=-lam_init, scalar2=None, op0=ALU.add), R=[ls.b], W=[ls.b])
    gh = SB(cx, [64, 1], F32, "gh")
    fw.dma(fw.SP, out=gh.t[:], in_=W["da_g_head"], W=[gh.b])
    fw.op(fw.DVE, lambda: nc.vector.tensor_scalar(out=gh.t[:], in0=gh.t[:], scalar1=1.0 - lam_init, scalar2=None, op0=ALU.mult), R=[gh.b], W=[gh.b])
    O1 = SB(cx, [64, L], F32, "O1")
    dd = [SB(cx, [64, CH], F32, "dd") for _ in range(2)]
    for h in range(4):
        def out1(qc, on):
            fw.op(fw.ACT, lambda: nc.scalar.copy(out=O1.t[:, qc * CH:(qc + 1) * CH], in_=on.t[:]), R=[on.b], W=[O1.b])

        def out2(qc, on, h=h):
            d_ = dd[qc % 2]
            fw.op(fw.DVE, lambda: nc.vector.scalar_tensor_tensor(out=d_.t[:], in0=on.t[:], scalar=ls.t[:, 3:4], in1=O1.t[:, qc * CH:(qc + 1) * CH],
                                                                 op0=ALU.mult, op1=ALU.add), R=[on.b, ls.b, O1.b], W=[d_.b])
            r = rstd_chunk(cx, [(d_.t[:], 64, [d_.b])], 64)
            fw.op(fw.DVE, lambda: nc.vector.scalar_tensor_tensor(out=d_.t[:], in0=d_.t[:], scalar=gh.t[:, 0:1], in1=r.t[0:64, :], op0=ALU.mult, op1=ALU.mult),
                  R=[d_.b, gh.b, r.b], W=[d_.b])
            fw.dma(fw.GD, out=sc["YDA"].t[h * 64:(h + 1) * 64, qc * CH:(qc + 1) * CH], in_=d_.t[:], R=[d_.b], W=[sc["YDA"].cb[qc]])
        fw.dma(fw.SP, out=at["QT"][0].t[0:32, :], in_=sc["QDA"].t[h * 64:h * 64 + 32, :], W=[at["QT"][0].b])
        fw.dma(fw.SP, out=at["QT"][1].t[32:64, :], in_=sc["QDA"].t[h * 64 + 32:h * 64 + 64, :], W=[at["QT"][1].b])
        fw.dma(fw.SP, out=at["KT"][0].t[0:64, :], in_=sc["KDA"].t[h * 64:(h + 1) * 64, :], W=[at["KT"][0].b])
        dg, fl = at["DG"][h % 2], at["FLR"][h % 2]
        fw.dma(fw.SP, out=dg.t[:], in_=cx.C["da_dg"][h].rearrange("a p f -> p a f"), W=[dg.b])
        fw.dma(fw.SP, out=fl.t[:], in_=cx.C["da_flr"][h].rearrange("a p f -> p a f"), W=[fl.b])
        for j, cb in enumerate([out1, out2]):
            for _ in attn_head(cx, at, j, j * 32, h, 32, h, cb, ki=0, kfull=128):
                pass


def phase_s5(cx, W, li):
    new_phase(cx)
    for _ in s5_main(cx, W, li, shared=False):
        pass
    s5_glu(cx, W, li)


def s5_main(cx, W, li, shared=False):
    fw, nc = cx.fw, cx.nc
    sc = cx.sc
    V = fw.DVE
    T = CH
    nb_ = 1 if shared else 2

    def tt(out, a, b, op, R, Wb):
        fw.op(V, lambda: nc.vector.tensor_tensor(out=out, in0=a, in1=b, op=op), R=R, W=Wb)

    def ts(out, a, s1, op0, R, Wb, s2=None, op1=None):
        if op1 is None:
            fw.op(V, lambda: nc.vector.tensor_scalar(out=out, in0=a, scalar1=s1, scalar2=None, op0=op0), R=R, W=Wb)
        else:
            fw.op(V, lambda: nc.vector.tensor_scalar(out=out, in0=a, scalar1=s1, scalar2=s2, op0=op0, op1=op1), R=R, W=Wb)

    def stt(out, a, sc_, b, op0, op1, R, Wb):
        fw.op(V, lambda: nc.vector.scalar_tensor_tensor(out=out, in0=a, scalar=sc_, in1=b, op0=op0, op1=op1), R=R, W=Wb)

    NP = 40
    pr = SB(cx, [128, NP, 16], F32, "s5p")
    PB = pr.b
    col = lambda i: pr.t[:, i, :]
    AR, AI, LDT, DT, ARDT, TH, SS, CC, M32, ZR, ZI, T1, T2, ABR, ABI, NR, DEN, FR, FI, ATR, ATI, N2 = range(22)
    fw.dma(fw.SP, out=col(AR), in_=W["s5_ar"], W=[PB])
    fw.dma(fw.SP, out=col(AI), in_=W["s5_ai"], W=[PB])
    fw.dma(fw.SP, out=col(LDT), in_=W["s5_ldt"], W=[PB])
    halfpi = SB(cx, [128, 1], F32, "halfpi")
    fw.op(V, lambda: nc.vector.memset(halfpi.t[:], math.pi / 2), W=[halfpi.b])
    fw.op(fw.ACT, lambda: nc.scalar.activation(out=col(DT), in_=col(LDT), func=AF.Exp), R=[PB], W=[PB])
    tt(col(ARDT), col(AR), col(DT), ALU.mult, [PB], [PB])
    tt(col(TH), col(AI), col(DT), ALU.mult, [PB], [PB])
    fw.op(fw.ACT, lambda: nc.scalar.activation(out=col(SS), in_=col(TH), func=AF.Sin, scale=1.0 / 32), R=[PB], W=[PB])
    fw.op(fw.ACT, lambda: nc.scalar.activation(out=col(CC), in_=col(TH), func=AF.Sin, scale=1.0 / 32, bias=halfpi.t[:, 0:1]), R=[PB, halfpi.b], W=[PB])
    fw.op(fw.ACT, lambda: nc.scalar.activation(out=col(M32), in_=col(ARDT), func=AF.Exp, scale=1.0 / 32), R=[PB], W=[PB])
    tt(col(ZR), col(M32), col(CC), ALU.mult, [PB], [PB])
    tt(col(ZI), col(M32), col(SS), ALU.mult, [PB], [PB])
    pw = SB(cx, [128, 10, 2, 16], F32, "s5pw")

    def square():
        tt(col(T1), col(ZR), col(ZR), ALU.mult, [PB], [PB])
        tt(col(T2), col(ZI), col(ZI), ALU.mult, [PB], [PB])
        tt(col(T2), col(T1), col(T2), ALU.subtract, [PB], [PB])
        tt(col(T1), col(ZR), col(ZI), ALU.mult, [PB], [PB])
        ts(col(ZI), col(T1), 2.0, ALU.mult, [PB], [PB])
        fw.op(V, lambda: nc.vector.tensor_copy(out=col(ZR), in_=col(T2)), R=[PB], W=[PB])
    for _ in range(5):
        square()
    for j in range(10):
        fw.op(V, lambda: nc.vector.tensor_copy(out=pw.t[:, j, 0, :], in_=col(ZR)), R=[PB], W=[pw.b])
        fw.op(V, lambda: nc.vector.tensor_copy(out=pw.t[:, j, 1, :], in_=col(ZI)), R=[PB], W=[pw.b])
        if j < 9:
            square()
    ABRa, ABIa = pw.t[:, 0, 0, :], pw.t[:, 0, 1, :]
    ts(col(NR), ABRa, -1.0, ALU.add, [pw.b], [PB])
    tt(col(T1), col(AR), col(AR), ALU.mult, [PB], [PB])
    tt(col(T2), col(AI), col(AI), ALU.mult, [PB], [PB])
    tt(col(DEN), col(T1), col(T2), ALU.add, [PB], [PB])
    fw.op(V, lambda: nc.vector.reciprocal(out=col(DEN), in_=col(DEN)), R=[PB], W=[PB])
    tt(col(T1), col(NR), col(AR), ALU.mult, [PB], [PB])
    tt(col(T2), ABIa, col(AI), ALU.mult, [PB, pw.b], [PB])
    tt(col(T1), col(T1), col(T2), ALU.add, [PB], [PB])
    tt(col(FR), col(T1), col(DEN), ALU.mult, [PB], [PB])
    tt(col(T1), ABIa, col(AR), ALU.mult, [PB, pw.b], [PB])
    tt(col(T2), col(NR), col(AI), ALU.mult, [PB], [PB])
    tt(col(T1), col(T1), col(T2), ALU.subtract, [PB], [PB])
    tt(col(FI), col(T1), col(DEN), ALU.mult, [PB], [PB])
    ts(col(N2), col(ARDT), -2.0, ALU.mult, [PB], [PB])
    bsrc = SB(cx, [128, 16, 2, 16], F32, "s5b")
    csrc = SB(cx, [128, 16, 2, 16], F32, "s5c")
    fw.dma(fw.SP, out=bsrc.t[:], in_=W["s5_b"], W=[bsrc.b])
    fw.dma(fw.SP, out=csrc.t[:], in_=W["s5_c"], W=[csrc.b])
    dsk = SB(cx, [32, 8], F32, "s5d")
    fw.dma(fw.SP, out=dsk.t[:], in_=W["s5_d"], W=[dsk.b])
    tau = SB(cx, [128, T], F32, "tau")
    fw.dma(fw.SP, out=tau.t[:], in_=cx.C["tau"], W=[tau.b])
    ones = SB(cx, [128, T], F32, "ones512")
    fw.op(V, lambda: nc.vector.memset(ones.t[:], 1.0), W=[ones.b])
    Ep = [SB(cx, [128, 2, T], F32, "Ep") for _ in range(nb_)] * (2 // nb_)
    Em = [SB(cx, [128, 2, T], F32, "Em") for _ in range(nb_)] * (2 // nb_)
    mg = SB(cx, [128, T], F32, "mg")
    Ub = [SB(cx, [32, L], BF16, "Ub") for _ in range(nb_)] * (2 // nb_)
    Y = SB(cx, [32, L], F32, "Y")
    bpad = SB(cx, [128, 2, 32], F32, "bpad")
    cpad = SB(cx, [128, 2, 32], F32, "cpad")
    bb = SB(cx, [128, 2, 16], F32, "bb")
    BT_ = [SB(cx, [32, 2, 128], BF16, "BTl") for _ in range(2)]
    CT_ = [SB(cx, [128, 2, 32], BF16, "CTl") for _ in range(2)]
    Pt = [SB(cx, [128, 2, T], F32, "Pt") for _ in range(2)]
    St = [SB(cx, [128, 2, T], F32, "St") for _ in range(2)]
    tm = [SB(cx, [128, 2, T], F32, "tm") for _ in range(2)]
    xb = [SB(cx, [128, 2, T], BF16, "xb") for _ in range(2)]
    ax = SB(cx, [128, 4], F32, "ax")
    yev = [SB(cx, [32, T], F32, "yev") for _ in range(2)]
    it = 0
    for k in range(8):
        ub = Ub[k % 2]
        fw.dma(fw.SP, out=Y.t[:], in_=sc["ZS5"].t[32 * k:32 * k + 32, :], W=[Y.b])
        fw.op(fw.ACT, lambda: nc.scalar.copy(out=ub.t[:], in_=Y.t[:]), R=[Y.b], W=[ub.b])
        fw.op(fw.ACT, lambda: nc.scalar.mul(out=Y.t[:], in_=Y.t[:], mul=dsk.t[:, k:k + 1]), R=[Y.b, dsk.b], W=[Y.b])
        for dr in range(2):
            cl = dr * 8 + k
            c1 = slice(cl, cl + 1)
            ep, em = Ep[(k * 2 + dr) % 2], Em[(k * 2 + dr) % 2]
            btl, ctl = BT_[(k * 2 + dr) % 2], CT_[(k * 2 + dr) % 2]
            fw.op(V, lambda: nc.vector.memset(ep.t[:, 0, 0:1], 1.0), W=[ep.b])
            fw.op(V, lambda: nc.vector.memset(ep.t[:, 1, 0:1], 0.0), W=[ep.b])
            for j in range(9):
                n_ = 1 << j
                p_r, p_i = pw.t[:, j, 0, c1], pw.t[:, j, 1, c1]
                ts(mg.t[:, 0:n_], ep.t[:, 1, 0:n_], p_i, ALU.mult, [ep.b, pw.b], [mg.b])
                stt(ep.t[:, 0, n_:2 * n_], ep.t[:, 0, 0:n_], p_r, mg.t[:, 0:n_], ALU.mult, ALU.subtract, [ep.b, pw.b, mg.b], [ep.b])
                ts(mg.t[:, 0:n_], ep.t[:, 1, 0:n_], p_r, ALU.mult, [ep.b, pw.b], [mg.b])
                stt(ep.t[:, 1, n_:2 * n_], ep.t[:, 0, 0:n_], p_i, mg.t[:, 0:n_], ALU.mult, ALU.add, [ep.b, pw.b, mg.b], [ep.b])
            fw.op(fw.ACT, lambda: nc.scalar.activation(out=mg.t[:], in_=tau.t[:], func=AF.Exp, scale=pr.t[:, N2, c1]), R=[tau.b, PB, mg.b], W=[mg.b])
            tt(em.t[:, 0, :], ep.t[:, 0, :], mg.t[:], ALU.mult, [ep.b, mg.b], [em.b])
            stt(em.t[:, 1, :], ep.t[:, 1, :], -1.0, mg.t[:], ALU.mult, ALU.mult, [ep.b, mg.b], [em.b])
            f_r, f_i = pr.t[:, FR, c1], pr.t[:, FI, c1]
            b_r, b_i = bsrc.t[:, cl, 0, :], bsrc.t[:, cl, 1, :]
            ts(bb.t[:, 0, :], b_i, f_i, ALU.mult, [bsrc.b, PB], [bb.b])
            stt(bb.t[:, 0, :], b_r, f_r, bb.t[:, 0, :], ALU.mult, ALU.subtract, [bsrc.b, PB, bb.b], [bb.b])
            ts(bb.t[:, 1, :], b_r, f_i, ALU.mult, [bsrc.b, PB], [bb.b])
            stt(bb.t[:, 1, :], b_i, f_r, bb.t[:, 1, :], ALU.mult, ALU.add, [bsrc.b, PB, bb.b], [bb.b])
            fw.op(V, lambda: nc.vector.memset(bpad.t[:], 0.0), W=[bpad.b])
            fw.op(V, lambda: nc.vector.memset(cpad.t[:], 0.0), W=[cpad.b])
            for hf in range(2):
                ps_ = slice(hf * 64, hf * 64 + 64)
                cs_ = slice(hf * 16, hf * 16 + 16)
                fw.op(V, lambda: nc.vector.tensor_copy(out=bpad.t[ps_, :, cs_], in_=bb.t[ps_, :, :]), R=[bb.b], W=[bpad.b])
                fw.op(V, lambda: nc.vector.tensor_copy(out=cpad.t[ps_, 0, cs_], in_=csrc.t[ps_, cl, 0, :]), R=[csrc.b], W=[cpad.b])
                ts(cpad.t[ps_, 1, cs_], csrc.t[ps_, cl, 1, :], -1.0, ALU.mult, [csrc.b], [cpad.b])
            fw.op(V, lambda: nc.vector.tensor_copy(out=ctl.t[:], in_=cpad.t[:]), R=[cpad.b], W=[ctl.b])
            psb = cx.ps[7] if shared else cx.ps[6]
            for ri in range(2):
                fw.op(fw.PE, lambda: nc.tensor.transpose(out=psb.t[0:32, ri * 128:(ri + 1) * 128], in_=bpad.t[:, ri, :], identity=cx.ident_f.t[:]),
                      R=[bpad.b, cx.ident_f.b], W=[psb.b], inc=(ri == 1))
            fw.op(V, lambda: nc.vector.tensor_copy(out=btl.t[:].rearrange("p a b -> p (a b)"), in_=psb.t[0:32, 0:256]), R=[psb.b], W=[btl.b])
            fw.op(V, lambda: nc.vector.memset(ax.t[:], 0.0), W=[ax.b])
            at_r, at_i = pw.t[:, 9, 0, c1], pw.t[:, 9, 1, c1]
            rv = (lambda a_: a_) if dr == 0 else (lambda a_: a_[:, ::-1])
            emr, emi = rv(em.t[:, 0, :]), rv(em.t[:, 1, :])
            epr, epi = rv(ep.t[:, 0, :]), rv(ep.t[:, 1, :])
            tiles = {}

            def tsl_of(ci):
                c = ci if dr == 0 else NCH - 1 - ci
                return slice(c * T, (c + 1) * T)

            def pe_in(ci):
                nonlocal it
                tsl = tsl_of(ci)
                if shared:
                    bur, bui = cx.ps[3 + (it % 2) * 2], cx.ps[4 + (it % 2) * 2]
                    yps = cx.ps[7]
                else:
                    bur, bui = cx.ps[(it % 2) * 2], cx.ps[(it % 2) * 2 + 1]
                    yps = cx.ps[4 + it % 2]
                tiles[ci] = (bur, bui, yps, Pt[it % 2], St[it % 2], tm[it % 2], xb[it % 2], yev[it % 2])
                it += 1
                fw.op(fw.PE, lambda: nc.tensor.matmul(bur.t[:, :], lhsT=btl.t[:, 0, :], rhs=ub.t[:, tsl], start=True, stop=True), R=[btl.b, ub.b], W=[bur.b])
                fw.op(fw.PE, lambda: nc.tensor.matmul(bui.t[:, :], lhsT=btl.t[:, 1, :], rhs=ub.t[:, tsl], start=True, stop=True), R=[btl.b, ub.b], W=[bui.b])

            def dve(ci):
                bur, bui, yps, P_, S_, t_, x_, ye = tiles[ci]
                tt(t_.t[:, 0, :], bui.t[:, :], emi, ALU.mult, [bui.b, em.b], [t_.b])
                tt(t_.t[:, 1, :], bur.t[:, :], emi, ALU.mult, [bur.b, em.b], [t_.b])
                tt(P_.t[:, 0, :], bur.t[:, :], emr, ALU.mult, [bur.b, em.b], [P_.b])
                tt(P_.t[:, 1, :], bui.t[:, :], emr, ALU.mult, [bui.b, em.b], [P_.b])
                tt(P_.t[:, 0, :], P_.t[:, 0, :], t_.t[:, 0, :], ALU.subtract, [P_.b, t_.b], [P_.b])
                tt(P_.t[:, 1, :], P_.t[:, 1, :], t_.t[:, 1, :], ALU.add, [P_.b, t_.b], [P_.b])
                for ri in range(2):
                    fw.op(V, lambda: nc.vector.tensor_tensor_scan(out=rv(S_.t[:, ri, :]), data0=ones.t[:, :], data1=rv(P_.t[:, ri, :]), initial=ax.t[:, ri:ri + 1],
                                                                  op0=ALU.mult, op1=ALU.add), R=[ones.b, P_.b, ax.b], W=[S_.b])
                e_ = T - 1 if dr == 0 else 0
                sre, sie = S_.t[:, 0, e_:e_ + 1], S_.t[:, 1, e_:e_ + 1]
                ts(ax.t[:, 2:3], sie, at_i, ALU.mult, [S_.b, pw.b], [ax.b])
                ts(ax.t[:, 3:4], sie, at_r, ALU.mult, [S_.b, pw.b], [ax.b])
                stt(ax.t[:, 0:1], sre, at_r, ax.t[:, 2:3], ALU.mult, ALU.subtract, [S_.b, pw.b, ax.b], [ax.b])
                stt(ax.t[:, 1:2], sre, at_i, ax.t[:, 3:4], ALU.mult, ALU.add, [S_.b, pw.b, ax.b], [ax.b])
                tt(t_.t[:, 0, :], S_.t[:, 1, :], epi, ALU.mult, [S_.b, ep.b], [t_.b])
                tt(t_.t[:, 1, :], S_.t[:, 0, :], epi, ALU.mult, [S_.b, ep.b], [t_.b])
                tt(P_.t[:, 0, :], S_.t[:, 0, :], epr, ALU.mult, [S_.b, ep.b], [P_.b])
                tt(P_.t[:, 1, :], S_.t[:, 1, :], epr, ALU.mult, [S_.b, ep.b], [P_.b])
                tt(x_.t[:, 0, :], P_.t[:, 0, :], t_.t[:, 0, :], ALU.subtract, [P_.b, t_.b], [x_.b])
                tt(x_.t[:, 1, :], P_.t[:, 1, :], t_.t[:, 1, :], ALU.add, [P_.b, t_.b], [x_.b])

            def pe_out(ci):
                bur, bui, yps, P_, S_, t_, x_, ye = tiles.pop(ci)
                tsl = tsl_of(ci)
                fw.op(fw.PE, lambda: nc.tensor.matmul(yps.t[0:32, :], lhsT=ctl.t[:, 0, :], rhs=x_.t[:, 0, :], start=True, stop=False), R=[ctl.b, x_.b], W=[yps.b], inc=False)
                fw.op(fw.PE, lambda: nc.tensor.matmul(yps.t[0:32, :], lhsT=ctl.t[:, 1, :], rhs=x_.t[:, 1, :], start=False, stop=True), R=[ctl.b, x_.b], W=[yps.b])
                fw.op(fw.ACT, lambda: nc.scalar.copy(out=ye.t[:], in_=yps.t[0:32, :]), R=[yps.b], W=[ye.b])
                fw.op(fw.POOL, lambda: nc.gpsimd.tensor_tensor(out=Y.t[:, tsl], in0=Y.t[:, tsl], in1=ye.t[:], op=ALU.add), R=[Y.b, ye.b], W=[Y.b])

            pe_in(0)
            for ci in range(NCH):
                if ci + 1 < NCH:
                    pe_in(ci + 1)
                if ci >= 1:
                    pe_out(ci - 1)
                yield
                dve(ci)
            pe_out(NCH - 1)
        fw.op(fw.ACT, lambda: nc.scalar.activation(out=Y.t[:], in_=Y.t[:], func=AF.Gelu), R=[Y.b], W=[Y.b])
        fw.dma(fw.GD, out=sc["GS5"].t[32 * k:32 * k + 32, :], in_=Y.t[:], R=[Y.b], W=sc["GS5"].cb)


def s5_glu(cx, W, li):
    fw, nc = cx.fw, cx.nc
    sc = cx.sc
    V = fw.DVE
    new_phase(cx)
    cx.wstg = [SB(cx, [128, 1024], F32, "wstg") for _ in range(2)]
    wgl = SB(cx, [128, 2, 256], BF16, "wgl")
    load_w_bf16(cx, wgl, W["s5_w_glu"], 2, 256)
    bgl = SB(cx, [128, 2], F32, "bgl")
    fw.dma(fw.SP, out=bgl.t[:], in_=W["s5_b_glu"], W=[bgl.b])
    gts = [SB(cx, [128, 2, CH], F32, "gt") for _ in range(2)]
    gbs = [SB(cx, [128, 2, CH], BF16, "gb") for _ in range(2)]
    sgs = [SB(cx, [128, CH], F32, "sg") for _ in range(2)]
    kq = 0
    for c in range(NCH):
        tsl = slice(c * CH, (c + 1) * CH)
        g, gbf = gts[c % 2], gbs[c % 2]
        fw.dma(fw.SP, out=g.t[:], in_=sc["GS5"].t[:, tsl].rearrange("(j p) t -> p j t", p=128), R=[sc["GS5"].cb[c]], W=[g.b])
        fw.op(fw.ACT, lambda: nc.scalar.copy(out=gbf.t[:], in_=g.t[:]), R=[g.b], W=[gbf.b])
        for m in range(2):
            ps = cx.ps[2 + kq % 6]
            kq += 1
            for j in range(2):
                fw.op(fw.PE, lambda: nc.tensor.matmul(ps.t[:, :], lhsT=wgl.t[:, j, m * 128:(m + 1) * 128], rhs=gbf.t[:, j, :], start=(j == 0), stop=(j == 1)),
                      R=[wgl.b, gbf.b], W=[ps.b], inc=(j == 1))
            sg = sgs[m]
            fw.op(fw.ACT, lambda: nc.scalar.activation(out=sg.t[:], in_=ps.t[:], func=AF.Sigmoid, bias=bgl.t[:, m:m + 1], scale=1.0), R=[ps.b, bgl.b], W=[sg.b])
            fw.op(V, lambda: nc.vector.tensor_tensor(out=g.t[:, m, :], in0=g.t[:, m, :], in1=sg.t[:], op=ALU.mult), R=[g.b, sg.b], W=[g.b])
        fw.dma(fw.GD, out=sc["YS5"].t[:, tsl].rearrange("(j p) t -> p j t", p=128), in_=g.t[:], R=[g.b], W=[sc["YS5"].cb[c]])


NK1 = 65
HC = 32


def fft_fwd_group(cx, hy, lhs_of, K, on_bank):
    fw, nc = cx.fw, cx.nc
    Yb = hy["Ybuf"]
    for c in range(HC):
        ps = cx.ps[hy["pi"] % 4]
        hy["pi"] += 1
        fw.op(fw.PE, lambda: nc.tensor.matmul(ps.t[:, 0:4 * NK1], lhsT=lhs_of(c), rhs=hy["F4"].t[0:K, :], start=True, stop=True), R=hy["lhsR"] + [hy["F4"].b], W=[ps.b])
        src = ps.t[:, 0:4 * NK1].rearrange("p (a k) -> p k a", a=4)
        if c % 2 == 0:
            fw.op(fw.ACT, lambda: nc.scalar.copy(out=Yb.t[:, :, :, c], in_=src), R=[ps.b], W=[Yb.b])
        else:
            fw.op(fw.DVE, lambda: nc.vector.tensor_copy(out=Yb.t[:, :, :, c], in_=src), R=[ps.b], W=[Yb.b])
    G = hy["G"]
    nb = (NK1 + 7) // 8
    for b in range(nb):
        k1s = list(range(b * 8, min(b * 8 + 8, NK1)))
        ps = cx.ps[4 + hy["pj"] % 4]
        hy["pj"] += 1
        for i, k1 in enumerate(k1s):
            o = ps.t[:, i * 2 * HC:(i + 1) * 2 * HC]
            fw.op(fw.PE, lambda: nc.tensor.matmul(o, lhsT=G.t[:, k1, 0, :], rhs=Yb.t[:, k1, 0:2, :].rearrange("p a c -> p (a c)"), start=True, stop=False),
                  R=[G.b, Yb.b], W=[ps.b], inc=False)
            fw.op(fw.PE, lambda: nc.tensor.matmul(o, lhsT=G.t[:, k1, 1, :], rhs=Yb.t[:, k1, 2:4, :].rearrange("p a c -> p (a c)"), start=False, stop=True),
                  R=[G.b, Yb.b], W=[ps.b], inc=(i == len(k1s) - 1))
        on_bank(b, k1s, ps)


def phase_hy_filter(cx, W, li):
    fw, nc = cx.fw, cx.nc
    new_phase(cx)
    sc = cx.sc
    V = fw.DVE
    hy = {"pi": 0, "pj": 0}
    hy["G"] = SB(cx, [128, NK1, 2, 128], BF16, "G")
    for i in range(5):
        fw.dma(fw.SP, out=hy["G"].t[:, i * 13:(i + 1) * 13], in_=cx.C["fftG"][:, i * 13:(i + 1) * 13], W=[hy["G"].b])
    hy["F4"] = SB(cx, [128, 4 * NK1], BF16, "F4")
    fw.dma(fw.SP, out=hy["F4"].t[:], in_=cx.C["fftF4"], W=[hy["F4"].b])
    hy["Ybuf"] = SB(cx, [128, NK1, 4, HC], BF16, "Ybuf")
    w1 = SB(cx, [33, 64], F32, "w1")
    w2 = SB(cx, [64, 64], F32, "w2")
    b12 = SB(cx, [64, 2], F32, "b12")
    fw.dma(fw.SP, out=w1.t[:], in_=W["hy_f_w1"], W=[w1.b])
    fw.dma(fw.SP, out=w2.t[:], in_=W["hy_f_w2"], W=[w2.b])
    fw.dma(fw.SP, out=b12.t[:], in_=W["hy_f_b12"], W=[b12.b])
    w3s = SB(cx, [64, 2, 512], F32, "w3s")
    nr = SB(cx, [128, 512], F32, "nr")
    for half in range(2):
        for o in range(2):
            c0 = o * 512 + half * 256
            fw.dma(fw.SP, out=w3s.t[:, half, o * 256:(o + 1) * 256], in_=W["hy_f_w3"][:, c0:c0 + 256], W=[w3s.b])
            fw.dma(fw.SP, out=nr.t[half * 64:(half + 1) * 64, o * 256:(o + 1) * 256], in_=W["hy_log_decay"][c0:c0 + 256].partition_broadcast(64), W=[nr.b])
    fw.op(fw.ACT, lambda: nc.scalar.activation(out=nr.t[:], in_=nr.t[:], func=AF.Exp), R=[nr.b], W=[nr.b])
    fw.op(V, lambda: nc.vector.tensor_scalar(out=nr.t[:], in0=nr.t[:], scalar1=-1.0, scalar2=None, op0=ALU.mult), R=[nr.b], W=[nr.b])
    tpos = SB(cx, [128, 128], F32, "tpos")
    fw.dma(fw.SP, out=tpos.t[:], in_=cx.C["tpos"], W=[tpos.b])
    h2T = SB(cx, [64, 2 * L], F32, "h2T")
    fts = [SB(cx, [33, CH], F32, "ft")] * 2
    xf = [SB(cx, [64, CH], F32, "xf") for _ in range(2)]
    xi = [SB(cx, [64, CH], mybir.dt.int32, "xi")] * 2
    h1 = [SB(cx, [64, CH], F32, "h1") for _ in range(2)]
    twopi = 2 * math.pi

    def sin_from(ps_ap, psb, bias_ap, out_ap, outb, i):
        x_, xi_ = xf[i % 2], xi[i % 2]
        fw.op(V, lambda: nc.vector.tensor_scalar(out=x_.t[:], in0=ps_ap, scalar1=bias_ap, scalar2=None, op0=ALU.add), R=[psb, b12.b], W=[x_.b])
        fw.op(V, lambda: nc.vector.tensor_scalar(out=xi_.t[:], in0=x_.t[:], scalar1=1.0 / twopi, scalar2=None, op0=ALU.mult), R=[x_.b], W=[xi_.b])
        fw.op(V, lambda: nc.vector.tensor_copy(out=out_ap, in_=xi_.t[:]), R=[xi_.b], W=[outb])
        fw.op(V, lambda: nc.vector.scalar_tensor_tensor(out=x_.t[:], in0=out_ap, scalar=-twopi, in1=x_.t[:], op0=ALU.mult, op1=ALU.add), R=[outb, x_.b], W=[x_.b])
        fw.op(fw.ACT, lambda: nc.scalar.activation(out=out_ap, in_=x_.t[:], func=AF.Sin), R=[x_.b], W=[outb])
    for c in range(32):
        ft = fts[c % 2]
        fw.dma(fw.SP, out=ft.t[:], in_=cx.C["featsT"][:, c * CH:(c + 1) * CH], W=[ft.b])
        ps = cx.ps[c % 2]
        fw.op(fw.PE, lambda: nc.tensor.matmul(ps.t[0:64, :], lhsT=w1.t[:], rhs=ft.t[:], start=True, stop=True), R=[w1.b, ft.b], W=[ps.b])
        sin_from(ps.t[0:64, :], ps.b, b12.t[:, 0:1], h1[c % 2].t[:], h1[c % 2].b, c)
        ps2 = cx.ps[2 + c % 2]
        fw.op(fw.PE, lambda: nc.tensor.matmul(ps2.t[0:64, :], lhsT=w2.t[:], rhs=h1[c % 2].t[:], start=True, stop=True), R=[w2.b, h1[c % 2].b], W=[ps2.b])
        sin_from(ps2.t[0:64, :], ps2.b, b12.t[:, 1:2], h2T.t[:, c * CH:(c + 1) * CH], h2T.b, c)
    HB = SB(cx, [128, 128, 128], BF16, "HB")
    acc4 = SB(cx, [128, 512], F32, "acc4")
    dec = [SB(cx, [128, 512], F32, "dec") for _ in range(2)]
    hh = [SB(cx, [128, 512], F32, "hh") for _ in range(2)]
    ab = [SB(cx, [128, 512], F32, "ab") for _ in range(1)] * 2
    rnt = SB(cx, [128, 512], F32, "rnt")
    hst = [SB(cx, [128, NK1, 2, HC], BF16, "hst") for _ in range(1)]
    hy["lhsR"] = [HB.b]
    for q in range(4):
        qs = slice(q * 128, (q + 1) * 128)
        fw.op(V, lambda: nc.vector.memset(acc4.t[:], 0.0), W=[acc4.b])
        for bt in range(32):
            ps = cx.ps[bt % 2]
            d_, h_, a_ = dec[bt % 2], hh[bt % 2], ab[bt % 2]
            for i in range(4):
                n2 = bt * 4 + i
                fw.op(fw.PE, lambda: nc.tensor.matmul(ps.t[0:64, i * 128:(i + 1) * 128], lhsT=h2T.t[:, n2:L:128], rhs=w3s.t[:, 0, qs], start=True, stop=True),
                      R=[h2T.b, w3s.b], W=[ps.b], inc=False)
                fw.op(fw.PE, lambda: nc.tensor.matmul(ps.t[64:128, i * 128:(i + 1) * 128], lhsT=h2T.t[:, L + n2:2 * L:128], rhs=w3s.t[:, 1, qs], start=True, stop=True),
                      R=[h2T.b, w3s.b], W=[ps.b], inc=(i == 3))
                fw.op(fw.ACT, lambda: nc.scalar.activation(out=d_.t[:, i * 128:(i + 1) * 128], in_=nr.t[:, qs], func=AF.Exp, scale=tpos.t[:, n2:n2 + 1]),
                      R=[nr.b, tpos.b], W=[d_.b])
            fw.op(V, lambda: nc.vector.tensor_tensor(out=h_.t[:], in0=ps.t[:], in1=d_.t[:], op=ALU.mult), R=[ps.b, d_.b], W=[h_.b])
            if bt == 0:
                fw.op(V, lambda: nc.vector.memset(h_.t[64:65, 0:128], 0.0), W=[h_.b])
            fw.op(fw.ACT, lambda: nc.scalar.activation(out=a_.t[:], in_=h_.t[:], func=AF.Abs), R=[h_.b], W=[a_.b])
            fw.op(fw.POOL, lambda: nc.gpsimd.tensor_tensor(out=acc4.t[:], in0=acc4.t[:], in1=a_.t[:], op=ALU.add), R=[acc4.b, a_.b], W=[acc4.b])
            fw.op(V, lambda: nc.vector.tensor_copy(out=HB.t[:, :, bt * 4:(bt + 1) * 4].rearrange("p c n -> p n c"), in_=h_.t[:].rearrange("p (n c) -> p n c", c=128)),
                  R=[h_.b], W=[HB.b])
        ps = cx.ps[2]
        for i in range(4):
            fw.op(fw.PE, lambda: nc.tensor.matmul(ps.t[:, 0:128], lhsT=cx.ones_f.t[:], rhs=acc4.t[:, i * 128:(i + 1) * 128], start=(i == 0), stop=(i == 3)),
                  R=[cx.ones_f.b, acc4.b], W=[ps.b], inc=(i == 3))
        fw.op(V, lambda: nc.vector.reciprocal(out=rnt.t[:, qs], in_=ps.t[:, 0:128]), R=[ps.b], W=[rnt.b])
        for g in range(4):
            gi = q * 4 + g
            hs = hst[0]

            def on_bank(b, k1s, ps, hs=hs):
                src = ps.t[:, 0:len(k1s) * 2 * HC]
                dst = hs.t[:, k1s[0]:k1s[-1] + 1, :, :].rearrange("p k a c -> p (k a c)")
                if b % 2 == 0:
                    fw.op(fw.ACT, lambda: nc.scalar.copy(out=dst, in_=src), R=[ps.b], W=[hs.b])
                else:
                    fw.op(V, lambda: nc.vector.tensor_copy(out=dst, in_=src), R=[ps.b], W=[hs.b])
            fft_fwd_group(cx, hy, lambda c, g=g: HB.t[:, g * HC + c, :], 128, on_bank)
            fw.dma(fw.GD, out=sc["HSPEC"].t[gi], in_=hs.t[:].rearrange("p k a c -> p (k a c)"), R=[hs.b], W=[sc["HSPEC"].cb[gi]])
    fw.dma(fw.GD, out=sc["RN"].t, in_=rnt.t[:], R=[rnt.b], W=sc["RN"].cb)


def phase_hy(cx, W, li):
    fw, nc = cx.fw, cx.nc
    sc = cx.sc
    V = fw.DVE
    new_phase(cx)
    cw = SB(cx, [128, 6, 4], F32, "cw")
    fw.dma(fw.SP, out=cw.t[:], in_=W["hy_conv"], W=[cw.b])
    zt = [SB(cx, [128, L], F32, "zt") for _ in range(2)]
    zo = [SB(cx, [128, L], F32, "zo") for _ in range(2)]
    for j in range(6):
        z, o = zt[j % 2], zo[j % 2]
        fw.dma(fw.SP, out=z.t[:], in_=sc["ZHY"].t[j * 128:(j + 1) * 128, :], W=[z.b])
        fw.op(V, lambda: nc.vector.tensor_scalar(out=o.t[:], in0=z.t[:], scalar1=cw.t[:, j, 1:2], scalar2=cw.t[:, j, 3:4], op0=ALU.mult, op1=ALU.add),
              R=[z.b, cw.b], W=[o.b])
        fw.op(V, lambda: nc.vector.scalar_tensor_tensor(out=o.t[:, 1:L], in0=z.t[:, 0:L - 1], scalar=cw.t[:, j, 0:1], in1=o.t[:, 1:L], op0=ALU.mult, op1=ALU.add),
              R=[z.b, cw.b, o.b], W=[o.b])
        fw.op(V, lambda: nc.vector.scalar_tensor_tensor(out=o.t[:, 0:L - 1], in0=z.t[:, 1:L], scalar=cw.t[:, j, 2:3], in1=o.t[:, 0:L - 1], op0=ALU.mult, op1=ALU.add),
              R=[z.b, cw.b, o.b], W=[o.b])
        fw.dma(fw.GD, out=sc["ZC"].t[j * 128:(j + 1) * 128, :], in_=o.t[:], R=[o.b], W=sc["ZC"].cb)
    if getattr(cx, "hy_sc_only", False):
        return
    new_phase(cx)
    hy = {"pi": 0, "pj": 0}
    hy["G"] = SB(cx, [128, NK1, 2, 128], BF16, "G")
    for i in range(5):
        fw.dma(fw.SP, out=hy["G"].t[:, i * 13:(i + 1) * 13], in_=cx.C["fftG"][:, i * 13:(i + 1) * 13], W=[hy["G"].b])
    hy["F4"] = SB(cx, [128, 4 * NK1], BF16, "F4")
    fw.dma(fw.SP, out=hy["F4"].t[:], in_=cx.C["fftF4"], W=[hy["F4"].b])
    Q = SB(cx, [NK1, 128, 2, 64], BF16, "Q")
    for i in range(4):
        fw.dma(fw.SP, out=Q.t[:, i * 32:(i + 1) * 32], in_=cx.C["fftQ"][:, i * 32:(i + 1) * 32], W=[Q.b])
    FI = SB(cx, [128, 2, 256], BF16, "FI")
    fw.dma(fw.SP, out=FI.t[:], in_=cx.C["fftFI"], W=[FI.b])
    hy["Ybuf"] = SB(cx, [128, NK1, 4, HC], BF16, "Ybuf")
    Yb = hy["Ybuf"]
    Tb_ap = Yb.t[0:NK1].rearrange("p k a c -> p (k a c)")[:, 0:2 * 128 * HC].rearrange("p (a n c) -> p a n c", a=2, c=HC)
    Hg = [SB(cx, [128, NK1, 2, HC], BF16, "Hg") for _ in range(2)]
    Zb = SB(cx, [128, HC, 2, NK1], BF16, "Zb")
    Vf = SB(cx, [64, HC, 128], F32, "Vf")
    X1 = SB(cx, [64, HC, 128], F32, "X1")
    G1 = SB(cx, [64, HC, 128], F32, "G1")
    Ub = SB(cx, [64, HC, 128], BF16, "Ub")
    U2 = SB(cx, [64, HC, 128], BF16, "U2")
    RN = SB(cx, [64, 512], F32, "RN")
    fw.dma(fw.SP, out=RN.t[:], in_=sc["RN"].t[0:64, :], R=sc["RN"].cb, W=[RN.b])
    SK = SB(cx, [64, 512], F32, "SK")
    fw.dma(fw.SP, out=SK.t[:], in_=W["hy_skip"].rearrange("a b -> (a b)").partition_broadcast(64), W=[SK.b])
    tA = [SB(cx, [128, 8 * HC], F32, "tA") for _ in range(2)]
    tB = [SB(cx, [128, 8 * HC], F32, "tB") for _ in range(2)]
    e1 = [SB(cx, [64, 16 * HC], F32, "e1") for _ in range(2)]
    e2 = [SB(cx, [64, 16 * HC], F32, "e2") for _ in range(2)]
    zc = sc["ZC"].t

    def ld(dst, row0):
        src = zc[row0:row0 + HC, :].rearrange("c (a b) -> a c b", b=128)
        for i in range(2):
            fw.dma(fw.SP, out=dst.t[:, i * 16:(i + 1) * 16, :], in_=src[:, i * 16:(i + 1) * 16, :], R=sc["ZC"].cb, W=[dst.b])

    def conv(g, o, ub, v_f, gate, out_f32, out_bf):
        hg = Hg[o]
        fw.dma(fw.SP, out=hg.t[:].rearrange("p k a c -> p (k a c)"), in_=sc["HSPEC"].t[o * 8 + g], R=[sc["HSPEC"].cb[o * 8 + g]], W=[hg.b])
        hy["lhsR"] = [ub.b]

        def on_bank(b, k1s, ps):
            nk = len(k1s)
            ks = slice(k1s[0], k1s[-1] + 1)
            pv = ps.t[:, 0:nk * 2 * HC].rearrange("p (k a c) -> p k a c", a=2, c=HC)
            xr, xi_ = pv[:, :, 0, :], pv[:, :, 1, :]
            hr, hi = hg.t[:, ks, 0, :], hg.t[:, ks, 1, :]
            a_, b_ = tA[b % 2], tB[b % 2]
            av = a_.t[:, 0:nk * HC].rearrange("p (k c) -> p k c", c=HC)
            bv = b_.t[:, 0:nk * HC].rearrange("p (k c) -> p k c", c=HC)
            zr = Zb.t[:, :, 0, ks].rearrange("p c k -> p k c")
            zi = Zb.t[:, :, 1, ks].rearrange("p c k -> p k c")
            fw.op(V, lambda: nc.vector.tensor_tensor(out=av, in0=xr, in1=hr, op=ALU.mult), R=[ps.b, hg.b], W=[a_.b])
            fw.op(V, lambda: nc.vector.tensor_tensor(out=bv, in0=xi_, in1=hi, op=ALU.mult), R=[ps.b, hg.b], W=[b_.b])
            fw.op(fw.POOL, lambda: nc.gpsimd.tensor_tensor(out=zr, in0=av, in1=bv, op=ALU.subtract), R=[a_.b, b_.b], W=[Zb.b])
            a2, b2 = tA[(b + 1) % 2], tB[(b + 1) % 2]
            av2 = a2.t[:, 0:nk * HC].rearrange("p (k c) -> p k c", c=HC)
            bv2 = b2.t[:, 0:nk * HC].rearrange("p (k c) -> p k c", c=HC)
            fw.op(V, lambda: nc.vector.tensor_tensor(out=av2, in0=xr, in1=hi, op=ALU.mult), R=[ps.b, hg.b], W=[a2.b])
            fw.op(V, lambda: nc.vector.tensor_tensor(out=bv2, in0=xi_, in1=hr, op=ALU.mult), R=[ps.b, hg.b], W=[b2.b])
            fw.op(fw.POOL, lambda: nc.gpsimd.tensor_tensor(out=zi, in0=av2, in1=bv2, op=ALU.add), R=[a2.b, b2.b], W=[Zb.b])
        if cx.hy_stop <= 1:
            return
        fft_fwd_group(cx, hy, lambda c: ub.t[:, c, :], 64, on_bank)
        if cx.hy_stop <= 2:
            return
        for c in range(HC):
            ps = cx.ps[hy["pi"] % 4]
            hy["pi"] += 1
            fw.op(fw.PE, lambda: nc.tensor.matmul(ps.t[0:NK1, 0:256], lhsT=Zb.t[:, c, 0, :], rhs=FI.t[:, 0, :], start=True, stop=False), R=[Zb.b, FI.b], W=[ps.b], inc=False)
            fw.op(fw.PE, lambda: nc.tensor.matmul(ps.t[0:NK1, 0:256], lhsT=Zb.t[:, c, 1, :], rhs=FI.t[:, 1, :], start=False, stop=True), R=[Zb.b, FI.b], W=[ps.b])
            src = ps.t[0:NK1, 0:256].rearrange("p (a n) -> p a n", a=2)
            if c % 2 == 0:
                fw.op(fw.ACT, lambda: nc.scalar.copy(out=Tb_ap[:, :, :, c], in_=src), R=[ps.b], W=[Yb.b])
            else:
                fw.op(V, lambda: nc.vector.tensor_copy(out=Tb_ap[:, :, :, c], in_=src), R=[ps.b], W=[Yb.b])
        if cx.hy_stop <= 3:
            return
        cs = slice(o * 256 + g * HC, o * 256 + (g + 1) * HC)
        for nb in range(8):
            ps = cx.ps[4 + hy["pj"] % 4]
            hy["pj"] += 1
            for j in range(16):
                n2 = nb * 16 + j
                oo = ps.t[0:64, j * HC:(j + 1) * HC]
                fw.op(fw.PE, lambda: nc.tensor.matmul(oo, lhsT=Q.t[:, n2, 0, :], rhs=Tb_ap[:, 0, n2, :], start=True, stop=False), R=[Q.b, Yb.b], W=[ps.b], inc=False)
                fw.op(fw.PE, lambda: nc.tensor.matmul(oo, lhsT=Q.t[:, n2, 1, :], rhs=Tb_ap[:, 1, n2, :], start=False, stop=True), R=[Q.b, Yb.b], W=[ps.b], inc=(j == 15))
            pv = ps.t[0:64, 0:16 * HC].rearrange("p (n c) -> p n c", c=HC)
            ns = slice(nb * 16, (nb + 1) * 16)
            a_, b_ = e1[nb % 2], e2[nb % 2]
            av = a_.t[:].rearrange("p (n c) -> p n c", c=HC)
            bv = b_.t[:].rearrange("p (n c) -> p n c", c=HC)
            fw.op(V, lambda: nc.vector.tensor_tensor(out=av, in0=pv, in1=RN.t[:, cs].unsqueeze(1).broadcast_to([64, 16, HC]), op=ALU.mult), R=[ps.b, RN.b], W=[a_.b])
            fw.op(fw.POOL, lambda: nc.gpsimd.tensor_tensor(out=bv, in0=v_f.t[:, :, ns].rearrange("p c n -> p n c"), in1=SK.t[:, cs].unsqueeze(1).broadcast_to([64, 16, HC]), op=ALU.mult),
                  R=[v_f.b, SK.b], W=[b_.b])
            fw.op(V, lambda: nc.vector.tensor_tensor(out=av, in0=av, in1=bv, op=ALU.add), R=[a_.b, b_.b], W=[a_.b])
            fw.op(V, lambda: nc.vector.tensor_tensor(out=out_f32.t[:, :, ns].rearrange("p c n -> p n c"), in0=av, in1=gate.t[:, :, ns].rearrange("p c n -> p n c"), op=ALU.mult),
                  R=[a_.b, gate.b], W=[out_f32.b])
            if out_bf is not None:
                fw.op(fw.ACT, lambda: nc.scalar.copy(out=out_bf.t[:, :, ns], in_=out_f32.t[:, :, ns]), R=[out_f32.b], W=[out_bf.b])

    for g_ in range(cx.hy_lim):
        g = cx.hy_g0 if cx.hy_same else g_
        ld(Vf, g * HC)
        ld(X1, 256 + g * HC)
        fw.op(fw.ACT, lambda: nc.scalar.copy(out=Ub.t[:], in_=Vf.t[:]), R=[Vf.b], W=[Ub.b])
        conv(g, 0, Ub, Vf, X1, G1, U2)
        ld(X1, 512 + g * HC)
        conv(g, 1, U2, G1, X1, Vf, None)
        dst = sc["YHY"].t[g * HC:(g + 1) * HC, :].rearrange("c (a b) -> a c b", b=128)
        for i in range(2):
            fw.dma(fw.GD, out=dst[:, i * 16:(i + 1) * 16, :], in_=Vf.t[:, i * 16:(i + 1) * 16, :], R=[Vf.b], W=sc["YHY"].cb)


def phase_f1(cx, W, li):
    fw, nc = cx.fw, cx.nc
    new_phase(cx)
    sc = cx.sc
    cx.wstg = [SB(cx, [128, 1024], F32, "wstg") for _ in range(2)]
    gb = SB(cx, [128, 8], F32, "gb")
    fw.dma(fw.SP, out=gb.t[:], in_=W["g_branch"], W=[gb.b])
    wout = SB(cx, [128, 8, D], BF16, "wout")
    load_w_bf16(cx, wout, W["w_out"], 8, D, gain=gb)
    ys = [SB(cx, [128, 8, CH], F32, "ys") for _ in range(2)]
    hts = [SB(cx, [128, 8, CH], F32, "ht") for _ in range(2)]
    mixT = SB(cx, [128, 8, CH], BF16, "mixT")
    k = 0
    srcs = ["YHY", "YS5", "YDA", "YMLA"]
    for c in range(NCH):
        tsl = slice(c * CH, (c + 1) * CH)
        y, ht = ys[c % 2], hts[c % 2]
        for bi, nm in enumerate(srcs):
            fw.dma(fw.SP, out=y.t[:, 2 * bi:2 * bi + 2, :], in_=sc[nm].t[:, tsl].rearrange("(j p) t -> p j t", p=128), R=[sc[nm].cb[c]], W=[y.b])
        fw.dma(fw.SP, out=ht.t[:], in_=cx.HT.t[:, tsl].rearrange("(j p) t -> p j t", p=128), R=[cx.HT.cb[c]], W=[ht.b])
        for bi in range(4):
            if bi == 2:
                fw.op(fw.ACT, lambda: nc.scalar.copy(out=mixT.t[:, 4:6, :], in_=y.t[:, 4:6, :]), R=[y.b], W=[mixT.b])
                continue
            r = rstd_chunk(cx, [(y.t[:, 2 * bi + jj, :], 128, [y.b]) for jj in range(2)], 256)
            fw.op(fw.DVE, lambda: nc.vector.tensor_tensor(out=mixT.t[:, 2 * bi:2 * bi + 2, :], in0=y.t[:, 2 * bi:2 * bi + 2, :],
                                                          in1=r.t[:].unsqueeze(1).broadcast_to([128, 2, CH]), op=ALU.mult), R=[y.b, r.b], W=[mixT.b])
        for m in range(8):
            ps = cx.ps[2 + (k % 6)]
            k += 1
            for j in range(8):
                fw.op(fw.PE, lambda: nc.tensor.matmul(ps.t[:, :], lhsT=wout.t[:, j, m * 128:(m + 1) * 128], rhs=mixT.t[:, j, :], start=(j == 0), stop=(j == 7)),
                      R=[wout.b, mixT.b], W=[ps.b], inc=(j == 7))
            fw.op(fw.DVE, lambda: nc.vector.tensor_tensor(out=ht.t[:, m, :], in0=ht.t[:, m, :], in1=ps.t[:, :], op=ALU.add), R=[ht.b, ps.b], W=[ht.b])
        fw.dma(fw.GD, out=cx.HT.t[:, tsl].rearrange("(j p) t -> p j t", p=128), in_=ht.t[:], R=[ht.b], W=[cx.HT.cb[c]])


def phase_f2(cx, W, li):
    fw, nc = cx.fw, cx.nc
    new_phase(cx)
    n = 256
    nchunks = L // n
    cx.wstg = [SB(cx, [128, 1024], F32, "wstg") for _ in range(2)]
    gf = SB(cx, [128, 8], F32, "gf")
    fw.dma(fw.SP, out=gf.t[:], in_=W["g_ffn"], W=[gf.b])
    wg = SB(cx, [128, 8, FFN], BF16, "wg")
    wu = SB(cx, [128, 8, FFN], BF16, "wu")
    wd = SB(cx, [128, 22, D], BF16, "wd")
    kq = 0
    for wt, src in [(wg, W["w_gate"]), (wu, W["w_up"])]:
        for j in range(8):
            for c0 in range(0, FFN, 1024):
                cn = min(1024, FFN - c0)
                stg = cx.wstg[kq % 2]
                fw.dma(fw.SP, out=stg.t[:, 0:cn], in_=src[j * 128:(j + 1) * 128, c0:c0 + cn], W=[stg.b])
                if kq % 2 == 0:
                    fw.op(fw.ACT, lambda: nc.scalar.mul(out=wt.t[:, j, c0:c0 + cn], in_=stg.t[:, 0:cn], mul=gf.t[:, j:j + 1]), R=[stg.b, gf.b], W=[wt.b])
                else:
                    fw.op(fw.DVE, lambda: nc.vector.tensor_scalar(out=wt.t[:, j, c0:c0 + cn], in0=stg.t[:, 0:cn], scalar1=gf.t[:, j:j + 1], scalar2=None, op0=ALU.mult),
                          R=[stg.b, gf.b], W=[wt.b])
                kq += 1
    for i in range(22):
        stg = cx.wstg[kq % 2]
        fw.dma(fw.SP, out=stg.t[:, :], in_=W["w_down"][i * 128:(i + 1) * 128, :], W=[stg.b])
        evac(cx, kq, wd.t[:, i, :], stg.t[:, :], R=[stg.b], W=[wd.b])
        kq += 1
    hts = [SB(cx, [128, 8, n], F32, "ht") for _ in range(2)]
    fT = SB(cx, [128, 8, n], BF16, "fT")
    act = SB(cx, [128, 22, n], BF16, "act")
    sg = [SB(cx, [128, n], F32, "sg") for _ in range(2)]
    k = 0
    for c in range(nchunks):
        tsl = slice(c * n, (c + 1) * n)
        cb = cx.HT.cb[c // 2]
        ht = hts[c % 2]
        fw.dma(fw.SP, out=ht.t[:], in_=cx.HT.t[:, tsl].rearrange("(j p) t -> p j t", p=128), R=[cb], W=[ht.b])
        r = rstd_chunk(cx, [(ht.t[:, j, :], 128, [ht.b]) for j in range(8)], D, n=n)
        fw.op(fw.DVE, lambda: nc.vector.tensor_tensor(out=fT.t[:], in0=ht.t[:], in1=r.t[:, 0:n].unsqueeze(1).broadcast_to([128, 8, n]), op=ALU.mult),
              R=[ht.b, r.b], W=[fT.b])
        for i in range(22):
            ps = cx.ps[2 + (k % 6)]
            k += 1
            for half, wt in enumerate([wg, wu]):
                for j in range(8):
                    fw.op(fw.PE, lambda: nc.tensor.matmul(ps.t[:, half * n:(half + 1) * n], lhsT=wt.t[:, j, i * 128:(i + 1) * 128], rhs=fT.t[:, j, :],
                                                          start=(j == 0), stop=(j == 7)), R=[wt.b, fT.b], W=[ps.b], inc=(j == 7))
            s_ = sg[i % 2]
            fw.op(fw.ACT, lambda: nc.scalar.activation(out=s_.t[:], in_=ps.t[:, 0:n], func=AF.Silu), R=[ps.b], W=[s_.b])
            fw.op(fw.DVE, lambda: nc.vector.tensor_tensor(out=act.t[:, i, :], in0=s_.t[:], in1=ps.t[:, n:2 * n], op=ALU.mult), R=[s_.b, ps.b], W=[act.b])
        for m in range(8):
            ps = cx.ps[2 + (k % 6)]
            k += 1
            for i in range(22):
                fw.op(fw.PE, lambda: nc.tensor.matmul(ps.t[:, 0:n], lhsT=wd.t[:, i, m * 128:(m + 1) * 128], rhs=act.t[:, i, :], start=(i == 0), stop=(i == 21)),
                      R=[wd.b, act.b], W=[ps.b], inc=(i == 21))
            fw.op(fw.DVE, lambda: nc.vector.tensor_tensor(out=ht.t[:, m, :], in0=ht.t[:, m, :], in1=ps.t[:, 0:n], op=ALU.add), R=[ht.b, ps.b], W=[ht.b])
        fw.dma(fw.GD, out=cx.HT.t[:, tsl].rearrange("(j p) t -> p j t", p=128), in_=ht.t[:], R=[ht.b], W=[cb])


def phase_f3(cx, W, li, p_ap):
    fw, nc = cx.fw, cx.nc
    new_phase(cx)
    cx.wstg = [SB(cx, [128, 1024], F32, "wstg") for _ in range(2)]
    gp = SB(cx, [128, 8], F32, "gp")
    fw.dma(fw.SP, out=gp.t[:], in_=W["g_ple"], W=[gp.b])
    wpg = SB(cx, [128, 8, D], BF16, "wpg")
    load_w_bf16(cx, wpg, W["w_ple_gate"], 8, D, gain=gp)
    wpl = SB(cx, [128, 2, D], BF16, "wpl")
    load_w_bf16(cx, wpl, W["w_ple"], 2, D)
    hts = [SB(cx, [128, 8, CH], F32, "ht") for _ in range(2)]
    pcs = [SB(cx, [128, 4, 256], F32, "pc") for _ in range(2)]
    fT = SB(cx, [128, 8, CH], BF16, "fT")
    pT = SB(cx, [128, 2, CH], BF16, "pT")
    sg = [SB(cx, [128, CH], F32, "sg") for _ in range(2)]
    k = 0
    for c in range(NCH):
        tsl = slice(c * CH, (c + 1) * CH)
        ht, pc = hts[c % 2], pcs[c % 2]
        fw.dma(fw.SP, out=ht.t[:], in_=cx.HT.t[:, tsl].rearrange("(j p) t -> p j t", p=128), R=[cx.HT.cb[c]], W=[ht.b])
        fw.dma(fw.SP, out=pc.t[:], in_=p_ap[li, tsl, :].rearrange("(s p) d -> p s d", p=128), W=[pc.b])
        r = rstd_chunk(cx, [(ht.t[:, j, :], 128, [ht.b]) for j in range(8)], D)
        fw.op(fw.DVE, lambda: nc.vector.tensor_tensor(out=fT.t[:], in0=ht.t[:], in1=r.t[:].unsqueeze(1).broadcast_to([128, 8, CH]), op=ALU.mult),
              R=[ht.b, r.b], W=[fT.b])
        for ct in range(2):
            ps = cx.ps[2 + (k % 6)]
            k += 1
            for s in range(4):
                fw.op(fw.PE, lambda: nc.tensor.transpose(out=ps.t[:, s * 128:(s + 1) * 128], in_=pc.t[:, s, ct * 128:(ct + 1) * 128], identity=cx.ident_f.t[:]),
                      R=[pc.b, cx.ident_f.b], W=[ps.b], inc=(s == 3))
            evac(cx, k, pT.t[:, ct, :], ps.t[:], R=[ps.b], W=[pT.b])
        for m in range(8):
            ps = cx.ps[2 + (k % 6)]
            k += 1
            for j in range(8):
                fw.op(fw.PE, lambda: nc.tensor.matmul(ps.t[:, :], lhsT=wpg.t[:, j, m * 128:(m + 1) * 128], rhs=fT.t[:, j, :], start=(j == 0), stop=(j == 7)),
                      R=[wpg.b, fT.b], W=[ps.b], inc=(j == 7))
            ps2 = cx.ps[2 + (k % 6)]
            k += 1
            for ct in range(2):
                fw.op(fw.PE, lambda: nc.tensor.matmul(ps2.t[:, :], lhsT=wpl.t[:, ct, m * 128:(m + 1) * 128], rhs=pT.t[:, ct, :], start=(ct == 0), stop=(ct == 1)),
                      R=[wpl.b, pT.b], W=[ps2.b], inc=(ct == 1))
            s_ = sg[m % 2]
            fw.op(fw.ACT, lambda: nc.scalar.activation(out=s_.t[:], in_=ps.t[:], func=AF.Sigmoid), R=[ps.b], W=[s_.b])
            fw.op(fw.DVE, lambda: nc.vector.tensor_tensor(out=s_.t[:], in0=s_.t[:], in1=ps2.t[:], op=ALU.mult), R=[s_.b, ps2.b], W=[s_.b])
            fw.op(fw.DVE, lambda: nc.vector.tensor_tensor(out=ht.t[:, m, :], in0=ht.t[:, m, :], in1=s_.t[:], op=ALU.add), R=[ht.b, s_.b], W=[ht.b])
        fw.dma(fw.GD, out=cx.HT.t[:, tsl].rearrange("(j p) t -> p j t", p=128), in_=ht.t[:], R=[ht.b], W=[cx.HT.cb[c]])


def host_consts():
    C = {}
    C["ident_f"] = np.eye(128, dtype=np.float32)
    C["ones_f"] = np.ones((128, 128), dtype=np.float32)
    inv = 10000.0 ** (-np.arange(0, 32, 2, dtype=np.float32) / 32)
    ang = (np.arange(L, dtype=np.float32)[:, None] * inv[None, :]).astype(np.float32)
    cos, sin = np.cos(ang).T.astype(np.float32), np.sin(ang).T.astype(np.float32)
    sq = np.float32(96 ** -0.5)
    C["cs_q"] = np.stack([np.tile(cos, (4, 1)) * sq, np.tile(sin, (4, 1)) * sq]).astype(np.float32)
    C["cs_k"] = np.stack([cos, sin]).astype(np.float32)
    sel = np.zeros((65, 64), np.float32)
    sel[64, :] = 1.0
    C["sel"] = sel
    bf = ml_dtypes.bfloat16
    N = 2 * L
    n1 = np.arange(128, dtype=np.float64)
    k1 = np.arange(NK1, dtype=np.float64)
    a1 = 2 * np.pi * np.outer(n1, k1) / 128.0
    Fr, Fi = np.cos(a1), -np.sin(a1)
    C["fftF4"] = np.concatenate([Fr, Fi, -Fi, Fr], axis=1).astype(bf)
    n2 = np.arange(128, dtype=np.float64)
    k2 = np.arange(128, dtype=np.float64)
    kk = k1[None, :, None] + 128.0 * k2[None, None, :]
    ag = 2 * np.pi * ((n2[:, None, None] * kk) % N) / N
    C["fftG"] = np.stack([np.cos(ag), -np.sin(ag)], axis=2).astype(bf)
    ai_ = 2 * np.pi * np.outer(k2, n2) / 128.0
    Fc, Fs = np.cos(ai_), np.sin(ai_)
    C["fftFI"] = np.stack([np.concatenate([Fc, Fs], axis=1), np.concatenate([-Fs, Fc], axis=1)], axis=1).astype(bf)
    wk = np.where((k1 == 0) | (k1 == 64), 1.0, 2.0) / N
    nn = 128.0 * np.arange(64, dtype=np.float64)[None, None, :] + n2[None, :, None]
    aq = 2 * np.pi * ((k1[:, None, None] * nn) % N) / N
    C["fftQ"] = np.stack([np.cos(aq) * wk[:, None, None], -np.sin(aq) * wk[:, None, None]], axis=2).astype(bf)
    n = np.arange(N)
    pos = np.where(n < L, n, N - n).astype(np.float64)
    t = (pos.astype(np.float32) / np.float32(L)).astype(np.float32)
    bands = np.arange(1, 17, dtype=np.float32)
    ang2 = (np.float32(2.0 * math.pi) * t[:, None] * bands[None, :]).astype(np.float32)
    C["featsT"] = np.ascontiguousarray(np.concatenate([t[:, None], np.cos(ang2), np.sin(ang2)], axis=-1).T.astype(np.float32))
    C["tpos"] = np.ascontiguousarray(t.reshape(128, 128))
    C["tau"] = np.tile(np.arange(512, dtype=np.float32)[None, :], (128, 1))
    slopes = [2.0 ** (-8.0 * (h + 1) / 4) for h in range(4)]
    p = np.arange(128, dtype=np.float64)[:, None]
    dl = np.arange(-63, 64, dtype=np.float64)[None, :]
    f = np.arange(512, dtype=np.float64)
    bt, dg, flr = [], [], []
    for sl in slopes:
        left = sl * (p + 128 * dl)
        right = -sl * (p + 128 * dl - 511)
        bt.append(np.where(dl < 0, left, np.where(dl >= 4, right, 0.0)))
        dg.append(np.stack([-sl * np.abs(f[None, :] - 128 * d_ - p) for d_ in range(4)]))
        flr.append(np.stack([np.tile(np.exp(-sl * f)[None, :], (65, 1)), np.tile(np.exp(-sl * (511 - f))[None, :], (65, 1))]))
    C["da_bt"] = np.stack(bt).astype(np.float32)
    C["da_dg"] = np.stack(dg).astype(np.float32)
    C["da_flr"] = np.stack(flr).astype(np.float32)
    return C


PER_LAYER = ["g_mix", "w_in", "w_out", "g_branch", "hy_conv_w", "hy_conv_b", "hy_f_w1", "hy_f_b1", "hy_f_w2", "hy_f_b2", "hy_f_w3",
             "hy_log_decay", "hy_skip", "s5_a_re", "s5_a_im", "s5_log_dt", "s5_b_re", "s5_b_im", "s5_c_re", "s5_c_im", "s5_d",
             "s5_w_glu", "s5_b_glu", "da_lambda", "da_g_head", "mla_g_q", "mla_g_kv", "mla_w_uq", "mla_w_ukv", "g_ffn", "w_gate",
             "w_up", "w_down", "g_ple", "w_ple_gate", "w_ple"]


def pj(v, kt):
    v = np.asarray(v, dtype=np.float32).reshape(-1)
    out = np.zeros((kt * 128,), np.float32)
    out[:v.shape[0]] = v
    return np.ascontiguousarray(out.reshape(kt, 128).T)


def host_layout(inputs, nl):
    Wd = {}
    for li in range(nl):
        g = lambda n: np.asarray(inputs[n][li], dtype=np.float32)
        Wd[f"g_mix__{li}"] = pj(g("g_mix"), 8)
        Wd[f"w_in__{li}"] = g("w_in")
        Wd[f"mla_g_q__{li}"] = pj(g("mla_g_q"), 2)
        Wd[f"mla_g_kv__{li}"] = pj(g("mla_g_kv"), 1)
        Wd[f"mla_w_uq__{li}"] = g("mla_w_uq")
        Wd[f"mla_w_ukv__{li}"] = g("mla_w_ukv")
        Wd[f"da_lambda__{li}"] = g("da_lambda")
        def s5l(a):
            a = np.asarray(a, np.float32)
            rest = a.shape[2:]
            a = a.reshape((2, 8, 128) + rest)
            a = np.moveaxis(a, 2, 0)
            return np.ascontiguousarray(a.reshape((128, 16) + rest))
        Wd[f"s5_ar__{li}"] = s5l(g("s5_a_re").reshape(2, 1024))
        Wd[f"s5_ai__{li}"] = s5l(g("s5_a_im").reshape(2, 1024))
        Wd[f"s5_ldt__{li}"] = s5l(np.repeat(g("s5_log_dt"), 64, axis=1))
        Wd[f"s5_b__{li}"] = s5l(np.stack([g("s5_b_re").reshape(2, 1024, 16), g("s5_b_im").reshape(2, 1024, 16)], axis=2))
        cre = np.transpose(g("s5_c_re"), (0, 1, 3, 2)).reshape(2, 1024, 16)
        cim = np.transpose(g("s5_c_im"), (0, 1, 3, 2)).reshape(2, 1024, 16)
        Wd[f"s5_c__{li}"] = s5l(np.stack([cre, cim], axis=2))
        Wd[f"s5_d__{li}"] = np.ascontiguousarray(g("s5_d").reshape(8, 32).T)
        Wd[f"s5_w_glu__{li}"] = g("s5_w_glu")
        Wd[f"s5_b_glu__{li}"] = pj(g("s5_b_glu"), 2)
        Wd[f"hy_f_w1__{li}"] = g("hy_f_w1")
        Wd[f"hy_f_w2__{li}"] = g("hy_f_w2")
        Wd[f"hy_f_b12__{li}"] = np.ascontiguousarray(np.stack([g("hy_f_b1"), g("hy_f_b2")], axis=1))
        Wd[f"hy_f_w3__{li}"] = g("hy_f_w3")
        Wd[f"hy_log_decay__{li}"] = g("hy_log_decay")
        Wd[f"hy_skip__{li}"] = g("hy_skip")
        cwb = np.concatenate([g("hy_conv_w"), g("hy_conv_b")[None, :]], axis=0)
        Wd[f"hy_conv__{li}"] = np.ascontiguousarray(cwb.T.reshape(6, 128, 4).transpose(1, 0, 2))
        gbr = g("g_branch")
        Wd[f"g_branch__{li}"] = pj(np.concatenate([gbr[0], gbr[1], np.ones(256, np.float32), gbr[2]]), 8)
        Wd[f"w_out__{li}"] = g("w_out")
        Wd[f"g_ffn__{li}"] = pj(g("g_ffn"), 8)
        Wd[f"w_gate__{li}"] = g("w_gate")
        Wd[f"w_up__{li}"] = g("w_up")
        Wd[f"w_down__{li}"] = g("w_down")
        Wd[f"g_ple__{li}"] = pj(g("g_ple"), 8)
        Wd[f"w_ple_gate__{li}"] = g("w_ple_gate")
        Wd[f"w_ple__{li}"] = g("w_ple")
        Wd[f"da_g_head__{li}"] = g("da_g_head").reshape(64, 1)
    Wd["g_final"] = pj(inputs["g_final"], 8)
    return Wd


def build(nl, Wd_shapes, C_shapes, stages, dbg=()):
    nc = bass.Bass("TRN2", target_bir_lowering=False)
    cx = Ctx()
    cx.nc = nc
    x_ap = nc.dram_tensor("x", [L, D], F32, kind="ExternalInput").ap()
    p_ap = nc.dram_tensor("p", [nl, L, 256], F32, kind="ExternalInput").ap()
    out_ap = nc.dram_tensor("out", [L, D], F32, kind="ExternalOutput").ap()
    WA = {k: nc.dram_tensor(k, list(shp), F32, kind="ExternalInput").ap() for k, shp in Wd_shapes.items()}
    cx.C = {k: nc.dram_tensor("c_" + k, list(shp), dt, kind="ExternalInput").ap() for k, (shp, dt) in C_shapes.items()}
    with ExitStack() as es:
        cx.es = es
        cx.phase_es = None
        cx.tcount = 0
        cx.fw = fw = FW(nc, es)
        cx.ps = [Tl(es.enter_context(nc.psum_tensor(f"ps{i}", [128, 512], F32)), f"ps{i}") for i in range(8)]
        P = lambda n, shp, dt: Tl(es.enter_context(nc.sbuf_tensor(n, shp, dt)), n)
        cx.ones_f = P("ones_f", [128, 128], F32)
        cx.ident_f = P("ident_f", [128, 128], F32)
        cx.eps_t = P("eps_t", [128, 1], F32)
        cx.sq = [P(f"sq{i}", [128, CH], F32) for i in range(2)]
        cx.rstd = [P(f"rstd{i}", [128, CH], F32) for i in range(2)]
        cx.rstd_i = 0
        fw.dma(fw.SP, out=cx.ones_f.t[:], in_=cx.C["ones_f"], W=[cx.ones_f.b])
        fw.dma(fw.SP, out=cx.ident_f.t[:], in_=cx.C["ident_f"], W=[cx.ident_f.b])
        fw.op(fw.DVE, lambda: nc.vector.memset(cx.eps_t.t[:], EPS), W=[cx.eps_t.b])
        cx.OUTB = [Buf() for _ in range(NCH)]
        cx.HT = dram(cx, "HT", [D, L], F32)
        sc = cx.sc = {"HT": cx.HT}
        sc["ZHY"] = dram(cx, "ZHY", [768, L], F32)
        sc["ZS5"] = dram(cx, "ZS5", [256, L], F32)
        sc["QDA"] = dram(cx, "QDA", [256, L], BF16)
        sc["KDA"] = dram(cx, "KDA", [256, L], BF16)
        sc["VDA"] = dram(cx, "VDA", [L, 256], BF16)
        sc["QMLA"] = dram(cx, "QMLA", [4, 96, L], BF16)
        sc["KMLA"] = dram(cx, "KMLA", [4, 96, L], BF16)
        sc["VMLA"] = dram(cx, "VMLA", [L, 256], BF16)
        sc["YMLA"] = dram(cx, "YMLA", [256, L], F32)
        sc["YHY"] = dram(cx, "YHY", [256, L], F32)
        sc["YS5"] = dram(cx, "YS5", [256, L], F32)
        sc["GS5"] = dram(cx, "GS5", [256, L], F32)
        sc["ZC"] = dram(cx, "ZC", [768, L], F32)
        sc["HSPEC"] = dram(cx, "HSPEC", [16, 128, NK1 * 2 * HC], BF16)
        sc["RN"] = dram(cx, "RN", [128, 512], F32)
        sc["YDA"] = dram(cx, "YDA", [256, L], F32)
        cx.slopes = [2.0 ** (-8.0 * (h + 1) / 4) for h in range(4)]
        cx.window = 70.0
        cx.dbgt = 'dbgt' in stages
        cx.hy_sc_only = 'hysc' in stages
        cx.hy_lim = 8
        cx.hy_stop = 9
        cx.hy_same = 'hysame' in stages
        cx.hy_g0 = 1 if 'hyg1' in stages else 0
        for st_ in stages:
            if st_.startswith('hystop'):
                cx.hy_stop = int(st_[6:])
                cx.hy_lim = 1
            if st_.startswith('hylim'):
                cx.hy_lim = int(st_[5:])
        if "x0" in stages:
            phase_x0(cx, x_ap)
        for li in range(nl):
            W = {k.rsplit("__", 1)[0]: v for k, v in WA.items() if k.endswith(f"__{li}")}
            if "a" in stages:
                phase_a(cx, W, li)
            if "mlas5" in stages:
                phase_mla_s5(cx, W, li)
            if "mla" in stages:
                phase_mla(cx)
            if "da" in stages:
                phase_da(cx, W, li)
            if "s5" in stages:
                phase_s5(cx, W, li)
            if "hyf" in stages:
                phase_hy_filter(cx, W, li)
            if "hy" in stages:
                phase_hy(cx, W, li)
            if "inj" in stages:
                new_phase(cx)
                for nm in ["YHY", "YS5"]:
                    src = nc.dram_tensor("inj_" + nm, [256, L], F32, kind="ExternalInput").ap()
                    fw.dma(fw.SP, out=sc[nm].t, in_=src)
            if "f1" in stages:
                phase_f1(cx, W, li)
            if "f2" in stages:
                phase_f2(cx, W, li)
            if "f3" in stages:
                phase_f3(cx, W, li, p_ap)
        if "out" in stages:
            phase_out(cx, WA["g_final"], out_ap)
        new_phase(cx)
        for name in dbg:
            src = sc[name].t
            dst = nc.dram_tensor("dbg_" + name, list(src.shape), src.dtype, kind="ExternalOutput").ap()
            fw.dma(fw.SP, out=dst, in_=src)
        fw.final_wait()
        fw.check_deadlock()
        cx.phase_es.close()
    print("instructions:", fw.n_inst)
    return nc


ALL_STAGES = ["x0", "a", "mlas5", "da", "hyf", "hy", "f1", "f2", "f3", "out"]
_CACHE = {}


def kernel(**inputs):
    nl = 4
    nb = 8
    inputs = {k: np.asarray(v) for k, v in inputs.items()}
    Wd = host_layout(inputs, nl)
    C = host_consts()
    key = "prog"
    if key not in _CACHE:
        _CACHE[key] = build(nl, {k: v.shape for k, v in Wd.items()},
                            {k: (v.shape, BF16 if v.dtype == ml_dtypes.bfloat16 else F32) for k, v in C.items()}, ALL_STAGES)
    nc = _CACHE[key]
    x = np.asarray(inputs["x"], dtype=np.float32)
    p = np.asarray(inputs["p"], dtype=np.float32)
    in_maps = []
    for b in range(nb):
        m = {"x": np.ascontiguousarray(x[b]), "p": np.ascontiguousarray(p[:, b])}
        m.update(Wd)
        m.update({"c_" + k: v for k, v in C.items()})
        in_maps.append(m)
    res = run_bass_kernel_spmd(nc, in_maps, core_ids=list(range(nb)))
    return np.stack([np.asarray(res.results[b]["out"], dtype=np.float32) for b in range(nb)], axis=0)
```
